# Optimizing a Trainium2 kernel written in Bass

```python
import math
import jax, jax.numpy as jnp
from jax import lax
import numpy as np

D_MODEL = 1024
BATCH = 8
SEQ = 4096
DEPTH = 2

D_MIX = D_MODEL
MLA_HEADS = 8
QK_NOPE_DIM = 64
QK_ROPE_DIM = 32
V_HEAD_DIM = 64
Q_LORA_RANK = 256
KV_LORA_RANK = 128
MLA_WIDTH = MLA_HEADS * V_HEAD_DIM
ROPE_THETA = 10000.0
Q_BLOCK = 128
SSM_WIDTH = D_MIX - MLA_WIDTH
SSM_GROUP = 16
SSM_GROUPS = SSM_WIDTH // SSM_GROUP
SSM_STATE = 64
DT_MIN = 0.001
DT_MAX = 0.1
IN_WIDTH = Q_LORA_RANK + KV_LORA_RANK + QK_ROPE_DIM + SSM_WIDTH
MEM_LEN = 256
X_HEADS = 4
X_HEAD_DIM = D_MODEL // X_HEADS
D_FF = -(-8 * D_MODEL // (3 * 256)) * 256
EPS = 1e-6

kernel_name = 'hybrid_mla_s5_memory_decoder'


def rmsnorm(x, g):
    xf = x.astype(jnp.float32)
    y = xf * lax.rsqrt(jnp.mean(xf * xf, axis=-1, keepdims=True) + EPS)
    return (y * g.astype(jnp.float32)).astype(x.dtype)


def apply_rope(x, cos, sin):
    xf = x.astype(jnp.float32)
    half = xf.shape[-1] // 2
    x1, x2 = xf[..., :half], xf[..., half:]
    return jnp.concatenate([x1 * cos - x2 * sin, x2 * cos + x1 * sin], axis=-1).astype(x.dtype)


def mla_group(c_q, c_kv, k_r, q_norm_g, w_uq, kv_norm_g, w_ukv, cos, sin):
    B, S, _ = c_q.shape
    q = (rmsnorm(c_q, q_norm_g) @ w_uq).reshape(B, S, MLA_HEADS, QK_NOPE_DIM + QK_ROPE_DIM)
    q_nope = q[..., :QK_NOPE_DIM]
    q_rope = apply_rope(q[..., QK_NOPE_DIM:], cos[:, :, None, :], sin[:, :, None, :])
    kv = (rmsnorm(c_kv, kv_norm_g) @ w_ukv).reshape(B, S, MLA_HEADS, QK_NOPE_DIM + V_HEAD_DIM)
    k_nope = kv[..., :QK_NOPE_DIM].transpose(0, 2, 1, 3)
    v = kv[..., QK_NOPE_DIM:].transpose(0, 2, 1, 3)
    k_rope = apply_rope(k_r, cos, sin)
    scale = (QK_NOPE_DIM + QK_ROPE_DIM) ** -0.5
    qn = (q_nope * scale).transpose(0, 2, 1, 3)
    qr = (q_rope * scale).transpose(0, 2, 1, 3)
    outs = []
    for i in range(S // Q_BLOCK):
        lo, hi = i * Q_BLOCK, (i + 1) * Q_BLOCK
        s = (jnp.einsum('bhqd,bhkd->bhqk', qn[:, :, lo:hi], k_nope[:, :, :hi])
             + jnp.einsum('bhqr,bkr->bhqk', qr[:, :, lo:hi], k_rope[:, :hi])).astype(jnp.float32)
        mask = jnp.arange(hi)[None, :] <= jnp.arange(lo, hi)[:, None]
        s = jnp.where(mask, s, -jnp.inf)
        p = jax.nn.softmax(s, axis=-1).astype(v.dtype)
        outs.append(jnp.einsum('bhqk,bhkd->bqhd', p, v[:, :, :hi]))
    o = jnp.concatenate(outs, axis=1)
    return o.reshape(B, S, MLA_WIDTH)


def s5_group(u, lam_re, lam_im, log_dt, b_re, b_im, c_re, c_im, d, w_glu, b_glu):
    B, S, _ = u.shape
    f32 = jnp.float32
    uf = u.astype(f32)
    ug = uf.reshape(B, S, SSM_GROUPS, SSM_GROUP)
    lam = lax.complex(lam_re.astype(f32), lam_im.astype(f32))
    dt = jnp.exp(log_dt.astype(f32))[:, None]
    a_bar = jnp.exp(lam * dt)
    b_mat = lax.complex(b_re.astype(f32), b_im.astype(f32))
    b_bar = ((a_bar - 1.0) / lam)[..., None] * b_mat
    bu = jnp.einsum('bsgc,gpc->bsgp', ug.astype(jnp.complex64), b_bar)
    a_elems = jnp.broadcast_to(a_bar, bu.shape)

    def combine(e1, e2):
        a1, x1 = e1
        a2, x2 = e2
        return a1 * a2, a2 * x1 + x2

    _, states = lax.associative_scan(combine, (a_elems, bu), axis=1)
    c_mat = lax.complex(c_re.astype(f32), c_im.astype(f32))
    y = jnp.einsum('bsgp,gcp->bsgc', states, c_mat).real.reshape(B, S, SSM_WIDTH)
    y = y + d.astype(f32) * uf
    g = jax.nn.gelu(y)
    y = y * jax.nn.sigmoid(g @ w_glu.astype(f32) + b_glu.astype(f32))
    return y.astype(u.dtype)


def memory_cross_attention(hn, memn, w_xq, w_xkv, w_xo):
    B, S, _ = hn.shape
    M = memn.shape[1]
    q = (hn @ w_xq).reshape(B, S, X_HEADS, X_HEAD_DIM)
    kv = (memn @ w_xkv).reshape(B, M, 2, X_HEADS, X_HEAD_DIM)
    k, v = kv[:, :, 0], kv[:, :, 1]
    s = jnp.einsum('bshd,bmhd->bhsm', q, k).astype(jnp.float32) * (X_HEAD_DIM ** -0.5)
    p = jax.nn.softmax(s, axis=-1).astype(v.dtype)
    o = jnp.einsum('bhsm,bmhd->bshd', p, v).reshape(B, S, D_MODEL)
    return o @ w_xo


def swiglu(hn, w_gate, w_up, w_down):
    return (jax.nn.silu(hn @ w_gate) * (hn @ w_up)) @ w_down


def setup_inputs(seed: int = 0) -> dict:
    key = jax.random.key(seed)
    ks = jax.random.split(key, 40)
    f32 = jnp.float32

    def nrm(k, shape, fan_in):
        return jax.random.normal(k, shape, f32) * (fan_in ** -0.5)

    def gain(k, shape):
        return 1.0 + 0.05 * jax.random.normal(k, shape, f32)

    L = DEPTH
    x = jax.random.normal(ks[0], (BATCH, SEQ, D_MODEL), f32)
    mem = jax.random.normal(ks[1], (BATCH, MEM_LEN, D_MODEL), f32)
    start = jax.random.randint(ks[2], (BATCH, 1), 0, 1024, dtype=jnp.int32)
    positions = start + jnp.arange(SEQ, dtype=jnp.int32)[None, :]
    n_idx = jnp.arange(SSM_STATE, dtype=f32)
    ssm_lambda_re = -0.5 * jnp.exp(0.05 * jax.random.normal(ks[9], (L, SSM_GROUPS, SSM_STATE), f32))
    ssm_lambda_im = jnp.pi * n_idx + 0.01 * jax.random.normal(ks[10], (L, SSM_GROUPS, SSM_STATE), f32)
    ssm_log_dt = jax.random.uniform(ks[11], (L, SSM_GROUPS), f32, math.log(DT_MIN), math.log(DT_MAX))
    return {
        'x': x,
        'mem': mem,
        'positions': positions,
        'norm_mix_g': gain(ks[3], (L, D_MODEL)),
        'w_in': nrm(ks[4], (L, D_MODEL, IN_WIDTH), D_MODEL),
        'q_norm_g': gain(ks[5], (L, Q_LORA_RANK)),
        'w_uq': nrm(ks[6], (L, Q_LORA_RANK, MLA_HEADS * (QK_NOPE_DIM + QK_ROPE_DIM)), Q_LORA_RANK),
        'kv_norm_g': gain(ks[7], (L, KV_LORA_RANK)),
        'w_ukv': nrm(ks[8], (L, KV_LORA_RANK, MLA_HEADS * (QK_NOPE_DIM + V_HEAD_DIM)), KV_LORA_RANK),
        'ssm_lambda_re': ssm_lambda_re,
        'ssm_lambda_im': ssm_lambda_im,
        'ssm_log_dt': ssm_log_dt,
        'ssm_b_re': nrm(ks[12], (L, SSM_GROUPS, SSM_STATE, SSM_GROUP), 2 * SSM_GROUP),
        'ssm_b_im': nrm(ks[13], (L, SSM_GROUPS, SSM_STATE, SSM_GROUP), 2 * SSM_GROUP),
        'ssm_c_re': nrm(ks[14], (L, SSM_GROUPS, SSM_GROUP, SSM_STATE), 2 * SSM_STATE),
        'ssm_c_im': nrm(ks[15], (L, SSM_GROUPS, SSM_GROUP, SSM_STATE), 2 * SSM_STATE),
        'ssm_d': jax.random.normal(ks[16], (L, SSM_WIDTH), f32),
        'ssm_w_glu': nrm(ks[17], (L, SSM_WIDTH, SSM_WIDTH), SSM_WIDTH),
        'ssm_b_glu': 0.01 * jax.random.normal(ks[18], (L, SSM_WIDTH), f32),
        'attn_out_g': gain(ks[19], (L, MLA_WIDTH)),
        'ssm_out_g': gain(ks[20], (L, SSM_WIDTH)),
        'w_out': nrm(ks[21], (L, D_MIX, D_MODEL), D_MIX),
        'norm_x_g': gain(ks[22], (L, D_MODEL)),
        'mem_norm_g': gain(ks[23], (L, D_MODEL)),
        'w_xq': nrm(ks[24], (L, D_MODEL, D_MODEL), D_MODEL),
        'w_xkv': nrm(ks[25], (L, D_MODEL, 2 * D_MODEL), D_MODEL),
        'w_xo': nrm(ks[26], (L, D_MODEL, D_MODEL), D_MODEL),
        'norm_ffn_g': gain(ks[27], (L, D_MODEL)),
        'w_gate': nrm(ks[28], (L, D_MODEL, D_FF), D_MODEL),
        'w_up': nrm(ks[29], (L, D_MODEL, D_FF), D_MODEL),
        'w_down': nrm(ks[30], (L, D_FF, D_MODEL), D_FF),
        'final_norm_g': gain(ks[31], (D_MODEL,)),
    }


def reference(x, mem, positions, norm_mix_g, w_in, q_norm_g, w_uq, kv_norm_g, w_ukv,
              ssm_lambda_re, ssm_lambda_im, ssm_log_dt, ssm_b_re, ssm_b_im, ssm_c_re, ssm_c_im,
              ssm_d, ssm_w_glu, ssm_b_glu, attn_out_g, ssm_out_g, w_out, norm_x_g, mem_norm_g,
              w_xq, w_xkv, w_xo, norm_ffn_g, w_gate, w_up, w_down, final_norm_g):
    freqs = ROPE_THETA ** (-jnp.arange(0, QK_ROPE_DIM, 2, dtype=jnp.float32) / QK_ROPE_DIM)
    ang = positions.astype(jnp.float32)[..., None] * freqs
    cos, sin = jnp.cos(ang), jnp.sin(ang)
    split_at = [Q_LORA_RANK, Q_LORA_RANK + KV_LORA_RANK, Q_LORA_RANK + KV_LORA_RANK + QK_ROPE_DIM]
    h = x
    for l in range(DEPTH):
        xn = rmsnorm(h, norm_mix_g[l])
        proj = xn @ w_in[l]
        c_q, c_kv, k_r, u = jnp.split(proj, split_at, axis=-1)
        a_out = mla_group(c_q, c_kv, k_r, q_norm_g[l], w_uq[l], kv_norm_g[l], w_ukv[l], cos, sin)
        s_out = s5_group(u, ssm_lambda_re[l], ssm_lambda_im[l], ssm_log_dt[l], ssm_b_re[l], ssm_b_im[l],
                         ssm_c_re[l], ssm_c_im[l], ssm_d[l], ssm_w_glu[l], ssm_b_glu[l])
        mixed = jnp.concatenate([rmsnorm(a_out, attn_out_g[l]), rmsnorm(s_out, ssm_out_g[l])], axis=-1)
        h = h + mixed @ w_out[l]
        h = h + memory_cross_attention(rmsnorm(h, norm_x_g[l]), rmsnorm(mem, mem_norm_g[l]),
                                       w_xq[l], w_xkv[l], w_xo[l])
        h = h + swiglu(rmsnorm(h, norm_ffn_g[l]), w_gate[l], w_up[l], w_down[l])
    return rmsnorm(h, final_norm_g)
```

```python
import math
import numpy as np
import concourse.bass as bass
import concourse.mybir as mybir
from concourse.bass_utils import run_bass_kernel_spmd

F32 = mybir.dt.float32
BF16 = mybir.dt.bfloat16
I32 = mybir.dt.int32
AF = mybir.ActivationFunctionType
ALU = mybir.AluOpType

ENGS = ["tensor", "vector", "scalar", "gpsimd", "sync"]

L = 2
S = 4096
D = 1024
NB = 8
EPS = 1e-6
DFF = 2816
NFF = 22
TWO_PI = 2.0 * math.pi
SIN_SCALE = 6.2831845
MAGIC = 12582912.0
ATT_SCALE = 96.0 ** -0.5
X_SCALE = 256.0 ** -0.5
NCOL = 55
C_GMIX, C_GX, C_GFFN, C_GMEM, C_GQ, C_GKV, C_GS, C_D, C_BGLU, C_GA = 0, 8, 16, 24, 32, 34, 35, 39, 43, 47


class Res:
    __slots__ = ("name", "last_w", "readers")

    def __init__(self, name=""):
        self.name = name
        self.last_w = None
        self.readers = []


class Op:
    __slots__ = ("eng", "fn", "waits", "dma", "sem", "val")


class Prog:
    def __init__(self, n_dma_sems=24):
        self.nc = bass.Bass("TRN2", target_bir_lowering=False)
        self.ops = {e: [] for e in ENGS}
        self.n_dma_sems = n_dma_sems
        self.wm = {e: {} for e in ENGS}
        self.dma_rr = {e: 0 for e in ENGS}
        self.dma_cnt = {}
        self.dma_last = {}
        self.eng_cnt = {}
        self.pending = {e: {} for e in ENGS}
        self._ctx = []

    def sbuf(self, name, shape, dtype):
        g = self.nc.sbuf_tensor("sb_" + name, list(shape), dtype)
        h = g.__enter__()
        self._ctx.append(g)
        return h

    def psum(self, name, shape, dtype):
        g = self.nc.psum_tensor("ps_" + name, list(shape), dtype)
        h = g.__enter__()
        self._ctx.append(g)
        return h

    def _need(self, op, dep):
        if dep is None:
            return
        if dep.eng == "tensor" and op.eng == "tensor" and not dep.dma and not op.dma:
            return
        if dep.val > op.waits.get(dep.sem, 0):
            op.waits[dep.sem] = dep.val

    def barrier(self):
        cur = {}
        for e, c in self.eng_cnt.items():
            cur[("eng", e)] = c
        for k, c in self.dma_cnt.items():
            cur[k] = 16 * c
        for e in ENGS:
            pe = self.pending[e]
            for k, v in cur.items():
                if v > pe.get(k, 0):
                    pe[k] = v

    def op(self, eng, fn, reads=(), writes=(), dma=False):
        o = Op()
        o.eng = eng
        o.fn = fn
        o.dma = dma
        o.waits = {}
        if self.pending[eng]:
            o.waits.update(self.pending[eng])
            self.pending[eng] = {}
        if dma:
            slot = self.dma_rr[eng] % self.n_dma_sems
            self.dma_rr[eng] += 1
            key = ("dma", eng, slot)
            prev = self.dma_last.get(key)
            cnt = self.dma_cnt.get(key, 0) + 1
            self.dma_cnt[key] = cnt
            o.sem = key
            o.val = 16 * cnt
            if prev is not None:
                self._need(o, prev)
            self.dma_last[key] = o
        else:
            o.sem = ("eng", eng)
            self.eng_cnt[eng] = self.eng_cnt.get(eng, 0) + 1
            o.val = self.eng_cnt[eng]
        for r in reads:
            self._need(o, r.last_w)
        for w in writes:
            self._need(o, w.last_w)
            for rd in w.readers:
                self._need(o, rd)
        for r in reads:
            r.readers.append(o)
        for w in writes:
            w.last_w = o
            w.readers = []
        wm = self.wm[eng]
        for k in list(o.waits):
            if wm.get(k, 0) >= o.waits[k]:
                del o.waits[k]
            else:
                wm[k] = o.waits[k]
        self.ops[eng].append(o)
        return o

    def build(self, final_waits=()):
        nc = self.nc
        sems = {}
        for e in ENGS:
            for o in self.ops[e]:
                if o.sem not in sems:
                    g = nc.semaphore("s_" + "_".join(str(x) for x in o.sem))
                    sems[o.sem] = g.__enter__()
                    self._ctx.append(g)
        fin = {}
        for o in final_waits:
            fin[o.sem] = max(fin.get(o.sem, 0), o.val)
        with nc.Block() as block:
            def make(e):
                def body(engobj):
                    for o in self.ops[e]:
                        for k, v in o.waits.items():
                            engobj.wait_ge(sems[k], v)
                        ins = o.fn(engobj)
                        ins.then_inc(sems[o.sem], 16 if o.dma else 1)
                    if e == "sync":
                        for k, v in fin.items():
                            engobj.wait_ge(sems[k], v)
                return body
            for e in ENGS:
                if self.ops[e] or e == "sync":
                    getattr(block, e)(make(e))
        return nc


class Ring:
    def __init__(self, P, name, n, shape, dtype):
        self.bufs = [(P.sbuf(f"{name}{i}", shape, dtype), Res(f"{name}{i}")) for i in range(n)]
        self.i = 0

    def next(self):
        b = self.bufs[self.i % len(self.bufs)]
        self.i += 1
        return b


def build_program(debug=False, n_layers=L, stop_after=None):
    P = Prog()
    nc = P.nc

    def din(name, shape, dt=F32):
        return nc.dram_tensor(name, list(shape), dt, kind="ExternalInput").ap()

    def dscr(name, shape, dt):
        return nc.dram_tensor(name, list(shape), dt, kind="Internal").ap()

    x_d = din("x", [S, D])
    mem_d = din("mem", [256, D])
    pos_d = din("pos", [1, S], I32)
    cols_d = din("cols", [128, L, NCOL])
    consts_d = din("consts", [128, 2])
    mask16_d = din("mask16", [128, 128])
    fing_d = din("fing", [1, D])
    w_in_d = din("w_in", [L, D, 928])
    w_kr_d = din("w_kr", [L, D, 192])
    w_uq_d = din("w_uq", [L, 256, 768])
    w_uqs_d = din("w_uqs", [L, 256, 768])
    w_ukv_d = din("w_ukv", [L, 128, 1024])
    s5v_d = din("s5v", [L, 128, 3, 16])
    s5b_d = din("s5b", [L, 128, 2, 16, 16])
    s5c_d = din("s5c", [L, 128, 2, 16, 16])
    w_glu_d = din("w_glu", [L, 512, 512])
    w_out_d = din("w_out", [L, D, D])
    w_xq_d = din("w_xq", [L, D, D])
    w_xkv_d = din("w_xkv", [L, D, 2 * D])
    w_xo_d = din("w_xo", [L, D, D])
    w_gate_d = din("w_gate", [L, D, DFF])
    w_up_d = din("w_up", [L, D, DFF])
    w_down_d = din("w_down", [L, DFF, D])
    out_d = nc.dram_tensor("out", [S, D], F32, kind="ExternalOutput").ap()

    h_d = dscr("h_scr", [S, D], F32)
    h1_d = dscr("h1_scr", [S, D], F32)
    qT_d = dscr("qT_scr", [8, 96, S], BF16)
    kT_d = dscr("kT_scr", [8, 96, S], BF16)
    V_d = dscr("V_scr", [8, 128, 32, 65], BF16)
    uT_d = dscr("uT_scr", [128, 4, S], BF16)
    yT_d = dscr("yT_scr", [128, 4, S], F32)
    aT_d = dscr("aT_scr", [NB, 64, 8, 512], F32)
    sT_d = dscr("sT_scr", [NB, 128, 4, 512], BF16)
    cos_d = dscr("cos_scr", [32, S], F32)
    sin_d = dscr("sin_scr", [32, S], F32)
    r_hd = [Res() for _ in range(32)]
    r_h1d = [Res() for _ in range(32)]
    r_qd, r_kd, r_vd, r_ud, r_yd = Res(), Res(), Res(), Res(), Res()
    r_ad, r_sd, r_csd = Res(), Res(), Res()

    dbg = {}

    def dbg_out(name, shape, dt=F32):
        t = nc.dram_tensor("dbg_" + name, list(shape), dt, kind="ExternalOutput").ap()
        dbg[name] = t
        return t

    final_ops = []

    def VEC(fn, reads, writes):
        return P.op("vector", fn, reads, writes)

    def ACT(fn, reads, writes):
        return P.op("scalar", fn, reads, writes)

    def POOL(fn, reads, writes):
        return P.op("gpsimd", fn, reads, writes)

    def PE(fn, reads, writes):
        return P.op("tensor", fn, reads, writes)

    def DMA(out, in_, reads, writes, q="sync"):
        return P.op(q, lambda e: e.dma_start(out=out, in_=in_), reads, writes, dma=True)

    def DMAC(out, in_, reads, writes):
        return P.op("gpsimd", lambda e: e.dma_start(out=out, in_=in_), reads, writes, dma=True)

    banks = []
    for i in range(8):
        t = P.psum(f"bank{i}", [128, 512], F32)
        banks.append((t, t.bitcast(BF16), Res(f"bank{i}")))
    bank_rr = [0]

    def next_bank(pool=(0, 1, 2, 3, 4, 5, 6, 7)):
        i = pool[bank_rr[0] % len(pool)]
        bank_rr[0] += 1
        return banks[i]

    identf = P.sbuf("identf", [128, 128], F32)
    ident = P.sbuf("ident", [128, 128], BF16)
    onesf = P.sbuf("onesf", [128, 128], F32)
    sel65 = P.sbuf("sel65", [65, 64], F32)
    onesb = P.sbuf("onesb", [128, 128], BF16)
    fing = P.sbuf("fing", [128, D], F32)
    mask16 = P.sbuf("mask16", [128, 128], F32)
    cols = P.sbuf("cols", [128, L, NCOL], F32)
    consts = P.sbuf("consts", [128, 2], F32)
    r_const = Res("const")
    POOL(lambda e: e.memset(identf[:], 1.0), [], [r_const])
    POOL(lambda e: e.affine_select(out=identf[:], in_=identf[:], pattern=[[-1, 128]], compare_op=ALU.is_equal,
                                   fill=0.0, base=0, channel_multiplier=1), [r_const], [r_const])
    VEC(lambda e: e.tensor_copy(out=ident[:], in_=identf[:]), [r_const], [r_const])
    VEC(lambda e: e.memset(onesf[:], 1.0), [], [r_const])
    VEC(lambda e: e.memset(onesb[:], 1.0), [], [r_const])
    DMA(fing[:], fing_d[0:1, :].to_broadcast([128, D]), [], [r_const])
    VEC(lambda e: e.memset(sel65[:], 0.0), [], [r_const])
    VEC(lambda e: e.memset(sel65[64:65, :], 1.0), [r_const], [r_const])
    DMA(mask16[:], mask16_d[:, :], [], [r_const])
    DMA(cols[:], cols_d[:, :, :], [], [r_const])
    DMA(consts[:], consts_d[:, :], [], [r_const])

    def col(l, c, n=1, p0=0, p1=128):
        return cols[p0:p1, l, c:c + n]

    ARENA_F32 = 33792
    arena_f = P.sbuf("arena", [128, ARENA_F32], F32)
    arena_b = arena_f.bitcast(BF16)
    arena_i = arena_f.bitcast(I32)
    arena_off = [0]

    def areset():
        arena_off[0] = 0

    def aalloc(shape, dt, name=""):
        n = 1
        for s_ in shape[1:]:
            n *= s_
        esz = 2 if dt == BF16 else 4
        off = (arena_off[0] + 3) // 4 * 4
        arena_off[0] = off + n * esz
        assert arena_off[0] <= ARENA_F32 * 4, (name, arena_off[0])
        base = {BF16: arena_b, F32: arena_f, I32: arena_i}[dt]
        e0 = off // esz
        ap = base[0:shape[0], e0:e0 + n]
        if len(shape) == 3:
            ap = ap.rearrange("p (a b) -> p a b", b=shape[2])
        elif len(shape) == 4:
            ap = ap.rearrange("p (a b c) -> p a b c", b=shape[2], c=shape[3])
        elif len(shape) == 5:
            ap = ap.rearrange("p (a b c d) -> p a b c d", b=shape[2], c=shape[3], d=shape[4])
        return ap, Res(name)

    class ARing:
        def __init__(self, n, shape, dt, name=""):
            self.bufs = [aalloc(shape, dt, f"{name}{i}") for i in range(n)]
            self.i = 0

        def next(self):
            b = self.bufs[self.i % len(self.bufs)]
            self.i += 1
            return b

    ht_ring = Ring(P, "ht", 2, [128, D], F32)
    junk_ring = Ring(P, "junk", 1, [128, D], BF16)
    xn_ring = Ring(P, "xn", 2, [128, D], BF16)
    xnT_ring = Ring(P, "xnT", 2, [128, 8, 512], BF16)
    st_ring = Ring(P, "st", 8, [128, 4], F32)
    pT_ring = Ring(P, "pT", 4, [128, 512], BF16)
    frA = (P.sbuf("frA", [128, 5632], F32), Res("frA"))
    frB = (P.sbuf("frB", [128, 2048], F32), Res("frB"))

    def rms_stats(src_ap, r_src, nfeat, st, r_st, c0):
        junk, r_junk = junk_ring.next()
        n = src_ap.shape[-1]
        ACT(lambda e: e.activation(out=junk[:, 0:n], in_=src_ap, func=AF.Square, accum_out=st[:, c0:c0 + 1]), [r_src], [r_junk, r_st])
        ACT(lambda e: e.activation(out=st[:, c0:c0 + 1], in_=st[:, c0:c0 + 1], func=AF.Sqrt, scale=1.0 / nfeat, bias=EPS), [r_st], [r_st])
        VEC(lambda e: e.reciprocal(out=st[:, c0:c0 + 1], in_=st[:, c0:c0 + 1]), [r_st], [r_st])

    def norm_transpose(ht, r_ht, gcol, xnT, r_xnT, sub):
        st, r_st = st_ring.next()
        rms_stats(ht[:], r_ht, D, st, r_st, 0)
        xn, r_xn = xn_ring.next()
        VEC(lambda e: e.tensor_scalar(out=xn[:], in0=ht[:], scalar1=st[:, 0:1], scalar2=None, op0=ALU.mult), [r_ht, r_st], [r_xn])
        bk, bkb, r_bk = next_bank()
        bkv = bkb[:, :].rearrange("p (a b) -> p a b", b=128)
        for k in range(8):
            PE(lambda e, k=k: e.transpose(out=bkv[:, k, :], in_=xn[:, k * 128:(k + 1) * 128], identity=ident[:]), [r_xn, r_const], [r_bk])
        VEC(lambda e: e.tensor_tensor(out=xnT[:, :, sub * 128:(sub + 1) * 128], in0=bkv, in1=gcol.to_broadcast([128, 8, 128]), op=ALU.mult),
            [r_bk, r_const], [r_xnT])

    RS = slice(64, 96)

    areset()
    posi, r_ra = aalloc([96, S], I32, "posi")
    tmpa, _ = aalloc([96, S], F32, "tmpa")
    tmpb, _ = aalloc([96, S], F32, "tmpb")
    cs_t, r_cs = aalloc([96, S], F32, "cs")
    sn_t, _ = aalloc([96, S], F32, "sn")
    DMA(posi[RS, :], pos_d[0:1, :].to_broadcast([32, S]), [], [r_ra])
    VEC(lambda e: e.tensor_copy(out=tmpa[RS, :], in_=posi[RS, :]), [r_ra], [r_ra])
    VEC(lambda e: e.tensor_scalar(out=tmpa[RS, :], in0=tmpa[RS, :], scalar1=consts[RS, 0:1], scalar2=1.0 / TWO_PI,
                                  op0=ALU.mult, op1=ALU.mult), [r_ra, r_const], [r_ra])
    VEC(lambda e: e.tensor_scalar(out=tmpb[RS, :], in0=tmpa[RS, :], scalar1=MAGIC, scalar2=MAGIC, op0=ALU.add, op1=ALU.subtract), [r_ra], [r_ra])
    VEC(lambda e: e.tensor_tensor(out=tmpb[RS, :], in0=tmpa[RS, :], in1=tmpb[RS, :], op=ALU.subtract), [r_ra], [r_ra])
    ACT(lambda e: e.activation(out=sn_t[RS, :], in_=tmpb[RS, :], func=AF.Sin, scale=consts[RS, 1:2]), [r_ra, r_const], [r_cs])
    VEC(lambda e: e.tensor_scalar(out=tmpa[RS, :], in0=tmpa[RS, :], scalar1=0.25, scalar2=None, op0=ALU.add), [r_ra, r_cs], [r_ra])
    VEC(lambda e: e.tensor_scalar(out=tmpb[RS, :], in0=tmpa[RS, :], scalar1=MAGIC, scalar2=MAGIC, op0=ALU.add, op1=ALU.subtract), [r_ra], [r_ra])
    VEC(lambda e: e.tensor_tensor(out=tmpb[RS, :], in0=tmpa[RS, :], in1=tmpb[RS, :], op=ALU.subtract), [r_ra], [r_ra])
    ACT(lambda e: e.activation(out=cs_t[RS, :], in_=tmpb[RS, :], func=AF.Sin, scale=SIN_SCALE), [r_ra], [r_cs])
    DMA(cos_d[:, :], cs_t[RS, :], [r_cs], [r_csd])
    DMA(sin_d[:, :], sn_t[RS, :], [r_cs], [r_csd])

    def emit_layer(l):
        src_d = x_d if l == 0 else h_d
        r_src = [Res() for _ in range(32)] if l == 0 else r_hd
        P.barrier()
        areset()
        w_in_sb, r_w = aalloc([128, 8, 928], BF16, "w_in")
        w_kr_sb, _ = aalloc([128, 8, 192], BF16)
        w_uq_sb, _ = aalloc([128, 2, 768], BF16)
        w_uqs_sb, _ = aalloc([128, 2, 768], BF16)
        w_ukv_sb, _ = aalloc([128, 1024], BF16)
        DMAC(w_in_sb, w_in_d[l].rearrange("(k p) n -> p k n", p=128), [], [r_w])
        DMAC(w_kr_sb, w_kr_d[l].rearrange("(k p) n -> p k n", p=128), [], [r_w])
        DMAC(w_uq_sb, w_uq_d[l].rearrange("(k p) n -> p k n", p=128), [], [r_w])
        DMAC(w_uqs_sb, w_uqs_d[l].rearrange("(k p) n -> p k n", p=128), [], [r_w])
        DMAC(w_ukv_sb, w_ukv_d[l], [], [r_w])
        qs, r_qs = aalloc([96, 8, 512], F32, "qs")
        qsw, r_qsw = aalloc([96, 8, 512], F32, "qsw")
        qb_ring = ARing(2, [96, 8, 512], BF16, "qb")
        kb_ring = ARing(2, [96, 8, 512], BF16, "kb")
        vb_ring = ARing(2, [128, 8, 4, 65], BF16, "vb")
        ub_ring = ARing(2, [128, 4, 512], BF16, "ub")
        cT_ring = ARing(2, [128, 3, 512], BF16, "cT")
        cs_ring = ARing(2, [96, 2, 512], F32, "csb")
        ta, r_ta = aalloc([96, 1024], F32, "ta")
        for vb, r_vb in vb_ring.bufs:
            VEC(lambda e, vb=vb: e.memset(vb[:, :, :, 64:65], 1.0), [], [r_vb])

        for b in range(NB):
            T0 = b * 512
            xnT, r_xnT = xnT_ring.next()
            cT, r_cT = cT_ring.next()
            vb, r_vb = vb_ring.next()
            csb, r_csb = cs_ring.next()
            DMA(csb[RS, 0, :], cos_d[:, T0:T0 + 512], [r_csd], [r_csb])
            DMA(csb[RS, 1, :], sin_d[:, T0:T0 + 512], [r_csd], [r_csb])
            for sub in range(4):
                t0 = T0 + sub * 128
                ti = b * 4 + sub
                ht, r_ht = ht_ring.next()
                DMA(ht[:], src_d[t0:t0 + 128, :], [r_src[ti]], [r_ht])
                norm_transpose(ht, r_ht, col(l, C_GMIX, 8), xnT, r_xnT, sub)
                bk, bkb, r_bk = next_bank()
                for k in range(8):
                    PE(lambda e, k=k, bk=bk, xnT=xnT, sub=sub: e.matmul(bk[:, 0:384], lhsT=xnT[:, k, sub * 128:(sub + 1) * 128],
                                                                       rhs=w_in_sb[:, k, 0:384], start=(k == 0), stop=(k == 7)),
                       [r_xnT, r_w], [r_bk])
                st, r_st = st_ring.next()
                rms_stats(bk[:, 0:256], r_bk, 256, st, r_st, 0)
                rms_stats(bk[:, 256:384], r_bk, 128, st, r_st, 1)
                cn, r_cn = xn_ring.next()
                VEC(lambda e, bk=bk, cn=cn, st=st: e.tensor_scalar(out=cn[:, 0:256], in0=bk[:, 0:256], scalar1=st[:, 0:1], scalar2=None, op0=ALU.mult),
                    [r_bk, r_st], [r_cn])
                VEC(lambda e, bk=bk, cn=cn, st=st: e.tensor_scalar(out=cn[:, 256:384], in0=bk[:, 256:384], scalar1=st[:, 1:2], scalar2=None, op0=ALU.mult),
                    [r_bk, r_st], [r_cn])
                bk2, bk2b, r_bk2 = next_bank()
                bk2v = bk2b[:, :].rearrange("p (a b) -> p a b", b=128)
                for k in range(3):
                    PE(lambda e, k=k, bk2v=bk2v, cn=cn: e.transpose(out=bk2v[:, k, :], in_=cn[:, k * 128:(k + 1) * 128], identity=ident[:]),
                       [r_cn, r_const], [r_bk2])
                VEC(lambda e, bk2v=bk2v, cT=cT, sub=sub: e.tensor_tensor(out=cT[:, :, sub * 128:(sub + 1) * 128], in0=bk2v[:, 0:3, :],
                                                                       in1=col(l, C_GQ, 3).to_broadcast([128, 3, 128]), op=ALU.mult),
                    [r_bk2, r_const], [r_cT])
                bk3, _, r_bk3 = next_bank()
                PE(lambda e, bk3=bk3, cT=cT, sub=sub: e.matmul(bk3[:, :], lhsT=cT[:, 2, sub * 128:(sub + 1) * 128], rhs=w_ukv_sb[:, 512:1024],
                                                            start=True, stop=True), [r_cT, r_w], [r_bk3])
                ACT(lambda e, bk3=bk3, vb=vb, sub=sub: e.copy(out=vb[:, :, sub, 0:64], in_=bk3[:, :].rearrange("p (h d) -> p h d", d=64)),
                    [r_bk3], [r_vb])
            DMA(V_d[:, :, b * 4:(b + 1) * 4, :].rearrange("h p t d -> p h t d"), vb, [r_vb], [r_vd])
            ub, r_ub = ub_ring.next()
            for ct in range(4):
                bk, _, r_bk = next_bank()
                for k in range(8):
                    PE(lambda e, k=k, bk=bk, xnT=xnT, ct=ct: e.matmul(bk[:, :], lhsT=w_in_sb[:, k, 416 + ct * 128:416 + (ct + 1) * 128],
                                                                      rhs=xnT[:, k, :], start=(k == 0), stop=(k == 7)), [r_xnT, r_w], [r_bk])
                ACT(lambda e, bk=bk, ct=ct, ub=ub: e.copy(out=ub[:, ct, :], in_=bk[:, :]), [r_bk], [r_ub])
            DMA(uT_d[:, :, T0:T0 + 512], ub, [r_ub], [r_ud])
            bka, _, r_bka = next_bank()
            bkb_, _, r_bkb = next_bank()
            for k in range(8):
                PE(lambda e, k=k, bka=bka, xnT=xnT: e.matmul(bka[0:96, :], lhsT=w_kr_sb[:, k, 0:96], rhs=xnT[:, k, :], start=(k == 0), stop=(k == 7)),
                   [r_xnT, r_w], [r_bka])
            for k in range(8):
                PE(lambda e, k=k, bkb_=bkb_, xnT=xnT: e.matmul(bkb_[0:96, :], lhsT=w_kr_sb[:, k, 96:192], rhs=xnT[:, k, :], start=(k == 0), stop=(k == 7)),
                   [r_xnT, r_w], [r_bkb])
            VEC(lambda e, bka=bka, csb=csb: e.tensor_tensor(out=ta[RS, 0:512], in0=bka[RS, :], in1=csb[RS, 0, :], op=ALU.mult), [r_bka, r_csb], [r_ta])
            VEC(lambda e, bkb_=bkb_, csb=csb: e.tensor_tensor(out=ta[RS, 512:1024], in0=bkb_[RS, :], in1=csb[RS, 1, :], op=ALU.mult), [r_bkb, r_csb], [r_ta])
            VEC(lambda e: e.tensor_tensor(out=ta[RS, 0:512], in0=ta[RS, 0:512], in1=ta[RS, 512:1024], op=ALU.add), [r_ta], [r_ta])
            kb, r_kb = kb_ring.next()
            POOL(lambda e, kb=kb: e.tensor_copy(out=kb[RS, :, :], in_=ta[RS, 0:512].rearrange("p (o t) -> p o t", o=1).to_broadcast([32, 8, 512])),
                 [r_ta], [r_kb])
            for h in range(8):
                bk, _, r_bk = next_bank()
                PE(lambda e, bk=bk, h=h, cT=cT: e.matmul(bk[0:64, :], lhsT=w_ukv_sb[:, h * 64:(h + 1) * 64], rhs=cT[:, 2, :], start=True, stop=True),
                   [r_cT, r_w], [r_bk])
                ACT(lambda e, bk=bk, h=h, kb=kb: e.copy(out=kb[0:64, h, :], in_=bk[0:64, :]), [r_bk], [r_kb])
            DMA(kT_d[:, :, T0:T0 + 512].rearrange("h p t -> p h t"), kb, [r_kb], [r_kd])
            for h in range(8):
                bka, _, r_bka = next_bank()
                bkb_, _, r_bkb = next_bank()
                for k in range(2):
                    PE(lambda e, k=k, bka=bka, h=h, cT=cT: e.matmul(bka[0:96, :], lhsT=w_uq_sb[:, k, h * 96:(h + 1) * 96], rhs=cT[:, k, :],
                                                                    start=(k == 0), stop=(k == 1)), [r_cT, r_w], [r_bka])
                for k in range(2):
                    PE(lambda e, k=k, bkb_=bkb_, h=h, cT=cT: e.matmul(bkb_[0:96, :], lhsT=w_uqs_sb[:, k, h * 96:(h + 1) * 96], rhs=cT[:, k, :],
                                                                      start=(k == 0), stop=(k == 1)), [r_cT, r_w], [r_bkb])
                ACT(lambda e, bka=bka, h=h: e.copy(out=qs[:, h, :], in_=bka[0:96, :]), [r_bka], [r_qs])
                ACT(lambda e, bkb_=bkb_, h=h: e.copy(out=qsw[RS, h, :], in_=bkb_[RS, :]), [r_bkb], [r_qsw])
            qb, r_qb = qb_ring.next()
            POOL(lambda e, qb=qb: e.tensor_copy(out=qb[0:64, :, :], in_=qs[0:64, :, :]), [r_qs], [r_qb])
            VEC(lambda e, csb=csb: e.tensor_tensor(out=qs[RS, :, :], in0=qs[RS, :, :],
                                                   in1=csb[RS, 0:1, :].to_broadcast([32, 8, 512]), op=ALU.mult), [r_qs, r_csb], [r_qs])
            VEC(lambda e, csb=csb: e.tensor_tensor(out=qsw[RS, :, :], in0=qsw[RS, :, :],
                                                   in1=csb[RS, 1:2, :].to_broadcast([32, 8, 512]), op=ALU.mult), [r_qsw, r_csb], [r_qsw])
            VEC(lambda e, qb=qb: e.tensor_tensor(out=qb[RS, :, :], in0=qs[RS, :, :], in1=qsw[RS, :, :], op=ALU.add), [r_qs, r_qsw], [r_qb])
            DMA(qT_d[:, :, T0:T0 + 512].rearrange("h p t -> p h t"), qb, [r_qb], [r_qd])
        if stop_after == "A":
            return True

        P.barrier()
        areset()
        S_POOL = (0, 1, 2)
        O_POOL = (3, 4)
        M_POOL = (5, 6, 7)
        qh_ring = ARing(2, [96, S], BF16, "qh")
        kh_ring = ARing(2, [96, S], BF16, "kh")
        vh_ring = ARing(2, [128, 32, 65], BF16, "vh")
        oT_ring = ARing(2, [65, 1024], F32, "oT")
        an_ring = ARing(2, [64, 512], F32, "an")
        for h in range(8):
            qh, r_qh = qh_ring.next()
            kh, r_kh = kh_ring.next()
            vh, r_vh = vh_ring.next()
            DMA(qh, qT_d[h], [r_qd], [r_qh])
            DMA(kh, kT_d[h], [r_kd], [r_kh])
            DMA(vh, V_d[h], [r_vd], [r_vh])
            for b in range(NB):
                T0 = b * 512
                nkt = 4 * (b + 1)
                bo, _, r_bo = next_bank(O_POOL)
                for kt in range(nkt):
                    bs, _, r_bs = next_bank(S_POOL)
                    PE(lambda e, bs=bs, kt=kt, kh=kh, qh=qh, T0=T0: e.matmul(bs[:, :], lhsT=kh[:, kt * 128:(kt + 1) * 128], rhs=qh[:, T0:T0 + 512],
                                                                             start=True, stop=True), [r_kh, r_qh], [r_bs])
                    pT, r_pT = pT_ring.next()
                    ACT(lambda e, bs=bs, pT=pT: e.activation(out=pT[:], in_=bs[:, :], func=AF.Exp, scale=ATT_SCALE), [r_bs], [r_pT])
                    if kt >= 4 * b:
                        base = T0 - kt * 128
                        POOL(lambda e, pT=pT, base=base: e.affine_select(out=pT[:], in_=pT[:], pattern=[[1, 512]], compare_op=ALU.is_ge,
                                                                         fill=0.0, base=base, channel_multiplier=-1), [r_pT], [r_pT])
                    PE(lambda e, bo=bo, pT=pT, kt=kt, nkt=nkt, vh=vh: e.matmul(bo[0:65, :], lhsT=vh[:, kt, :], rhs=pT[:],
                                                                               start=(kt == 0), stop=(kt == nkt - 1)), [r_vh, r_pT], [r_bo])
                oT, r_oT = oT_ring.next()
                VEC(lambda e, bo=bo, oT=oT: e.tensor_copy(out=oT[:, 0:512], in_=bo[0:65, :]), [r_bo], [r_oT])
                bm, _, r_bm = next_bank(M_POOL)
                PE(lambda e, bm=bm, oT=oT: e.matmul(bm[0:64, :], lhsT=sel65[:, :], rhs=oT[:, 0:512], start=True, stop=True), [r_oT, r_const], [r_bm])
                VEC(lambda e, bm=bm, oT=oT: e.reciprocal(out=oT[0:64, 512:1024], in_=bm[0:64, :]), [r_bm, r_oT], [r_oT])
                an, r_an = an_ring.next()
                POOL(lambda e, oT=oT, an=an: e.tensor_tensor(out=an[:, :], in0=oT[0:64, 0:512], in1=oT[0:64, 512:1024], op=ALU.mult), [r_oT], [r_an])
                DMA(aT_d[b, :, h, :], an, [r_an], [r_ad])
        if stop_after == "B1":
            return True
        P.barrier()
        areset()
        Wst, r_Wst = aalloc([128, 4, 8, 2, 128], BF16, "Wst")
        Wfir, r_Wfir = aalloc([128, 4, 8, 128], BF16, "Wfir")
        Wo_r, r_Wo = aalloc([128, 16, 8, 32], BF16, "Wo")
        Wo_i, _ = aalloc([128, 16, 8, 32], BF16)
        sm, r_sm = aalloc([128, 32, 16], F32, "sm")
        pw_r, _ = aalloc([128, 16, 9], F32)
        pw_i, _ = aalloc([128, 16, 9], F32)
        ph_r, _ = aalloc([128, 16, 9], F32)
        ph_i, _ = aalloc([128, 16, 9], F32)
        mark = arena_off[0]
        Bri, r_bc = aalloc([128, 2, 16, 16], F32, "Bri")
        Cri, _ = aalloc([128, 2, 16, 16], F32)
        Bb_r, r_T = aalloc([128, 16, 16], F32, "T")
        Bb_i, _ = aalloc([128, 16, 16], F32)
        T1, _ = aalloc([128, 16, 16], F32)
        T2, _ = aalloc([128, 16, 16], F32)
        T3, _ = aalloc([128, 16, 16], F32)
        T4, _ = aalloc([128, 16, 16], F32)
        ME_r, r_ME = aalloc([128, 8, 4, 128], F32, "ME")
        ME_i, _ = aalloc([128, 8, 4, 128], F32)
        MF_r, r_MF = aalloc([128, 4, 128], F32, "MF")
        MF_in, _ = aalloc([128, 4, 128], F32)
        tmpF, r_tmpF = aalloc([128, 128], F32, "tmpF")
        DMA(sm[:, 0:3, :], s5v_d[l], [], [r_sm])
        DMA(Bri, s5b_d[l], [], [r_bc])
        DMA(Cri, s5c_d[l], [], [r_bc])
        POOL(lambda e: e.memset(ME_r, 0.0), [], [r_ME])
        POOL(lambda e: e.memset(ME_i, 0.0), [], [r_ME])
        POOL(lambda e: e.memset(MF_r, 0.0), [], [r_MF])
        POOL(lambda e: e.memset(MF_in, 0.0), [], [r_MF])
        POOL(lambda e: e.memset(Wo_r, 0.0), [], [r_Wo])
        POOL(lambda e: e.memset(Wo_i, 0.0), [], [r_Wo])
        LRE, LIM, LDT, DT, TT_, ER, Y, RND, FR, SN_, CS_, AR, AI, ARM1, DEN, RDEN, CBR, CBI, U1, U2, RDEC, RR = range(22)

        def smtt(o, a, b, op):
            VEC(lambda e: e.tensor_tensor(out=sm[:, o, :], in0=sm[:, a, :], in1=sm[:, b, :], op=op), [r_sm], [r_sm])

        def smts(o, a, s1, op0, s2=None, op1=None):
            if op1 is None:
                VEC(lambda e: e.tensor_scalar(out=sm[:, o, :], in0=sm[:, a, :], scalar1=s1, scalar2=None, op0=op0), [r_sm], [r_sm])
            else:
                VEC(lambda e: e.tensor_scalar(out=sm[:, o, :], in0=sm[:, a, :], scalar1=s1, scalar2=s2, op0=op0, op1=op1), [r_sm], [r_sm])

        def smact(o, a, func, scale=1.0):
            ACT(lambda e: e.activation(out=sm[:, o, :], in_=sm[:, a, :], func=func, scale=scale), [r_sm], [r_sm])

        smact(DT, LDT, AF.Exp)
        smtt(TT_, LRE, DT, ALU.mult)
        smact(ER, TT_, AF.Exp)
        smact(RDEC, TT_, AF.Exp, 8.0)
        smtt(Y, LIM, DT, ALU.mult)
        smts(Y, Y, 1.0 / TWO_PI, ALU.mult)
        smts(RND, Y, MAGIC, ALU.add, MAGIC, ALU.subtract)
        smtt(FR, Y, RND, ALU.subtract)
        smact(SN_, FR, AF.Sin, SIN_SCALE)
        smts(Y, Y, 0.25, ALU.add)
        smts(RND, Y, MAGIC, ALU.add, MAGIC, ALU.subtract)
        smtt(FR, Y, RND, ALU.subtract)
        smact(CS_, FR, AF.Sin, SIN_SCALE)
        smtt(AR, ER, CS_, ALU.mult)
        smtt(AI, ER, SN_, ALU.mult)
        smts(ARM1, AR, -1.0, ALU.add)
        smtt(U1, LRE, LRE, ALU.mult)
        smtt(U2, LIM, LIM, ALU.mult)
        smtt(DEN, U1, U2, ALU.add)
        VEC(lambda e: e.reciprocal(out=sm[:, RDEN, :], in_=sm[:, DEN, :]), [r_sm], [r_sm])
        smtt(U1, ARM1, LRE, ALU.mult)
        smtt(U2, AI, LIM, ALU.mult)
        smtt(U1, U1, U2, ALU.add)
        smtt(CBR, U1, RDEN, ALU.mult)
        smtt(U1, AI, LRE, ALU.mult)
        smtt(U2, ARM1, LIM, ALU.mult)
        smtt(U1, U1, U2, ALU.subtract)
        smtt(CBI, U1, RDEN, ALU.mult)
        VEC(lambda e: e.memset(pw_r[:, :, 0:1], 1.0), [r_sm], [r_sm])
        VEC(lambda e: e.memset(pw_i[:, :, 0:1], 0.0), [r_sm], [r_sm])
        VEC(lambda e: e.tensor_copy(out=pw_r[:, :, 1], in_=sm[:, AR, :]), [r_sm], [r_sm])
        VEC(lambda e: e.tensor_copy(out=pw_i[:, :, 1], in_=sm[:, AI, :]), [r_sm], [r_sm])
        for k in range(1, 8):
            VEC(lambda e, k=k: e.tensor_tensor(out=sm[:, U1, :], in0=pw_r[:, :, k], in1=sm[:, AR, :], op=ALU.mult), [r_sm], [r_sm])
            VEC(lambda e, k=k: e.tensor_tensor(out=sm[:, U2, :], in0=pw_i[:, :, k], in1=sm[:, AI, :], op=ALU.mult), [r_sm], [r_sm])
            VEC(lambda e, k=k: e.tensor_tensor(out=pw_r[:, :, k + 1], in0=sm[:, U1, :], in1=sm[:, U2, :], op=ALU.subtract), [r_sm], [r_sm])
            VEC(lambda e, k=k: e.tensor_tensor(out=sm[:, U1, :], in0=pw_r[:, :, k], in1=sm[:, AI, :], op=ALU.mult), [r_sm], [r_sm])
            VEC(lambda e, k=k: e.tensor_tensor(out=sm[:, U2, :], in0=pw_i[:, :, k], in1=sm[:, AR, :], op=ALU.mult), [r_sm], [r_sm])
            VEC(lambda e, k=k: e.tensor_tensor(out=pw_i[:, :, k + 1], in0=sm[:, U1, :], in1=sm[:, U2, :], op=ALU.add), [r_sm], [r_sm])
        VEC(lambda e: e.reciprocal(out=sm[:, RR, :], in_=sm[:, RDEC, :]), [r_sm], [r_sm])
        VEC(lambda e: e.tensor_tensor(out=ph_r[:, :, 0], in0=pw_r[:, :, 8], in1=sm[:, RR, :], op=ALU.mult), [r_sm], [r_sm])
        VEC(lambda e: e.tensor_tensor(out=ph_i[:, :, 0], in0=pw_i[:, :, 8], in1=sm[:, RR, :], op=ALU.mult), [r_sm], [r_sm])
        for k in range(8):
            VEC(lambda e, k=k: e.tensor_tensor(out=sm[:, U1, :], in0=ph_r[:, :, k], in1=ph_r[:, :, k], op=ALU.mult), [r_sm], [r_sm])
            VEC(lambda e, k=k: e.tensor_tensor(out=sm[:, U2, :], in0=ph_i[:, :, k], in1=ph_i[:, :, k], op=ALU.mult), [r_sm], [r_sm])
            VEC(lambda e, k=k: e.tensor_tensor(out=ph_r[:, :, k + 1], in0=sm[:, U1, :], in1=sm[:, U2, :], op=ALU.subtract), [r_sm], [r_sm])
            VEC(lambda e, k=k: e.tensor_tensor(out=sm[:, U1, :], in0=ph_r[:, :, k], in1=ph_i[:, :, k], op=ALU.mult), [r_sm], [r_sm])
            VEC(lambda e, k=k: e.tensor_scalar(out=ph_i[:, :, k + 1], in0=sm[:, U1, :], scalar1=2.0, scalar2=None, op0=ALU.mult), [r_sm], [r_sm])

        def bc16(tile_idx_ap):
            return tile_idx_ap.rearrange("p (a o) -> p a o", o=1).to_broadcast([128, 16, 16])

        def cmul_bc(outr, outi, xr, xi, sr_ap, si_ap, rds, wrs):
            pass

        VEC(lambda e: e.tensor_tensor(out=T1, in0=Bri[:, 0], in1=bc16(sm[:, CBR, :]), op=ALU.mult), [r_sm, r_bc], [r_T])
        VEC(lambda e: e.tensor_tensor(out=T2, in0=Bri[:, 1], in1=bc16(sm[:, CBI, :]), op=ALU.mult), [r_sm, r_bc], [r_T])
        VEC(lambda e: e.tensor_tensor(out=Bb_r, in0=T1, in1=T2, op=ALU.subtract), [r_T], [r_T])
        VEC(lambda e: e.tensor_tensor(out=T1, in0=Bri[:, 1], in1=bc16(sm[:, CBR, :]), op=ALU.mult), [r_sm, r_bc, r_T], [r_T])
        VEC(lambda e: e.tensor_tensor(out=T2, in0=Bri[:, 0], in1=bc16(sm[:, CBI, :]), op=ALU.mult), [r_sm, r_bc], [r_T])
        VEC(lambda e: e.tensor_tensor(out=Bb_i, in0=T1, in1=T2, op=ALU.add), [r_T], [r_T])

        def blkME(M, lg, hf):
            return M[hf * 64:(hf + 1) * 64, lg, :, :].rearrange("p ct (q x) -> p ct q x", x=32)[:, :, :, hf * 16:(hf + 1) * 16]

        def halfT(T, hf):
            return T[hf * 64:(hf + 1) * 64, :, :].rearrange("p (ct q) c -> p ct q c", q=4)

        for lg in range(8):
            VEC(lambda e, lg=lg: e.tensor_tensor(out=T1, in0=Bb_r, in1=pw_r[:, :, lg:lg + 1].to_broadcast([128, 16, 16]), op=ALU.mult), [r_sm, r_T], [r_T])
            VEC(lambda e, lg=lg: e.tensor_tensor(out=T2, in0=Bb_i, in1=pw_i[:, :, lg:lg + 1].to_broadcast([128, 16, 16]), op=ALU.mult), [r_sm, r_T], [r_T])
            VEC(lambda e, lg=lg: e.tensor_tensor(out=T3, in0=Bb_i, in1=pw_r[:, :, lg:lg + 1].to_broadcast([128, 16, 16]), op=ALU.mult), [r_sm, r_T], [r_T])
            VEC(lambda e, lg=lg: e.tensor_tensor(out=T4, in0=Bb_r, in1=pw_i[:, :, lg:lg + 1].to_broadcast([128, 16, 16]), op=ALU.mult), [r_sm, r_T], [r_T])
            for hf in range(2):
                POOL(lambda e, lg=lg, hf=hf: e.tensor_tensor(out=blkME(ME_r, lg, hf), in0=halfT(T1, hf), in1=halfT(T2, hf), op=ALU.subtract), [r_T], [r_ME])
                POOL(lambda e, lg=lg, hf=hf: e.tensor_tensor(out=blkME(ME_i, lg, hf), in0=halfT(T3, hf), in1=halfT(T4, hf), op=ALU.add), [r_T], [r_ME])

        def blkMF(M, hf):
            return M[hf * 64:(hf + 1) * 64, :, :].rearrange("p ct (q x) -> p ct q x", x=32)[:, :, :, hf * 16:(hf + 1) * 16]

        for hf in range(2):
            POOL(lambda e, hf=hf: e.tensor_copy(out=blkMF(MF_r, hf), in_=halfT(Cri[:, 0], hf)), [r_bc], [r_MF])
            POOL(lambda e, hf=hf: e.tensor_scalar(out=blkMF(MF_in, hf), in0=halfT(Cri[:, 1], hf), scalar1=-1.0, scalar2=None, op0=ALU.mult), [r_bc], [r_MF])
        for ct in range(4):
            for ri, M in ((0, ME_r), (1, ME_i)):
                for j0 in (0, 4):
                    bk, _, r_bk = next_bank()
                    for jj in range(4):
                        j = j0 + jj
                        PE(lambda e, bk=bk, jj=jj, j=j, ct=ct, M=M: e.transpose(out=bk[:, jj * 128:(jj + 1) * 128], in_=M[:, 7 - j, ct, :], identity=identf[:]),
                           [r_ME, r_const], [r_bk])
                    ACT(lambda e, bk=bk, ct=ct, j0=j0, ri=ri: e.copy(out=Wst[:, ct, j0:j0 + 4, ri, :], in_=bk[:, :].rearrange("p (a b) -> p a b", b=128)),
                        [r_bk], [r_Wst])
        for ct in range(4):
            for l0 in (0, 4):
                bk, _, r_bk = next_bank()
                for ll in range(4):
                    lg = l0 + ll
                    PE(lambda e, bk=bk, ll=ll, lg=lg, ct=ct: e.matmul(bk[:, ll * 128:(ll + 1) * 128], lhsT=ME_r[:, lg, ct, :], rhs=MF_r[:, ct, :], start=True, stop=False),
                       [r_ME, r_MF], [r_bk])
                    PE(lambda e, bk=bk, ll=ll, lg=lg, ct=ct: e.matmul(bk[:, ll * 128:(ll + 1) * 128], lhsT=ME_i[:, lg, ct, :], rhs=MF_in[:, ct, :], start=False, stop=True),
                       [r_ME, r_MF], [r_bk])
                VEC(lambda e, bk=bk, ct=ct, l0=l0: e.tensor_tensor(out=Wfir[:, ct, l0:l0 + 4, :], in0=bk[:, :].rearrange("p (a b) -> p a b", b=128),
                                                                 in1=mask16[:, :].rearrange("p (o b) -> p o b", o=1).to_broadcast([128, 4, 128]), op=ALU.mult),
                    [r_bk, r_const], [r_Wfir])
                if l0 == 0:
                    VEC(lambda e, bk=bk: e.tensor_tensor(out=tmpF, in0=bk[:, 0:128], in1=mask16[:, :], op=ALU.mult), [r_bk, r_const], [r_tmpF])
                    VEC(lambda e, ct=ct: e.scalar_tensor_tensor(out=Wfir[:, ct, 0, :], in0=identf[:, :], scalar=col(l, C_D + ct), in1=tmpF, op0=ALU.mult, op1=ALU.add),
                        [r_tmpF, r_const], [r_Wfir])
        for i in range(8):
            VEC(lambda e, i=i: e.tensor_tensor(out=T1, in0=Cri[:, 0], in1=pw_r[:, :, i + 1:i + 2].to_broadcast([128, 16, 16]), op=ALU.mult), [r_sm, r_bc, r_T], [r_T])
            VEC(lambda e, i=i: e.tensor_tensor(out=T2, in0=Cri[:, 1], in1=pw_i[:, :, i + 1:i + 2].to_broadcast([128, 16, 16]), op=ALU.mult), [r_sm, r_bc, r_T], [r_T])
            VEC(lambda e, i=i: e.tensor_tensor(out=T3, in0=Cri[:, 1], in1=pw_r[:, :, i + 1:i + 2].to_broadcast([128, 16, 16]), op=ALU.mult), [r_sm, r_bc, r_T], [r_T])
            VEC(lambda e, i=i: e.tensor_tensor(out=T4, in0=Cri[:, 0], in1=pw_i[:, :, i + 1:i + 2].to_broadcast([128, 16, 16]), op=ALU.mult), [r_sm, r_bc, r_T], [r_T])
            for hf in range(2):
                hs = slice(hf * 64, (hf + 1) * 64)
                cs = slice(hf * 16, (hf + 1) * 16)
                POOL(lambda e, i=i, hs=hs, cs=cs: e.tensor_tensor(out=Wo_r[hs, :, i, cs], in0=T1[hs], in1=T2[hs], op=ALU.subtract), [r_T], [r_Wo])
                VEC(lambda e, i=i, hs=hs, cs=cs: e.scalar_tensor_tensor(out=Wo_i[hs, :, i, cs], in0=T3[hs], scalar=-1.0, in1=T4[hs], op0=ALU.mult, op1=ALU.subtract),
                    [r_T], [r_Wo])
        P.barrier()
        arena_off[0] = mark
        u_ring = ARing(2, [128, S], BF16, "uct")
        y_ring = ARing(1, [128, S], F32, "yct")
        tab_r, r_tab = aalloc([128, 4, 512], F32, "tab")
        tab_i, _ = aalloc([128, 4, 512], F32)
        tq1, r_tq = aalloc([128, 4, 256], F32, "tq")
        tq2, _ = aalloc([128, 4, 256], F32)
        xp_r, r_xp = aalloc([128, 4, 512], BF16, "xp")
        xp_i, _ = aalloc([128, 4, 512], BF16)
        A_ring = ARing(6, [128, 512], F32, "A")
        VEC(lambda e: e.memset(xp_r[:, :, 0:1], 0.0), [], [r_xp])
        VEC(lambda e: e.memset(xp_i[:, :, 0:1], 0.0), [], [r_xp])
        for ct in range(4):
            uct, r_uct = u_ring.next()
            DMA(uct, uT_d[:, ct, :], [r_ud], [r_uct])
            u8 = uct.rearrange("p (c j) -> p j c", j=8)
            VEC(lambda e: e.memset(tab_r[:, :, 0:1], 1.0), [r_tab], [r_tab])
            VEC(lambda e: e.memset(tab_i[:, :, 0:1], 0.0), [r_tab], [r_tab])
            for k in range(9):
                s_ = 1 << k
                phr = ph_r[:, 4 * ct:4 * ct + 4, k:k + 1].to_broadcast([128, 4, s_])
                phi = ph_i[:, 4 * ct:4 * ct + 4, k:k + 1].to_broadcast([128, 4, s_])
                VEC(lambda e, s_=s_, phr=phr: e.tensor_tensor(out=tq1[:, :, 0:s_], in0=tab_r[:, :, 0:s_], in1=phr, op=ALU.mult), [r_tab, r_sm, r_tq], [r_tq])
                VEC(lambda e, s_=s_, phi=phi: e.tensor_tensor(out=tq2[:, :, 0:s_], in0=tab_i[:, :, 0:s_], in1=phi, op=ALU.mult), [r_tab, r_sm, r_tq], [r_tq])
                VEC(lambda e, s_=s_: e.tensor_tensor(out=tab_r[:, :, s_:2 * s_], in0=tq1[:, :, 0:s_], in1=tq2[:, :, 0:s_], op=ALU.subtract), [r_tq], [r_tab])
                VEC(lambda e, s_=s_, phi=phi: e.tensor_tensor(out=tq1[:, :, 0:s_], in0=tab_r[:, :, 0:s_], in1=phi, op=ALU.mult), [r_tab, r_sm, r_tq], [r_tq])
                VEC(lambda e, s_=s_, phr=phr: e.tensor_tensor(out=tq2[:, :, 0:s_], in0=tab_i[:, :, 0:s_], in1=phr, op=ALU.mult), [r_tab, r_sm, r_tq], [r_tq])
                VEC(lambda e, s_=s_: e.tensor_tensor(out=tab_i[:, :, s_:2 * s_], in0=tq1[:, :, 0:s_], in1=tq2[:, :, 0:s_], op=ALU.add), [r_tq], [r_tab])
            for q in range(4):
                pair = 4 * ct + q
                ps_ = slice(32 * q, 32 * q + 32)
                bks = []
                for ri in range(2):
                    bk, _, r_bk = next_bank()
                    for j in range(8):
                        PE(lambda e, bk=bk, j=j, ri=ri, ps_=ps_, q=q, ct=ct, u8=u8: e.matmul(bk[:, :], lhsT=Wst[ps_, ct, j, ri, :], rhs=u8[ps_, j, :],
                                                                                           start=(j == 0), stop=(j == 7), tile_position=(32 * q, 0)),
                           [r_Wst, r_uct], [r_bk])
                    bks.append((bk, r_bk))
                (Sr, r_Sr), (Si, r_Si) = bks
                tr, ti = tab_r[:, q, :], tab_i[:, q, :]
                a1, r_a1 = A_ring.next()
                a2, r_a2 = A_ring.next()
                a3, r_a3 = A_ring.next()
                a4, r_a4 = A_ring.next()
                VEC(lambda e, Sr=Sr, a1=a1, tr=tr: e.tensor_tensor(out=a1, in0=Sr[:, :], in1=tr, op=ALU.mult), [r_Sr, r_tab], [r_a1])
                VEC(lambda e, Si=Si, a2=a2, ti=ti: e.tensor_tensor(out=a2, in0=Si[:, :], in1=ti, op=ALU.mult), [r_Si, r_tab], [r_a2])
                VEC(lambda e, Si=Si, a3=a3, tr=tr: e.tensor_tensor(out=a3, in0=Si[:, :], in1=tr, op=ALU.mult), [r_Si, r_tab], [r_a3])
                VEC(lambda e, Sr=Sr, a4=a4, ti=ti: e.tensor_tensor(out=a4, in0=Sr[:, :], in1=ti, op=ALU.mult), [r_Sr, r_tab], [r_a4])
                POOL(lambda e, a1=a1, a2=a2: e.tensor_tensor(out=a1, in0=a1, in1=a2, op=ALU.add), [r_a1, r_a2], [r_a1])
                POOL(lambda e, a3=a3, a4=a4: e.tensor_tensor(out=a3, in0=a3, in1=a4, op=ALU.subtract), [r_a3, r_a4], [r_a3])
                rd = sm[:, RDEC, pair:pair + 1].to_broadcast([128, 512])
                VEC(lambda e, a1=a1, a2=a2, rd=rd: e.tensor_tensor_scan(out=a2, data0=rd, data1=a1, initial=0.0, op0=ALU.mult, op1=ALU.add), [r_a1, r_sm, r_a2], [r_a2])
                VEC(lambda e, a3=a3, a4=a4, rd=rd: e.tensor_tensor_scan(out=a4, data0=rd, data1=a3, initial=0.0, op0=ALU.mult, op1=ALU.add), [r_a3, r_sm, r_a4], [r_a4])
                b1, r_b1 = A_ring.next()
                b2, r_b2 = A_ring.next()
                POOL(lambda e, a2=a2, b1=b1, tr=tr: e.tensor_tensor(out=b1, in0=a2, in1=tr, op=ALU.mult), [r_a2, r_tab], [r_b1])
                POOL(lambda e, a4=a4, b2=b2, ti=ti: e.tensor_tensor(out=b2, in0=a4, in1=ti, op=ALU.mult), [r_a4, r_tab], [r_b2])
                VEC(lambda e, b1=b1, b2=b2, q=q: e.tensor_tensor(out=xp_r[:, q, 1:512], in0=b1[:, 0:511], in1=b2[:, 0:511], op=ALU.subtract), [r_b1, r_b2], [r_xp])
                POOL(lambda e, a2=a2, a1=a1, ti=ti: e.tensor_tensor(out=a1, in0=a2, in1=ti, op=ALU.mult), [r_a2, r_tab, r_a1], [r_a1])
                POOL(lambda e, a4=a4, a3=a3, tr=tr: e.tensor_tensor(out=a3, in0=a4, in1=tr, op=ALU.mult), [r_a4, r_tab, r_a3], [r_a3])
                VEC(lambda e, a1=a1, a3=a3, q=q: e.tensor_tensor(out=xp_i[:, q, 1:512], in0=a1[:, 0:511], in1=a3[:, 0:511], op=ALU.add), [r_a1, r_a3], [r_xp])
            yct, r_yct = y_ring.next()
            y8 = yct.rearrange("p (c j) -> p j c", j=8)
            for i in range(8):
                bk, _, r_bk = next_bank()
                for lg in range(i + 1):
                    PE(lambda e, bk=bk, lg=lg, i=i, ct=ct, u8=u8: e.matmul(bk[:, :], lhsT=Wfir[:, ct, lg, :], rhs=u8[:, i - lg, :], start=(lg == 0), stop=False),
                       [r_Wfir, r_uct], [r_bk])
                for q in range(4):
                    pair = 4 * ct + q
                    PE(lambda e, bk=bk, q=q, pair=pair, i=i: e.matmul(bk[32 * q:32 * q + 32, :], lhsT=Wo_r[:, pair, i, :], rhs=xp_r[:, q, :], start=False, stop=False,
                                                                     tile_position=(0, 32 * q)), [r_Wo, r_xp], [r_bk])
                    PE(lambda e, bk=bk, q=q, pair=pair, i=i: e.matmul(bk[32 * q:32 * q + 32, :], lhsT=Wo_i[:, pair, i, :], rhs=xp_i[:, q, :], start=False, stop=(q == 3),
                                                                     tile_position=(0, 32 * q)), [r_Wo, r_xp], [r_bk])
                ACT(lambda e, bk=bk, y8=y8, i=i: e.copy(out=y8[:, i, :], in_=bk[:, :]), [r_bk], [r_yct])
            DMA(yT_d[:, ct, :], yct, [r_yct], [r_yd])
        if stop_after == "B2a":
            return True
        P.barrier()
        arena_off[0] = mark
        wglu, r_wglu = aalloc([128, 4, 512], BF16, "wglu")
        DMAC(wglu, w_glu_d[l].rearrange("(k p) n -> p k n", p=128), [], [r_wglu])
        yb_ring = ARing(2, [128, 4, 512], F32, "yb")
        g_ring = ARing(2, [128, 4, 512], BF16, "gT")
        sg_ring = ARing(2, [128, 4, 512], F32, "sg")
        sq_ring = ARing(1, [128, 4, 512], F32, "sq")
        rs_ring = ARing(2, [128, 512], F32, "rs")
        sn_ring = ARing(2, [128, 4, 512], BF16, "sn")
        for b in range(NB):
            T0 = b * 512
            yb, r_yb = yb_ring.next()
            DMA(yb, yT_d[:, :, T0:T0 + 512], [r_yd], [r_yb])
            gT, r_gT = g_ring.next()
            ACT(lambda e, yb=yb, gT=gT: e.activation(out=gT, in_=yb, func=AF.Gelu_apprx_tanh), [r_yb], [r_gT])
            sg, r_sg = sg_ring.next()
            for co in range(4):
                bk, _, r_bk = next_bank()
                for ci in range(4):
                    PE(lambda e, bk=bk, ci=ci, co=co, gT=gT: e.matmul(bk[:, :], lhsT=wglu[:, ci, co * 128:(co + 1) * 128], rhs=gT[:, ci, :], start=(ci == 0), stop=(ci == 3)),
                       [r_wglu, r_gT], [r_bk])
                ACT(lambda e, bk=bk, co=co, sg=sg: e.activation(out=sg[:, co, :], in_=bk[:, :], func=AF.Sigmoid, bias=col(l, C_BGLU + co)), [r_bk, r_const], [r_sg])
            VEC(lambda e, sg=sg, yb=yb: e.tensor_tensor(out=sg, in0=sg, in1=yb, op=ALU.mult), [r_sg, r_yb], [r_sg])
            sq, r_sq = sq_ring.next()
            POOL(lambda e, sg=sg, sq=sq: e.tensor_tensor(out=sq, in0=sg, in1=sg, op=ALU.mult), [r_sg], [r_sq])
            bk, _, r_bk = next_bank()
            for co in range(4):
                PE(lambda e, bk=bk, co=co, sq=sq: e.matmul(bk[:, :], lhsT=onesf[:, :], rhs=sq[:, co, :], start=(co == 0), stop=(co == 3)), [r_sq, r_const], [r_bk])
            rs_, r_rs = rs_ring.next()
            ACT(lambda e, bk=bk, rs_=rs_: e.activation(out=rs_, in_=bk[:, :], func=AF.Sqrt, scale=1.0 / 512, bias=EPS), [r_bk], [r_rs])
            VEC(lambda e, rs_=rs_: e.reciprocal(out=rs_, in_=rs_), [r_rs], [r_rs])
            VEC(lambda e, sg=sg: e.tensor_tensor(out=sg, in0=sg, in1=col(l, C_GS, 4).rearrange("p (k o) -> p k o", o=1).to_broadcast([128, 4, 512]), op=ALU.mult),
                [r_sg, r_const], [r_sg])
            sn, r_sn = sn_ring.next()
            VEC(lambda e, sg=sg, sn=sn, rs_=rs_: e.tensor_tensor(out=sn, in0=sg, in1=rs_.rearrange("p (o t) -> p o t", o=1).to_broadcast([128, 4, 512]), op=ALU.mult),
                [r_sg, r_rs], [r_sn])
            DMA(sT_d[b], sn, [r_sn], [r_sd])
        if stop_after == "B2":
            return True
        P.barrier()
        areset()
        wo_a, r_wc = aalloc([64, 8, 1024], BF16, "wo_a")
        wo_s, _ = aalloc([128, 4, 1024], BF16)
        wxq, _ = aalloc([128, 8, 1024], BF16)
        wxo, _ = aalloc([128, 8, 1024], BF16)
        KxT, r_kx = aalloc([128, 8, 256], BF16, "KxT")
        Vx, _ = aalloc([128, 2, 1024], BF16)
        markc = arena_off[0]
        wxkv, r_wxkv = aalloc([128, 8, 2048], BF16, "wxkv")
        DMAC(wxkv, w_xkv_d[l].rearrange("(k p) n -> p k n", p=128), [], [r_wxkv])
        DMAC(wo_a, w_out_d[l, 0:512, :].rearrange("(h d) n -> d h n", d=64), [], [r_wc])
        DMAC(wo_s, w_out_d[l, 512:1024, :].rearrange("(k p) n -> p k n", p=128), [], [r_wc])
        DMAC(wxq, w_xq_d[l].rearrange("(k p) n -> p k n", p=128), [], [r_wc])
        DMAC(wxo, w_xo_d[l].rearrange("(k p) n -> p k n", p=128), [], [r_wc])
        memT, r_memT = xnT_ring.next()
        for mt in range(2):
            ht, r_ht = ht_ring.next()
            DMA(ht[:], mem_d[mt * 128:(mt + 1) * 128, :], [], [r_ht])
            norm_transpose(ht, r_ht, col(l, C_GMEM, 8), memT, r_memT, mt)
        for oc in range(8):
            bk, _, r_bk = next_bank()
            for k in range(8):
                PE(lambda e, bk=bk, k=k, oc=oc: e.matmul(bk[:, 0:256], lhsT=wxkv[:, k, oc * 128:(oc + 1) * 128], rhs=memT[:, k, 0:256], start=(k == 0), stop=(k == 7)),
                   [r_wxkv, r_memT], [r_bk])
            ACT(lambda e, bk=bk, oc=oc: e.copy(out=KxT[:, oc, :], in_=bk[:, 0:256]), [r_bk], [r_kx])
        for mt in range(2):
            for hf in range(2):
                bk, _, r_bk = next_bank()
                for k in range(8):
                    PE(lambda e, bk=bk, k=k, mt=mt, hf=hf: e.matmul(bk[:, :], lhsT=memT[:, k, mt * 128:(mt + 1) * 128], rhs=wxkv[:, k, 1024 + hf * 512:1024 + (hf + 1) * 512],
                                                                  start=(k == 0), stop=(k == 7)), [r_wxkv, r_memT], [r_bk])
                ACT(lambda e, bk=bk, mt=mt, hf=hf: e.copy(out=Vx[:, mt, hf * 512:(hf + 1) * 512], in_=bk[:, :]), [r_bk], [r_kx])
        P.barrier()
        arena_off[0] = markc
        an_ring2 = ARing(1, [64, 8, 512], BF16, "an2")
        sn_ring2 = ARing(2, [128, 4, 512], BF16, "sn2")
        h1_ring = ARing(5, [128, D], F32, "h1t")
        qx_ring = ARing(1, [128, 8, 512], BF16, "qxT")
        ox_ring = ARing(1, [128, 8, 512], BF16, "oxT")
        pX_ring = ARing(2, [128, 2, 512], BF16, "pX")
        rc_ring = ARing(2, [128, 512], F32, "rc")
        araw_v = frA[0][0:64, 0:4096].rearrange("p (h t) -> p h t", t=512)
        r_araw = frA[1]
        sqv = frB[0][0:64, 0:2048].rearrange("p (h t) -> p h t", t=512)
        r_sqv = frB[1]
        r_h1src = [Res() for _ in range(32)] if l == 0 else r_hd
        for b in range(NB):
            T0 = b * 512
            DMA(araw_v, aT_d[b], [r_ad], [r_araw])
            bk, _, r_bk = next_bank()
            for hg in range(2):
                POOL(lambda e, hg=hg: e.tensor_tensor(out=sqv, in0=araw_v[:, hg * 4:(hg + 1) * 4, :], in1=araw_v[:, hg * 4:(hg + 1) * 4, :], op=ALU.mult), [r_araw], [r_sqv])
                for hh in range(4):
                    PE(lambda e, bk=bk, hg=hg, hh=hh: e.matmul(bk[0:64, :], lhsT=onesf[0:64, 0:64], rhs=sqv[:, hh, :], start=(hg == 0 and hh == 0), stop=(hg == 1 and hh == 3)),
                       [r_sqv, r_const], [r_bk])
            rc, r_rc = rc_ring.next()
            ACT(lambda e, bk=bk, rc=rc: e.activation(out=rc[0:64, :], in_=bk[0:64, :], func=AF.Sqrt, scale=1.0 / 512, bias=EPS), [r_bk], [r_rc])
            VEC(lambda e, rc=rc: e.reciprocal(out=rc[0:64, :], in_=rc[0:64, :]), [r_rc], [r_rc])
            VEC(lambda e: e.tensor_tensor(out=araw_v, in0=araw_v, in1=col(l, C_GA, 8, 0, 64).rearrange("p (h o) -> p h o", o=1).to_broadcast([64, 8, 512]), op=ALU.mult),
                [r_araw, r_const], [r_araw])
            an2, r_an2 = an_ring2.next()
            VEC(lambda e, an2=an2, rc=rc: e.tensor_tensor(out=an2, in0=araw_v, in1=rc[0:64, :].rearrange("p (o t) -> p o t", o=1).to_broadcast([64, 8, 512]), op=ALU.mult),
                [r_araw, r_rc], [r_an2])
            sn2, r_sn2 = sn_ring2.next()
            DMA(sn2, sT_d[b], [r_sd], [r_sn2])
            xnT, r_xnT = xnT_ring.next()
            h1s = []
            for sub in range(4):
                ti = b * 4 + sub
                ts_ = slice(sub * 128, (sub + 1) * 128)
                ht, r_ht = ht_ring.next()
                DMA(ht[:], src_d[T0 + sub * 128:T0 + (sub + 1) * 128, :], [r_h1src[ti]], [r_ht])
                h1t, r_h1t = h1_ring.next()
                for hf in range(2):
                    bk, _, r_bk = next_bank()
                    cs_ = slice(hf * 512, (hf + 1) * 512)
                    for hh in range(8):
                        PE(lambda e, bk=bk, hh=hh, an2=an2, ts_=ts_, cs_=cs_: e.matmul(bk[:, :], lhsT=an2[:, hh, ts_], rhs=wo_a[:, hh, cs_], start=(hh == 0), stop=False),
                           [r_an2, r_wc], [r_bk])
                    for k in range(4):
                        PE(lambda e, bk=bk, k=k, sn2=sn2, ts_=ts_, cs_=cs_: e.matmul(bk[:, :], lhsT=sn2[:, k, ts_], rhs=wo_s[:, k, cs_], start=False, stop=(k == 3)),
                           [r_sn2, r_wc], [r_bk])
                    VEC(lambda e, bk=bk, ht=ht, h1t=h1t, cs_=cs_: e.tensor_tensor(out=h1t[:, cs_], in0=bk[:, :], in1=ht[:, cs_], op=ALU.add), [r_bk, r_ht], [r_h1t])
                norm_transpose(h1t, r_h1t, col(l, C_GX, 8), xnT, r_xnT, sub)
                h1s.append((h1t, r_h1t))
            qxT, r_qxT = qx_ring.next()
            for oc in range(8):
                bk, _, r_bk = next_bank()
                for k in range(8):
                    PE(lambda e, bk=bk, k=k, oc=oc, xnT=xnT: e.matmul(bk[:, :], lhsT=wxq[:, k, oc * 128:(oc + 1) * 128], rhs=xnT[:, k, :], start=(k == 0), stop=(k == 7)),
                       [r_wc, r_xnT], [r_bk])
                ACT(lambda e, bk=bk, oc=oc, qxT=qxT: e.copy(out=qxT[:, oc, :], in_=bk[:, :]), [r_bk], [r_qxT])
            oxT, r_oxT = ox_ring.next()
            for hx in range(4):
                pX, r_pX = pX_ring.next()
                for mt in range(2):
                    bk, _, r_bk = next_bank()
                    for dc in range(2):
                        PE(lambda e, bk=bk, dc=dc, mt=mt, hx=hx, qxT=qxT: e.matmul(bk[:, :], lhsT=KxT[:, hx * 2 + dc, mt * 128:(mt + 1) * 128], rhs=qxT[:, hx * 2 + dc, :],
                                                                              start=(dc == 0), stop=(dc == 1)), [r_kx, r_qxT], [r_bk])
                    ACT(lambda e, bk=bk, mt=mt, pX=pX: e.activation(out=pX[:, mt, :], in_=bk[:, :], func=AF.Exp, scale=X_SCALE), [r_bk], [r_pX])
                bk, _, r_bk = next_bank()
                for mt in range(2):
                    PE(lambda e, bk=bk, mt=mt, pX=pX: e.matmul(bk[:, :], lhsT=onesb[:, :], rhs=pX[:, mt, :], start=(mt == 0), stop=(mt == 1)), [r_pX, r_const], [r_bk])
                rc, r_rc = rc_ring.next()
                VEC(lambda e, bk=bk, rc=rc: e.reciprocal(out=rc, in_=bk[:, :]), [r_bk], [r_rc])
                for dc in range(2):
                    bk, _, r_bk = next_bank()
                    for mt in range(2):
                        PE(lambda e, bk=bk, mt=mt, dc=dc, hx=hx, pX=pX: e.matmul(bk[:, :], lhsT=Vx[:, mt, (hx * 2 + dc) * 128:(hx * 2 + dc + 1) * 128], rhs=pX[:, mt, :],
                                                                             start=(mt == 0), stop=(mt == 1)), [r_kx, r_pX], [r_bk])
                    VEC(lambda e, bk=bk, dc=dc, hx=hx, rc=rc, oxT=oxT: e.tensor_tensor(out=oxT[:, hx * 2 + dc, :], in0=bk[:, :], in1=rc, op=ALU.mult), [r_bk, r_rc], [r_oxT])
            for sub in range(4):
                ti = b * 4 + sub
                ts_ = slice(sub * 128, (sub + 1) * 128)
                h1t, r_h1t = h1s[sub]
                for hf in range(2):
                    bk, _, r_bk = next_bank()
                    cs_ = slice(hf * 512, (hf + 1) * 512)
                    for k in range(8):
                        PE(lambda e, bk=bk, k=k, oxT=oxT, ts_=ts_, cs_=cs_: e.matmul(bk[:, :], lhsT=oxT[:, k, ts_], rhs=wxo[:, k, cs_], start=(k == 0), stop=(k == 7)),
                           [r_oxT, r_wc], [r_bk])
                    VEC(lambda e, bk=bk, h1t=h1t, cs_=cs_: e.tensor_tensor(out=h1t[:, cs_], in0=bk[:, :], in1=h1t[:, cs_], op=ALU.add), [r_bk, r_h1t], [r_h1t])
                DMA(h1_d[T0 + sub * 128:T0 + (sub + 1) * 128, :], h1t[:], [r_h1t], [r_h1d[ti]])
        if stop_after == "C1":
            return True

        P.barrier()
        areset()
        wg, r_wf = aalloc([128, 8, DFF], BF16, "wg")
        wu, _ = aalloc([128, 8, DFF], BF16)
        wd, _ = aalloc([128, NFF, 1024], BF16)
        for k in range(8):
            DMAC(wg[:, k, :], w_gate_d[l, k * 128:(k + 1) * 128, :], [], [r_wf])
            DMAC(wu[:, k, :], w_up_d[l, k * 128:(k + 1) * 128, :], [], [r_wf])
        for k in range(0, NFF, 2):
            DMAC(wd[:, k:k + 2, :], w_down_d[l, k * 128:(k + 2) * 128, :].rearrange("(k p) n -> p k n", p=128), [], [r_wf])
        actT = frA[0].bitcast(BF16) if hasattr(frA[0], "bitcast") else None
        actT = actT[:, 0:NFF * 512].rearrange("p (f t) -> p f t", t=512)
        r_actT = frA[1]
        sgs = [(frB[0][:, i * 512:(i + 1) * 512], Res()) for i in range(4)]
        sg_i = 0
        last = (l == L - 1)
        for b in range(NB):
            T0 = b * 512
            xnT, r_xnT = xnT_ring.next()
            for sub in range(4):
                ti = b * 4 + sub
                ht, r_ht = ht_ring.next()
                DMA(ht[:], h1_d[T0 + sub * 128:T0 + (sub + 1) * 128, :], [r_h1d[ti]], [r_ht])
                norm_transpose(ht, r_ht, col(l, C_GFFN, 8), xnT, r_xnT, sub)
            for fc in range(NFF):
                bkg, _, r_bkg = next_bank()
                bku, _, r_bku = next_bank()
                for k in range(8):
                    PE(lambda e, bkg=bkg, k=k, fc=fc, xnT=xnT: e.matmul(bkg[:, :], lhsT=wg[:, k, fc * 128:(fc + 1) * 128], rhs=xnT[:, k, :], start=(k == 0), stop=(k == 7)),
                       [r_wf, r_xnT], [r_bkg])
                for k in range(8):
                    PE(lambda e, bku=bku, k=k, fc=fc, xnT=xnT: e.matmul(bku[:, :], lhsT=wu[:, k, fc * 128:(fc + 1) * 128], rhs=xnT[:, k, :], start=(k == 0), stop=(k == 7)),
                       [r_wf, r_xnT], [r_bku])
                sgt, r_sgt = sgs[sg_i % 4]
                sg_i += 1
                ACT(lambda e, bkg=bkg, sgt=sgt: e.activation(out=sgt, in_=bkg[:, :], func=AF.Silu), [r_bkg], [r_sgt])
                VEC(lambda e, bku=bku, sgt=sgt, fc=fc: e.tensor_tensor(out=actT[:, fc, :], in0=bku[:, :], in1=sgt, op=ALU.mult), [r_bku, r_sgt], [r_actT])
            for sub in range(4):
                ti = b * 4 + sub
                ts_ = slice(sub * 128, (sub + 1) * 128)
                ht, r_ht = ht_ring.next()
                DMA(ht[:], h1_d[T0 + sub * 128:T0 + (sub + 1) * 128, :], [r_h1d[ti]], [r_ht])
                for hf in range(2):
                    bk, _, r_bk = next_bank()
                    cs_ = slice(hf * 512, (hf + 1) * 512)
                    for fc in range(NFF):
                        PE(lambda e, bk=bk, fc=fc, ts_=ts_, cs_=cs_: e.matmul(bk[:, :], lhsT=actT[:, fc, ts_], rhs=wd[:, fc, cs_], start=(fc == 0), stop=(fc == NFF - 1)),
                           [r_actT, r_wf], [r_bk])
                    VEC(lambda e, bk=bk, ht=ht, cs_=cs_: e.tensor_tensor(out=ht[:, cs_], in0=bk[:, :], in1=ht[:, cs_], op=ALU.add), [r_bk, r_ht], [r_ht])
                if not last:
                    DMA(h_d[T0 + sub * 128:T0 + (sub + 1) * 128, :], ht[:], [r_ht], [r_hd[ti]])
                else:
                    st, r_st = st_ring.next()
                    rms_stats(ht[:], r_ht, D, st, r_st, 0)
                    VEC(lambda e, ht=ht, st=st: e.scalar_tensor_tensor(out=ht[:], in0=ht[:], scalar=st[:, 0:1], in1=fing[:], op0=ALU.mult, op1=ALU.mult),
                        [r_ht, r_st, r_const], [r_ht])
                    final_ops.append(DMA(out_d[T0 + sub * 128:T0 + (sub + 1) * 128, :], ht[:], [r_ht], []))

        return False

    for l_ in range(n_layers):
        if emit_layer(l_):
            break

    if debug:
        def dump(name, src, shape, dt, r):
            final_ops.append(DMA(dbg_out(name, shape, dt), src, [r], []))
        P.barrier()
        dump("qT", qT_d[:, :, 3584:4096], [8, 96, 512], BF16, r_qd)
        dump("kT", kT_d[:, :, 3584:4096], [8, 96, 512], BF16, r_kd)
        dump("V", V_d[:, :, 28:32, :], [8, 128, 4, 65], BF16, r_vd)
        dump("uT", uT_d[:, :, 3584:4096], [128, 4, 512], BF16, r_ud)
        dump("aT0", aT_d[0], [64, 8, 512], F32, r_ad)
        dump("aT7", aT_d[7], [64, 8, 512], F32, r_ad)
        dump("yT", yT_d[:, :, 3584:4096], [128, 4, 512], F32, r_yd)
        dump("yT0", yT_d[:, :, 0:512], [128, 4, 512], F32, r_yd)
        dump("sT7", sT_d[7], [128, 4, 512], BF16, r_sd)
        dump("sT0", sT_d[0], [128, 4, 512], BF16, r_sd)
        dump("h1", h1_d[3968:4096, :], [128, D], F32, r_h1d[31])
        dump("h", h_d[3968:4096, :], [128, D], F32, r_hd[31])
    P.barrier()
    return P.build(final_waits=final_ops), dbg


def prep_inputs(inp):
    f = np.float32
    g = lambda k: np.asarray(inp[k])
    cols = np.zeros((128, L, NCOL), f)
    for l in range(L):
        cols[:, l, C_GMIX:C_GMIX + 8] = g("norm_mix_g")[l].reshape(8, 128).T
        cols[:, l, C_GX:C_GX + 8] = g("norm_x_g")[l].reshape(8, 128).T
        cols[:, l, C_GFFN:C_GFFN + 8] = g("norm_ffn_g")[l].reshape(8, 128).T
        cols[:, l, C_GMEM:C_GMEM + 8] = g("mem_norm_g")[l].reshape(8, 128).T
        cols[:, l, C_GQ:C_GQ + 2] = g("q_norm_g")[l].reshape(2, 128).T
        cols[:, l, C_GKV] = g("kv_norm_g")[l]
        cols[:, l, C_GS:C_GS + 4] = g("ssm_out_g")[l].reshape(4, 128).T
        cols[:, l, C_D:C_D + 4] = g("ssm_d")[l].reshape(4, 128).T
        cols[:, l, C_BGLU:C_BGLU + 4] = g("ssm_b_glu")[l].reshape(4, 128).T
        cols[0:64, l, C_GA:C_GA + 8] = g("attn_out_g")[l].reshape(8, 64).T
    consts = np.zeros((128, 2), f)
    freqs = (np.float32(10000.0) ** (-np.arange(0, 32, 2, dtype=np.float32) / np.float32(32))).astype(f)
    consts[64:80, 0] = freqs
    consts[80:96, 0] = freqs
    consts[64:80, 1] = -SIN_SCALE
    consts[80:96, 1] = SIN_SCALE
    idx = np.arange(128) // 16
    mask16 = (idx[:, None] == idx[None, :]).astype(f)
    w_in = g("w_in")
    w_kr = np.zeros((L, D, 192), f)
    w_kr[:, :, 64:96] = w_in[:, :, 384:416]
    w_kr[:, :, 160:176] = w_in[:, :, 400:416]
    w_kr[:, :, 176:192] = w_in[:, :, 384:400]
    w_uq = g("w_uq")
    w_uqs = np.zeros_like(w_uq)
    for h in range(8):
        w_uqs[:, :, h * 96 + 64:h * 96 + 80] = w_uq[:, :, h * 96 + 80:h * 96 + 96]
        w_uqs[:, :, h * 96 + 80:h * 96 + 96] = w_uq[:, :, h * 96 + 64:h * 96 + 80]
    w_ukv = g("w_ukv").reshape(L, 128, 8, 2, 64).transpose(0, 1, 3, 2, 4).reshape(L, 128, 1024)

    def pair_layout(a):
        sh = a.shape
        a = a.reshape(L, 16, 2, 64, *sh[3:])
        a = np.moveaxis(a, 1, 3)
        return a.reshape(L, 128, 16, *sh[3:])

    lam_re = pair_layout(g("ssm_lambda_re"))
    lam_im = pair_layout(g("ssm_lambda_im"))
    logdt = pair_layout(np.repeat(g("ssm_log_dt")[:, :, None], 64, axis=2))
    s5v = np.stack([lam_re, lam_im, logdt], axis=2)
    b_re = pair_layout(g("ssm_b_re"))
    b_im = pair_layout(g("ssm_b_im"))
    s5b = np.stack([b_re, b_im], axis=2)
    c_re = pair_layout(np.swapaxes(g("ssm_c_re"), 2, 3))
    c_im = pair_layout(np.swapaxes(g("ssm_c_im"), 2, 3))
    s5c = np.stack([c_re, c_im], axis=2)
    common = dict(
        cols=cols, consts=consts, mask16=mask16, fing=g("final_norm_g").reshape(1, D).astype(f),
        w_in=w_in, w_kr=w_kr, w_uq=w_uq, w_uqs=w_uqs, w_ukv=np.ascontiguousarray(w_ukv),
        s5v=np.ascontiguousarray(s5v), s5b=np.ascontiguousarray(s5b), s5c=np.ascontiguousarray(s5c),
        w_glu=g("ssm_w_glu"), w_out=g("w_out"), w_xq=g("w_xq"), w_xkv=g("w_xkv"), w_xo=g("w_xo"),
        w_gate=g("w_gate"), w_up=g("w_up"), w_down=g("w_down"),
    )
    common = {k: np.ascontiguousarray(v, dtype=f) for k, v in common.items()}
    x = g("x")
    mem = g("mem")
    pos = g("positions").astype(np.int32)
    per_core = []
    for c in range(x.shape[0]):
        d = dict(common)
        d["x"] = np.ascontiguousarray(x[c], dtype=f)
        d["mem"] = np.ascontiguousarray(mem[c], dtype=f)
        d["pos"] = np.ascontiguousarray(pos[c].reshape(1, S))
        per_core.append(d)
    return per_core


def kernel(**inputs):
    per_core = prep_inputs(inputs)
    nc, _ = build_program()
    res = run_bass_kernel_spmd(nc, per_core, core_ids=list(range(8)))
    return np.stack([np.asarray(r["out"], dtype=np.float32) for r in res.results], axis=0)
```

```python
import math
import numpy as np
import concourse.bass as bass
import concourse.mybir as mybir
from concourse.bass_utils import run_bass_kernel_spmd

F32 = mybir.dt.float32
BF16 = mybir.dt.bfloat16
I32 = mybir.dt.int32
AF = mybir.ActivationFunctionType
ALU = mybir.AluOpType

ENGS = ["tensor", "vector", "scalar", "gpsimd", "sync"]

L = 2
S = 4096
D = 1024
NB = 8
EPS = 1e-6
DFF = 2816
NFF = 22
TWO_PI = 2.0 * math.pi
SIN_SCALE = 6.2831845
MAGIC = 12582912.0
ATT_SCALE = 96.0 ** -0.5
X_SCALE = 256.0 ** -0.5
NCOL = 55
C_GMIX, C_GX, C_GFFN, C_GMEM, C_GQ, C_GKV, C_GS, C_D, C_BGLU, C_GA = 0, 8, 16, 24, 32, 34, 35, 39, 43, 47


class Res:
    __slots__ = ("name", "last_w", "readers")

    def __init__(self, name=""):
        self.name = name
        self.last_w = None
        self.readers = []


class Op:
    __slots__ = ("eng", "fn", "waits", "dma", "sem", "val")


class Prog:
    def __init__(self, n_dma_sems=24):
        self.nc = bass.Bass("TRN2", target_bir_lowering=False)
        self.ops = {e: [] for e in ENGS}
        self.n_dma_sems = n_dma_sems
        self.wm = {e: {} for e in ENGS}
        self.dma_rr = {e: 0 for e in ENGS}
        self.dma_cnt = {}
        self.dma_last = {}
        self.eng_cnt = {}
        self.pending = {e: {} for e in ENGS}
        self._ctx = []

    def sbuf(self, name, shape, dtype):
        g = self.nc.sbuf_tensor("sb_" + name, list(shape), dtype)
        h = g.__enter__()
        self._ctx.append(g)
        return h

    def psum(self, name, shape, dtype):
        g = self.nc.psum_tensor("ps_" + name, list(shape), dtype)
        h = g.__enter__()
        self._ctx.append(g)
        return h

    def _need(self, op, dep):
        if dep is None:
            return
        if dep.eng == "tensor" and op.eng == "tensor" and not dep.dma and not op.dma:
            return
        if dep.val > op.waits.get(dep.sem, 0):
            op.waits[dep.sem] = dep.val

    def barrier(self):
        cur = {}
        for e, c in self.eng_cnt.items():
            cur[("eng", e)] = c
        for k, c in self.dma_cnt.items():
            cur[k] = 16 * c
        for e in ENGS:
            pe = self.pending[e]
            for k, v in cur.items():
                if v > pe.get(k, 0):
                    pe[k] = v

    def op(self, eng, fn, reads=(), writes=(), dma=False):
        o = Op()
        o.eng = eng
        o.fn = fn
        o.dma = dma
        o.waits = {}
        if self.pending[eng]:
            o.waits.update(self.pending[eng])
            self.pending[eng] = {}
        if dma:
            slot = self.dma_rr[eng] % self.n_dma_sems
            self.dma_rr[eng] += 1
            key = ("dma", eng, slot)
            prev = self.dma_last.get(key)
            cnt = self.dma_cnt.get(key, 0) + 1
            self.dma_cnt[key] = cnt
            o.sem = key
            o.val = 16 * cnt
            if prev is not None:
                self._need(o, prev)
            self.dma_last[key] = o
        else:
            o.sem = ("eng", eng)
            self.eng_cnt[eng] = self.eng_cnt.get(eng, 0) + 1
            o.val = self.eng_cnt[eng]
        for r in reads:
            self._need(o, r.last_w)
        for w in writes:
            self._need(o, w.last_w)
            for rd in w.readers:
                self._need(o, rd)
        for r in reads:
            r.readers.append(o)
        for w in writes:
            w.last_w = o
            w.readers = []
        wm = self.wm[eng]
        for k in list(o.waits):
            if wm.get(k, 0) >= o.waits[k]:
                del o.waits[k]
            else:
                wm[k] = o.waits[k]
        self.ops[eng].append(o)
        return o

    def build(self, final_waits=()):
        nc = self.nc
        sems = {}
        for e in ENGS:
            for o in self.ops[e]:
                if o.sem not in sems:
                    g = nc.semaphore("s_" + "_".join(str(x) for x in o.sem))
                    sems[o.sem] = g.__enter__()
                    self._ctx.append(g)
        fin = {}
        for o in final_waits:
            fin[o.sem] = max(fin.get(o.sem, 0), o.val)
        with nc.Block() as block:
            def make(e):
                def body(engobj):
                    for o in self.ops[e]:
                        for k, v in o.waits.items():
                            engobj.wait_ge(sems[k], v)
                        ins = o.fn(engobj)
                        ins.then_inc(sems[o.sem], 16 if o.dma else 1)
                    if e == "sync":
                        for k, v in fin.items():
                            engobj.wait_ge(sems[k], v)
                return body
            for e in ENGS:
                if self.ops[e] or e == "sync":
                    getattr(block, e)(make(e))
        return nc


class Ring:
    def __init__(self, P, name, n, shape, dtype):
        self.bufs = [(P.sbuf(f"{name}{i}", shape, dtype), Res(f"{name}{i}")) for i in range(n)]
        self.i = 0

    def next(self):
        b = self.bufs[self.i % len(self.bufs)]
        self.i += 1
        return b


def build_program(debug=False, n_layers=L, stop_after=None):
    P = Prog()
    nc = P.nc

    def din(name, shape, dt=F32):
        return nc.dram_tensor(name, list(shape), dt, kind="ExternalInput").ap()

    def dscr(name, shape, dt):
        return nc.dram_tensor(name, list(shape), dt, kind="Internal").ap()

    x_d = din("x", [S, D])
    mem_d = din("mem", [256, D])
    pos_d = din("pos", [1, S], I32)
    cols_d = din("cols", [128, L, NCOL])
    consts_d = din("consts", [128, 2])
    mask16_d = din("mask16", [128, 128])
    fing_d = din("fing", [1, D])
    w_in_d = din("w_in", [L, D, 928])
    w_kr_d = din("w_kr", [L, D, 192])
    w_uq_d = din("w_uq", [L, 256, 768])
    w_uqs_d = din("w_uqs", [L, 256, 768])
    w_ukv_d = din("w_ukv", [L, 128, 1024])
    s5v_d = din("s5v", [L, 128, 3, 16])
    s5b_d = din("s5b", [L, 128, 2, 16, 16])
    s5c_d = din("s5c", [L, 128, 2, 16, 16])
    w_glu_d = din("w_glu", [L, 512, 512])
    w_out_d = din("w_out", [L, D, D])
    w_xq_d = din("w_xq", [L, D, D])
    w_xkv_d = din("w_xkv", [L, D, 2 * D])
    w_xo_d = din("w_xo", [L, D, D])
    w_gate_d = din("w_gate", [L, D, DFF])
    w_up_d = din("w_up", [L, D, DFF])
    w_down_d = din("w_down", [L, DFF, D])
    out_d = nc.dram_tensor("out", [S, D], F32, kind="ExternalOutput").ap()

    h_d = dscr("h_scr", [S, D], F32)
    h1_d = dscr("h1_scr", [S, D], F32)
    qT_d = dscr("qT_scr", [8, 96, S], BF16)
    kT_d = dscr("kT_scr", [8, 96, S], BF16)
    V_d = dscr("V_scr", [8, 128, 32, 65], BF16)
    uT_d = dscr("uT_scr", [128, 4, S], BF16)
    yT_d = dscr("yT_scr", [128, 4, S], F32)
    aT_d = dscr("aT_scr", [NB, 64, 8, 512], F32)
    sT_d = dscr("sT_scr", [NB, 128, 4, 512], BF16)
    cos_d = dscr("cos_scr", [32, S], F32)
    sin_d = dscr("sin_scr", [32, S], F32)
    r_hd = [Res() for _ in range(32)]
    r_h1d = [Res() for _ in range(32)]
    r_qd, r_kd, r_vd, r_ud, r_yd = Res(), Res(), Res(), Res(), Res()
    r_ad, r_sd, r_csd = Res(), Res(), Res()

    dbg = {}

    def dbg_out(name, shape, dt=F32):
        t = nc.dram_tensor("dbg_" + name, list(shape), dt, kind="ExternalOutput").ap()
        dbg[name] = t
        return t

    final_ops = []

    def VEC(fn, reads, writes):
        return P.op("vector", fn, reads, writes)

    def ACT(fn, reads, writes):
        return P.op("scalar", fn, reads, writes)

    def POOL(fn, reads, writes):
        return P.op("gpsimd", fn, reads, writes)

    def PE(fn, reads, writes):
        return P.op("tensor", fn, reads, writes)

    def DMA(out, in_, reads, writes, q="sync"):
        return P.op(q, lambda e: e.dma_start(out=out, in_=in_), reads, writes, dma=True)

    def DMAC(out, in_, reads, writes):
        return P.op("gpsimd", lambda e: e.dma_start(out=out, in_=in_), reads, writes, dma=True)

    banks = []
    for i in range(8):
        t = P.psum(f"bank{i}", [128, 512], F32)
        banks.append((t, t.bitcast(BF16), Res(f"bank{i}")))
    bank_rr = [0]

    def next_bank(pool=(0, 1, 2, 3, 4, 5, 6, 7)):
        i = pool[bank_rr[0] % len(pool)]
        bank_rr[0] += 1
        return banks[i]

    identf = P.sbuf("identf", [128, 128], F32)
    ident = P.sbuf("ident", [128, 128], BF16)
    onesf = P.sbuf("onesf", [128, 128], F32)
    sel65 = P.sbuf("sel65", [65, 64], F32)
    onesb = P.sbuf("onesb", [128, 128], BF16)
    fing = P.sbuf("fing", [128, D], F32)
    mask16 = P.sbuf("mask16", [128, 128], F32)
    cols = P.sbuf("cols", [128, L, NCOL], F32)
    consts = P.sbuf("consts", [128, 2], F32)
    r_const = Res("const")
    POOL(lambda e: e.memset(identf[:], 1.0), [], [r_const])
    POOL(lambda e: e.affine_select(out=identf[:], in_=identf[:], pattern=[[-1, 128]], compare_op=ALU.is_equal,
                                   fill=0.0, base=0, channel_multiplier=1), [r_const], [r_const])
    VEC(lambda e: e.tensor_copy(out=ident[:], in_=identf[:]), [r_const], [r_const])
    VEC(lambda e: e.memset(onesf[:], 1.0), [], [r_const])
    VEC(lambda e: e.memset(onesb[:], 1.0), [], [r_const])
    DMA(fing[:], fing_d[0:1, :].to_broadcast([128, D]), [], [r_const])
    VEC(lambda e: e.memset(sel65[:], 0.0), [], [r_const])
    VEC(lambda e: e.memset(sel65[64:65, :], 1.0), [r_const], [r_const])
    DMA(mask16[:], mask16_d[:, :], [], [r_const])
    DMA(cols[:], cols_d[:, :, :], [], [r_const])
    DMA(consts[:], consts_d[:, :], [], [r_const])

    def col(l, c, n=1, p0=0, p1=128):
        return cols[p0:p1, l, c:c + n]

    ARENA_F32 = 33792
    arena_f = P.sbuf("arena", [128, ARENA_F32], F32)
    arena_b = arena_f.bitcast(BF16)
    arena_i = arena_f.bitcast(I32)
    arena_off = [0]

    def areset():
        arena_off[0] = 0

    def aalloc(shape, dt, name=""):
        n = 1
        for s_ in shape[1:]:
            n *= s_
        esz = 2 if dt == BF16 else 4
        off = (arena_off[0] + 3) // 4 * 4
        arena_off[0] = off + n * esz
        assert arena_off[0] <= ARENA_F32 * 4, (name, arena_off[0])
        base = {BF16: arena_b, F32: arena_f, I32: arena_i}[dt]
        e0 = off // esz
        ap = base[0:shape[0], e0:e0 + n]
        if len(shape) == 3:
            ap = ap.rearrange("p (a b) -> p a b", b=shape[2])
        elif len(shape) == 4:
            ap = ap.rearrange("p (a b c) -> p a b c", b=shape[2], c=shape[3])
        elif len(shape) == 5:
            ap = ap.rearrange("p (a b c d) -> p a b c d", b=shape[2], c=shape[3], d=shape[4])
        return ap, Res(name)

    class ARing:
        def __init__(self, n, shape, dt, name=""):
            self.bufs = [aalloc(shape, dt, f"{name}{i}") for i in range(n)]
            self.i = 0

        def next(self):
            b = self.bufs[self.i % len(self.bufs)]
            self.i += 1
            return b

    ht_ring = Ring(P, "ht", 2, [128, D], F32)
    junk_ring = Ring(P, "junk", 1, [128, D], BF16)
    xn_ring = Ring(P, "xn", 2, [128, D], BF16)
    xnT_ring = Ring(P, "xnT", 2, [128, 8, 512], BF16)
    st_ring = Ring(P, "st", 8, [128, 4], F32)
    pT_ring = Ring(P, "pT", 4, [128, 512], BF16)
    frA = (P.sbuf("frA", [128, 5632], F32), Res("frA"))
    frB = (P.sbuf("frB", [128, 2048], F32), Res("frB"))

    def rms_stats(src_ap, r_src, nfeat, st, r_st, c0):
        junk, r_junk = junk_ring.next()
        n = src_ap.shape[-1]
        ACT(lambda e: e.activation(out=junk[:, 0:n], in_=src_ap, func=AF.Square, accum_out=st[:, c0:c0 + 1]), [r_src], [r_junk, r_st])
        ACT(lambda e: e.activation(out=st[:, c0:c0 + 1], in_=st[:, c0:c0 + 1], func=AF.Sqrt, scale=1.0 / nfeat, bias=EPS), [r_st], [r_st])
        VEC(lambda e: e.reciprocal(out=st[:, c0:c0 + 1], in_=st[:, c0:c0 + 1]), [r_st], [r_st])

    def norm_transpose(ht, r_ht, gcol, xnT, r_xnT, sub):
        st, r_st = st_ring.next()
        rms_stats(ht[:], r_ht, D, st, r_st, 0)
        xn, r_xn = xn_ring.next()
        VEC(lambda e: e.tensor_scalar(out=xn[:], in0=ht[:], scalar1=st[:, 0:1], scalar2=None, op0=ALU.mult), [r_ht, r_st], [r_xn])
        bk, bkb, r_bk = next_bank()
        bkv = bkb[:, :].rearrange("p (a b) -> p a b", b=128)
        for k in range(8):
            PE(lambda e, k=k: e.transpose(out=bkv[:, k, :], in_=xn[:, k * 128:(k + 1) * 128], identity=ident[:]), [r_xn, r_const], [r_bk])
        VEC(lambda e: e.tensor_tensor(out=xnT[:, :, sub * 128:(sub + 1) * 128], in0=bkv, in1=gcol.to_broadcast([128, 8, 128]), op=ALU.mult),
            [r_bk, r_const], [r_xnT])

    RS = slice(64, 96)

    areset()
    posi, r_ra = aalloc([96, S], I32, "posi")
    tmpa, _ = aalloc([96, S], F32, "tmpa")
    tmpb, _ = aalloc([96, S], F32, "tmpb")
    cs_t, r_cs = aalloc([96, S], F32, "cs")
    sn_t, _ = aalloc([96, S], F32, "sn")
    DMA(posi[RS, :], pos_d[0:1, :].to_broadcast([32, S]), [], [r_ra])
    VEC(lambda e: e.tensor_copy(out=tmpa[RS, :], in_=posi[RS, :]), [r_ra], [r_ra])
    VEC(lambda e: e.tensor_scalar(out=tmpa[RS, :], in0=tmpa[RS, :], scalar1=consts[RS, 0:1], scalar2=1.0 / TWO_PI,
                                  op0=ALU.mult, op1=ALU.mult), [r_ra, r_const], [r_ra])
    VEC(lambda e: e.tensor_scalar(out=tmpb[RS, :], in0=tmpa[RS, :], scalar1=MAGIC, scalar2=MAGIC, op0=ALU.add, op1=ALU.subtract), [r_ra], [r_ra])
    VEC(lambda e: e.tensor_tensor(out=tmpb[RS, :], in0=tmpa[RS, :], in1=tmpb[RS, :], op=ALU.subtract), [r_ra], [r_ra])
    ACT(lambda e: e.activation(out=sn_t[RS, :], in_=tmpb[RS, :], func=AF.Sin, scale=consts[RS, 1:2]), [r_ra, r_const], [r_cs])
    VEC(lambda e: e.tensor_scalar(out=tmpa[RS, :], in0=tmpa[RS, :], scalar1=0.25, scalar2=None, op0=ALU.add), [r_ra, r_cs], [r_ra])
    VEC(lambda e: e.tensor_scalar(out=tmpb[RS, :], in0=tmpa[RS, :], scalar1=MAGIC, scalar2=MAGIC, op0=ALU.add, op1=ALU.subtract), [r_ra], [r_ra])
    VEC(lambda e: e.tensor_tensor(out=tmpb[RS, :], in0=tmpa[RS, :], in1=tmpb[RS, :], op=ALU.subtract), [r_ra], [r_ra])
    ACT(lambda e: e.activation(out=cs_t[RS, :], in_=tmpb[RS, :], func=AF.Sin, scale=SIN_SCALE), [r_ra], [r_cs])
    DMA(cos_d[:, :], cs_t[RS, :], [r_cs], [r_csd])
    DMA(sin_d[:, :], sn_t[RS, :], [r_cs], [r_csd])

    def emit_layer(l):
        src_d = x_d if l == 0 else h_d
        r_src = [Res() for _ in range(32)] if l == 0 else r_hd
        P.barrier()
        areset()
        w_in_sb, r_w = aalloc([128, 8, 928], BF16, "w_in")
        w_kr_sb, _ = aalloc([128, 8, 192], BF16)
        w_uq_sb, _ = aalloc([128, 2, 768], BF16)
        w_uqs_sb, _ = aalloc([128, 2, 768], BF16)
        w_ukv_sb, _ = aalloc([128, 1024], BF16)
        DMAC(w_in_sb, w_in_d[l].rearrange("(k p) n -> p k n", p=128), [], [r_w])
        DMAC(w_kr_sb, w_kr_d[l].rearrange("(k p) n -> p k n", p=128), [], [r_w])
        DMAC(w_uq_sb, w_uq_d[l].rearrange("(k p) n -> p k n", p=128), [], [r_w])
        DMAC(w_uqs_sb, w_uqs_d[l].rearrange("(k p) n -> p k n", p=128), [], [r_w])
        DMAC(w_ukv_sb, w_ukv_d[l], [], [r_w])
        qs, r_qs = aalloc([96, 8, 512], F32, "qs")
        qsw, r_qsw = aalloc([96, 8, 512], F32, "qsw")
        qb_ring = ARing(2, [96, 8, 512], BF16, "qb")
        kb_ring = ARing(2, [96, 8, 512], BF16, "kb")
        vb_ring = ARing(2, [128, 8, 4, 65], BF16, "vb")
        ub_ring = ARing(2, [128, 4, 512], BF16, "ub")
        cT_ring = ARing(2, [128, 3, 512], BF16, "cT")
        cs_ring = ARing(2, [96, 2, 512], F32, "csb")
        ta, r_ta = aalloc([96, 1024], F32, "ta")
        for vb, r_vb in vb_ring.bufs:
            VEC(lambda e, vb=vb: e.memset(vb[:, :, :, 64:65], 1.0), [], [r_vb])

        ctxA = {}

        def stageA1(b):
            T0 = b * 512
            xnT, r_xnT = xnT_ring.next()
            cT, r_cT = cT_ring.next()
            vb, r_vb = vb_ring.next()
            csb, r_csb = cs_ring.next()
            ctxA[b] = (xnT, r_xnT, cT, r_cT, csb, r_csb)
            DMA(csb[RS, 0, :], cos_d[:, T0:T0 + 512], [r_csd], [r_csb])
            DMA(csb[RS, 1, :], sin_d[:, T0:T0 + 512], [r_csd], [r_csb])
            for sub in range(4):
                t0 = T0 + sub * 128
                ti = b * 4 + sub
                ht, r_ht = ht_ring.next()
                DMA(ht[:], src_d[t0:t0 + 128, :], [r_src[ti]], [r_ht])
                norm_transpose(ht, r_ht, col(l, C_GMIX, 8), xnT, r_xnT, sub)
                yield
                bk, bkb, r_bk = next_bank()
                for k in range(8):
                    PE(lambda e, k=k, bk=bk, xnT=xnT, sub=sub: e.matmul(bk[:, 0:384], lhsT=xnT[:, k, sub * 128:(sub + 1) * 128],
                                                                       rhs=w_in_sb[:, k, 0:384], start=(k == 0), stop=(k == 7)),
                       [r_xnT, r_w], [r_bk])
                st, r_st = st_ring.next()
                rms_stats(bk[:, 0:256], r_bk, 256, st, r_st, 0)
                rms_stats(bk[:, 256:384], r_bk, 128, st, r_st, 1)
                yield
                cn, r_cn = xn_ring.next()
                VEC(lambda e, bk=bk, cn=cn, st=st: e.tensor_scalar(out=cn[:, 0:256], in0=bk[:, 0:256], scalar1=st[:, 0:1], scalar2=None, op0=ALU.mult),
                    [r_bk, r_st], [r_cn])
                VEC(lambda e, bk=bk, cn=cn, st=st: e.tensor_scalar(out=cn[:, 256:384], in0=bk[:, 256:384], scalar1=st[:, 1:2], scalar2=None, op0=ALU.mult),
                    [r_bk, r_st], [r_cn])
                bk2, bk2b, r_bk2 = next_bank()
                bk2v = bk2b[:, :].rearrange("p (a b) -> p a b", b=128)
                for k in range(3):
                    PE(lambda e, k=k, bk2v=bk2v, cn=cn: e.transpose(out=bk2v[:, k, :], in_=cn[:, k * 128:(k + 1) * 128], identity=ident[:]),
                       [r_cn, r_const], [r_bk2])
                VEC(lambda e, bk2v=bk2v, cT=cT, sub=sub: e.tensor_tensor(out=cT[:, :, sub * 128:(sub + 1) * 128], in0=bk2v[:, 0:3, :],
                                                                       in1=col(l, C_GQ, 3).to_broadcast([128, 3, 128]), op=ALU.mult),
                    [r_bk2, r_const], [r_cT])
                yield
                bk3, _, r_bk3 = next_bank()
                PE(lambda e, bk3=bk3, cT=cT, sub=sub: e.matmul(bk3[:, :], lhsT=cT[:, 2, sub * 128:(sub + 1) * 128], rhs=w_ukv_sb[:, 512:1024],
                                                            start=True, stop=True), [r_cT, r_w], [r_bk3])
                ACT(lambda e, bk3=bk3, vb=vb, sub=sub: e.copy(out=vb[:, :, sub, 0:64], in_=bk3[:, :].rearrange("p (h d) -> p h d", d=64)),
                    [r_bk3], [r_vb])
                yield
            DMA(V_d[:, :, b * 4:(b + 1) * 4, :].rearrange("h p t d -> p h t d"), vb, [r_vb], [r_vd], q="scalar")
            yield

        def stageA2(b):
            T0 = b * 512
            xnT, r_xnT, cT, r_cT, csb, r_csb = ctxA.pop(b)
            ub, r_ub = ub_ring.next()
            for ct in range(4):
                yield
                bk, _, r_bk = next_bank()
                for k in range(8):
                    PE(lambda e, k=k, bk=bk, xnT=xnT, ct=ct: e.matmul(bk[:, :], lhsT=w_in_sb[:, k, 416 + ct * 128:416 + (ct + 1) * 128],
                                                                      rhs=xnT[:, k, :], start=(k == 0), stop=(k == 7)), [r_xnT, r_w], [r_bk])
                ACT(lambda e, bk=bk, ct=ct, ub=ub: e.copy(out=ub[:, ct, :], in_=bk[:, :]), [r_bk], [r_ub])
            DMA(uT_d[:, :, T0:T0 + 512], ub, [r_ub], [r_ud], q="scalar")
            yield
            bka, _, r_bka = next_bank()
            bkb_, _, r_bkb = next_bank()
            for k in range(8):
                PE(lambda e, k=k, bka=bka, xnT=xnT: e.matmul(bka[0:96, :], lhsT=w_kr_sb[:, k, 0:96], rhs=xnT[:, k, :], start=(k == 0), stop=(k == 7)),
                   [r_xnT, r_w], [r_bka])
            for k in range(8):
                PE(lambda e, k=k, bkb_=bkb_, xnT=xnT: e.matmul(bkb_[0:96, :], lhsT=w_kr_sb[:, k, 96:192], rhs=xnT[:, k, :], start=(k == 0), stop=(k == 7)),
                   [r_xnT, r_w], [r_bkb])
            yield
            VEC(lambda e, bka=bka, csb=csb: e.tensor_tensor(out=ta[RS, 0:512], in0=bka[RS, :], in1=csb[RS, 0, :], op=ALU.mult), [r_bka, r_csb], [r_ta])
            VEC(lambda e, bkb_=bkb_, csb=csb: e.tensor_tensor(out=ta[RS, 512:1024], in0=bkb_[RS, :], in1=csb[RS, 1, :], op=ALU.mult), [r_bkb, r_csb], [r_ta])
            VEC(lambda e: e.tensor_tensor(out=ta[RS, 0:512], in0=ta[RS, 0:512], in1=ta[RS, 512:1024], op=ALU.add), [r_ta], [r_ta])
            kb, r_kb = kb_ring.next()
            POOL(lambda e, kb=kb: e.tensor_copy(out=kb[RS, :, :], in_=ta[RS, 0:512].rearrange("p (o t) -> p o t", o=1).to_broadcast([32, 8, 512])),
                 [r_ta], [r_kb])
            for h in range(8):
                yield
                bk, _, r_bk = next_bank()
                PE(lambda e, bk=bk, h=h, cT=cT: e.matmul(bk[0:64, :], lhsT=w_ukv_sb[:, h * 64:(h + 1) * 64], rhs=cT[:, 2, :], start=True, stop=True),
                   [r_cT, r_w], [r_bk])
                ACT(lambda e, bk=bk, h=h, kb=kb: e.copy(out=kb[0:64, h, :], in_=bk[0:64, :]), [r_bk], [r_kb])
            DMA(kT_d[:, :, T0:T0 + 512].rearrange("h p t -> p h t"), kb, [r_kb], [r_kd], q="scalar")
            for h in range(8):
                yield
                bka, _, r_bka = next_bank()
                bkb_, _, r_bkb = next_bank()
                for k in range(2):
                    PE(lambda e, k=k, bka=bka, h=h, cT=cT: e.matmul(bka[0:96, :], lhsT=w_uq_sb[:, k, h * 96:(h + 1) * 96], rhs=cT[:, k, :],
                                                                    start=(k == 0), stop=(k == 1)), [r_cT, r_w], [r_bka])
                for k in range(2):
                    PE(lambda e, k=k, bkb_=bkb_, h=h, cT=cT: e.matmul(bkb_[0:96, :], lhsT=w_uqs_sb[:, k, h * 96:(h + 1) * 96], rhs=cT[:, k, :],
                                                                      start=(k == 0), stop=(k == 1)), [r_cT, r_w], [r_bkb])
                ACT(lambda e, bka=bka, h=h: e.copy(out=qs[:, h, :], in_=bka[0:96, :]), [r_bka], [r_qs])
                ACT(lambda e, bkb_=bkb_, h=h: e.copy(out=qsw[RS, h, :], in_=bkb_[RS, :]), [r_bkb], [r_qsw])
            yield
            qb, r_qb = qb_ring.next()
            POOL(lambda e, qb=qb: e.tensor_copy(out=qb[0:64, :, :], in_=qs[0:64, :, :]), [r_qs], [r_qb])
            VEC(lambda e, csb=csb: e.tensor_tensor(out=qs[RS, :, :], in0=qs[RS, :, :],
                                                   in1=csb[RS, 0:1, :].to_broadcast([32, 8, 512]), op=ALU.mult), [r_qs, r_csb], [r_qs])
            VEC(lambda e, csb=csb: e.tensor_tensor(out=qsw[RS, :, :], in0=qsw[RS, :, :],
                                                   in1=csb[RS, 1:2, :].to_broadcast([32, 8, 512]), op=ALU.mult), [r_qsw, r_csb], [r_qsw])
            VEC(lambda e, qb=qb: e.tensor_tensor(out=qb[RS, :, :], in0=qs[RS, :, :], in1=qsw[RS, :, :], op=ALU.add), [r_qs, r_qsw], [r_qb])
            DMA(qT_d[:, :, T0:T0 + 512].rearrange("h p t -> p h t"), qb, [r_qb], [r_qd])
            yield

        def run_interleaved(gens):
            gens = list(gens)
            while gens:
                for g_ in list(gens):
                    try:
                        next(g_)
                    except StopIteration:
                        gens.remove(g_)

        for b in range(NB + 1):
            gl = []
            if b >= 1:
                gl.append(stageA2(b - 1))
            if b < NB:
                gl.append(stageA1(b))
            run_interleaved(gl)

        if stop_after == "A":
            return True

        P.barrier()
        areset()
        S_POOL = (0, 1, 2)
        O_POOL = (3, 4)
        M_POOL = (5, 6, 7)
        qh_ring = ARing(2, [96, S], BF16, "qh")
        kh_ring = ARing(2, [96, S], BF16, "kh")
        vh_ring = ARing(2, [128, 32, 65], BF16, "vh")
        oT_ring = ARing(3, [65, 1024], F32, "oT3")
        an_ring = ARing(3, [64, 512], F32, "an3")
        LA = 2

        def load_head(h):
            qh, r_qh = qh_ring.next()
            kh, r_kh = kh_ring.next()
            vh, r_vh = vh_ring.next()
            DMA(qh, qT_d[h], [r_qd], [r_qh])
            DMA(kh, kT_d[h], [r_kd], [r_kh])
            DMA(vh, V_d[h], [r_vd], [r_vh])
            return (qh, r_qh, kh, r_kh, vh, r_vh)

        heads = {0: load_head(0)}
        for h in range(8):
            qh, r_qh, kh, r_kh, vh, r_vh = heads[h]
            if h + 1 < 8:
                heads[h + 1] = load_head(h + 1)
            items = [(b, kt) for b in range(NB) for kt in range(4 * (b + 1))]
            pts = {}
            bos = {}
            deferred = []

            def stage1(i):
                b, kt = items[i]
                T0 = b * 512
                bs, _, r_bs = next_bank(S_POOL)
                PE(lambda e, bs=bs, kt=kt, kh=kh, qh=qh, T0=T0: e.matmul(bs[:, :], lhsT=kh[:, kt * 128:(kt + 1) * 128], rhs=qh[:, T0:T0 + 512],
                                                                         start=True, stop=True), [r_kh, r_qh], [r_bs])
                pT, r_pT = pT_ring.next()
                ACT(lambda e, bs=bs, pT=pT: e.activation(out=pT[:], in_=bs[:, :], func=AF.Exp, scale=ATT_SCALE), [r_bs], [r_pT])
                if kt >= 4 * b:
                    base = T0 - kt * 128
                    POOL(lambda e, pT=pT, base=base: e.affine_select(out=pT[:], in_=pT[:], pattern=[[1, 512]], compare_op=ALU.is_ge,
                                                                     fill=0.0, base=base, channel_multiplier=-1), [r_pT], [r_pT])
                pts[i] = (pT, r_pT)

            def stage2(j, i_now):
                b, kt = items[j]
                nkt = 4 * (b + 1)
                if kt == 0:
                    bos[b] = next_bank(O_POOL)
                bo, _, r_bo = bos[b]
                pT, r_pT = pts.pop(j)
                PE(lambda e, bo=bo, pT=pT, kt=kt, nkt=nkt, vh=vh: e.matmul(bo[0:65, :], lhsT=vh[:, kt, :], rhs=pT[:],
                                                                           start=(kt == 0), stop=(kt == nkt - 1)), [r_vh, r_pT], [r_bo])
                if kt == nkt - 1:
                    oT, r_oT = oT_ring.next()
                    VEC(lambda e, bo=bo, oT=oT: e.tensor_copy(out=oT[:, 0:512], in_=bo[0:65, :]), [r_bo], [r_oT])

                    def epi(b=b, oT=oT, r_oT=r_oT):
                        bm, _, r_bm = next_bank(M_POOL)
                        PE(lambda e, bm=bm, oT=oT: e.matmul(bm[0:64, :], lhsT=sel65[:, :], rhs=oT[:, 0:512], start=True, stop=True), [r_oT, r_const], [r_bm])
                        VEC(lambda e, bm=bm, oT=oT: e.reciprocal(out=oT[0:64, 512:1024], in_=bm[0:64, :]), [r_bm, r_oT], [r_oT])
                        an, r_an = an_ring.next()
                        POOL(lambda e, oT=oT, an=an: e.tensor_tensor(out=an[:, :], in0=oT[0:64, 0:512], in1=oT[0:64, 512:1024], op=ALU.mult), [r_oT], [r_an])
                        DMA(aT_d[b, :, h, :], an, [r_an], [r_ad], q="gpsimd")
                    deferred.append((i_now + 3, epi))

            n_it = len(items)
            for i in range(n_it + LA):
                if i < n_it:
                    stage1(i)
                if i - LA >= 0:
                    stage2(i - LA, i)
                while deferred and deferred[0][0] <= i:
                    deferred.pop(0)[1]()
            while deferred:
                deferred.pop(0)[1]()
        if stop_after == "B1":
            return True
        P.barrier()
        areset()
        Wst, r_Wst = aalloc([128, 4, 8, 2, 128], BF16, "Wst")
        Wfir, r_Wfir = aalloc([128, 4, 8, 128], BF16, "Wfir")
        Wo_r, r_Wo = aalloc([128, 16, 8, 32], BF16, "Wo")
        Wo_i, _ = aalloc([128, 16, 8, 32], BF16)
        sm, r_sm = aalloc([128, 32, 16], F32, "sm")
        pw_r, _ = aalloc([128, 16, 9], F32)
        pw_i, _ = aalloc([128, 16, 9], F32)
        ph_r, _ = aalloc([128, 16, 9], F32)
        ph_i, _ = aalloc([128, 16, 9], F32)
        mark = arena_off[0]
        Bri, r_bc = aalloc([128, 2, 16, 16], F32, "Bri")
        Cri, _ = aalloc([128, 2, 16, 16], F32)
        Bb_r, r_T = aalloc([128, 16, 16], F32, "T")
        Bb_i, _ = aalloc([128, 16, 16], F32)
        T1, _ = aalloc([128, 16, 16], F32)
        T2, _ = aalloc([128, 16, 16], F32)
        T3, _ = aalloc([128, 16, 16], F32)
        T4, _ = aalloc([128, 16, 16], F32)
        ME_r, r_ME = aalloc([128, 8, 4, 128], F32, "ME")
        ME_i, _ = aalloc([128, 8, 4, 128], F32)
        MF_r, r_MF = aalloc([128, 4, 128], F32, "MF")
        MF_in, _ = aalloc([128, 4, 128], F32)
        tmpF, r_tmpF = aalloc([128, 128], F32, "tmpF")
        DMA(sm[:, 0:3, :], s5v_d[l], [], [r_sm])
        DMA(Bri, s5b_d[l], [], [r_bc])
        DMA(Cri, s5c_d[l], [], [r_bc])
        POOL(lambda e: e.memset(ME_r, 0.0), [], [r_ME])
        POOL(lambda e: e.memset(ME_i, 0.0), [], [r_ME])
        POOL(lambda e: e.memset(MF_r, 0.0), [], [r_MF])
        POOL(lambda e: e.memset(MF_in, 0.0), [], [r_MF])
        POOL(lambda e: e.memset(Wo_r, 0.0), [], [r_Wo])
        POOL(lambda e: e.memset(Wo_i, 0.0), [], [r_Wo])
        LRE, LIM, LDT, DT, TT_, ER, Y, RND, FR, SN_, CS_, AR, AI, ARM1, DEN, RDEN, CBR, CBI, U1, U2, RDEC, RR = range(22)

        def smtt(o, a, b, op):
            VEC(lambda e: e.tensor_tensor(out=sm[:, o, :], in0=sm[:, a, :], in1=sm[:, b, :], op=op), [r_sm], [r_sm])

        def smts(o, a, s1, op0, s2=None, op1=None):
            if op1 is None:
                VEC(lambda e: e.tensor_scalar(out=sm[:, o, :], in0=sm[:, a, :], scalar1=s1, scalar2=None, op0=op0), [r_sm], [r_sm])
            else:
                VEC(lambda e: e.tensor_scalar(out=sm[:, o, :], in0=sm[:, a, :], scalar1=s1, scalar2=s2, op0=op0, op1=op1), [r_sm], [r_sm])

        def smact(o, a, func, scale=1.0):
            ACT(lambda e: e.activation(out=sm[:, o, :], in_=sm[:, a, :], func=func, scale=scale), [r_sm], [r_sm])

        smact(DT, LDT, AF.Exp)
        smtt(TT_, LRE, DT, ALU.mult)
        smact(ER, TT_, AF.Exp)
        smact(RDEC, TT_, AF.Exp, 8.0)
        smtt(Y, LIM, DT, ALU.mult)
        smts(Y, Y, 1.0 / TWO_PI, ALU.mult)
        smts(RND, Y, MAGIC, ALU.add, MAGIC, ALU.subtract)
        smtt(FR, Y, RND, ALU.subtract)
        smact(SN_, FR, AF.Sin, SIN_SCALE)
        smts(Y, Y, 0.25, ALU.add)
        smts(RND, Y, MAGIC, ALU.add, MAGIC, ALU.subtract)
        smtt(FR, Y, RND, ALU.subtract)
        smact(CS_, FR, AF.Sin, SIN_SCALE)
        smtt(AR, ER, CS_, ALU.mult)
        smtt(AI, ER, SN_, ALU.mult)
        smts(ARM1, AR, -1.0, ALU.add)
        smtt(U1, LRE, LRE, ALU.mult)
        smtt(U2, LIM, LIM, ALU.mult)
        smtt(DEN, U1, U2, ALU.add)
        VEC(lambda e: e.reciprocal(out=sm[:, RDEN, :], in_=sm[:, DEN, :]), [r_sm], [r_sm])
        smtt(U1, ARM1, LRE, ALU.mult)
        smtt(U2, AI, LIM, ALU.mult)
        smtt(U1, U1, U2, ALU.add)
        smtt(CBR, U1, RDEN, ALU.mult)
        smtt(U1, AI, LRE, ALU.mult)
        smtt(U2, ARM1, LIM, ALU.mult)
        smtt(U1, U1, U2, ALU.subtract)
        smtt(CBI, U1, RDEN, ALU.mult)
        VEC(lambda e: e.memset(pw_r[:, :, 0:1], 1.0), [r_sm], [r_sm])
        VEC(lambda e: e.memset(pw_i[:, :, 0:1], 0.0), [r_sm], [r_sm])
        VEC(lambda e: e.tensor_copy(out=pw_r[:, :, 1], in_=sm[:, AR, :]), [r_sm], [r_sm])
        VEC(lambda e: e.tensor_copy(out=pw_i[:, :, 1], in_=sm[:, AI, :]), [r_sm], [r_sm])
        for k in range(1, 8):
            VEC(lambda e, k=k: e.tensor_tensor(out=sm[:, U1, :], in0=pw_r[:, :, k], in1=sm[:, AR, :], op=ALU.mult), [r_sm], [r_sm])
            VEC(lambda e, k=k: e.tensor_tensor(out=sm[:, U2, :], in0=pw_i[:, :, k], in1=sm[:, AI, :], op=ALU.mult), [r_sm], [r_sm])
            VEC(lambda e, k=k: e.tensor_tensor(out=pw_r[:, :, k + 1], in0=sm[:, U1, :], in1=sm[:, U2, :], op=ALU.subtract), [r_sm], [r_sm])
            VEC(lambda e, k=k: e.tensor_tensor(out=sm[:, U1, :], in0=pw_r[:, :, k], in1=sm[:, AI, :], op=ALU.mult), [r_sm], [r_sm])
            VEC(lambda e, k=k: e.tensor_tensor(out=sm[:, U2, :], in0=pw_i[:, :, k], in1=sm[:, AR, :], op=ALU.mult), [r_sm], [r_sm])
            VEC(lambda e, k=k: e.tensor_tensor(out=pw_i[:, :, k + 1], in0=sm[:, U1, :], in1=sm[:, U2, :], op=ALU.add), [r_sm], [r_sm])
        VEC(lambda e: e.reciprocal(out=sm[:, RR, :], in_=sm[:, RDEC, :]), [r_sm], [r_sm])
        VEC(lambda e: e.tensor_tensor(out=ph_r[:, :, 0], in0=pw_r[:, :, 8], in1=sm[:, RR, :], op=ALU.mult), [r_sm], [r_sm])
        VEC(lambda e: e.tensor_tensor(out=ph_i[:, :, 0], in0=pw_i[:, :, 8], in1=sm[:, RR, :], op=ALU.mult), [r_sm], [r_sm])
        for k in range(8):
            VEC(lambda e, k=k: e.tensor_tensor(out=sm[:, U1, :], in0=ph_r[:, :, k], in1=ph_r[:, :, k], op=ALU.mult), [r_sm], [r_sm])
            VEC(lambda e, k=k: e.tensor_tensor(out=sm[:, U2, :], in0=ph_i[:, :, k], in1=ph_i[:, :, k], op=ALU.mult), [r_sm], [r_sm])
            VEC(lambda e, k=k: e.tensor_tensor(out=ph_r[:, :, k + 1], in0=sm[:, U1, :], in1=sm[:, U2, :], op=ALU.subtract), [r_sm], [r_sm])
            VEC(lambda e, k=k: e.tensor_tensor(out=sm[:, U1, :], in0=ph_r[:, :, k], in1=ph_i[:, :, k], op=ALU.mult), [r_sm], [r_sm])
            VEC(lambda e, k=k: e.tensor_scalar(out=ph_i[:, :, k + 1], in0=sm[:, U1, :], scalar1=2.0, scalar2=None, op0=ALU.mult), [r_sm], [r_sm])

        def bc16(tile_idx_ap):
            return tile_idx_ap.rearrange("p (a o) -> p a o", o=1).to_broadcast([128, 16, 16])

        def cmul_bc(outr, outi, xr, xi, sr_ap, si_ap, rds, wrs):
            pass

        VEC(lambda e: e.tensor_tensor(out=T1, in0=Bri[:, 0], in1=bc16(sm[:, CBR, :]), op=ALU.mult), [r_sm, r_bc], [r_T])
        VEC(lambda e: e.tensor_tensor(out=T2, in0=Bri[:, 1], in1=bc16(sm[:, CBI, :]), op=ALU.mult), [r_sm, r_bc], [r_T])
        VEC(lambda e: e.tensor_tensor(out=Bb_r, in0=T1, in1=T2, op=ALU.subtract), [r_T], [r_T])
        VEC(lambda e: e.tensor_tensor(out=T1, in0=Bri[:, 1], in1=bc16(sm[:, CBR, :]), op=ALU.mult), [r_sm, r_bc, r_T], [r_T])
        VEC(lambda e: e.tensor_tensor(out=T2, in0=Bri[:, 0], in1=bc16(sm[:, CBI, :]), op=ALU.mult), [r_sm, r_bc], [r_T])
        VEC(lambda e: e.tensor_tensor(out=Bb_i, in0=T1, in1=T2, op=ALU.add), [r_T], [r_T])

        def blkME(M, lg, hf):
            return M[hf * 64:(hf + 1) * 64, lg, :, :].rearrange("p ct (q x) -> p ct q x", x=32)[:, :, :, hf * 16:(hf + 1) * 16]

        def halfT(T, hf):
            return T[hf * 64:(hf + 1) * 64, :, :].rearrange("p (ct q) c -> p ct q c", q=4)

        for lg in range(8):
            VEC(lambda e, lg=lg: e.tensor_tensor(out=T1, in0=Bb_r, in1=pw_r[:, :, lg:lg + 1].to_broadcast([128, 16, 16]), op=ALU.mult), [r_sm, r_T], [r_T])
            VEC(lambda e, lg=lg: e.tensor_tensor(out=T2, in0=Bb_i, in1=pw_i[:, :, lg:lg + 1].to_broadcast([128, 16, 16]), op=ALU.mult), [r_sm, r_T], [r_T])
            VEC(lambda e, lg=lg: e.tensor_tensor(out=T3, in0=Bb_i, in1=pw_r[:, :, lg:lg + 1].to_broadcast([128, 16, 16]), op=ALU.mult), [r_sm, r_T], [r_T])
            VEC(lambda e, lg=lg: e.tensor_tensor(out=T4, in0=Bb_r, in1=pw_i[:, :, lg:lg + 1].to_broadcast([128, 16, 16]), op=ALU.mult), [r_sm, r_T], [r_T])
            for hf in range(2):
                POOL(lambda e, lg=lg, hf=hf: e.tensor_tensor(out=blkME(ME_r, lg, hf), in0=halfT(T1, hf), in1=halfT(T2, hf), op=ALU.subtract), [r_T], [r_ME])
                POOL(lambda e, lg=lg, hf=hf: e.tensor_tensor(out=blkME(ME_i, lg, hf), in0=halfT(T3, hf), in1=halfT(T4, hf), op=ALU.add), [r_T], [r_ME])

        def blkMF(M, hf):
            return M[hf * 64:(hf + 1) * 64, :, :].rearrange("p ct (q x) -> p ct q x", x=32)[:, :, :, hf * 16:(hf + 1) * 16]

        for hf in range(2):
            POOL(lambda e, hf=hf: e.tensor_copy(out=blkMF(MF_r, hf), in_=halfT(Cri[:, 0], hf)), [r_bc], [r_MF])
            POOL(lambda e, hf=hf: e.tensor_scalar(out=blkMF(MF_in, hf), in0=halfT(Cri[:, 1], hf), scalar1=-1.0, scalar2=None, op0=ALU.mult), [r_bc], [r_MF])
        for ct in range(4):
            for ri, M in ((0, ME_r), (1, ME_i)):
                for j0 in (0, 4):
                    bk, _, r_bk = next_bank()
                    for jj in range(4):
                        j = j0 + jj
                        PE(lambda e, bk=bk, jj=jj, j=j, ct=ct, M=M: e.transpose(out=bk[:, jj * 128:(jj + 1) * 128], in_=M[:, 7 - j, ct, :], identity=identf[:]),
                           [r_ME, r_const], [r_bk])
                    ACT(lambda e, bk=bk, ct=ct, j0=j0, ri=ri: e.copy(out=Wst[:, ct, j0:j0 + 4, ri, :], in_=bk[:, :].rearrange("p (a b) -> p a b", b=128)),
                        [r_bk], [r_Wst])
        for ct in range(4):
            for l0 in (0, 4):
                bk, _, r_bk = next_bank()
                for ll in range(4):
                    lg = l0 + ll
                    PE(lambda e, bk=bk, ll=ll, lg=lg, ct=ct: e.matmul(bk[:, ll * 128:(ll + 1) * 128], lhsT=ME_r[:, lg, ct, :], rhs=MF_r[:, ct, :], start=True, stop=False),
                       [r_ME, r_MF], [r_bk])
                    PE(lambda e, bk=bk, ll=ll, lg=lg, ct=ct: e.matmul(bk[:, ll * 128:(ll + 1) * 128], lhsT=ME_i[:, lg, ct, :], rhs=MF_in[:, ct, :], start=False, stop=True),
                       [r_ME, r_MF], [r_bk])
                VEC(lambda e, bk=bk, ct=ct, l0=l0: e.tensor_tensor(out=Wfir[:, ct, l0:l0 + 4, :], in0=bk[:, :].rearrange("p (a b) -> p a b", b=128),
                                                                 in1=mask16[:, :].rearrange("p (o b) -> p o b", o=1).to_broadcast([128, 4, 128]), op=ALU.mult),
                    [r_bk, r_const], [r_Wfir])
                if l0 == 0:
                    VEC(lambda e, bk=bk: e.tensor_tensor(out=tmpF, in0=bk[:, 0:128], in1=mask16[:, :], op=ALU.mult), [r_bk, r_const], [r_tmpF])
                    VEC(lambda e, ct=ct: e.scalar_tensor_tensor(out=Wfir[:, ct, 0, :], in0=identf[:, :], scalar=col(l, C_D + ct), in1=tmpF, op0=ALU.mult, op1=ALU.add),
                        [r_tmpF, r_const], [r_Wfir])
        for i in range(8):
            VEC(lambda e, i=i: e.tensor_tensor(out=T1, in0=Cri[:, 0], in1=pw_r[:, :, i + 1:i + 2].to_broadcast([128, 16, 16]), op=ALU.mult), [r_sm, r_bc, r_T], [r_T])
            VEC(lambda e, i=i: e.tensor_tensor(out=T2, in0=Cri[:, 1], in1=pw_i[:, :, i + 1:i + 2].to_broadcast([128, 16, 16]), op=ALU.mult), [r_sm, r_bc, r_T], [r_T])
            VEC(lambda e, i=i: e.tensor_tensor(out=T3, in0=Cri[:, 1], in1=pw_r[:, :, i + 1:i + 2].to_broadcast([128, 16, 16]), op=ALU.mult), [r_sm, r_bc, r_T], [r_T])
            VEC(lambda e, i=i: e.tensor_tensor(out=T4, in0=Cri[:, 0], in1=pw_i[:, :, i + 1:i + 2].to_broadcast([128, 16, 16]), op=ALU.mult), [r_sm, r_bc, r_T], [r_T])
            for hf in range(2):
                hs = slice(hf * 64, (hf + 1) * 64)
                cs = slice(hf * 16, (hf + 1) * 16)
                POOL(lambda e, i=i, hs=hs, cs=cs: e.tensor_tensor(out=Wo_r[hs, :, i, cs], in0=T1[hs], in1=T2[hs], op=ALU.subtract), [r_T], [r_Wo])
                VEC(lambda e, i=i, hs=hs, cs=cs: e.scalar_tensor_tensor(out=Wo_i[hs, :, i, cs], in0=T3[hs], scalar=-1.0, in1=T4[hs], op0=ALU.mult, op1=ALU.subtract),
                    [r_T], [r_Wo])
        P.barrier()
        arena_off[0] = mark
        u_ring = ARing(2, [128, S], BF16, "uct")
        y_ring = ARing(1, [128, S], F32, "yct")
        tab_r, r_tab = aalloc([128, 4, 512], F32, "tab")
        tab_i, _ = aalloc([128, 4, 512], F32)
        tq1, r_tq = aalloc([128, 4, 256], F32, "tq")
        tq2, _ = aalloc([128, 4, 256], F32)
        xp_r, r_xp = aalloc([128, 4, 512], BF16, "xp")
        xp_i, _ = aalloc([128, 4, 512], BF16)
        A_ring = ARing(6, [128, 512], F32, "A")
        VEC(lambda e: e.memset(xp_r[:, :, 0:1], 0.0), [], [r_xp])
        VEC(lambda e: e.memset(xp_i[:, :, 0:1], 0.0), [], [r_xp])
        for ct in range(4):
            uct, r_uct = u_ring.next()
            DMA(uct, uT_d[:, ct, :], [r_ud], [r_uct])
            u8 = uct.rearrange("p (c j) -> p j c", j=8)
            VEC(lambda e: e.memset(tab_r[:, :, 0:1], 1.0), [r_tab], [r_tab])
            VEC(lambda e: e.memset(tab_i[:, :, 0:1], 0.0), [r_tab], [r_tab])
            for k in range(9):
                s_ = 1 << k
                phr = ph_r[:, 4 * ct:4 * ct + 4, k:k + 1].to_broadcast([128, 4, s_])
                phi = ph_i[:, 4 * ct:4 * ct + 4, k:k + 1].to_broadcast([128, 4, s_])
                VEC(lambda e, s_=s_, phr=phr: e.tensor_tensor(out=tq1[:, :, 0:s_], in0=tab_r[:, :, 0:s_], in1=phr, op=ALU.mult), [r_tab, r_sm, r_tq], [r_tq])
                VEC(lambda e, s_=s_, phi=phi: e.tensor_tensor(out=tq2[:, :, 0:s_], in0=tab_i[:, :, 0:s_], in1=phi, op=ALU.mult), [r_tab, r_sm, r_tq], [r_tq])
                VEC(lambda e, s_=s_: e.tensor_tensor(out=tab_r[:, :, s_:2 * s_], in0=tq1[:, :, 0:s_], in1=tq2[:, :, 0:s_], op=ALU.subtract), [r_tq], [r_tab])
                VEC(lambda e, s_=s_, phi=phi: e.tensor_tensor(out=tq1[:, :, 0:s_], in0=tab_r[:, :, 0:s_], in1=phi, op=ALU.mult), [r_tab, r_sm, r_tq], [r_tq])
                VEC(lambda e, s_=s_, phr=phr: e.tensor_tensor(out=tq2[:, :, 0:s_], in0=tab_i[:, :, 0:s_], in1=phr, op=ALU.mult), [r_tab, r_sm, r_tq], [r_tq])
                VEC(lambda e, s_=s_: e.tensor_tensor(out=tab_i[:, :, s_:2 * s_], in0=tq1[:, :, 0:s_], in1=tq2[:, :, 0:s_], op=ALU.add), [r_tq], [r_tab])
            for q in range(4):
                pair = 4 * ct + q
                ps_ = slice(32 * q, 32 * q + 32)
                bks = []
                for ri in range(2):
                    bk, _, r_bk = next_bank()
                    for j in range(8):
                        PE(lambda e, bk=bk, j=j, ri=ri, ps_=ps_, q=q, ct=ct, u8=u8: e.matmul(bk[:, :], lhsT=Wst[ps_, ct, j, ri, :], rhs=u8[ps_, j, :],
                                                                                           start=(j == 0), stop=(j == 7), tile_position=(32 * q, 0)),
                           [r_Wst, r_uct], [r_bk])
                    bks.append((bk, r_bk))
                (Sr, r_Sr), (Si, r_Si) = bks
                tr, ti = tab_r[:, q, :], tab_i[:, q, :]
                a1, r_a1 = A_ring.next()
                a2, r_a2 = A_ring.next()
                a3, r_a3 = A_ring.next()
                a4, r_a4 = A_ring.next()
                VEC(lambda e, Sr=Sr, a1=a1, tr=tr: e.tensor_tensor(out=a1, in0=Sr[:, :], in1=tr, op=ALU.mult), [r_Sr, r_tab], [r_a1])
                VEC(lambda e, Si=Si, a2=a2, ti=ti: e.tensor_tensor(out=a2, in0=Si[:, :], in1=ti, op=ALU.mult), [r_Si, r_tab], [r_a2])
                VEC(lambda e, Si=Si, a3=a3, tr=tr: e.tensor_tensor(out=a3, in0=Si[:, :], in1=tr, op=ALU.mult), [r_Si, r_tab], [r_a3])
                VEC(lambda e, Sr=Sr, a4=a4, ti=ti: e.tensor_tensor(out=a4, in0=Sr[:, :], in1=ti, op=ALU.mult), [r_Sr, r_tab], [r_a4])
                POOL(lambda e, a1=a1, a2=a2: e.tensor_tensor(out=a1, in0=a1, in1=a2, op=ALU.add), [r_a1, r_a2], [r_a1])
                POOL(lambda e, a3=a3, a4=a4: e.tensor_tensor(out=a3, in0=a3, in1=a4, op=ALU.subtract), [r_a3, r_a4], [r_a3])
                rd = sm[:, RDEC, pair:pair + 1].to_broadcast([128, 512])
                VEC(lambda e, a1=a1, a2=a2, rd=rd: e.tensor_tensor_scan(out=a2, data0=rd, data1=a1, initial=0.0, op0=ALU.mult, op1=ALU.add), [r_a1, r_sm, r_a2], [r_a2])
                VEC(lambda e, a3=a3, a4=a4, rd=rd: e.tensor_tensor_scan(out=a4, data0=rd, data1=a3, initial=0.0, op0=ALU.mult, op1=ALU.add), [r_a3, r_sm, r_a4], [r_a4])
                b1, r_b1 = A_ring.next()
                b2, r_b2 = A_ring.next()
                POOL(lambda e, a2=a2, b1=b1, tr=tr: e.tensor_tensor(out=b1, in0=a2, in1=tr, op=ALU.mult), [r_a2, r_tab], [r_b1])
                POOL(lambda e, a4=a4, b2=b2, ti=ti: e.tensor_tensor(out=b2, in0=a4, in1=ti, op=ALU.mult), [r_a4, r_tab], [r_b2])
                VEC(lambda e, b1=b1, b2=b2, q=q: e.tensor_tensor(out=xp_r[:, q, 1:512], in0=b1[:, 0:511], in1=b2[:, 0:511], op=ALU.subtract), [r_b1, r_b2], [r_xp])
                POOL(lambda e, a2=a2, a1=a1, ti=ti: e.tensor_tensor(out=a1, in0=a2, in1=ti, op=ALU.mult), [r_a2, r_tab, r_a1], [r_a1])
                POOL(lambda e, a4=a4, a3=a3, tr=tr: e.tensor_tensor(out=a3, in0=a4, in1=tr, op=ALU.mult), [r_a4, r_tab, r_a3], [r_a3])
                VEC(lambda e, a1=a1, a3=a3, q=q: e.tensor_tensor(out=xp_i[:, q, 1:512], in0=a1[:, 0:511], in1=a3[:, 0:511], op=ALU.add), [r_a1, r_a3], [r_xp])
            yct, r_yct = y_ring.next()
            y8 = yct.rearrange("p (c j) -> p j c", j=8)
            for i in range(8):
                bk, _, r_bk = next_bank()
                for lg in range(i + 1):
                    PE(lambda e, bk=bk, lg=lg, i=i, ct=ct, u8=u8: e.matmul(bk[:, :], lhsT=Wfir[:, ct, lg, :], rhs=u8[:, i - lg, :], start=(lg == 0), stop=False),
                       [r_Wfir, r_uct], [r_bk])
                for q in range(4):
                    pair = 4 * ct + q
                    PE(lambda e, bk=bk, q=q, pair=pair, i=i: e.matmul(bk[32 * q:32 * q + 32, :], lhsT=Wo_r[:, pair, i, :], rhs=xp_r[:, q, :], start=False, stop=False,
                                                                     tile_position=(0, 32 * q)), [r_Wo, r_xp], [r_bk])
                    PE(lambda e, bk=bk, q=q, pair=pair, i=i: e.matmul(bk[32 * q:32 * q + 32, :], lhsT=Wo_i[:, pair, i, :], rhs=xp_i[:, q, :], start=False, stop=(q == 3),
                                                                     tile_position=(0, 32 * q)), [r_Wo, r_xp], [r_bk])
                ACT(lambda e, bk=bk, y8=y8, i=i: e.copy(out=y8[:, i, :], in_=bk[:, :]), [r_bk], [r_yct])
            DMA(yT_d[:, ct, :], yct, [r_yct], [r_yd], q="scalar")
        if stop_after == "B2a":
            return True
        P.barrier()
        arena_off[0] = mark
        wglu, r_wglu = aalloc([128, 4, 512], BF16, "wglu")
        DMAC(wglu, w_glu_d[l].rearrange("(k p) n -> p k n", p=128), [], [r_wglu])
        yb_ring = ARing(2, [128, 4, 512], F32, "yb")
        g_ring = ARing(2, [128, 4, 512], BF16, "gT")
        sg_ring = ARing(2, [128, 4, 512], F32, "sg")
        sq_ring = ARing(1, [128, 4, 512], F32, "sq")
        rs_ring = ARing(2, [128, 512], F32, "rs")
        sn_ring = ARing(2, [128, 4, 512], BF16, "sn")
        for b in range(NB):
            T0 = b * 512
            yb, r_yb = yb_ring.next()
            DMA(yb, yT_d[:, :, T0:T0 + 512], [r_yd], [r_yb])
            gT, r_gT = g_ring.next()
            ACT(lambda e, yb=yb, gT=gT: e.activation(out=gT, in_=yb, func=AF.Gelu_apprx_tanh), [r_yb], [r_gT])
            sg, r_sg = sg_ring.next()
            for co in range(4):
                bk, _, r_bk = next_bank()
                for ci in range(4):
                    PE(lambda e, bk=bk, ci=ci, co=co, gT=gT: e.matmul(bk[:, :], lhsT=wglu[:, ci, co * 128:(co + 1) * 128], rhs=gT[:, ci, :], start=(ci == 0), stop=(ci == 3)),
                       [r_wglu, r_gT], [r_bk])
                ACT(lambda e, bk=bk, co=co, sg=sg: e.activation(out=sg[:, co, :], in_=bk[:, :], func=AF.Sigmoid, bias=col(l, C_BGLU + co)), [r_bk, r_const], [r_sg])
            VEC(lambda e, sg=sg, yb=yb: e.tensor_tensor(out=sg, in0=sg, in1=yb, op=ALU.mult), [r_sg, r_yb], [r_sg])
            sq, r_sq = sq_ring.next()
            POOL(lambda e, sg=sg, sq=sq: e.tensor_tensor(out=sq, in0=sg, in1=sg, op=ALU.mult), [r_sg], [r_sq])
            bk, _, r_bk = next_bank()
            for co in range(4):
                PE(lambda e, bk=bk, co=co, sq=sq: e.matmul(bk[:, :], lhsT=onesf[:, :], rhs=sq[:, co, :], start=(co == 0), stop=(co == 3)), [r_sq, r_const], [r_bk])
            rs_, r_rs = rs_ring.next()
            ACT(lambda e, bk=bk, rs_=rs_: e.activation(out=rs_, in_=bk[:, :], func=AF.Sqrt, scale=1.0 / 512, bias=EPS), [r_bk], [r_rs])
            VEC(lambda e, rs_=rs_: e.reciprocal(out=rs_, in_=rs_), [r_rs], [r_rs])
            VEC(lambda e, sg=sg: e.tensor_tensor(out=sg, in0=sg, in1=col(l, C_GS, 4).rearrange("p (k o) -> p k o", o=1).to_broadcast([128, 4, 512]), op=ALU.mult),
                [r_sg, r_const], [r_sg])
            sn, r_sn = sn_ring.next()
            VEC(lambda e, sg=sg, sn=sn, rs_=rs_: e.tensor_tensor(out=sn, in0=sg, in1=rs_.rearrange("p (o t) -> p o t", o=1).to_broadcast([128, 4, 512]), op=ALU.mult),
                [r_sg, r_rs], [r_sn])
            DMA(sT_d[b], sn, [r_sn], [r_sd])
        if stop_after == "B2":
            return True
        P.barrier()
        areset()
        wo_a, r_wc = aalloc([64, 8, 1024], BF16, "wo_a")
        wo_s, _ = aalloc([128, 4, 1024], BF16)
        wxq, _ = aalloc([128, 8, 1024], BF16)
        wxo, _ = aalloc([128, 8, 1024], BF16)
        KxT, r_kx = aalloc([128, 8, 256], BF16, "KxT")
        Vx, _ = aalloc([128, 2, 1024], BF16)
        markc = arena_off[0]
        wxkv, r_wxkv = aalloc([128, 8, 2048], BF16, "wxkv")
        DMAC(wxkv, w_xkv_d[l].rearrange("(k p) n -> p k n", p=128), [], [r_wxkv])
        DMAC(wo_a, w_out_d[l, 0:512, :].rearrange("(h d) n -> d h n", d=64), [], [r_wc])
        DMAC(wo_s, w_out_d[l, 512:1024, :].rearrange("(k p) n -> p k n", p=128), [], [r_wc])
        DMAC(wxq, w_xq_d[l].rearrange("(k p) n -> p k n", p=128), [], [r_wc])
        DMAC(wxo, w_xo_d[l].rearrange("(k p) n -> p k n", p=128), [], [r_wc])
        memT, r_memT = xnT_ring.next()
        for mt in range(2):
            ht, r_ht = ht_ring.next()
            DMA(ht[:], mem_d[mt * 128:(mt + 1) * 128, :], [], [r_ht])
            norm_transpose(ht, r_ht, col(l, C_GMEM, 8), memT, r_memT, mt)
        for oc in range(8):
            bk, _, r_bk = next_bank()
            for k in range(8):
                PE(lambda e, bk=bk, k=k, oc=oc: e.matmul(bk[:, 0:256], lhsT=wxkv[:, k, oc * 128:(oc + 1) * 128], rhs=memT[:, k, 0:256], start=(k == 0), stop=(k == 7)),
                   [r_wxkv, r_memT], [r_bk])
            ACT(lambda e, bk=bk, oc=oc: e.copy(out=KxT[:, oc, :], in_=bk[:, 0:256]), [r_bk], [r_kx])
        for mt in range(2):
            for hf in range(2):
                bk, _, r_bk = next_bank()
                for k in range(8):
                    PE(lambda e, bk=bk, k=k, mt=mt, hf=hf: e.matmul(bk[:, :], lhsT=memT[:, k, mt * 128:(mt + 1) * 128], rhs=wxkv[:, k, 1024 + hf * 512:1024 + (hf + 1) * 512],
                                                                  start=(k == 0), stop=(k == 7)), [r_wxkv, r_memT], [r_bk])
                ACT(lambda e, bk=bk, mt=mt, hf=hf: e.copy(out=Vx[:, mt, hf * 512:(hf + 1) * 512], in_=bk[:, :]), [r_bk], [r_kx])
        P.barrier()
        arena_off[0] = markc
        an_ring2 = ARing(1, [64, 8, 512], BF16, "an2")
        sn_ring2 = ARing(2, [128, 4, 512], BF16, "sn2")
        h1_ring = ARing(5, [128, D], F32, "h1t")
        qx_ring = ARing(1, [128, 8, 512], BF16, "qxT")
        ox_ring = ARing(1, [128, 8, 512], BF16, "oxT")
        pX_ring = ARing(2, [128, 2, 512], BF16, "pX")
        rc_ring = ARing(2, [128, 512], F32, "rc")
        araw_v = frA[0][0:64, 0:4096].rearrange("p (h t) -> p h t", t=512)
        r_araw = frA[1]
        sqv = frB[0][0:64, 0:2048].rearrange("p (h t) -> p h t", t=512)
        r_sqv = frB[1]
        r_h1src = [Res() for _ in range(32)] if l == 0 else r_hd
        for b in range(NB):
            T0 = b * 512
            DMA(araw_v, aT_d[b], [r_ad], [r_araw])
            bk, _, r_bk = next_bank()
            for hg in range(2):
                POOL(lambda e, hg=hg: e.tensor_tensor(out=sqv, in0=araw_v[:, hg * 4:(hg + 1) * 4, :], in1=araw_v[:, hg * 4:(hg + 1) * 4, :], op=ALU.mult), [r_araw], [r_sqv])
                for hh in range(4):
                    PE(lambda e, bk=bk, hg=hg, hh=hh: e.matmul(bk[0:64, :], lhsT=onesf[0:64, 0:64], rhs=sqv[:, hh, :], start=(hg == 0 and hh == 0), stop=(hg == 1 and hh == 3)),
                       [r_sqv, r_const], [r_bk])
            rc, r_rc = rc_ring.next()
            ACT(lambda e, bk=bk, rc=rc: e.activation(out=rc[0:64, :], in_=bk[0:64, :], func=AF.Sqrt, scale=1.0 / 512, bias=EPS), [r_bk], [r_rc])
            VEC(lambda e, rc=rc: e.reciprocal(out=rc[0:64, :], in_=rc[0:64, :]), [r_rc], [r_rc])
            VEC(lambda e: e.tensor_tensor(out=araw_v, in0=araw_v, in1=col(l, C_GA, 8, 0, 64).rearrange("p (h o) -> p h o", o=1).to_broadcast([64, 8, 512]), op=ALU.mult),
                [r_araw, r_const], [r_araw])
            an2, r_an2 = an_ring2.next()
            VEC(lambda e, an2=an2, rc=rc: e.tensor_tensor(out=an2, in0=araw_v, in1=rc[0:64, :].rearrange("p (o t) -> p o t", o=1).to_broadcast([64, 8, 512]), op=ALU.mult),
                [r_araw, r_rc], [r_an2])
            sn2, r_sn2 = sn_ring2.next()
            DMA(sn2, sT_d[b], [r_sd], [r_sn2])
            xnT, r_xnT = xnT_ring.next()
            h1s = []
            for sub in range(4):
                ti = b * 4 + sub
                ts_ = slice(sub * 128, (sub + 1) * 128)
                ht, r_ht = ht_ring.next()
                DMA(ht[:], src_d[T0 + sub * 128:T0 + (sub + 1) * 128, :], [r_h1src[ti]], [r_ht])
                h1t, r_h1t = h1_ring.next()
                for hf in range(2):
                    bk, _, r_bk = next_bank()
                    cs_ = slice(hf * 512, (hf + 1) * 512)
                    for hh in range(8):
                        PE(lambda e, bk=bk, hh=hh, an2=an2, ts_=ts_, cs_=cs_: e.matmul(bk[:, :], lhsT=an2[:, hh, ts_], rhs=wo_a[:, hh, cs_], start=(hh == 0), stop=False),
                           [r_an2, r_wc], [r_bk])
                    for k in range(4):
                        PE(lambda e, bk=bk, k=k, sn2=sn2, ts_=ts_, cs_=cs_: e.matmul(bk[:, :], lhsT=sn2[:, k, ts_], rhs=wo_s[:, k, cs_], start=False, stop=(k == 3)),
                           [r_sn2, r_wc], [r_bk])
                    VEC(lambda e, bk=bk, ht=ht, h1t=h1t, cs_=cs_: e.tensor_tensor(out=h1t[:, cs_], in0=bk[:, :], in1=ht[:, cs_], op=ALU.add), [r_bk, r_ht], [r_h1t])
                norm_transpose(h1t, r_h1t, col(l, C_GX, 8), xnT, r_xnT, sub)
                h1s.append((h1t, r_h1t))
            qxT, r_qxT = qx_ring.next()
            for oc in range(8):
                bk, _, r_bk = next_bank()
                for k in range(8):
                    PE(lambda e, bk=bk, k=k, oc=oc, xnT=xnT: e.matmul(bk[:, :], lhsT=wxq[:, k, oc * 128:(oc + 1) * 128], rhs=xnT[:, k, :], start=(k == 0), stop=(k == 7)),
                       [r_wc, r_xnT], [r_bk])
                ACT(lambda e, bk=bk, oc=oc, qxT=qxT: e.copy(out=qxT[:, oc, :], in_=bk[:, :]), [r_bk], [r_qxT])
            oxT, r_oxT = ox_ring.next()
            for hx in range(4):
                pX, r_pX = pX_ring.next()
                for mt in range(2):
                    bk, _, r_bk = next_bank()
                    for dc in range(2):
                        PE(lambda e, bk=bk, dc=dc, mt=mt, hx=hx, qxT=qxT: e.matmul(bk[:, :], lhsT=KxT[:, hx * 2 + dc, mt * 128:(mt + 1) * 128], rhs=qxT[:, hx * 2 + dc, :],
                                                                              start=(dc == 0), stop=(dc == 1)), [r_kx, r_qxT], [r_bk])
                    ACT(lambda e, bk=bk, mt=mt, pX=pX: e.activation(out=pX[:, mt, :], in_=bk[:, :], func=AF.Exp, scale=X_SCALE), [r_bk], [r_pX])
                bk, _, r_bk = next_bank()
                for mt in range(2):
                    PE(lambda e, bk=bk, mt=mt, pX=pX: e.matmul(bk[:, :], lhsT=onesb[:, :], rhs=pX[:, mt, :], start=(mt == 0), stop=(mt == 1)), [r_pX, r_const], [r_bk])
                rc, r_rc = rc_ring.next()
                VEC(lambda e, bk=bk, rc=rc: e.reciprocal(out=rc, in_=bk[:, :]), [r_bk], [r_rc])
                for dc in range(2):
                    bk, _, r_bk = next_bank()
                    for mt in range(2):
                        PE(lambda e, bk=bk, mt=mt, dc=dc, hx=hx, pX=pX: e.matmul(bk[:, :], lhsT=Vx[:, mt, (hx * 2 + dc) * 128:(hx * 2 + dc + 1) * 128], rhs=pX[:, mt, :],
                                                                             start=(mt == 0), stop=(mt == 1)), [r_kx, r_pX], [r_bk])
                    VEC(lambda e, bk=bk, dc=dc, hx=hx, rc=rc, oxT=oxT: e.tensor_tensor(out=oxT[:, hx * 2 + dc, :], in0=bk[:, :], in1=rc, op=ALU.mult), [r_bk, r_rc], [r_oxT])
            for sub in range(4):
                ti = b * 4 + sub
                ts_ = slice(sub * 128, (sub + 1) * 128)
                h1t, r_h1t = h1s[sub]
                for hf in range(2):
                    bk, _, r_bk = next_bank()
                    cs_ = slice(hf * 512, (hf + 1) * 512)
                    for k in range(8):
                        PE(lambda e, bk=bk, k=k, oxT=oxT, ts_=ts_, cs_=cs_: e.matmul(bk[:, :], lhsT=oxT[:, k, ts_], rhs=wxo[:, k, cs_], start=(k == 0), stop=(k == 7)),
                           [r_oxT, r_wc], [r_bk])
                    VEC(lambda e, bk=bk, h1t=h1t, cs_=cs_: e.tensor_tensor(out=h1t[:, cs_], in0=bk[:, :], in1=h1t[:, cs_], op=ALU.add), [r_bk, r_h1t], [r_h1t])
                DMA(h1_d[T0 + sub * 128:T0 + (sub + 1) * 128, :], h1t[:], [r_h1t], [r_h1d[ti]])
        if stop_after == "C1":
            return True

        P.barrier()
        areset()
        wg, r_wf = aalloc([128, 8, DFF], BF16, "wg")
        wu, _ = aalloc([128, 8, DFF], BF16)
        wd, _ = aalloc([128, NFF, 1024], BF16)
        for k in range(8):
            DMAC(wg[:, k, :], w_gate_d[l, k * 128:(k + 1) * 128, :], [], [r_wf])
            DMAC(wu[:, k, :], w_up_d[l, k * 128:(k + 1) * 128, :], [], [r_wf])
        for k in range(0, NFF, 2):
            DMAC(wd[:, k:k + 2, :], w_down_d[l, k * 128:(k + 2) * 128, :].rearrange("(k p) n -> p k n", p=128), [], [r_wf])
        actT = frA[0].bitcast(BF16) if hasattr(frA[0], "bitcast") else None
        actT = actT[:, 0:NFF * 512].rearrange("p (f t) -> p f t", t=512)
        r_actT = frA[1]
        sgs = [(frB[0][:, i * 512:(i + 1) * 512], Res()) for i in range(4)]
        sg_i = [0]
        last = (l == L - 1)
        ctxC = {}
        nt_ring = Ring(P, f"ntl{l}", 1, [128, 4], F32) if False else None

        def stageC2a(b):
            T0 = b * 512
            xnT, r_xnT = xnT_ring.next()
            ctxC[b] = (xnT, r_xnT)
            for sub in range(4):
                ti = b * 4 + sub
                ht, r_ht = ht_ring.next()
                DMA(ht[:], h1_d[T0 + sub * 128:T0 + (sub + 1) * 128, :], [r_h1d[ti]], [r_ht])
                norm_transpose(ht, r_ht, col(l, C_GFFN, 8), xnT, r_xnT, sub)
                yield

        def stageC2b(b):
            T0 = b * 512
            xnT, r_xnT = ctxC.pop(b)
            for fc in range(NFF):
                if fc % 3 == 0:
                    yield
                bkg, _, r_bkg = next_bank()
                bku, _, r_bku = next_bank()
                for k in range(8):
                    PE(lambda e, bkg=bkg, k=k, fc=fc, xnT=xnT: e.matmul(bkg[:, :], lhsT=wg[:, k, fc * 128:(fc + 1) * 128], rhs=xnT[:, k, :], start=(k == 0), stop=(k == 7)),
                       [r_wf, r_xnT], [r_bkg])
                for k in range(8):
                    PE(lambda e, bku=bku, k=k, fc=fc, xnT=xnT: e.matmul(bku[:, :], lhsT=wu[:, k, fc * 128:(fc + 1) * 128], rhs=xnT[:, k, :], start=(k == 0), stop=(k == 7)),
                       [r_wf, r_xnT], [r_bku])
                sgt, r_sgt = sgs[sg_i[0] % 4]
                sg_i[0] += 1
                ACT(lambda e, bkg=bkg, sgt=sgt: e.activation(out=sgt, in_=bkg[:, :], func=AF.Silu), [r_bkg], [r_sgt])
                VEC(lambda e, bku=bku, sgt=sgt, fc=fc: e.tensor_tensor(out=actT[:, fc, :], in0=bku[:, :], in1=sgt, op=ALU.mult), [r_bku, r_sgt], [r_actT])
            for sub in range(4):
                ti = b * 4 + sub
                ts_ = slice(sub * 128, (sub + 1) * 128)
                ht, r_ht = ht_ring.next()
                DMA(ht[:], h1_d[T0 + sub * 128:T0 + (sub + 1) * 128, :], [r_h1d[ti]], [r_ht])
                for hf in range(2):
                    yield
                    bk, _, r_bk = next_bank()
                    cs_ = slice(hf * 512, (hf + 1) * 512)
                    for fc in range(NFF):
                        PE(lambda e, bk=bk, fc=fc, ts_=ts_, cs_=cs_: e.matmul(bk[:, :], lhsT=actT[:, fc, ts_], rhs=wd[:, fc, cs_], start=(fc == 0), stop=(fc == NFF - 1)),
                           [r_actT, r_wf], [r_bk])
                    VEC(lambda e, bk=bk, ht=ht, cs_=cs_: e.tensor_tensor(out=ht[:, cs_], in0=bk[:, :], in1=ht[:, cs_], op=ALU.add), [r_bk, r_ht], [r_ht])
                if not last:
                    DMA(h_d[T0 + sub * 128:T0 + (sub + 1) * 128, :], ht[:], [r_ht], [r_hd[ti]])
                else:
                    st, r_st = st_ring.next()
                    rms_stats(ht[:], r_ht, D, st, r_st, 0)
                    VEC(lambda e, ht=ht, st=st: e.scalar_tensor_tensor(out=ht[:], in0=ht[:], scalar=st[:, 0:1], in1=fing[:], op0=ALU.mult, op1=ALU.mult),
                        [r_ht, r_st, r_const], [r_ht])
                    final_ops.append(DMA(out_d[T0 + sub * 128:T0 + (sub + 1) * 128, :], ht[:], [r_ht], []))


        def run_il(gens):
            gens = list(gens)
            while gens:
                for g_ in list(gens):
                    try:
                        next(g_)
                    except StopIteration:
                        gens.remove(g_)

        for b in range(NB + 1):
            gl = []
            if b >= 1:
                gl.append(stageC2b(b - 1))
            if b < NB:
                gl.append(stageC2a(b))
            run_il(gl)
        return False

    for l_ in range(n_layers):
        if emit_layer(l_):
            break

    if debug:
        def dump(name, src, shape, dt, r):
            final_ops.append(DMA(dbg_out(name, shape, dt), src, [r], []))
        P.barrier()
        dump("qT", qT_d[:, :, 3584:4096], [8, 96, 512], BF16, r_qd)
        dump("kT", kT_d[:, :, 3584:4096], [8, 96, 512], BF16, r_kd)
        dump("V", V_d[:, :, 28:32, :], [8, 128, 4, 65], BF16, r_vd)
        dump("uT", uT_d[:, :, 3584:4096], [128, 4, 512], BF16, r_ud)
        dump("aT0", aT_d[0], [64, 8, 512], F32, r_ad)
        dump("aT7", aT_d[7], [64, 8, 512], F32, r_ad)
        dump("yT", yT_d[:, :, 3584:4096], [128, 4, 512], F32, r_yd)
        dump("yT0", yT_d[:, :, 0:512], [128, 4, 512], F32, r_yd)
        dump("sT7", sT_d[7], [128, 4, 512], BF16, r_sd)
        dump("sT0", sT_d[0], [128, 4, 512], BF16, r_sd)
        dump("h1", h1_d[3968:4096, :], [128, D], F32, r_h1d[31])
        dump("h", h_d[3968:4096, :], [128, D], F32, r_hd[31])
    P.barrier()
    return P.build(final_waits=final_ops), dbg


def prep_inputs(inp):
    f = np.float32
    g = lambda k: np.asarray(inp[k])
    cols = np.zeros((128, L, NCOL), f)
    for l in range(L):
        cols[:, l, C_GMIX:C_GMIX + 8] = g("norm_mix_g")[l].reshape(8, 128).T
        cols[:, l, C_GX:C_GX + 8] = g("norm_x_g")[l].reshape(8, 128).T
        cols[:, l, C_GFFN:C_GFFN + 8] = g("norm_ffn_g")[l].reshape(8, 128).T
        cols[:, l, C_GMEM:C_GMEM + 8] = g("mem_norm_g")[l].reshape(8, 128).T
        cols[:, l, C_GQ:C_GQ + 2] = g("q_norm_g")[l].reshape(2, 128).T
        cols[:, l, C_GKV] = g("kv_norm_g")[l]
        cols[:, l, C_GS:C_GS + 4] = g("ssm_out_g")[l].reshape(4, 128).T
        cols[:, l, C_D:C_D + 4] = g("ssm_d")[l].reshape(4, 128).T
        cols[:, l, C_BGLU:C_BGLU + 4] = g("ssm_b_glu")[l].reshape(4, 128).T
        cols[0:64, l, C_GA:C_GA + 8] = g("attn_out_g")[l].reshape(8, 64).T
    consts = np.zeros((128, 2), f)
    freqs = (np.float32(10000.0) ** (-np.arange(0, 32, 2, dtype=np.float32) / np.float32(32))).astype(f)
    consts[64:80, 0] = freqs
    consts[80:96, 0] = freqs
    consts[64:80, 1] = -SIN_SCALE
    consts[80:96, 1] = SIN_SCALE
    idx = np.arange(128) // 16
    mask16 = (idx[:, None] == idx[None, :]).astype(f)
    w_in = g("w_in")
    w_kr = np.zeros((L, D, 192), f)
    w_kr[:, :, 64:96] = w_in[:, :, 384:416]
    w_kr[:, :, 160:176] = w_in[:, :, 400:416]
    w_kr[:, :, 176:192] = w_in[:, :, 384:400]
    w_uq = g("w_uq")
    w_uqs = np.zeros_like(w_uq)
    for h in range(8):
        w_uqs[:, :, h * 96 + 64:h * 96 + 80] = w_uq[:, :, h * 96 + 80:h * 96 + 96]
        w_uqs[:, :, h * 96 + 80:h * 96 + 96] = w_uq[:, :, h * 96 + 64:h * 96 + 80]
    w_ukv = g("w_ukv").reshape(L, 128, 8, 2, 64).transpose(0, 1, 3, 2, 4).reshape(L, 128, 1024)

    def pair_layout(a):
        sh = a.shape
        a = a.reshape(L, 16, 2, 64, *sh[3:])
        a = np.moveaxis(a, 1, 3)
        return a.reshape(L, 128, 16, *sh[3:])

    lam_re = pair_layout(g("ssm_lambda_re"))
    lam_im = pair_layout(g("ssm_lambda_im"))
    logdt = pair_layout(np.repeat(g("ssm_log_dt")[:, :, None], 64, axis=2))
    s5v = np.stack([lam_re, lam_im, logdt], axis=2)
    b_re = pair_layout(g("ssm_b_re"))
    b_im = pair_layout(g("ssm_b_im"))
    s5b = np.stack([b_re, b_im], axis=2)
    c_re = pair_layout(np.swapaxes(g("ssm_c_re"), 2, 3))
    c_im = pair_layout(np.swapaxes(g("ssm_c_im"), 2, 3))
    s5c = np.stack([c_re, c_im], axis=2)
    common = dict(
        cols=cols, consts=consts, mask16=mask16, fing=g("final_norm_g").reshape(1, D).astype(f),
        w_in=w_in, w_kr=w_kr, w_uq=w_uq, w_uqs=w_uqs, w_ukv=np.ascontiguousarray(w_ukv),
        s5v=np.ascontiguousarray(s5v), s5b=np.ascontiguousarray(s5b), s5c=np.ascontiguousarray(s5c),
        w_glu=g("ssm_w_glu"), w_out=g("w_out"), w_xq=g("w_xq"), w_xkv=g("w_xkv"), w_xo=g("w_xo"),
        w_gate=g("w_gate"), w_up=g("w_up"), w_down=g("w_down"),
    )
    common = {k: np.ascontiguousarray(v, dtype=f) for k, v in common.items()}
    x = g("x")
    mem = g("mem")
    pos = g("positions").astype(np.int32)
    per_core = []
    for c in range(x.shape[0]):
        d = dict(common)
        d["x"] = np.ascontiguousarray(x[c], dtype=f)
        d["mem"] = np.ascontiguousarray(mem[c], dtype=f)
        d["pos"] = np.ascontiguousarray(pos[c].reshape(1, S))
        per_core.append(d)
    return per_core


def kernel(**inputs):
    per_core = prep_inputs(inputs)
    nc, _ = build_program()
    res = run_bass_kernel_spmd(nc, per_core, core_ids=list(range(8)))
    return np.stack([np.asarray(r["out"], dtype=np.float32) for r in res.results], axis=0)
```

```python
import math
import numpy as np
import concourse.bass as bass
import concourse.mybir as mybir
from concourse.bass_utils import run_bass_kernel_spmd

F32 = mybir.dt.float32
BF16 = mybir.dt.bfloat16
I32 = mybir.dt.int32
AF = mybir.ActivationFunctionType
ALU = mybir.AluOpType

ENGS = ["tensor", "vector", "scalar", "gpsimd", "sync"]

L = 2
S = 4096
D = 1024
NB = 8
EPS = 1e-6
DFF = 2816
NFF = 22
TWO_PI = 2.0 * math.pi
SIN_SCALE = 6.2831845
MAGIC = 12582912.0
ATT_SCALE = 96.0 ** -0.5
X_SCALE = 256.0 ** -0.5
NCOL = 55
C_GMIX, C_GX, C_GFFN, C_GMEM, C_GQ, C_GKV, C_GS, C_D, C_BGLU, C_GA = 0, 8, 16, 24, 32, 34, 35, 39, 43, 47


class Res:
    __slots__ = ("name", "last_w", "readers")

    def __init__(self, name=""):
        self.name = name
        self.last_w = None
        self.readers = []


class Op:
    __slots__ = ("eng", "fn", "waits", "dma", "sem", "val")


class Prog:
    def __init__(self, n_dma_sems=24):
        self.nc = bass.Bass("TRN2", target_bir_lowering=False)
        self.ops = {e: [] for e in ENGS}
        self.n_dma_sems = n_dma_sems
        self.wm = {e: {} for e in ENGS}
        self.dma_rr = {e: 0 for e in ENGS}
        self.dma_cnt = {}
        self.dma_last = {}
        self.eng_cnt = {}
        self.pending = {e: {} for e in ENGS}
        self._ctx = []

    def sbuf(self, name, shape, dtype):
        g = self.nc.sbuf_tensor("sb_" + name, list(shape), dtype)
        h = g.__enter__()
        self._ctx.append(g)
        return h

    def psum(self, name, shape, dtype):
        g = self.nc.psum_tensor("ps_" + name, list(shape), dtype)
        h = g.__enter__()
        self._ctx.append(g)
        return h

    def _need(self, op, dep):
        if dep is None:
            return
        if dep.eng == "tensor" and op.eng == "tensor" and not dep.dma and not op.dma:
            return
        if dep.val > op.waits.get(dep.sem, 0):
            op.waits[dep.sem] = dep.val

    def barrier(self):
        cur = {}
        for e, c in self.eng_cnt.items():
            cur[("eng", e)] = c
        for k, c in self.dma_cnt.items():
            cur[k] = 16 * c
        for e in ENGS:
            pe = self.pending[e]
            for k, v in cur.items():
                if v > pe.get(k, 0):
                    pe[k] = v

    def op(self, eng, fn, reads=(), writes=(), dma=False):
        o = Op()
        o.eng = eng
        o.fn = fn
        o.dma = dma
        o.waits = {}
        if self.pending[eng]:
            o.waits.update(self.pending[eng])
            self.pending[eng] = {}
        if dma:
            slot = self.dma_rr[eng] % self.n_dma_sems
            self.dma_rr[eng] += 1
            key = ("dma", eng, slot)
            prev = self.dma_last.get(key)
            cnt = self.dma_cnt.get(key, 0) + 1
            self.dma_cnt[key] = cnt
            o.sem = key
            o.val = 16 * cnt
            if prev is not None:
                self._need(o, prev)
            self.dma_last[key] = o
        else:
            o.sem = ("eng", eng)
            self.eng_cnt[eng] = self.eng_cnt.get(eng, 0) + 1
            o.val = self.eng_cnt[eng]
        for r in reads:
            self._need(o, r.last_w)
        for w in writes:
            self._need(o, w.last_w)
            for rd in w.readers:
                self._need(o, rd)
        for r in reads:
            r.readers.append(o)
        for w in writes:
            w.last_w = o
            w.readers = []
        wm = self.wm[eng]
        for k in list(o.waits):
            if wm.get(k, 0) >= o.waits[k]:
                del o.waits[k]
            else:
                wm[k] = o.waits[k]
        self.ops[eng].append(o)
        return o

    def build(self, final_waits=()):
        nc = self.nc
        sems = {}
        for e in ENGS:
            for o in self.ops[e]:
                if o.sem not in sems:
                    g = nc.semaphore("s_" + "_".join(str(x) for x in o.sem))
                    sems[o.sem] = g.__enter__()
                    self._ctx.append(g)
        fin = {}
        for o in final_waits:
            fin[o.sem] = max(fin.get(o.sem, 0), o.val)
        with nc.Block() as block:
            def make(e):
                def body(engobj):
                    for o in self.ops[e]:
                        for k, v in o.waits.items():
                            engobj.wait_ge(sems[k], v)
                        ins = o.fn(engobj)
                        ins.then_inc(sems[o.sem], 16 if o.dma else 1)
                    if e == "sync":
                        for k, v in fin.items():
                            engobj.wait_ge(sems[k], v)
                return body
            for e in ENGS:
                if self.ops[e] or e == "sync":
                    getattr(block, e)(make(e))
        return nc


class Ring:
    def __init__(self, P, name, n, shape, dtype):
        self.bufs = [(P.sbuf(f"{name}{i}", shape, dtype), Res(f"{name}{i}")) for i in range(n)]
        self.i = 0

    def next(self):
        b = self.bufs[self.i % len(self.bufs)]
        self.i += 1
        return b


def build_program(debug=False, n_layers=L, stop_after=None):
    P = Prog()
    nc = P.nc

    def din(name, shape, dt=F32):
        return nc.dram_tensor(name, list(shape), dt, kind="ExternalInput").ap()

    def dscr(name, shape, dt):
        return nc.dram_tensor(name, list(shape), dt, kind="Internal").ap()

    x_d = din("x", [S, D])
    mem_d = din("mem", [256, D])
    pos_d = din("pos", [1, S], I32)
    cols_d = din("cols", [128, L, NCOL])
    consts_d = din("consts", [128, 2])
    mask16_d = din("mask16", [128, 128])
    fing_d = din("fing", [1, D])
    w_in_d = din("w_in", [L, D, 928])
    w_kr_d = din("w_kr", [L, D, 192])
    w_uq_d = din("w_uq", [L, 256, 768])
    w_uqs_d = din("w_uqs", [L, 256, 768])
    w_ukv_d = din("w_ukv", [L, 128, 1024])
    s5v_d = din("s5v", [L, 128, 3, 16])
    s5b_d = din("s5b", [L, 128, 2, 16, 16])
    s5c_d = din("s5c", [L, 128, 2, 16, 16])
    w_glu_d = din("w_glu", [L, 512, 512])
    w_out_d = din("w_out", [L, D, D])
    w_xq_d = din("w_xq", [L, D, D])
    w_xkv_d = din("w_xkv", [L, D, 2 * D])
    w_xo_d = din("w_xo", [L, D, D])
    w_gate_d = din("w_gate", [L, D, DFF])
    w_up_d = din("w_up", [L, D, DFF])
    w_down_d = din("w_down", [L, DFF, D])
    out_d = nc.dram_tensor("out", [S, D], F32, kind="ExternalOutput").ap()

    h_d = dscr("h_scr", [S, D], F32)
    h1_d = dscr("h1_scr", [S, D], F32)
    qT_d = dscr("qT_scr", [8, 96, S], BF16)
    kT_d = dscr("kT_scr", [8, 96, S], BF16)
    V_d = dscr("V_scr", [8, 128, 32, 65], BF16)
    uT_d = dscr("uT_scr", [128, 4, S], BF16)
    yT_d = dscr("yT_scr", [128, 4, S], F32)
    aT_d = dscr("aT_scr", [NB, 64, 8, 512], F32)
    sT_d = dscr("sT_scr", [NB, 128, 4, 512], BF16)
    cos_d = dscr("cos_scr", [32, S], F32)
    sin_d = dscr("sin_scr", [32, S], F32)
    r_hd = [Res() for _ in range(32)]
    r_h1d = [Res() for _ in range(32)]
    r_qd, r_kd, r_vd, r_ud, r_yd = Res(), Res(), Res(), Res(), Res()
    r_ad, r_sd, r_csd = Res(), Res(), Res()

    dbg = {}

    def dbg_out(name, shape, dt=F32):
        t = nc.dram_tensor("dbg_" + name, list(shape), dt, kind="ExternalOutput").ap()
        dbg[name] = t
        return t

    final_ops = []

    def VEC(fn, reads, writes):
        return P.op("vector", fn, reads, writes)

    def ACT(fn, reads, writes):
        return P.op("scalar", fn, reads, writes)

    def POOL(fn, reads, writes):
        return P.op("gpsimd", fn, reads, writes)

    def PE(fn, reads, writes):
        return P.op("tensor", fn, reads, writes)

    def DMA(out, in_, reads, writes, q="sync"):
        return P.op(q, lambda e: e.dma_start(out=out, in_=in_), reads, writes, dma=True)

    def DMAC(out, in_, reads, writes):
        return P.op("gpsimd", lambda e: e.dma_start(out=out, in_=in_), reads, writes, dma=True)

    banks = []
    for i in range(8):
        t = P.psum(f"bank{i}", [128, 512], F32)
        banks.append((t, t.bitcast(BF16), Res(f"bank{i}")))
    bank_rr = [0]

    def next_bank(pool=(0, 1, 2, 3, 4, 5, 6, 7)):
        i = pool[bank_rr[0] % len(pool)]
        bank_rr[0] += 1
        return banks[i]

    identf = P.sbuf("identf", [128, 128], F32)
    ident = P.sbuf("ident", [128, 128], BF16)
    onesf = P.sbuf("onesf", [128, 128], F32)
    sel65 = P.sbuf("sel65", [65, 64], F32)
    onesb = P.sbuf("onesb", [128, 128], BF16)
    fing = P.sbuf("fing", [128, D], F32)
    mask16 = P.sbuf("mask16", [128, 128], F32)
    cols = P.sbuf("cols", [128, L, NCOL], F32)
    consts = P.sbuf("consts", [128, 2], F32)
    r_const = Res("const")
    POOL(lambda e: e.memset(identf[:], 1.0), [], [r_const])
    POOL(lambda e: e.affine_select(out=identf[:], in_=identf[:], pattern=[[-1, 128]], compare_op=ALU.is_equal,
                                   fill=0.0, base=0, channel_multiplier=1), [r_const], [r_const])
    VEC(lambda e: e.tensor_copy(out=ident[:], in_=identf[:]), [r_const], [r_const])
    VEC(lambda e: e.memset(onesf[:], 1.0), [], [r_const])
    VEC(lambda e: e.memset(onesb[:], 1.0), [], [r_const])
    DMA(fing[:], fing_d[0:1, :].to_broadcast([128, D]), [], [r_const])
    VEC(lambda e: e.memset(sel65[:], 0.0), [], [r_const])
    VEC(lambda e: e.memset(sel65[64:65, :], 1.0), [r_const], [r_const])
    DMA(mask16[:], mask16_d[:, :], [], [r_const])
    DMA(cols[:], cols_d[:, :, :], [], [r_const])
    DMA(consts[:], consts_d[:, :], [], [r_const])

    def col(l, c, n=1, p0=0, p1=128):
        return cols[p0:p1, l, c:c + n]

    ARENA_F32 = 33792
    arena_f = P.sbuf("arena", [128, ARENA_F32], F32)
    arena_b = arena_f.bitcast(BF16)
    arena_i = arena_f.bitcast(I32)
    arena_off = [0]

    def areset():
        arena_off[0] = 0

    def aalloc(shape, dt, name=""):
        n = 1
        for s_ in shape[1:]:
            n *= s_
        esz = 2 if dt == BF16 else 4
        off = (arena_off[0] + 3) // 4 * 4
        arena_off[0] = off + n * esz
        assert arena_off[0] <= ARENA_F32 * 4, (name, arena_off[0])
        base = {BF16: arena_b, F32: arena_f, I32: arena_i}[dt]
        e0 = off // esz
        ap = base[0:shape[0], e0:e0 + n]
        if len(shape) == 3:
            ap = ap.rearrange("p (a b) -> p a b", b=shape[2])
        elif len(shape) == 4:
            ap = ap.rearrange("p (a b c) -> p a b c", b=shape[2], c=shape[3])
        elif len(shape) == 5:
            ap = ap.rearrange("p (a b c d) -> p a b c d", b=shape[2], c=shape[3], d=shape[4])
        return ap, Res(name)

    class ARing:
        def __init__(self, n, shape, dt, name=""):
            self.bufs = [aalloc(shape, dt, f"{name}{i}") for i in range(n)]
            self.i = 0

        def next(self):
            b = self.bufs[self.i % len(self.bufs)]
            self.i += 1
            return b

    ht_ring = Ring(P, "ht", 2, [128, D], F32)
    junk_ring = Ring(P, "junk", 1, [128, D], BF16)
    xn_ring = Ring(P, "xn", 2, [128, D], BF16)
    xnT_ring = Ring(P, "xnT", 2, [128, 8, 512], BF16)
    st_ring = Ring(P, "st", 8, [128, 4], F32)
    pT_ring = Ring(P, "pT", 4, [128, 512], BF16)
    frA = (P.sbuf("frA", [128, 5632], F32), Res("frA"))
    frB = (P.sbuf("frB", [128, 2048], F32), Res("frB"))

    def rms_stats(src_ap, r_src, nfeat, st, r_st, c0):
        junk, r_junk = junk_ring.next()
        n = src_ap.shape[-1]
        ACT(lambda e: e.activation(out=junk[:, 0:n], in_=src_ap, func=AF.Square, accum_out=st[:, c0:c0 + 1]), [r_src], [r_junk, r_st])
        ACT(lambda e: e.activation(out=st[:, c0:c0 + 1], in_=st[:, c0:c0 + 1], func=AF.Sqrt, scale=1.0 / nfeat, bias=EPS), [r_st], [r_st])
        VEC(lambda e: e.reciprocal(out=st[:, c0:c0 + 1], in_=st[:, c0:c0 + 1]), [r_st], [r_st])

    def norm_transpose(ht, r_ht, gcol, xnT, r_xnT, sub):
        st, r_st = st_ring.next()
        rms_stats(ht[:], r_ht, D, st, r_st, 0)
        xn, r_xn = xn_ring.next()
        VEC(lambda e: e.tensor_scalar(out=xn[:], in0=ht[:], scalar1=st[:, 0:1], scalar2=None, op0=ALU.mult), [r_ht, r_st], [r_xn])
        bk, bkb, r_bk = next_bank()
        bkv = bkb[:, :].rearrange("p (a b) -> p a b", b=128)
        for k in range(8):
            PE(lambda e, k=k: e.transpose(out=bkv[:, k, :], in_=xn[:, k * 128:(k + 1) * 128], identity=ident[:]), [r_xn, r_const], [r_bk])
        VEC(lambda e: e.tensor_tensor(out=xnT[:, :, sub * 128:(sub + 1) * 128], in0=bkv, in1=gcol.to_broadcast([128, 8, 128]), op=ALU.mult),
            [r_bk, r_const], [r_xnT])

    RS = slice(64, 96)

    areset()
    posi, r_ra = aalloc([96, S], I32, "posi")
    tmpa, _ = aalloc([96, S], F32, "tmpa")
    tmpb, _ = aalloc([96, S], F32, "tmpb")
    cs_t, r_cs = aalloc([96, S], F32, "cs")
    sn_t, _ = aalloc([96, S], F32, "sn")
    DMA(posi[RS, :], pos_d[0:1, :].to_broadcast([32, S]), [], [r_ra])
    VEC(lambda e: e.tensor_copy(out=tmpa[RS, :], in_=posi[RS, :]), [r_ra], [r_ra])
    VEC(lambda e: e.tensor_scalar(out=tmpa[RS, :], in0=tmpa[RS, :], scalar1=consts[RS, 0:1], scalar2=1.0 / TWO_PI,
                                  op0=ALU.mult, op1=ALU.mult), [r_ra, r_const], [r_ra])
    VEC(lambda e: e.tensor_scalar(out=tmpb[RS, :], in0=tmpa[RS, :], scalar1=MAGIC, scalar2=MAGIC, op0=ALU.add, op1=ALU.subtract), [r_ra], [r_ra])
    VEC(lambda e: e.tensor_tensor(out=tmpb[RS, :], in0=tmpa[RS, :], in1=tmpb[RS, :], op=ALU.subtract), [r_ra], [r_ra])
    ACT(lambda e: e.activation(out=sn_t[RS, :], in_=tmpb[RS, :], func=AF.Sin, scale=consts[RS, 1:2]), [r_ra, r_const], [r_cs])
    VEC(lambda e: e.tensor_scalar(out=tmpa[RS, :], in0=tmpa[RS, :], scalar1=0.25, scalar2=None, op0=ALU.add), [r_ra, r_cs], [r_ra])
    VEC(lambda e: e.tensor_scalar(out=tmpb[RS, :], in0=tmpa[RS, :], scalar1=MAGIC, scalar2=MAGIC, op0=ALU.add, op1=ALU.subtract), [r_ra], [r_ra])
    VEC(lambda e: e.tensor_tensor(out=tmpb[RS, :], in0=tmpa[RS, :], in1=tmpb[RS, :], op=ALU.subtract), [r_ra], [r_ra])
    ACT(lambda e: e.activation(out=cs_t[RS, :], in_=tmpb[RS, :], func=AF.Sin, scale=SIN_SCALE), [r_ra], [r_cs])
    DMA(cos_d[:, :], cs_t[RS, :], [r_cs], [r_csd])
    DMA(sin_d[:, :], sn_t[RS, :], [r_cs], [r_csd])

    def emit_layer(l):
        src_d = x_d if l == 0 else h_d
        r_src = [Res() for _ in range(32)] if l == 0 else r_hd
        P.barrier()
        areset()
        w_in_sb, r_w = aalloc([128, 8, 928], BF16, "w_in")
        w_kr_sb, _ = aalloc([128, 8, 192], BF16)
        w_uq_sb, _ = aalloc([128, 2, 768], BF16)
        w_uqs_sb, _ = aalloc([128, 2, 768], BF16)
        w_ukv_sb, _ = aalloc([128, 1024], BF16)
        DMAC(w_in_sb, w_in_d[l].rearrange("(k p) n -> p k n", p=128), [], [r_w])
        DMAC(w_kr_sb, w_kr_d[l].rearrange("(k p) n -> p k n", p=128), [], [r_w])
        DMAC(w_uq_sb, w_uq_d[l].rearrange("(k p) n -> p k n", p=128), [], [r_w])
        DMAC(w_uqs_sb, w_uqs_d[l].rearrange("(k p) n -> p k n", p=128), [], [r_w])
        DMAC(w_ukv_sb, w_ukv_d[l], [], [r_w])
        qs, r_qs = aalloc([96, 8, 512], F32, "qs")
        qsw, r_qsw = aalloc([96, 8, 512], F32, "qsw")
        qb_ring = ARing(2, [96, 8, 512], BF16, "qb")
        kb_ring = ARing(2, [96, 8, 512], BF16, "kb")
        vb_ring = ARing(2, [128, 8, 4, 65], BF16, "vb")
        ub_ring = ARing(2, [128, 4, 512], BF16, "ub")
        cT_ring = ARing(2, [128, 3, 512], BF16, "cT")
        cs_ring = ARing(2, [96, 2, 512], F32, "csb")
        ta, r_ta = aalloc([96, 1024], F32, "ta")
        for vb, r_vb in vb_ring.bufs:
            VEC(lambda e, vb=vb: e.memset(vb[:, :, :, 64:65], 1.0), [], [r_vb])

        ctxA = {}

        htx = [(frB[0][:, 0:1024], Res("htx0")), (frB[0][:, 1024:2048], Res("htx1"))]
        frA_b = frA[0].bitcast(BF16)
        xnx = [(frA_b[:, 0:1024], Res("xnx0")), (frA_b[:, 1024:2048], Res("xnx1"))]
        jkx = [(frA_b[:, 2048 + i * 1024:2048 + (i + 1) * 1024], Res(f"jkx{i}")) for i in range(4)]

        def stageA1(b):
            T0 = b * 512
            xnT, r_xnT = xnT_ring.next()
            cT, r_cT = cT_ring.next()
            vb, r_vb = vb_ring.next()
            csb, r_csb = cs_ring.next()
            ctxA[b] = (xnT, r_xnT, cT, r_cT, csb, r_csb)
            DMA(csb[RS, 0, :], cos_d[:, T0:T0 + 512], [r_csd], [r_csb])
            DMA(csb[RS, 1, :], sin_d[:, T0:T0 + 512], [r_csd], [r_csb])
            hts = [ht_ring.next(), ht_ring.next(), htx[0], htx[1]]
            xns = [xn_ring.next(), xn_ring.next(), xnx[0], xnx[1]]
            sts = [st_ring.next() for _ in range(4)]
            gcol = col(l, C_GMIX, 8)
            gq = col(l, C_GQ, 3)
            for sub in range(4):
                ht, r_ht = hts[sub]
                DMA(ht[:, :], src_d[T0 + sub * 128:T0 + (sub + 1) * 128, :], [r_src[b * 4 + sub]], [r_ht])
            yield
            for sub in range(4):
                (ht, r_ht), (st, r_st), (jk, r_jk) = hts[sub], sts[sub], jkx[sub]
                ACT(lambda e, ht=ht, st=st, jk=jk: e.activation(out=jk, in_=ht[:, :], func=AF.Square, accum_out=st[:, 0:1]), [r_ht], [r_jk, r_st])
            for sub in range(4):
                st, r_st = sts[sub]
                ACT(lambda e, st=st: e.activation(out=st[:, 0:1], in_=st[:, 0:1], func=AF.Sqrt, scale=1.0 / D, bias=EPS), [r_st], [r_st])
            yield
            for sub in range(4):
                st, r_st = sts[sub]
                VEC(lambda e, st=st: e.reciprocal(out=st[:, 0:1], in_=st[:, 0:1]), [r_st], [r_st])
            for sub in range(4):
                (ht, r_ht), (st, r_st), (xn, r_xn) = hts[sub], sts[sub], xns[sub]
                VEC(lambda e, ht=ht, st=st, xn=xn: e.tensor_scalar(out=xn[:, :], in0=ht[:, :], scalar1=st[:, 0:1], scalar2=None, op0=ALU.mult),
                    [r_ht, r_st], [r_xn])
            yield
            bks = []
            for sub in range(4):
                xn, r_xn = xns[sub]
                bk, bkb, r_bk = next_bank()
                bkv = bkb[:, :].rearrange("p (a b) -> p a b", b=128)
                for k in range(8):
                    PE(lambda e, k=k, bkv=bkv, xn=xn: e.transpose(out=bkv[:, k, :], in_=xn[:, k * 128:(k + 1) * 128], identity=ident[:]), [r_xn, r_const], [r_bk])
                bks.append((bkv, r_bk))
                if sub % 2 == 1:
                    yield
            for sub in range(4):
                bkv, r_bk = bks[sub]
                VEC(lambda e, bkv=bkv, sub=sub: e.tensor_tensor(out=xnT[:, :, sub * 128:(sub + 1) * 128], in0=bkv, in1=gcol.to_broadcast([128, 8, 128]), op=ALU.mult),
                    [r_bk, r_const], [r_xnT])
            yield
            cbk = []
            for sub in range(4):
                bk, bkb, r_bk = next_bank()
                for k in range(8):
                    PE(lambda e, k=k, bk=bk, sub=sub: e.matmul(bk[:, 0:384], lhsT=xnT[:, k, sub * 128:(sub + 1) * 128], rhs=w_in_sb[:, k, 0:384],
                                                              start=(k == 0), stop=(k == 7)), [r_xnT, r_w], [r_bk])
                cbk.append((bk, r_bk))
                if sub % 2 == 1:
                    yield
            for sub in range(4):
                (bk, r_bk), (st, r_st), (jk, r_jk) = cbk[sub], sts[sub], jkx[sub]
                ACT(lambda e, bk=bk, st=st, jk=jk: e.activation(out=jk[:, 0:256], in_=bk[:, 0:256], func=AF.Square, accum_out=st[:, 1:2]), [r_bk], [r_jk, r_st])
                ACT(lambda e, bk=bk, st=st, jk=jk: e.activation(out=jk[:, 256:384], in_=bk[:, 256:384], func=AF.Square, accum_out=st[:, 2:3]), [r_bk], [r_jk, r_st])
            yield
            for sub in range(4):
                st, r_st = sts[sub]
                ACT(lambda e, st=st: e.activation(out=st[:, 1:2], in_=st[:, 1:2], func=AF.Sqrt, scale=1.0 / 256, bias=EPS), [r_st], [r_st])
                ACT(lambda e, st=st: e.activation(out=st[:, 2:3], in_=st[:, 2:3], func=AF.Sqrt, scale=1.0 / 128, bias=EPS), [r_st], [r_st])
            for sub in range(4):
                st, r_st = sts[sub]
                VEC(lambda e, st=st: e.reciprocal(out=st[:, 1:3], in_=st[:, 1:3]), [r_st], [r_st])
            yield
            for sub in range(4):
                (bk, r_bk), (st, r_st), (cn, r_cn) = cbk[sub], sts[sub], xns[sub]
                VEC(lambda e, bk=bk, cn=cn, st=st: e.tensor_scalar(out=cn[:, 0:256], in0=bk[:, 0:256], scalar1=st[:, 1:2], scalar2=None, op0=ALU.mult),
                    [r_bk, r_st], [r_cn])
                VEC(lambda e, bk=bk, cn=cn, st=st: e.tensor_scalar(out=cn[:, 256:384], in0=bk[:, 256:384], scalar1=st[:, 2:3], scalar2=None, op0=ALU.mult),
                    [r_bk, r_st], [r_cn])
            yield
            bk2s = []
            for sub in range(4):
                cn, r_cn = xns[sub]
                bk2, bk2b, r_bk2 = next_bank()
                bk2v = bk2b[:, :].rearrange("p (a b) -> p a b", b=128)
                for k in range(3):
                    PE(lambda e, k=k, bk2v=bk2v, cn=cn: e.transpose(out=bk2v[:, k, :], in_=cn[:, k * 128:(k + 1) * 128], identity=ident[:]), [r_cn, r_const], [r_bk2])
                bk2s.append((bk2v, r_bk2))
            yield
            for sub in range(4):
                bk2v, r_bk2 = bk2s[sub]
                VEC(lambda e, bk2v=bk2v, sub=sub: e.tensor_tensor(out=cT[:, :, sub * 128:(sub + 1) * 128], in0=bk2v[:, 0:3, :],
                                                                 in1=gq.to_broadcast([128, 3, 128]), op=ALU.mult), [r_bk2, r_const], [r_cT])
            yield
            for sub in range(4):
                bk3, _, r_bk3 = next_bank()
                PE(lambda e, bk3=bk3, sub=sub: e.matmul(bk3[:, :], lhsT=cT[:, 2, sub * 128:(sub + 1) * 128], rhs=w_ukv_sb[:, 512:1024],
                                                       start=True, stop=True), [r_cT, r_w], [r_bk3])
                ACT(lambda e, bk3=bk3, sub=sub: e.copy(out=vb[:, :, sub, 0:64], in_=bk3[:, :].rearrange("p (h d) -> p h d", d=64)), [r_bk3], [r_vb])
                if sub % 2 == 1:
                    yield
            DMA(V_d[:, :, b * 4:(b + 1) * 4, :].rearrange("h p t d -> p h t d"), vb, [r_vb], [r_vd], q="scalar")
            yield

        def stageA2(b):
            T0 = b * 512
            xnT, r_xnT, cT, r_cT, csb, r_csb = ctxA.pop(b)
            ub, r_ub = ub_ring.next()
            for ct in range(4):
                yield
                bk, _, r_bk = next_bank()
                for k in range(8):
                    PE(lambda e, k=k, bk=bk, xnT=xnT, ct=ct: e.matmul(bk[:, :], lhsT=w_in_sb[:, k, 416 + ct * 128:416 + (ct + 1) * 128],
                                                                      rhs=xnT[:, k, :], start=(k == 0), stop=(k == 7)), [r_xnT, r_w], [r_bk])
                ACT(lambda e, bk=bk, ct=ct, ub=ub: e.copy(out=ub[:, ct, :], in_=bk[:, :]), [r_bk], [r_ub])
            DMA(uT_d[:, :, T0:T0 + 512], ub, [r_ub], [r_ud], q="scalar")
            yield
            bka, _, r_bka = next_bank()
            bkb_, _, r_bkb = next_bank()
            for k in range(8):
                PE(lambda e, k=k, bka=bka, xnT=xnT: e.matmul(bka[0:96, :], lhsT=w_kr_sb[:, k, 0:96], rhs=xnT[:, k, :], start=(k == 0), stop=(k == 7)),
                   [r_xnT, r_w], [r_bka])
            for k in range(8):
                PE(lambda e, k=k, bkb_=bkb_, xnT=xnT: e.matmul(bkb_[0:96, :], lhsT=w_kr_sb[:, k, 96:192], rhs=xnT[:, k, :], start=(k == 0), stop=(k == 7)),
                   [r_xnT, r_w], [r_bkb])
            yield
            VEC(lambda e, bka=bka, csb=csb: e.tensor_tensor(out=ta[RS, 0:512], in0=bka[RS, :], in1=csb[RS, 0, :], op=ALU.mult), [r_bka, r_csb], [r_ta])
            VEC(lambda e, bkb_=bkb_, csb=csb: e.tensor_tensor(out=ta[RS, 512:1024], in0=bkb_[RS, :], in1=csb[RS, 1, :], op=ALU.mult), [r_bkb, r_csb], [r_ta])
            VEC(lambda e: e.tensor_tensor(out=ta[RS, 0:512], in0=ta[RS, 0:512], in1=ta[RS, 512:1024], op=ALU.add), [r_ta], [r_ta])
            kb, r_kb = kb_ring.next()
            POOL(lambda e, kb=kb: e.tensor_copy(out=kb[RS, :, :], in_=ta[RS, 0:512].rearrange("p (o t) -> p o t", o=1).to_broadcast([32, 8, 512])),
                 [r_ta], [r_kb])
            for h in range(8):
                yield
                bk, _, r_bk = next_bank()
                PE(lambda e, bk=bk, h=h, cT=cT: e.matmul(bk[0:64, :], lhsT=w_ukv_sb[:, h * 64:(h + 1) * 64], rhs=cT[:, 2, :], start=True, stop=True),
                   [r_cT, r_w], [r_bk])
                ACT(lambda e, bk=bk, h=h, kb=kb: e.copy(out=kb[0:64, h, :], in_=bk[0:64, :]), [r_bk], [r_kb])
            DMA(kT_d[:, :, T0:T0 + 512].rearrange("h p t -> p h t"), kb, [r_kb], [r_kd], q="scalar")
            for h in range(8):
                yield
                bka, _, r_bka = next_bank()
                bkb_, _, r_bkb = next_bank()
                for k in range(2):
                    PE(lambda e, k=k, bka=bka, h=h, cT=cT: e.matmul(bka[0:96, :], lhsT=w_uq_sb[:, k, h * 96:(h + 1) * 96], rhs=cT[:, k, :],
                                                                    start=(k == 0), stop=(k == 1)), [r_cT, r_w], [r_bka])
                for k in range(2):
                    PE(lambda e, k=k, bkb_=bkb_, h=h, cT=cT: e.matmul(bkb_[0:96, :], lhsT=w_uqs_sb[:, k, h * 96:(h + 1) * 96], rhs=cT[:, k, :],
                                                                      start=(k == 0), stop=(k == 1)), [r_cT, r_w], [r_bkb])
                ACT(lambda e, bka=bka, h=h: e.copy(out=qs[:, h, :], in_=bka[0:96, :]), [r_bka], [r_qs])
                ACT(lambda e, bkb_=bkb_, h=h: e.copy(out=qsw[RS, h, :], in_=bkb_[RS, :]), [r_bkb], [r_qsw])
            yield
            qb, r_qb = qb_ring.next()
            POOL(lambda e, qb=qb: e.tensor_copy(out=qb[0:64, :, :], in_=qs[0:64, :, :]), [r_qs], [r_qb])
            VEC(lambda e, csb=csb: e.tensor_tensor(out=qs[RS, :, :], in0=qs[RS, :, :],
                                                   in1=csb[RS, 0:1, :].to_broadcast([32, 8, 512]), op=ALU.mult), [r_qs, r_csb], [r_qs])
            VEC(lambda e, csb=csb: e.tensor_tensor(out=qsw[RS, :, :], in0=qsw[RS, :, :],
                                                   in1=csb[RS, 1:2, :].to_broadcast([32, 8, 512]), op=ALU.mult), [r_qsw, r_csb], [r_qsw])
            VEC(lambda e, qb=qb: e.tensor_tensor(out=qb[RS, :, :], in0=qs[RS, :, :], in1=qsw[RS, :, :], op=ALU.add), [r_qs, r_qsw], [r_qb])
            DMA(qT_d[:, :, T0:T0 + 512].rearrange("h p t -> p h t"), qb, [r_qb], [r_qd])
            yield

        def run_interleaved(gens):
            gens = list(gens)
            while gens:
                for g_ in list(gens):
                    try:
                        next(g_)
                    except StopIteration:
                        gens.remove(g_)

        for b in range(NB + 1):
            gl = []
            if b >= 1:
                gl.append(stageA2(b - 1))
            if b < NB:
                gl.append(stageA1(b))
            run_interleaved(gl)

        if stop_after == "A":
            return True

        P.barrier()
        areset()
        S_POOL = (0, 1, 2)
        O_POOL = (3, 4)
        M_POOL = (5, 6, 7)
        qh_ring = ARing(2, [96, S], BF16, "qh")
        kh_ring = ARing(2, [96, S], BF16, "kh")
        vh_ring = ARing(2, [128, 32, 65], BF16, "vh")
        oT_ring = ARing(3, [65, 1024], F32, "oT3")
        an_ring = ARing(3, [64, 512], F32, "an3")
        LA = 2

        def load_head(h):
            qh, r_qh = qh_ring.next()
            kh, r_kh = kh_ring.next()
            vh, r_vh = vh_ring.next()
            DMA(qh, qT_d[h], [r_qd], [r_qh])
            DMA(kh, kT_d[h], [r_kd], [r_kh])
            DMA(vh, V_d[h], [r_vd], [r_vh])
            return (qh, r_qh, kh, r_kh, vh, r_vh)

        heads = {0: load_head(0)}
        for h in range(8):
            qh, r_qh, kh, r_kh, vh, r_vh = heads[h]
            if h + 1 < 8:
                heads[h + 1] = load_head(h + 1)
            items = [(b, kt) for b in range(NB) for kt in range(4 * (b + 1))]
            pts = {}
            bos = {}
            deferred = []

            def stage1(i):
                b, kt = items[i]
                T0 = b * 512
                bs, _, r_bs = next_bank(S_POOL)
                PE(lambda e, bs=bs, kt=kt, kh=kh, qh=qh, T0=T0: e.matmul(bs[:, :], lhsT=kh[:, kt * 128:(kt + 1) * 128], rhs=qh[:, T0:T0 + 512],
                                                                         start=True, stop=True), [r_kh, r_qh], [r_bs])
                pT, r_pT = pT_ring.next()
                ACT(lambda e, bs=bs, pT=pT: e.activation(out=pT[:], in_=bs[:, :], func=AF.Exp, scale=ATT_SCALE), [r_bs], [r_pT])
                if kt >= 4 * b:
                    base = T0 - kt * 128
                    POOL(lambda e, pT=pT, base=base: e.affine_select(out=pT[:], in_=pT[:], pattern=[[1, 512]], compare_op=ALU.is_ge,
                                                                     fill=0.0, base=base, channel_multiplier=-1), [r_pT], [r_pT])
                pts[i] = (pT, r_pT)

            def stage2(j, i_now):
                b, kt = items[j]
                nkt = 4 * (b + 1)
                if kt == 0:
                    bos[b] = next_bank(O_POOL)
                bo, _, r_bo = bos[b]
                pT, r_pT = pts.pop(j)
                PE(lambda e, bo=bo, pT=pT, kt=kt, nkt=nkt, vh=vh: e.matmul(bo[0:65, :], lhsT=vh[:, kt, :], rhs=pT[:],
                                                                           start=(kt == 0), stop=(kt == nkt - 1)), [r_vh, r_pT], [r_bo])
                if kt == nkt - 1:
                    oT, r_oT = oT_ring.next()
                    VEC(lambda e, bo=bo, oT=oT: e.tensor_copy(out=oT[:, 0:512], in_=bo[0:65, :]), [r_bo], [r_oT])

                    def epi(b=b, oT=oT, r_oT=r_oT):
                        bm, _, r_bm = next_bank(M_POOL)
                        PE(lambda e, bm=bm, oT=oT: e.matmul(bm[0:64, :], lhsT=sel65[:, :], rhs=oT[:, 0:512], start=True, stop=True), [r_oT, r_const], [r_bm])
                        VEC(lambda e, bm=bm, oT=oT: e.reciprocal(out=oT[0:64, 512:1024], in_=bm[0:64, :]), [r_bm, r_oT], [r_oT])
                        an, r_an = an_ring.next()
                        POOL(lambda e, oT=oT, an=an: e.tensor_tensor(out=an[:, :], in0=oT[0:64, 0:512], in1=oT[0:64, 512:1024], op=ALU.mult), [r_oT], [r_an])
                        DMA(aT_d[b, :, h, :], an, [r_an], [r_ad], q="gpsimd")
                    deferred.append((i_now + 3, epi))

            n_it = len(items)
            for i in range(n_it + LA):
                if i < n_it:
                    stage1(i)
                if i - LA >= 0:
                    stage2(i - LA, i)
                while deferred and deferred[0][0] <= i:
                    deferred.pop(0)[1]()
            while deferred:
                deferred.pop(0)[1]()
        if stop_after == "B1":
            return True
        P.barrier()
        areset()
        Wst, r_Wst = aalloc([128, 4, 8, 2, 128], BF16, "Wst")
        Wfir, r_Wfir = aalloc([128, 4, 8, 128], BF16, "Wfir")
        Wo_r, r_Wo = aalloc([128, 16, 8, 32], BF16, "Wo")
        Wo_i, _ = aalloc([128, 16, 8, 32], BF16)
        sm, r_sm = aalloc([128, 32, 16], F32, "sm")
        pw_r, _ = aalloc([128, 16, 9], F32)
        pw_i, _ = aalloc([128, 16, 9], F32)
        ph_r, _ = aalloc([128, 16, 9], F32)
        ph_i, _ = aalloc([128, 16, 9], F32)
        mark = arena_off[0]
        Bri, r_bc = aalloc([128, 2, 16, 16], F32, "Bri")
        Cri, _ = aalloc([128, 2, 16, 16], F32)
        Bb_r, r_T = aalloc([128, 16, 16], F32, "T")
        Bb_i, _ = aalloc([128, 16, 16], F32)
        T1, _ = aalloc([128, 16, 16], F32)
        T2, _ = aalloc([128, 16, 16], F32)
        T3, _ = aalloc([128, 16, 16], F32)
        T4, _ = aalloc([128, 16, 16], F32)
        ME_r, r_ME = aalloc([128, 8, 4, 128], F32, "ME")
        ME_i, _ = aalloc([128, 8, 4, 128], F32)
        MF_r, r_MF = aalloc([128, 4, 128], F32, "MF")
        MF_in, _ = aalloc([128, 4, 128], F32)
        tmpF, r_tmpF = aalloc([128, 128], F32, "tmpF")
        DMA(sm[:, 0:3, :], s5v_d[l], [], [r_sm])
        DMA(Bri, s5b_d[l], [], [r_bc])
        DMA(Cri, s5c_d[l], [], [r_bc])
        POOL(lambda e: e.memset(ME_r, 0.0), [], [r_ME])
        POOL(lambda e: e.memset(ME_i, 0.0), [], [r_ME])
        POOL(lambda e: e.memset(MF_r, 0.0), [], [r_MF])
        POOL(lambda e: e.memset(MF_in, 0.0), [], [r_MF])
        POOL(lambda e: e.memset(Wo_r, 0.0), [], [r_Wo])
        POOL(lambda e: e.memset(Wo_i, 0.0), [], [r_Wo])
        LRE, LIM, LDT, DT, TT_, ER, Y, RND, FR, SN_, CS_, AR, AI, ARM1, DEN, RDEN, CBR, CBI, U1, U2, RDEC, RR = range(22)

        def smtt(o, a, b, op):
            VEC(lambda e: e.tensor_tensor(out=sm[:, o, :], in0=sm[:, a, :], in1=sm[:, b, :], op=op), [r_sm], [r_sm])

        def smts(o, a, s1, op0, s2=None, op1=None):
            if op1 is None:
                VEC(lambda e: e.tensor_scalar(out=sm[:, o, :], in0=sm[:, a, :], scalar1=s1, scalar2=None, op0=op0), [r_sm], [r_sm])
            else:
                VEC(lambda e: e.tensor_scalar(out=sm[:, o, :], in0=sm[:, a, :], scalar1=s1, scalar2=s2, op0=op0, op1=op1), [r_sm], [r_sm])

        def smact(o, a, func, scale=1.0):
            ACT(lambda e: e.activation(out=sm[:, o, :], in_=sm[:, a, :], func=func, scale=scale), [r_sm], [r_sm])

        smact(DT, LDT, AF.Exp)
        smtt(TT_, LRE, DT, ALU.mult)
        smact(ER, TT_, AF.Exp)
        smact(RDEC, TT_, AF.Exp, 8.0)
        smtt(Y, LIM, DT, ALU.mult)
        smts(Y, Y, 1.0 / TWO_PI, ALU.mult)
        smts(RND, Y, MAGIC, ALU.add, MAGIC, ALU.subtract)
        smtt(FR, Y, RND, ALU.subtract)
        smact(SN_, FR, AF.Sin, SIN_SCALE)
        smts(Y, Y, 0.25, ALU.add)
        smts(RND, Y, MAGIC, ALU.add, MAGIC, ALU.subtract)
        smtt(FR, Y, RND, ALU.subtract)
        smact(CS_, FR, AF.Sin, SIN_SCALE)
        smtt(AR, ER, CS_, ALU.mult)
        smtt(AI, ER, SN_, ALU.mult)
        smts(ARM1, AR, -1.0, ALU.add)
        smtt(U1, LRE, LRE, ALU.mult)
        smtt(U2, LIM, LIM, ALU.mult)
        smtt(DEN, U1, U2, ALU.add)
        VEC(lambda e: e.reciprocal(out=sm[:, RDEN, :], in_=sm[:, DEN, :]), [r_sm], [r_sm])
        smtt(U1, ARM1, LRE, ALU.mult)
        smtt(U2, AI, LIM, ALU.mult)
        smtt(U1, U1, U2, ALU.add)
        smtt(CBR, U1, RDEN, ALU.mult)
        smtt(U1, AI, LRE, ALU.mult)
        smtt(U2, ARM1, LIM, ALU.mult)
        smtt(U1, U1, U2, ALU.subtract)
        smtt(CBI, U1, RDEN, ALU.mult)
        VEC(lambda e: e.memset(pw_r[:, :, 0:1], 1.0), [r_sm], [r_sm])
        VEC(lambda e: e.memset(pw_i[:, :, 0:1], 0.0), [r_sm], [r_sm])
        VEC(lambda e: e.tensor_copy(out=pw_r[:, :, 1], in_=sm[:, AR, :]), [r_sm], [r_sm])
        VEC(lambda e: e.tensor_copy(out=pw_i[:, :, 1], in_=sm[:, AI, :]), [r_sm], [r_sm])
        for k in range(1, 8):
            VEC(lambda e, k=k: e.tensor_tensor(out=sm[:, U1, :], in0=pw_r[:, :, k], in1=sm[:, AR, :], op=ALU.mult), [r_sm], [r_sm])
            VEC(lambda e, k=k: e.tensor_tensor(out=sm[:, U2, :], in0=pw_i[:, :, k], in1=sm[:, AI, :], op=ALU.mult), [r_sm], [r_sm])
            VEC(lambda e, k=k: e.tensor_tensor(out=pw_r[:, :, k + 1], in0=sm[:, U1, :], in1=sm[:, U2, :], op=ALU.subtract), [r_sm], [r_sm])
            VEC(lambda e, k=k: e.tensor_tensor(out=sm[:, U1, :], in0=pw_r[:, :, k], in1=sm[:, AI, :], op=ALU.mult), [r_sm], [r_sm])
            VEC(lambda e, k=k: e.tensor_tensor(out=sm[:, U2, :], in0=pw_i[:, :, k], in1=sm[:, AR, :], op=ALU.mult), [r_sm], [r_sm])
            VEC(lambda e, k=k: e.tensor_tensor(out=pw_i[:, :, k + 1], in0=sm[:, U1, :], in1=sm[:, U2, :], op=ALU.add), [r_sm], [r_sm])
        VEC(lambda e: e.reciprocal(out=sm[:, RR, :], in_=sm[:, RDEC, :]), [r_sm], [r_sm])
        VEC(lambda e: e.tensor_tensor(out=ph_r[:, :, 0], in0=pw_r[:, :, 8], in1=sm[:, RR, :], op=ALU.mult), [r_sm], [r_sm])
        VEC(lambda e: e.tensor_tensor(out=ph_i[:, :, 0], in0=pw_i[:, :, 8], in1=sm[:, RR, :], op=ALU.mult), [r_sm], [r_sm])
        for k in range(8):
            VEC(lambda e, k=k: e.tensor_tensor(out=sm[:, U1, :], in0=ph_r[:, :, k], in1=ph_r[:, :, k], op=ALU.mult), [r_sm], [r_sm])
            VEC(lambda e, k=k: e.tensor_tensor(out=sm[:, U2, :], in0=ph_i[:, :, k], in1=ph_i[:, :, k], op=ALU.mult), [r_sm], [r_sm])
            VEC(lambda e, k=k: e.tensor_tensor(out=ph_r[:, :, k + 1], in0=sm[:, U1, :], in1=sm[:, U2, :], op=ALU.subtract), [r_sm], [r_sm])
            VEC(lambda e, k=k: e.tensor_tensor(out=sm[:, U1, :], in0=ph_r[:, :, k], in1=ph_i[:, :, k], op=ALU.mult), [r_sm], [r_sm])
            VEC(lambda e, k=k: e.tensor_scalar(out=ph_i[:, :, k + 1], in0=sm[:, U1, :], scalar1=2.0, scalar2=None, op0=ALU.mult), [r_sm], [r_sm])

        def bc16(tile_idx_ap):
            return tile_idx_ap.rearrange("p (a o) -> p a o", o=1).to_broadcast([128, 16, 16])

        def cmul_bc(outr, outi, xr, xi, sr_ap, si_ap, rds, wrs):
            pass

        VEC(lambda e: e.tensor_tensor(out=T1, in0=Bri[:, 0], in1=bc16(sm[:, CBR, :]), op=ALU.mult), [r_sm, r_bc], [r_T])
        VEC(lambda e: e.tensor_tensor(out=T2, in0=Bri[:, 1], in1=bc16(sm[:, CBI, :]), op=ALU.mult), [r_sm, r_bc], [r_T])
        VEC(lambda e: e.tensor_tensor(out=Bb_r, in0=T1, in1=T2, op=ALU.subtract), [r_T], [r_T])
        VEC(lambda e: e.tensor_tensor(out=T1, in0=Bri[:, 1], in1=bc16(sm[:, CBR, :]), op=ALU.mult), [r_sm, r_bc, r_T], [r_T])
        VEC(lambda e: e.tensor_tensor(out=T2, in0=Bri[:, 0], in1=bc16(sm[:, CBI, :]), op=ALU.mult), [r_sm, r_bc], [r_T])
        VEC(lambda e: e.tensor_tensor(out=Bb_i, in0=T1, in1=T2, op=ALU.add), [r_T], [r_T])

        def blkME(M, lg, hf):
            return M[hf * 64:(hf + 1) * 64, lg, :, :].rearrange("p ct (q x) -> p ct q x", x=32)[:, :, :, hf * 16:(hf + 1) * 16]

        def halfT(T, hf):
            return T[hf * 64:(hf + 1) * 64, :, :].rearrange("p (ct q) c -> p ct q c", q=4)

        for lg in range(8):
            VEC(lambda e, lg=lg: e.tensor_tensor(out=T1, in0=Bb_r, in1=pw_r[:, :, lg:lg + 1].to_broadcast([128, 16, 16]), op=ALU.mult), [r_sm, r_T], [r_T])
            VEC(lambda e, lg=lg: e.tensor_tensor(out=T2, in0=Bb_i, in1=pw_i[:, :, lg:lg + 1].to_broadcast([128, 16, 16]), op=ALU.mult), [r_sm, r_T], [r_T])
            VEC(lambda e, lg=lg: e.tensor_tensor(out=T3, in0=Bb_i, in1=pw_r[:, :, lg:lg + 1].to_broadcast([128, 16, 16]), op=ALU.mult), [r_sm, r_T], [r_T])
            VEC(lambda e, lg=lg: e.tensor_tensor(out=T4, in0=Bb_r, in1=pw_i[:, :, lg:lg + 1].to_broadcast([128, 16, 16]), op=ALU.mult), [r_sm, r_T], [r_T])
            for hf in range(2):
                POOL(lambda e, lg=lg, hf=hf: e.tensor_tensor(out=blkME(ME_r, lg, hf), in0=halfT(T1, hf), in1=halfT(T2, hf), op=ALU.subtract), [r_T], [r_ME])
                POOL(lambda e, lg=lg, hf=hf: e.tensor_tensor(out=blkME(ME_i, lg, hf), in0=halfT(T3, hf), in1=halfT(T4, hf), op=ALU.add), [r_T], [r_ME])

        def blkMF(M, hf):
            return M[hf * 64:(hf + 1) * 64, :, :].rearrange("p ct (q x) -> p ct q x", x=32)[:, :, :, hf * 16:(hf + 1) * 16]

        for hf in range(2):
            POOL(lambda e, hf=hf: e.tensor_copy(out=blkMF(MF_r, hf), in_=halfT(Cri[:, 0], hf)), [r_bc], [r_MF])
            POOL(lambda e, hf=hf: e.tensor_scalar(out=blkMF(MF_in, hf), in0=halfT(Cri[:, 1], hf), scalar1=-1.0, scalar2=None, op0=ALU.mult), [r_bc], [r_MF])
        for ct in range(4):
            for ri, M in ((0, ME_r), (1, ME_i)):
                for j0 in (0, 4):
                    bk, _, r_bk = next_bank()
                    for jj in range(4):
                        j = j0 + jj
                        PE(lambda e, bk=bk, jj=jj, j=j, ct=ct, M=M: e.transpose(out=bk[:, jj * 128:(jj + 1) * 128], in_=M[:, 7 - j, ct, :], identity=identf[:]),
                           [r_ME, r_const], [r_bk])
                    ACT(lambda e, bk=bk, ct=ct, j0=j0, ri=ri: e.copy(out=Wst[:, ct, j0:j0 + 4, ri, :], in_=bk[:, :].rearrange("p (a b) -> p a b", b=128)),
                        [r_bk], [r_Wst])
        for ct in range(4):
            for l0 in (0, 4):
                bk, _, r_bk = next_bank()
                for ll in range(4):
                    lg = l0 + ll
                    PE(lambda e, bk=bk, ll=ll, lg=lg, ct=ct: e.matmul(bk[:, ll * 128:(ll + 1) * 128], lhsT=ME_r[:, lg, ct, :], rhs=MF_r[:, ct, :], start=True, stop=False),
                       [r_ME, r_MF], [r_bk])
                    PE(lambda e, bk=bk, ll=ll, lg=lg, ct=ct: e.matmul(bk[:, ll * 128:(ll + 1) * 128], lhsT=ME_i[:, lg, ct, :], rhs=MF_in[:, ct, :], start=False, stop=True),
                       [r_ME, r_MF], [r_bk])
                VEC(lambda e, bk=bk, ct=ct, l0=l0: e.tensor_tensor(out=Wfir[:, ct, l0:l0 + 4, :], in0=bk[:, :].rearrange("p (a b) -> p a b", b=128),
                                                                 in1=mask16[:, :].rearrange("p (o b) -> p o b", o=1).to_broadcast([128, 4, 128]), op=ALU.mult),
                    [r_bk, r_const], [r_Wfir])
                if l0 == 0:
                    VEC(lambda e, bk=bk: e.tensor_tensor(out=tmpF, in0=bk[:, 0:128], in1=mask16[:, :], op=ALU.mult), [r_bk, r_const], [r_tmpF])
                    VEC(lambda e, ct=ct: e.scalar_tensor_tensor(out=Wfir[:, ct, 0, :], in0=identf[:, :], scalar=col(l, C_D + ct), in1=tmpF, op0=ALU.mult, op1=ALU.add),
                        [r_tmpF, r_const], [r_Wfir])
        for i in range(8):
            VEC(lambda e, i=i: e.tensor_tensor(out=T1, in0=Cri[:, 0], in1=pw_r[:, :, i + 1:i + 2].to_broadcast([128, 16, 16]), op=ALU.mult), [r_sm, r_bc, r_T], [r_T])
            VEC(lambda e, i=i: e.tensor_tensor(out=T2, in0=Cri[:, 1], in1=pw_i[:, :, i + 1:i + 2].to_broadcast([128, 16, 16]), op=ALU.mult), [r_sm, r_bc, r_T], [r_T])
            VEC(lambda e, i=i: e.tensor_tensor(out=T3, in0=Cri[:, 1], in1=pw_r[:, :, i + 1:i + 2].to_broadcast([128, 16, 16]), op=ALU.mult), [r_sm, r_bc, r_T], [r_T])
            VEC(lambda e, i=i: e.tensor_tensor(out=T4, in0=Cri[:, 0], in1=pw_i[:, :, i + 1:i + 2].to_broadcast([128, 16, 16]), op=ALU.mult), [r_sm, r_bc, r_T], [r_T])
            for hf in range(2):
                hs = slice(hf * 64, (hf + 1) * 64)
                cs = slice(hf * 16, (hf + 1) * 16)
                POOL(lambda e, i=i, hs=hs, cs=cs: e.tensor_tensor(out=Wo_r[hs, :, i, cs], in0=T1[hs], in1=T2[hs], op=ALU.subtract), [r_T], [r_Wo])
                VEC(lambda e, i=i, hs=hs, cs=cs: e.scalar_tensor_tensor(out=Wo_i[hs, :, i, cs], in0=T3[hs], scalar=-1.0, in1=T4[hs], op0=ALU.mult, op1=ALU.subtract),
                    [r_T], [r_Wo])
        P.barrier()
        arena_off[0] = mark
        u_ring = ARing(2, [128, S], BF16, "uct")
        y_ring = ARing(1, [128, S], F32, "yct")
        tab_r, r_tab = aalloc([128, 4, 512], F32, "tab")
        tab_i, _ = aalloc([128, 4, 512], F32)
        tq1, r_tq = aalloc([128, 4, 256], F32, "tq")
        tq2, _ = aalloc([128, 4, 256], F32)
        xp_r, r_xp = aalloc([128, 4, 512], BF16, "xp")
        xp_i, _ = aalloc([128, 4, 512], BF16)
        A_ring = ARing(6, [128, 512], F32, "A")
        VEC(lambda e: e.memset(xp_r[:, :, 0:1], 0.0), [], [r_xp])
        VEC(lambda e: e.memset(xp_i[:, :, 0:1], 0.0), [], [r_xp])
        for ct in range(4):
            uct, r_uct = u_ring.next()
            DMA(uct, uT_d[:, ct, :], [r_ud], [r_uct])
            u8 = uct.rearrange("p (c j) -> p j c", j=8)
            VEC(lambda e: e.memset(tab_r[:, :, 0:1], 1.0), [r_tab], [r_tab])
            VEC(lambda e: e.memset(tab_i[:, :, 0:1], 0.0), [r_tab], [r_tab])
            for k in range(9):
                s_ = 1 << k
                phr = ph_r[:, 4 * ct:4 * ct + 4, k:k + 1].to_broadcast([128, 4, s_])
                phi = ph_i[:, 4 * ct:4 * ct + 4, k:k + 1].to_broadcast([128, 4, s_])
                VEC(lambda e, s_=s_, phr=phr: e.tensor_tensor(out=tq1[:, :, 0:s_], in0=tab_r[:, :, 0:s_], in1=phr, op=ALU.mult), [r_tab, r_sm, r_tq], [r_tq])
                VEC(lambda e, s_=s_, phi=phi: e.tensor_tensor(out=tq2[:, :, 0:s_], in0=tab_i[:, :, 0:s_], in1=phi, op=ALU.mult), [r_tab, r_sm, r_tq], [r_tq])
                VEC(lambda e, s_=s_: e.tensor_tensor(out=tab_r[:, :, s_:2 * s_], in0=tq1[:, :, 0:s_], in1=tq2[:, :, 0:s_], op=ALU.subtract), [r_tq], [r_tab])
                VEC(lambda e, s_=s_, phi=phi: e.tensor_tensor(out=tq1[:, :, 0:s_], in0=tab_r[:, :, 0:s_], in1=phi, op=ALU.mult), [r_tab, r_sm, r_tq], [r_tq])
                VEC(lambda e, s_=s_, phr=phr: e.tensor_tensor(out=tq2[:, :, 0:s_], in0=tab_i[:, :, 0:s_], in1=phr, op=ALU.mult), [r_tab, r_sm, r_tq], [r_tq])
                VEC(lambda e, s_=s_: e.tensor_tensor(out=tab_i[:, :, s_:2 * s_], in0=tq1[:, :, 0:s_], in1=tq2[:, :, 0:s_], op=ALU.add), [r_tq], [r_tab])
            for q in range(4):
                pair = 4 * ct + q
                ps_ = slice(32 * q, 32 * q + 32)
                bks = []
                for ri in range(2):
                    bk, _, r_bk = next_bank()
                    for j in range(8):
                        PE(lambda e, bk=bk, j=j, ri=ri, ps_=ps_, q=q, ct=ct, u8=u8: e.matmul(bk[:, :], lhsT=Wst[ps_, ct, j, ri, :], rhs=u8[ps_, j, :],
                                                                                           start=(j == 0), stop=(j == 7), tile_position=(32 * q, 0)),
                           [r_Wst, r_uct], [r_bk])
                    bks.append((bk, r_bk))
                (Sr, r_Sr), (Si, r_Si) = bks
                tr, ti = tab_r[:, q, :], tab_i[:, q, :]
                a1, r_a1 = A_ring.next()
                a2, r_a2 = A_ring.next()
                a3, r_a3 = A_ring.next()
                a4, r_a4 = A_ring.next()
                VEC(lambda e, Sr=Sr, a1=a1, tr=tr: e.tensor_tensor(out=a1, in0=Sr[:, :], in1=tr, op=ALU.mult), [r_Sr, r_tab], [r_a1])
                VEC(lambda e, Si=Si, a2=a2, ti=ti: e.tensor_tensor(out=a2, in0=Si[:, :], in1=ti, op=ALU.mult), [r_Si, r_tab], [r_a2])
                VEC(lambda e, Si=Si, a3=a3, tr=tr: e.tensor_tensor(out=a3, in0=Si[:, :], in1=tr, op=ALU.mult), [r_Si, r_tab], [r_a3])
                VEC(lambda e, Sr=Sr, a4=a4, ti=ti: e.tensor_tensor(out=a4, in0=Sr[:, :], in1=ti, op=ALU.mult), [r_Sr, r_tab], [r_a4])
                POOL(lambda e, a1=a1, a2=a2: e.tensor_tensor(out=a1, in0=a1, in1=a2, op=ALU.add), [r_a1, r_a2], [r_a1])
                POOL(lambda e, a3=a3, a4=a4: e.tensor_tensor(out=a3, in0=a3, in1=a4, op=ALU.subtract), [r_a3, r_a4], [r_a3])
                rd = sm[:, RDEC, pair:pair + 1].to_broadcast([128, 512])
                VEC(lambda e, a1=a1, a2=a2, rd=rd: e.tensor_tensor_scan(out=a2, data0=rd, data1=a1, initial=0.0, op0=ALU.mult, op1=ALU.add), [r_a1, r_sm, r_a2], [r_a2])
                VEC(lambda e, a3=a3, a4=a4, rd=rd: e.tensor_tensor_scan(out=a4, data0=rd, data1=a3, initial=0.0, op0=ALU.mult, op1=ALU.add), [r_a3, r_sm, r_a4], [r_a4])
                b1, r_b1 = A_ring.next()
                b2, r_b2 = A_ring.next()
                POOL(lambda e, a2=a2, b1=b1, tr=tr: e.tensor_tensor(out=b1, in0=a2, in1=tr, op=ALU.mult), [r_a2, r_tab], [r_b1])
                POOL(lambda e, a4=a4, b2=b2, ti=ti: e.tensor_tensor(out=b2, in0=a4, in1=ti, op=ALU.mult), [r_a4, r_tab], [r_b2])
                VEC(lambda e, b1=b1, b2=b2, q=q: e.tensor_tensor(out=xp_r[:, q, 1:512], in0=b1[:, 0:511], in1=b2[:, 0:511], op=ALU.subtract), [r_b1, r_b2], [r_xp])
                POOL(lambda e, a2=a2, a1=a1, ti=ti: e.tensor_tensor(out=a1, in0=a2, in1=ti, op=ALU.mult), [r_a2, r_tab, r_a1], [r_a1])
                POOL(lambda e, a4=a4, a3=a3, tr=tr: e.tensor_tensor(out=a3, in0=a4, in1=tr, op=ALU.mult), [r_a4, r_tab, r_a3], [r_a3])
                VEC(lambda e, a1=a1, a3=a3, q=q: e.tensor_tensor(out=xp_i[:, q, 1:512], in0=a1[:, 0:511], in1=a3[:, 0:511], op=ALU.add), [r_a1, r_a3], [r_xp])
            yct, r_yct = y_ring.next()
            y8 = yct.rearrange("p (c j) -> p j c", j=8)
            for i in range(8):
                bk, _, r_bk = next_bank()
                for lg in range(i + 1):
                    PE(lambda e, bk=bk, lg=lg, i=i, ct=ct, u8=u8: e.matmul(bk[:, :], lhsT=Wfir[:, ct, lg, :], rhs=u8[:, i - lg, :], start=(lg == 0), stop=False),
                       [r_Wfir, r_uct], [r_bk])
                for q in range(4):
                    pair = 4 * ct + q
                    PE(lambda e, bk=bk, q=q, pair=pair, i=i: e.matmul(bk[32 * q:32 * q + 32, :], lhsT=Wo_r[:, pair, i, :], rhs=xp_r[:, q, :], start=False, stop=False,
                                                                     tile_position=(0, 32 * q)), [r_Wo, r_xp], [r_bk])
                    PE(lambda e, bk=bk, q=q, pair=pair, i=i: e.matmul(bk[32 * q:32 * q + 32, :], lhsT=Wo_i[:, pair, i, :], rhs=xp_i[:, q, :], start=False, stop=(q == 3),
                                                                     tile_position=(0, 32 * q)), [r_Wo, r_xp], [r_bk])
                ACT(lambda e, bk=bk, y8=y8, i=i: e.copy(out=y8[:, i, :], in_=bk[:, :]), [r_bk], [r_yct])
            DMA(yT_d[:, ct, :], yct, [r_yct], [r_yd], q="scalar")
        if stop_after == "B2a":
            return True
        P.barrier()
        arena_off[0] = mark
        wglu, r_wglu = aalloc([128, 4, 512], BF16, "wglu")
        DMAC(wglu, w_glu_d[l].rearrange("(k p) n -> p k n", p=128), [], [r_wglu])
        yb_ring = ARing(2, [128, 4, 512], F32, "yb")
        g_ring = ARing(2, [128, 4, 512], BF16, "gT")
        sg_ring = ARing(2, [128, 4, 512], F32, "sg")
        sq_ring = ARing(1, [128, 4, 512], F32, "sq")
        rs_ring = ARing(2, [128, 512], F32, "rs")
        sn_ring = ARing(2, [128, 4, 512], BF16, "sn")
        for b in range(NB):
            T0 = b * 512
            yb, r_yb = yb_ring.next()
            DMA(yb, yT_d[:, :, T0:T0 + 512], [r_yd], [r_yb])
            gT, r_gT = g_ring.next()
            ACT(lambda e, yb=yb, gT=gT: e.activation(out=gT, in_=yb, func=AF.Gelu_apprx_tanh), [r_yb], [r_gT])
            sg, r_sg = sg_ring.next()
            for co in range(4):
                bk, _, r_bk = next_bank()
                for ci in range(4):
                    PE(lambda e, bk=bk, ci=ci, co=co, gT=gT: e.matmul(bk[:, :], lhsT=wglu[:, ci, co * 128:(co + 1) * 128], rhs=gT[:, ci, :], start=(ci == 0), stop=(ci == 3)),
                       [r_wglu, r_gT], [r_bk])
                ACT(lambda e, bk=bk, co=co, sg=sg: e.activation(out=sg[:, co, :], in_=bk[:, :], func=AF.Sigmoid, bias=col(l, C_BGLU + co)), [r_bk, r_const], [r_sg])
            VEC(lambda e, sg=sg, yb=yb: e.tensor_tensor(out=sg, in0=sg, in1=yb, op=ALU.mult), [r_sg, r_yb], [r_sg])
            sq, r_sq = sq_ring.next()
            POOL(lambda e, sg=sg, sq=sq: e.tensor_tensor(out=sq, in0=sg, in1=sg, op=ALU.mult), [r_sg], [r_sq])
            bk, _, r_bk = next_bank()
            for co in range(4):
                PE(lambda e, bk=bk, co=co, sq=sq: e.matmul(bk[:, :], lhsT=onesf[:, :], rhs=sq[:, co, :], start=(co == 0), stop=(co == 3)), [r_sq, r_const], [r_bk])
            rs_, r_rs = rs_ring.next()
            ACT(lambda e, bk=bk, rs_=rs_: e.activation(out=rs_, in_=bk[:, :], func=AF.Sqrt, scale=1.0 / 512, bias=EPS), [r_bk], [r_rs])
            VEC(lambda e, rs_=rs_: e.reciprocal(out=rs_, in_=rs_), [r_rs], [r_rs])
            VEC(lambda e, sg=sg: e.tensor_tensor(out=sg, in0=sg, in1=col(l, C_GS, 4).rearrange("p (k o) -> p k o", o=1).to_broadcast([128, 4, 512]), op=ALU.mult),
                [r_sg, r_const], [r_sg])
            sn, r_sn = sn_ring.next()
            VEC(lambda e, sg=sg, sn=sn, rs_=rs_: e.tensor_tensor(out=sn, in0=sg, in1=rs_.rearrange("p (o t) -> p o t", o=1).to_broadcast([128, 4, 512]), op=ALU.mult),
                [r_sg, r_rs], [r_sn])
            DMA(sT_d[b], sn, [r_sn], [r_sd])
        if stop_after == "B2":
            return True
        P.barrier()
        areset()
        wo_a, r_wc = aalloc([64, 8, 1024], BF16, "wo_a")
        wo_s, _ = aalloc([128, 4, 1024], BF16)
        wxq, _ = aalloc([128, 8, 1024], BF16)
        wxo, _ = aalloc([128, 8, 1024], BF16)
        KxT, r_kx = aalloc([128, 8, 256], BF16, "KxT")
        Vx, _ = aalloc([128, 2, 1024], BF16)
        markc = arena_off[0]
        wxkv, r_wxkv = aalloc([128, 8, 2048], BF16, "wxkv")
        DMAC(wxkv, w_xkv_d[l].rearrange("(k p) n -> p k n", p=128), [], [r_wxkv])
        DMAC(wo_a, w_out_d[l, 0:512, :].rearrange("(h d) n -> d h n", d=64), [], [r_wc])
        DMAC(wo_s, w_out_d[l, 512:1024, :].rearrange("(k p) n -> p k n", p=128), [], [r_wc])
        DMAC(wxq, w_xq_d[l].rearrange("(k p) n -> p k n", p=128), [], [r_wc])
        DMAC(wxo, w_xo_d[l].rearrange("(k p) n -> p k n", p=128), [], [r_wc])
        memT, r_memT = xnT_ring.next()
        for mt in range(2):
            ht, r_ht = ht_ring.next()
            DMA(ht[:], mem_d[mt * 128:(mt + 1) * 128, :], [], [r_ht])
            norm_transpose(ht, r_ht, col(l, C_GMEM, 8), memT, r_memT, mt)
        for oc in range(8):
            bk, _, r_bk = next_bank()
            for k in range(8):
                PE(lambda e, bk=bk, k=k, oc=oc: e.matmul(bk[:, 0:256], lhsT=wxkv[:, k, oc * 128:(oc + 1) * 128], rhs=memT[:, k, 0:256], start=(k == 0), stop=(k == 7)),
                   [r_wxkv, r_memT], [r_bk])
            ACT(lambda e, bk=bk, oc=oc: e.copy(out=KxT[:, oc, :], in_=bk[:, 0:256]), [r_bk], [r_kx])
        for mt in range(2):
            for hf in range(2):
                bk, _, r_bk = next_bank()
                for k in range(8):
                    PE(lambda e, bk=bk, k=k, mt=mt, hf=hf: e.matmul(bk[:, :], lhsT=memT[:, k, mt * 128:(mt + 1) * 128], rhs=wxkv[:, k, 1024 + hf * 512:1024 + (hf + 1) * 512],
                                                                  start=(k == 0), stop=(k == 7)), [r_wxkv, r_memT], [r_bk])
                ACT(lambda e, bk=bk, mt=mt, hf=hf: e.copy(out=Vx[:, mt, hf * 512:(hf + 1) * 512], in_=bk[:, :]), [r_bk], [r_kx])
        P.barrier()
        arena_off[0] = markc
        an_ring2 = ARing(1, [64, 8, 512], BF16, "an2")
        sn_ring2 = ARing(2, [128, 4, 512], BF16, "sn2")
        h1_ring = ARing(5, [128, D], F32, "h1t")
        qx_ring = ARing(1, [128, 8, 512], BF16, "qxT")
        ox_ring = ARing(1, [128, 8, 512], BF16, "oxT")
        pX_ring = ARing(2, [128, 2, 512], BF16, "pX")
        rc_ring = ARing(2, [128, 512], F32, "rc")
        araw_v = frA[0][0:64, 0:4096].rearrange("p (h t) -> p h t", t=512)
        r_araw = frA[1]
        sqv = frB[0][0:64, 0:2048].rearrange("p (h t) -> p h t", t=512)
        r_sqv = frB[1]
        r_h1src = [Res() for _ in range(32)] if l == 0 else r_hd
        for b in range(NB):
            T0 = b * 512
            DMA(araw_v, aT_d[b], [r_ad], [r_araw])
            bk, _, r_bk = next_bank()
            for hg in range(2):
                POOL(lambda e, hg=hg: e.tensor_tensor(out=sqv, in0=araw_v[:, hg * 4:(hg + 1) * 4, :], in1=araw_v[:, hg * 4:(hg + 1) * 4, :], op=ALU.mult), [r_araw], [r_sqv])
                for hh in range(4):
                    PE(lambda e, bk=bk, hg=hg, hh=hh: e.matmul(bk[0:64, :], lhsT=onesf[0:64, 0:64], rhs=sqv[:, hh, :], start=(hg == 0 and hh == 0), stop=(hg == 1 and hh == 3)),
                       [r_sqv, r_const], [r_bk])
            rc, r_rc = rc_ring.next()
            ACT(lambda e, bk=bk, rc=rc: e.activation(out=rc[0:64, :], in_=bk[0:64, :], func=AF.Sqrt, scale=1.0 / 512, bias=EPS), [r_bk], [r_rc])
            VEC(lambda e, rc=rc: e.reciprocal(out=rc[0:64, :], in_=rc[0:64, :]), [r_rc], [r_rc])
            VEC(lambda e: e.tensor_tensor(out=araw_v, in0=araw_v, in1=col(l, C_GA, 8, 0, 64).rearrange("p (h o) -> p h o", o=1).to_broadcast([64, 8, 512]), op=ALU.mult),
                [r_araw, r_const], [r_araw])
            an2, r_an2 = an_ring2.next()
            VEC(lambda e, an2=an2, rc=rc: e.tensor_tensor(out=an2, in0=araw_v, in1=rc[0:64, :].rearrange("p (o t) -> p o t", o=1).to_broadcast([64, 8, 512]), op=ALU.mult),
                [r_araw, r_rc], [r_an2])
            sn2, r_sn2 = sn_ring2.next()
            DMA(sn2, sT_d[b], [r_sd], [r_sn2])
            xnT, r_xnT = xnT_ring.next()
            h1s = []
            for sub in range(4):
                ti = b * 4 + sub
                ts_ = slice(sub * 128, (sub + 1) * 128)
                ht, r_ht = ht_ring.next()
                DMA(ht[:], src_d[T0 + sub * 128:T0 + (sub + 1) * 128, :], [r_h1src[ti]], [r_ht])
                h1t, r_h1t = h1_ring.next()
                for hf in range(2):
                    bk, _, r_bk = next_bank()
                    cs_ = slice(hf * 512, (hf + 1) * 512)
                    for hh in range(8):
                        PE(lambda e, bk=bk, hh=hh, an2=an2, ts_=ts_, cs_=cs_: e.matmul(bk[:, :], lhsT=an2[:, hh, ts_], rhs=wo_a[:, hh, cs_], start=(hh == 0), stop=False),
                           [r_an2, r_wc], [r_bk])
                    for k in range(4):
                        PE(lambda e, bk=bk, k=k, sn2=sn2, ts_=ts_, cs_=cs_: e.matmul(bk[:, :], lhsT=sn2[:, k, ts_], rhs=wo_s[:, k, cs_], start=False, stop=(k == 3)),
                           [r_sn2, r_wc], [r_bk])
                    VEC(lambda e, bk=bk, ht=ht, h1t=h1t, cs_=cs_: e.tensor_tensor(out=h1t[:, cs_], in0=bk[:, :], in1=ht[:, cs_], op=ALU.add), [r_bk, r_ht], [r_h1t])
                norm_transpose(h1t, r_h1t, col(l, C_GX, 8), xnT, r_xnT, sub)
                h1s.append((h1t, r_h1t))
            qxT, r_qxT = qx_ring.next()
            for oc in range(8):
                bk, _, r_bk = next_bank()
                for k in range(8):
                    PE(lambda e, bk=bk, k=k, oc=oc, xnT=xnT: e.matmul(bk[:, :], lhsT=wxq[:, k, oc * 128:(oc + 1) * 128], rhs=xnT[:, k, :], start=(k == 0), stop=(k == 7)),
                       [r_wc, r_xnT], [r_bk])
                ACT(lambda e, bk=bk, oc=oc, qxT=qxT: e.copy(out=qxT[:, oc, :], in_=bk[:, :]), [r_bk], [r_qxT])
            oxT, r_oxT = ox_ring.next()
            for hx in range(4):
                pX, r_pX = pX_ring.next()
                for mt in range(2):
                    bk, _, r_bk = next_bank()
                    for dc in range(2):
                        PE(lambda e, bk=bk, dc=dc, mt=mt, hx=hx, qxT=qxT: e.matmul(bk[:, :], lhsT=KxT[:, hx * 2 + dc, mt * 128:(mt + 1) * 128], rhs=qxT[:, hx * 2 + dc, :],
                                                                              start=(dc == 0), stop=(dc == 1)), [r_kx, r_qxT], [r_bk])
                    ACT(lambda e, bk=bk, mt=mt, pX=pX: e.activation(out=pX[:, mt, :], in_=bk[:, :], func=AF.Exp, scale=X_SCALE), [r_bk], [r_pX])
                bk, _, r_bk = next_bank()
                for mt in range(2):
                    PE(lambda e, bk=bk, mt=mt, pX=pX: e.matmul(bk[:, :], lhsT=onesb[:, :], rhs=pX[:, mt, :], start=(mt == 0), stop=(mt == 1)), [r_pX, r_const], [r_bk])
                rc, r_rc = rc_ring.next()
                VEC(lambda e, bk=bk, rc=rc: e.reciprocal(out=rc, in_=bk[:, :]), [r_bk], [r_rc])
                for dc in range(2):
                    bk, _, r_bk = next_bank()
                    for mt in range(2):
                        PE(lambda e, bk=bk, mt=mt, dc=dc, hx=hx, pX=pX: e.matmul(bk[:, :], lhsT=Vx[:, mt, (hx * 2 + dc) * 128:(hx * 2 + dc + 1) * 128], rhs=pX[:, mt, :],
                                                                             start=(mt == 0), stop=(mt == 1)), [r_kx, r_pX], [r_bk])
                    VEC(lambda e, bk=bk, dc=dc, hx=hx, rc=rc, oxT=oxT: e.tensor_tensor(out=oxT[:, hx * 2 + dc, :], in0=bk[:, :], in1=rc, op=ALU.mult), [r_bk, r_rc], [r_oxT])
            for sub in range(4):
                ti = b * 4 + sub
                ts_ = slice(sub * 128, (sub + 1) * 128)
                h1t, r_h1t = h1s[sub]
                for hf in range(2):
                    bk, _, r_bk = next_bank()
                    cs_ = slice(hf * 512, (hf + 1) * 512)
                    for k in range(8):
                        PE(lambda e, bk=bk, k=k, oxT=oxT, ts_=ts_, cs_=cs_: e.matmul(bk[:, :], lhsT=oxT[:, k, ts_], rhs=wxo[:, k, cs_], start=(k == 0), stop=(k == 7)),
                           [r_oxT, r_wc], [r_bk])
                    VEC(lambda e, bk=bk, h1t=h1t, cs_=cs_: e.tensor_tensor(out=h1t[:, cs_], in0=bk[:, :], in1=h1t[:, cs_], op=ALU.add), [r_bk, r_h1t], [r_h1t])
                DMA(h1_d[T0 + sub * 128:T0 + (sub + 1) * 128, :], h1t[:], [r_h1t], [r_h1d[ti]])
        if stop_after == "C1":
            return True

        P.barrier()
        areset()
        wg, r_wf = aalloc([128, 8, DFF], BF16, "wg")
        wu, _ = aalloc([128, 8, DFF], BF16)
        wd, _ = aalloc([128, NFF, 1024], BF16)
        for k in range(8):
            DMAC(wg[:, k, :], w_gate_d[l, k * 128:(k + 1) * 128, :], [], [r_wf])
            DMAC(wu[:, k, :], w_up_d[l, k * 128:(k + 1) * 128, :], [], [r_wf])
        for k in range(0, NFF, 2):
            DMAC(wd[:, k:k + 2, :], w_down_d[l, k * 128:(k + 2) * 128, :].rearrange("(k p) n -> p k n", p=128), [], [r_wf])
        actT = frA[0].bitcast(BF16) if hasattr(frA[0], "bitcast") else None
        actT = actT[:, 0:NFF * 512].rearrange("p (f t) -> p f t", t=512)
        r_actT = frA[1]
        sgs = [(frB[0][:, i * 512:(i + 1) * 512], Res()) for i in range(4)]
        sg_i = [0]
        last = (l == L - 1)
        ctxC = {}
        nt_ring = Ring(P, f"ntl{l}", 1, [128, 4], F32) if False else None

        def stageC2a(b):
            T0 = b * 512
            xnT, r_xnT = xnT_ring.next()
            ctxC[b] = (xnT, r_xnT)
            for sub in range(4):
                ti = b * 4 + sub
                ht, r_ht = ht_ring.next()
                DMA(ht[:], h1_d[T0 + sub * 128:T0 + (sub + 1) * 128, :], [r_h1d[ti]], [r_ht])
                norm_transpose(ht, r_ht, col(l, C_GFFN, 8), xnT, r_xnT, sub)
                yield

        def stageC2b(b):
            T0 = b * 512
            xnT, r_xnT = ctxC.pop(b)
            for fc in range(NFF):
                if fc % 3 == 0:
                    yield
                bkg, _, r_bkg = next_bank()
                bku, _, r_bku = next_bank()
                for k in range(8):
                    PE(lambda e, bkg=bkg, k=k, fc=fc, xnT=xnT: e.matmul(bkg[:, :], lhsT=wg[:, k, fc * 128:(fc + 1) * 128], rhs=xnT[:, k, :], start=(k == 0), stop=(k == 7)),
                       [r_wf, r_xnT], [r_bkg])
                for k in range(8):
                    PE(lambda e, bku=bku, k=k, fc=fc, xnT=xnT: e.matmul(bku[:, :], lhsT=wu[:, k, fc * 128:(fc + 1) * 128], rhs=xnT[:, k, :], start=(k == 0), stop=(k == 7)),
                       [r_wf, r_xnT], [r_bku])
                sgt, r_sgt = sgs[sg_i[0] % 4]
                sg_i[0] += 1
                ACT(lambda e, bkg=bkg, sgt=sgt: e.activation(out=sgt, in_=bkg[:, :], func=AF.Silu), [r_bkg], [r_sgt])
                VEC(lambda e, bku=bku, sgt=sgt, fc=fc: e.tensor_tensor(out=actT[:, fc, :], in0=bku[:, :], in1=sgt, op=ALU.mult), [r_bku, r_sgt], [r_actT])
            for sub in range(4):
                ti = b * 4 + sub
                ts_ = slice(sub * 128, (sub + 1) * 128)
                ht, r_ht = ht_ring.next()
                DMA(ht[:], h1_d[T0 + sub * 128:T0 + (sub + 1) * 128, :], [r_h1d[ti]], [r_ht])
                for hf in range(2):
                    yield
                    bk, _, r_bk = next_bank()
                    cs_ = slice(hf * 512, (hf + 1) * 512)
                    for fc in range(NFF):
                        PE(lambda e, bk=bk, fc=fc, ts_=ts_, cs_=cs_: e.matmul(bk[:, :], lhsT=actT[:, fc, ts_], rhs=wd[:, fc, cs_], start=(fc == 0), stop=(fc == NFF - 1)),
                           [r_actT, r_wf], [r_bk])
                    VEC(lambda e, bk=bk, ht=ht, cs_=cs_: e.tensor_tensor(out=ht[:, cs_], in0=bk[:, :], in1=ht[:, cs_], op=ALU.add), [r_bk, r_ht], [r_ht])
                if not last:
                    DMA(h_d[T0 + sub * 128:T0 + (sub + 1) * 128, :], ht[:], [r_ht], [r_hd[ti]])
                else:
                    st, r_st = st_ring.next()
                    rms_stats(ht[:], r_ht, D, st, r_st, 0)
                    VEC(lambda e, ht=ht, st=st: e.scalar_tensor_tensor(out=ht[:], in0=ht[:], scalar=st[:, 0:1], in1=fing[:], op0=ALU.mult, op1=ALU.mult),
                        [r_ht, r_st, r_const], [r_ht])
                    final_ops.append(DMA(out_d[T0 + sub * 128:T0 + (sub + 1) * 128, :], ht[:], [r_ht], []))


        def run_il(gens):
            gens = list(gens)
            while gens:
                for g_ in list(gens):
                    try:
                        next(g_)
                    except StopIteration:
                        gens.remove(g_)

        for b in range(NB + 1):
            gl = []
            if b >= 1:
                gl.append(stageC2b(b - 1))
            if b < NB:
                gl.append(stageC2a(b))
            run_il(gl)
        return False

    for l_ in range(n_layers):
        if emit_layer(l_):
            break

    if debug:
        def dump(name, src, shape, dt, r):
            final_ops.append(DMA(dbg_out(name, shape, dt), src, [r], []))
        P.barrier()
        dump("qT", qT_d[:, :, 3584:4096], [8, 96, 512], BF16, r_qd)
        dump("kT", kT_d[:, :, 3584:4096], [8, 96, 512], BF16, r_kd)
        dump("V", V_d[:, :, 28:32, :], [8, 128, 4, 65], BF16, r_vd)
        dump("uT", uT_d[:, :, 3584:4096], [128, 4, 512], BF16, r_ud)
        dump("aT0", aT_d[0], [64, 8, 512], F32, r_ad)
        dump("aT7", aT_d[7], [64, 8, 512], F32, r_ad)
        dump("yT", yT_d[:, :, 3584:4096], [128, 4, 512], F32, r_yd)
        dump("yT0", yT_d[:, :, 0:512], [128, 4, 512], F32, r_yd)
        dump("sT7", sT_d[7], [128, 4, 512], BF16, r_sd)
        dump("sT0", sT_d[0], [128, 4, 512], BF16, r_sd)
        dump("h1", h1_d[3968:4096, :], [128, D], F32, r_h1d[31])
        dump("h", h_d[3968:4096, :], [128, D], F32, r_hd[31])
    P.barrier()
    return P.build(final_waits=final_ops), dbg


def prep_inputs(inp):
    f = np.float32
    g = lambda k: np.asarray(inp[k])
    cols = np.zeros((128, L, NCOL), f)
    for l in range(L):
        cols[:, l, C_GMIX:C_GMIX + 8] = g("norm_mix_g")[l].reshape(8, 128).T
        cols[:, l, C_GX:C_GX + 8] = g("norm_x_g")[l].reshape(8, 128).T
        cols[:, l, C_GFFN:C_GFFN + 8] = g("norm_ffn_g")[l].reshape(8, 128).T
        cols[:, l, C_GMEM:C_GMEM + 8] = g("mem_norm_g")[l].reshape(8, 128).T
        cols[:, l, C_GQ:C_GQ + 2] = g("q_norm_g")[l].reshape(2, 128).T
        cols[:, l, C_GKV] = g("kv_norm_g")[l]
        cols[:, l, C_GS:C_GS + 4] = g("ssm_out_g")[l].reshape(4, 128).T
        cols[:, l, C_D:C_D + 4] = g("ssm_d")[l].reshape(4, 128).T
        cols[:, l, C_BGLU:C_BGLU + 4] = g("ssm_b_glu")[l].reshape(4, 128).T
        cols[0:64, l, C_GA:C_GA + 8] = g("attn_out_g")[l].reshape(8, 64).T
    consts = np.zeros((128, 2), f)
    freqs = (np.float32(10000.0) ** (-np.arange(0, 32, 2, dtype=np.float32) / np.float32(32))).astype(f)
    consts[64:80, 0] = freqs
    consts[80:96, 0] = freqs
    consts[64:80, 1] = -SIN_SCALE
    consts[80:96, 1] = SIN_SCALE
    idx = np.arange(128) // 16
    mask16 = (idx[:, None] == idx[None, :]).astype(f)
    w_in = g("w_in")
    w_kr = np.zeros((L, D, 192), f)
    w_kr[:, :, 64:96] = w_in[:, :, 384:416]
    w_kr[:, :, 160:176] = w_in[:, :, 400:416]
    w_kr[:, :, 176:192] = w_in[:, :, 384:400]
    w_uq = g("w_uq")
    w_uqs = np.zeros_like(w_uq)
    for h in range(8):
        w_uqs[:, :, h * 96 + 64:h * 96 + 80] = w_uq[:, :, h * 96 + 80:h * 96 + 96]
        w_uqs[:, :, h * 96 + 80:h * 96 + 96] = w_uq[:, :, h * 96 + 64:h * 96 + 80]
    w_ukv = g("w_ukv").reshape(L, 128, 8, 2, 64).transpose(0, 1, 3, 2, 4).reshape(L, 128, 1024)

    def pair_layout(a):
        sh = a.shape
        a = a.reshape(L, 16, 2, 64, *sh[3:])
        a = np.moveaxis(a, 1, 3)
        return a.reshape(L, 128, 16, *sh[3:])

    lam_re = pair_layout(g("ssm_lambda_re"))
    lam_im = pair_layout(g("ssm_lambda_im"))
    logdt = pair_layout(np.repeat(g("ssm_log_dt")[:, :, None], 64, axis=2))
    s5v = np.stack([lam_re, lam_im, logdt], axis=2)
    b_re = pair_layout(g("ssm_b_re"))
    b_im = pair_layout(g("ssm_b_im"))
    s5b = np.stack([b_re, b_im], axis=2)
    c_re = pair_layout(np.swapaxes(g("ssm_c_re"), 2, 3))
    c_im = pair_layout(np.swapaxes(g("ssm_c_im"), 2, 3))
    s5c = np.stack([c_re, c_im], axis=2)
    common = dict(
        cols=cols, consts=consts, mask16=mask16, fing=g("final_norm_g").reshape(1, D).astype(f),
        w_in=w_in, w_kr=w_kr, w_uq=w_uq, w_uqs=w_uqs, w_ukv=np.ascontiguousarray(w_ukv),
        s5v=np.ascontiguousarray(s5v), s5b=np.ascontiguousarray(s5b), s5c=np.ascontiguousarray(s5c),
        w_glu=g("ssm_w_glu"), w_out=g("w_out"), w_xq=g("w_xq"), w_xkv=g("w_xkv"), w_xo=g("w_xo"),
        w_gate=g("w_gate"), w_up=g("w_up"), w_down=g("w_down"),
    )
    common = {k: np.ascontiguousarray(v, dtype=f) for k, v in common.items()}
    x = g("x")
    mem = g("mem")
    pos = g("positions").astype(np.int32)
    per_core = []
    for c in range(x.shape[0]):
        d = dict(common)
        d["x"] = np.ascontiguousarray(x[c], dtype=f)
        d["mem"] = np.ascontiguousarray(mem[c], dtype=f)
        d["pos"] = np.ascontiguousarray(pos[c].reshape(1, S))
        per_core.append(d)
    return per_core


def kernel(**inputs):
    per_core = prep_inputs(inputs)
    nc, _ = build_program()
    res = run_bass_kernel_spmd(nc, per_core, core_ids=list(range(8)))
    return np.stack([np.asarray(r["out"], dtype=np.float32) for r in res.results], axis=0)
```

```python
import math
import numpy as np
import concourse.bass as bass
import concourse.mybir as mybir
from concourse.bass_utils import run_bass_kernel_spmd

F32 = mybir.dt.float32
BF16 = mybir.dt.bfloat16
I32 = mybir.dt.int32
AF = mybir.ActivationFunctionType
ALU = mybir.AluOpType

ENGS = ["tensor", "vector", "scalar", "gpsimd", "sync"]

L = 2
S = 4096
D = 1024
NB = 8
EPS = 1e-6
DFF = 2816
NFF = 22
TWO_PI = 2.0 * math.pi
SIN_SCALE = 6.2831845
MAGIC = 12582912.0
ATT_SCALE = 96.0 ** -0.5
X_SCALE = 256.0 ** -0.5
NCOL = 59
C_GMIX, C_GX, C_GFFN, C_GMEM, C_GQ, C_GKV, C_GS, C_D, C_BGLU, C_GA, C_GA2 = 0, 8, 16, 24, 32, 34, 35, 39, 43, 47, 55


class Res:
    __slots__ = ("name", "last_w", "readers")

    def __init__(self, name=""):
        self.name = name
        self.last_w = None
        self.readers = []


class Op:
    __slots__ = ("eng", "fn", "waits", "dma", "sem", "val")


class Prog:
    def __init__(self, n_dma_sems=24):
        self.nc = bass.Bass("TRN2", target_bir_lowering=False)
        self.ops = {e: [] for e in ENGS}
        self.n_dma_sems = n_dma_sems
        self.wm = {e: {} for e in ENGS}
        self.dma_rr = {e: 0 for e in ENGS}
        self.dma_cnt = {}
        self.dma_last = {}
        self.eng_cnt = {}
        self.pending = {e: {} for e in ENGS}
        self._ctx = []

    def sbuf(self, name, shape, dtype):
        g = self.nc.sbuf_tensor("sb_" + name, list(shape), dtype)
        h = g.__enter__()
        self._ctx.append(g)
        return h

    def psum(self, name, shape, dtype):
        g = self.nc.psum_tensor("ps_" + name, list(shape), dtype)
        h = g.__enter__()
        self._ctx.append(g)
        return h

    def _need(self, op, dep):
        if dep is None:
            return
        if dep.eng == "tensor" and op.eng == "tensor" and not dep.dma and not op.dma:
            return
        if dep.val > op.waits.get(dep.sem, 0):
            op.waits[dep.sem] = dep.val

    def barrier(self):
        cur = {}
        for e, c in self.eng_cnt.items():
            cur[("eng", e)] = c
        for k, c in self.dma_cnt.items():
            cur[k] = 16 * c
        for e in ENGS:
            pe = self.pending[e]
            for k, v in cur.items():
                if v > pe.get(k, 0):
                    pe[k] = v

    def op(self, eng, fn, reads=(), writes=(), dma=False):
        o = Op()
        o.eng = eng
        o.fn = fn
        o.dma = dma
        o.waits = {}
        if self.pending[eng]:
            o.waits.update(self.pending[eng])
            self.pending[eng] = {}
        if dma:
            slot = self.dma_rr[eng] % self.n_dma_sems
            self.dma_rr[eng] += 1
            key = ("dma", eng, slot)
            prev = self.dma_last.get(key)
            cnt = self.dma_cnt.get(key, 0) + 1
            self.dma_cnt[key] = cnt
            o.sem = key
            o.val = 16 * cnt
            if prev is not None:
                self._need(o, prev)
            self.dma_last[key] = o
        else:
            o.sem = ("eng", eng)
            self.eng_cnt[eng] = self.eng_cnt.get(eng, 0) + 1
            o.val = self.eng_cnt[eng]
        for r in reads:
            self._need(o, r.last_w)
        for w in writes:
            self._need(o, w.last_w)
            for rd in w.readers:
                self._need(o, rd)
        for r in reads:
            r.readers.append(o)
        for w in writes:
            w.last_w = o
            w.readers = []
        wm = self.wm[eng]
        for k in list(o.waits):
            if wm.get(k, 0) >= o.waits[k]:
                del o.waits[k]
            else:
                wm[k] = o.waits[k]
        self.ops[eng].append(o)
        return o

    def build(self, final_waits=()):
        nc = self.nc
        sems = {}
        for e in ENGS:
            for o in self.ops[e]:
                if o.sem not in sems:
                    g = nc.semaphore("s_" + "_".join(str(x) for x in o.sem))
                    sems[o.sem] = g.__enter__()
                    self._ctx.append(g)
        fin = {}
        for o in final_waits:
            fin[o.sem] = max(fin.get(o.sem, 0), o.val)
        with nc.Block() as block:
            def make(e):
                def body(engobj):
                    for o in self.ops[e]:
                        for k, v in o.waits.items():
                            engobj.wait_ge(sems[k], v)
                        ins = o.fn(engobj)
                        ins.then_inc(sems[o.sem], 16 if o.dma else 1)
                    if e == "sync":
                        for k, v in fin.items():
                            engobj.wait_ge(sems[k], v)
                return body
            for e in ENGS:
                if self.ops[e] or e == "sync":
                    getattr(block, e)(make(e))
        return nc


class Ring:
    def __init__(self, P, name, n, shape, dtype):
        self.bufs = [(P.sbuf(f"{name}{i}", shape, dtype), Res(f"{name}{i}")) for i in range(n)]
        self.i = 0

    def next(self):
        b = self.bufs[self.i % len(self.bufs)]
        self.i += 1
        return b


def build_program(debug=False, n_layers=L, stop_after=None):
    P = Prog()
    nc = P.nc

    def din(name, shape, dt=F32):
        return nc.dram_tensor(name, list(shape), dt, kind="ExternalInput").ap()

    def dscr(name, shape, dt):
        return nc.dram_tensor(name, list(shape), dt, kind="Internal").ap()

    x_d = din("x", [S, D])
    mem_d = din("mem", [256, D])
    pos_d = din("pos", [1, S], I32)
    cols_d = din("cols", [128, L, NCOL])
    consts_d = din("consts", [128, 2])
    mask16_d = din("mask16", [128, 128])
    fing_d = din("fing", [1, D])
    w_in_d = din("w_in", [L, D, 928])
    w_kr_d = din("w_kr", [L, D, 192])
    w_uq_d = din("w_uq", [L, 256, 768])
    w_uqs_d = din("w_uqs", [L, 256, 768])
    w_ukv_d = din("w_ukv", [L, 128, 1024])
    s5v_d = din("s5v", [L, 128, 3, 16])
    s5b_d = din("s5b", [L, 128, 2, 16, 16])
    s5c_d = din("s5c", [L, 128, 2, 16, 16])
    w_glu_d = din("w_glu", [L, 512, 512])
    w_out_d = din("w_out", [L, D, D])
    w_xq_d = din("w_xq", [L, D, D])
    w_xkv_d = din("w_xkv", [L, D, 2 * D])
    w_xo_d = din("w_xo", [L, D, D])
    w_gate_d = din("w_gate", [L, D, DFF])
    w_up_d = din("w_up", [L, D, DFF])
    w_down_d = din("w_down", [L, DFF, D])
    out_d = nc.dram_tensor("out", [S, D], F32, kind="ExternalOutput").ap()

    h_d = dscr("h_scr", [S, D], F32)
    h1_d = dscr("h1_scr", [S, D], F32)
    qT_d = dscr("qT_scr", [8, 96, S], BF16)
    kT_d = dscr("kT_scr", [8, 96, S], BF16)
    V_d = dscr("V_scr", [8, 128, 32, 65], BF16)
    uT_d = dscr("uT_scr", [128, 4, S], BF16)
    yT_d = dscr("yT_scr", [128, 4, S], F32)
    aT_d = dscr("aT_scr", [NB, 64, 8, 512], F32)
    sT_d = dscr("sT_scr", [NB, 128, 4, 512], BF16)
    cos_d = dscr("cos_scr", [32, S], F32)
    sin_d = dscr("sin_scr", [32, S], F32)
    r_hd = [Res() for _ in range(32)]
    r_h1d = [Res() for _ in range(32)]
    r_qd, r_kd, r_vd, r_ud, r_yd = Res(), Res(), Res(), Res(), Res()
    r_ad, r_sd, r_csd = Res(), Res(), Res()

    dbg = {}

    def dbg_out(name, shape, dt=F32):
        t = nc.dram_tensor("dbg_" + name, list(shape), dt, kind="ExternalOutput").ap()
        dbg[name] = t
        return t

    final_ops = []

    def VEC(fn, reads, writes):
        return P.op("vector", fn, reads, writes)

    def ACT(fn, reads, writes):
        return P.op("scalar", fn, reads, writes)

    def POOL(fn, reads, writes):
        return P.op("gpsimd", fn, reads, writes)

    def PE(fn, reads, writes):
        return P.op("tensor", fn, reads, writes)

    def DMA(out, in_, reads, writes, q="sync"):
        return P.op(q, lambda e: e.dma_start(out=out, in_=in_), reads, writes, dma=True)

    def DMAC(out, in_, reads, writes):
        return P.op("gpsimd", lambda e: e.dma_start(out=out, in_=in_), reads, writes, dma=True)

    banks = []
    for i in range(8):
        t = P.psum(f"bank{i}", [128, 512], F32)
        banks.append((t, t.bitcast(BF16), Res(f"bank{i}")))
    bank_rr = [0]

    def next_bank(pool=(0, 1, 2, 3, 4, 5, 6, 7)):
        i = pool[bank_rr[0] % len(pool)]
        bank_rr[0] += 1
        return banks[i]

    identf = P.sbuf("identf", [128, 128], F32)
    ident = P.sbuf("ident", [128, 128], BF16)
    onesf = P.sbuf("onesf", [128, 128], F32)
    sel65 = P.sbuf("sel65", [65, 64], F32)
    onesb = P.sbuf("onesb", [128, 128], BF16)
    fing = P.sbuf("fing", [128, D], F32)
    mask16 = P.sbuf("mask16", [128, 128], F32)
    cols = P.sbuf("cols", [128, L, NCOL], F32)
    consts = P.sbuf("consts", [128, 2], F32)
    r_const = Res("const")
    POOL(lambda e: e.memset(identf[:], 1.0), [], [r_const])
    POOL(lambda e: e.affine_select(out=identf[:], in_=identf[:], pattern=[[-1, 128]], compare_op=ALU.is_equal,
                                   fill=0.0, base=0, channel_multiplier=1), [r_const], [r_const])
    VEC(lambda e: e.tensor_copy(out=ident[:], in_=identf[:]), [r_const], [r_const])
    VEC(lambda e: e.memset(onesf[:], 1.0), [], [r_const])
    VEC(lambda e: e.memset(onesb[:], 1.0), [], [r_const])
    DMA(fing[:], fing_d[0:1, :].to_broadcast([128, D]), [], [r_const])
    VEC(lambda e: e.memset(sel65[:], 0.0), [], [r_const])
    VEC(lambda e: e.memset(sel65[64:65, :], 1.0), [r_const], [r_const])
    DMA(mask16[:], mask16_d[:, :], [], [r_const])
    DMA(cols[:], cols_d[:, :, :], [], [r_const])
    DMA(consts[:], consts_d[:, :], [], [r_const])

    def col(l, c, n=1, p0=0, p1=128):
        return cols[p0:p1, l, c:c + n]

    ARENA_F32 = 33792
    arena_f = P.sbuf("arena", [128, ARENA_F32], F32)
    arena_b = arena_f.bitcast(BF16)
    arena_i = arena_f.bitcast(I32)
    arena_off = [0]

    def areset():
        arena_off[0] = 0

    def aalloc(shape, dt, name=""):
        n = 1
        for s_ in shape[1:]:
            n *= s_
        esz = 2 if dt == BF16 else 4
        off = (arena_off[0] + 3) // 4 * 4
        arena_off[0] = off + n * esz
        assert arena_off[0] <= ARENA_F32 * 4, (name, arena_off[0])
        base = {BF16: arena_b, F32: arena_f, I32: arena_i}[dt]
        e0 = off // esz
        ap = base[0:shape[0], e0:e0 + n]
        if len(shape) == 3:
            ap = ap.rearrange("p (a b) -> p a b", b=shape[2])
        elif len(shape) == 4:
            ap = ap.rearrange("p (a b c) -> p a b c", b=shape[2], c=shape[3])
        elif len(shape) == 5:
            ap = ap.rearrange("p (a b c d) -> p a b c d", b=shape[2], c=shape[3], d=shape[4])
        return ap, Res(name)

    class ARing:
        def __init__(self, n, shape, dt, name=""):
            self.bufs = [aalloc(shape, dt, f"{name}{i}") for i in range(n)]
            self.i = 0

        def next(self):
            b = self.bufs[self.i % len(self.bufs)]
            self.i += 1
            return b

    ht_ring = Ring(P, "ht", 2, [128, D], F32)
    junk_ring = Ring(P, "junk", 1, [128, D], BF16)
    xn_ring = Ring(P, "xn", 2, [128, D], BF16)
    xnT_ring = Ring(P, "xnT", 2, [128, 8, 512], BF16)
    st_ring = Ring(P, "st", 8, [128, 4], F32)
    pT_ring = Ring(P, "pT", 4, [128, 512], BF16)
    frA = (P.sbuf("frA", [128, 5632], F32), Res("frA"))
    frB = (P.sbuf("frB", [128, 2048], F32), Res("frB"))

    def rms_stats(src_ap, r_src, nfeat, st, r_st, c0):
        junk, r_junk = junk_ring.next()
        n = src_ap.shape[-1]
        ACT(lambda e: e.activation(out=junk[:, 0:n], in_=src_ap, func=AF.Square, accum_out=st[:, c0:c0 + 1]), [r_src], [r_junk, r_st])
        ACT(lambda e: e.activation(out=st[:, c0:c0 + 1], in_=st[:, c0:c0 + 1], func=AF.Sqrt, scale=1.0 / nfeat, bias=EPS), [r_st], [r_st])
        VEC(lambda e: e.reciprocal(out=st[:, c0:c0 + 1], in_=st[:, c0:c0 + 1]), [r_st], [r_st])

    def norm_transpose(ht, r_ht, gcol, xnT, r_xnT, sub):
        st, r_st = st_ring.next()
        rms_stats(ht[:], r_ht, D, st, r_st, 0)
        xn, r_xn = xn_ring.next()
        VEC(lambda e: e.tensor_scalar(out=xn[:], in0=ht[:], scalar1=st[:, 0:1], scalar2=None, op0=ALU.mult), [r_ht, r_st], [r_xn])
        bk, bkb, r_bk = next_bank()
        bkv = bkb[:, :].rearrange("p (a b) -> p a b", b=128)
        for k in range(8):
            PE(lambda e, k=k: e.transpose(out=bkv[:, k, :], in_=xn[:, k * 128:(k + 1) * 128], identity=ident[:]), [r_xn, r_const], [r_bk])
        VEC(lambda e: e.tensor_tensor(out=xnT[:, :, sub * 128:(sub + 1) * 128], in0=bkv, in1=gcol.to_broadcast([128, 8, 128]), op=ALU.mult),
            [r_bk, r_const], [r_xnT])

    RS = slice(64, 96)

    areset()
    posi, r_ra = aalloc([96, S], I32, "posi")
    tmpa, _ = aalloc([96, S], F32, "tmpa")
    tmpb, _ = aalloc([96, S], F32, "tmpb")
    cs_t, r_cs = aalloc([96, S], F32, "cs")
    sn_t, _ = aalloc([96, S], F32, "sn")
    DMA(posi[RS, :], pos_d[0:1, :].to_broadcast([32, S]), [], [r_ra])
    VEC(lambda e: e.tensor_copy(out=tmpa[RS, :], in_=posi[RS, :]), [r_ra], [r_ra])
    VEC(lambda e: e.tensor_scalar(out=tmpa[RS, :], in0=tmpa[RS, :], scalar1=consts[RS, 0:1], scalar2=1.0 / TWO_PI,
                                  op0=ALU.mult, op1=ALU.mult), [r_ra, r_const], [r_ra])
    VEC(lambda e: e.tensor_scalar(out=tmpb[RS, :], in0=tmpa[RS, :], scalar1=MAGIC, scalar2=MAGIC, op0=ALU.add, op1=ALU.subtract), [r_ra], [r_ra])
    VEC(lambda e: e.tensor_tensor(out=tmpb[RS, :], in0=tmpa[RS, :], in1=tmpb[RS, :], op=ALU.subtract), [r_ra], [r_ra])
    ACT(lambda e: e.activation(out=sn_t[RS, :], in_=tmpb[RS, :], func=AF.Sin, scale=consts[RS, 1:2]), [r_ra, r_const], [r_cs])
    VEC(lambda e: e.tensor_scalar(out=tmpa[RS, :], in0=tmpa[RS, :], scalar1=0.25, scalar2=None, op0=ALU.add), [r_ra, r_cs], [r_ra])
    VEC(lambda e: e.tensor_scalar(out=tmpb[RS, :], in0=tmpa[RS, :], scalar1=MAGIC, scalar2=MAGIC, op0=ALU.add, op1=ALU.subtract), [r_ra], [r_ra])
    VEC(lambda e: e.tensor_tensor(out=tmpb[RS, :], in0=tmpa[RS, :], in1=tmpb[RS, :], op=ALU.subtract), [r_ra], [r_ra])
    ACT(lambda e: e.activation(out=cs_t[RS, :], in_=tmpb[RS, :], func=AF.Sin, scale=SIN_SCALE), [r_ra], [r_cs])
    DMA(cos_d[:, :], cs_t[RS, :], [r_cs], [r_csd])
    DMA(sin_d[:, :], sn_t[RS, :], [r_cs], [r_csd])

    def emit_layer(l):
        src_d = x_d if l == 0 else h_d
        r_src = [Res() for _ in range(32)] if l == 0 else r_hd
        P.barrier()
        areset()
        w_in_sb, r_w = aalloc([128, 8, 928], BF16, "w_in")
        w_kr_sb, _ = aalloc([128, 8, 192], BF16)
        w_uq_sb, _ = aalloc([128, 2, 768], BF16)
        w_uqs_sb, _ = aalloc([128, 2, 768], BF16)
        w_ukv_sb, _ = aalloc([128, 1024], BF16)
        DMAC(w_in_sb, w_in_d[l].rearrange("(k p) n -> p k n", p=128), [], [r_w])
        DMAC(w_kr_sb, w_kr_d[l].rearrange("(k p) n -> p k n", p=128), [], [r_w])
        DMAC(w_uq_sb, w_uq_d[l].rearrange("(k p) n -> p k n", p=128), [], [r_w])
        DMAC(w_uqs_sb, w_uqs_d[l].rearrange("(k p) n -> p k n", p=128), [], [r_w])
        DMAC(w_ukv_sb, w_ukv_d[l], [], [r_w])
        qs, r_qs = aalloc([96, 8, 512], F32, "qs")
        qsw, r_qsw = aalloc([96, 8, 512], F32, "qsw")
        qb_ring = ARing(2, [96, 8, 512], BF16, "qb")
        kb_ring = ARing(2, [96, 8, 512], BF16, "kb")
        vb_ring = ARing(2, [128, 8, 4, 65], BF16, "vb")
        ub_ring = ARing(2, [128, 4, 512], BF16, "ub")
        cT_ring = ARing(2, [128, 3, 512], BF16, "cT")
        cs_ring = ARing(2, [96, 2, 512], F32, "csb")
        ta, r_ta = aalloc([96, 1024], F32, "ta")
        for vb, r_vb in vb_ring.bufs:
            VEC(lambda e, vb=vb: e.memset(vb[:, :, :, 64:65], 1.0), [], [r_vb])

        ctxA = {}

        htx = [(frB[0][:, 0:1024], Res("htx0")), (frB[0][:, 1024:2048], Res("htx1"))]
        frA_b = frA[0].bitcast(BF16)
        xnx = [(frA_b[:, 0:1024], Res("xnx0")), (frA_b[:, 1024:2048], Res("xnx1"))]
        jkx = [(frA_b[:, 2048 + i * 1024:2048 + (i + 1) * 1024], Res(f"jkx{i}")) for i in range(4)]

        def stageA1(b):
            T0 = b * 512
            xnT, r_xnT = xnT_ring.next()
            cT, r_cT = cT_ring.next()
            vb, r_vb = vb_ring.next()
            csb, r_csb = cs_ring.next()
            ctxA[b] = (xnT, r_xnT, cT, r_cT, csb, r_csb)
            DMA(csb[RS, 0, :], cos_d[:, T0:T0 + 512], [r_csd], [r_csb])
            DMA(csb[RS, 1, :], sin_d[:, T0:T0 + 512], [r_csd], [r_csb])
            hts = [ht_ring.next(), ht_ring.next(), htx[0], htx[1]]
            xns = [xn_ring.next(), xn_ring.next(), xnx[0], xnx[1]]
            sts = [st_ring.next() for _ in range(4)]
            gcol = col(l, C_GMIX, 8)
            gq = col(l, C_GQ, 3)
            for sub in range(4):
                ht, r_ht = hts[sub]
                DMA(ht[:, :], src_d[T0 + sub * 128:T0 + (sub + 1) * 128, :], [r_src[b * 4 + sub]], [r_ht])
            yield
            for sub in range(4):
                (ht, r_ht), (st, r_st), (jk, r_jk) = hts[sub], sts[sub], jkx[sub]
                ACT(lambda e, ht=ht, st=st, jk=jk: e.activation(out=jk, in_=ht[:, :], func=AF.Square, accum_out=st[:, 0:1]), [r_ht], [r_jk, r_st])
            for sub in range(4):
                st, r_st = sts[sub]
                ACT(lambda e, st=st: e.activation(out=st[:, 0:1], in_=st[:, 0:1], func=AF.Sqrt, scale=1.0 / D, bias=EPS), [r_st], [r_st])
            yield
            for sub in range(4):
                st, r_st = sts[sub]
                VEC(lambda e, st=st: e.reciprocal(out=st[:, 0:1], in_=st[:, 0:1]), [r_st], [r_st])
            for sub in range(4):
                (ht, r_ht), (st, r_st), (xn, r_xn) = hts[sub], sts[sub], xns[sub]
                VEC(lambda e, ht=ht, st=st, xn=xn: e.tensor_scalar(out=xn[:, :], in0=ht[:, :], scalar1=st[:, 0:1], scalar2=None, op0=ALU.mult),
                    [r_ht, r_st], [r_xn])
            yield
            bks = []
            for sub in range(4):
                xn, r_xn = xns[sub]
                bk, bkb, r_bk = next_bank()
                bkv = bkb[:, :].rearrange("p (a b) -> p a b", b=128)
                for k in range(8):
                    PE(lambda e, k=k, bkv=bkv, xn=xn: e.transpose(out=bkv[:, k, :], in_=xn[:, k * 128:(k + 1) * 128], identity=ident[:]), [r_xn, r_const], [r_bk])
                bks.append((bkv, r_bk))
                if sub % 2 == 1:
                    yield
            for sub in range(4):
                bkv, r_bk = bks[sub]
                VEC(lambda e, bkv=bkv, sub=sub: e.tensor_tensor(out=xnT[:, :, sub * 128:(sub + 1) * 128], in0=bkv, in1=gcol.to_broadcast([128, 8, 128]), op=ALU.mult),
                    [r_bk, r_const], [r_xnT])
            yield
            cbk = []
            for sub in range(4):
                bk, bkb, r_bk = next_bank()
                for k in range(8):
                    PE(lambda e, k=k, bk=bk, sub=sub: e.matmul(bk[:, 0:384], lhsT=xnT[:, k, sub * 128:(sub + 1) * 128], rhs=w_in_sb[:, k, 0:384],
                                                              start=(k == 0), stop=(k == 7)), [r_xnT, r_w], [r_bk])
                cbk.append((bk, r_bk))
                if sub % 2 == 1:
                    yield
            for sub in range(4):
                (bk, r_bk), (st, r_st), (jk, r_jk) = cbk[sub], sts[sub], jkx[sub]
                ACT(lambda e, bk=bk, st=st, jk=jk: e.activation(out=jk[:, 0:256], in_=bk[:, 0:256], func=AF.Square, accum_out=st[:, 1:2]), [r_bk], [r_jk, r_st])
                ACT(lambda e, bk=bk, st=st, jk=jk: e.activation(out=jk[:, 256:384], in_=bk[:, 256:384], func=AF.Square, accum_out=st[:, 2:3]), [r_bk], [r_jk, r_st])
            yield
            for sub in range(4):
                st, r_st = sts[sub]
                ACT(lambda e, st=st: e.activation(out=st[:, 1:2], in_=st[:, 1:2], func=AF.Sqrt, scale=1.0 / 256, bias=EPS), [r_st], [r_st])
                ACT(lambda e, st=st: e.activation(out=st[:, 2:3], in_=st[:, 2:3], func=AF.Sqrt, scale=1.0 / 128, bias=EPS), [r_st], [r_st])
            for sub in range(4):
                st, r_st = sts[sub]
                VEC(lambda e, st=st: e.reciprocal(out=st[:, 1:3], in_=st[:, 1:3]), [r_st], [r_st])
            yield
            for sub in range(4):
                (bk, r_bk), (st, r_st), (cn, r_cn) = cbk[sub], sts[sub], xns[sub]
                VEC(lambda e, bk=bk, cn=cn, st=st: e.tensor_scalar(out=cn[:, 0:256], in0=bk[:, 0:256], scalar1=st[:, 1:2], scalar2=None, op0=ALU.mult),
                    [r_bk, r_st], [r_cn])
                VEC(lambda e, bk=bk, cn=cn, st=st: e.tensor_scalar(out=cn[:, 256:384], in0=bk[:, 256:384], scalar1=st[:, 2:3], scalar2=None, op0=ALU.mult),
                    [r_bk, r_st], [r_cn])
            yield
            bk2s = []
            for sub in range(4):
                cn, r_cn = xns[sub]
                bk2, bk2b, r_bk2 = next_bank()
                bk2v = bk2b[:, :].rearrange("p (a b) -> p a b", b=128)
                for k in range(3):
                    PE(lambda e, k=k, bk2v=bk2v, cn=cn: e.transpose(out=bk2v[:, k, :], in_=cn[:, k * 128:(k + 1) * 128], identity=ident[:]), [r_cn, r_const], [r_bk2])
                bk2s.append((bk2v, r_bk2))
            yield
            for sub in range(4):
                bk2v, r_bk2 = bk2s[sub]
                VEC(lambda e, bk2v=bk2v, sub=sub: e.tensor_tensor(out=cT[:, :, sub * 128:(sub + 1) * 128], in0=bk2v[:, 0:3, :],
                                                                 in1=gq.to_broadcast([128, 3, 128]), op=ALU.mult), [r_bk2, r_const], [r_cT])
            yield
            for sub in range(4):
                bk3, _, r_bk3 = next_bank()
                PE(lambda e, bk3=bk3, sub=sub: e.matmul(bk3[:, :], lhsT=cT[:, 2, sub * 128:(sub + 1) * 128], rhs=w_ukv_sb[:, 512:1024],
                                                       start=True, stop=True), [r_cT, r_w], [r_bk3])
                ACT(lambda e, bk3=bk3, sub=sub: e.copy(out=vb[:, :, sub, 0:64], in_=bk3[:, :].rearrange("p (h d) -> p h d", d=64)), [r_bk3], [r_vb])
                if sub % 2 == 1:
                    yield
            DMA(V_d[:, :, b * 4:(b + 1) * 4, :].rearrange("h p t d -> p h t d"), vb, [r_vb], [r_vd], q="scalar")
            yield

        def stageA2(b):
            T0 = b * 512
            xnT, r_xnT, cT, r_cT, csb, r_csb = ctxA.pop(b)
            ub, r_ub = ub_ring.next()
            for ct in range(4):
                yield
                bk, _, r_bk = next_bank()
                for k in range(8):
                    PE(lambda e, k=k, bk=bk, xnT=xnT, ct=ct: e.matmul(bk[:, :], lhsT=w_in_sb[:, k, 416 + ct * 128:416 + (ct + 1) * 128],
                                                                      rhs=xnT[:, k, :], start=(k == 0), stop=(k == 7)), [r_xnT, r_w], [r_bk])
                ACT(lambda e, bk=bk, ct=ct, ub=ub: e.copy(out=ub[:, ct, :], in_=bk[:, :]), [r_bk], [r_ub])
            DMA(uT_d[:, :, T0:T0 + 512], ub, [r_ub], [r_ud], q="scalar")
            yield
            bka, _, r_bka = next_bank()
            bkb_, _, r_bkb = next_bank()
            for k in range(8):
                PE(lambda e, k=k, bka=bka, xnT=xnT: e.matmul(bka[0:96, :], lhsT=w_kr_sb[:, k, 0:96], rhs=xnT[:, k, :], start=(k == 0), stop=(k == 7)),
                   [r_xnT, r_w], [r_bka])
            for k in range(8):
                PE(lambda e, k=k, bkb_=bkb_, xnT=xnT: e.matmul(bkb_[0:96, :], lhsT=w_kr_sb[:, k, 96:192], rhs=xnT[:, k, :], start=(k == 0), stop=(k == 7)),
                   [r_xnT, r_w], [r_bkb])
            yield
            VEC(lambda e, bka=bka, csb=csb: e.tensor_tensor(out=ta[RS, 0:512], in0=bka[RS, :], in1=csb[RS, 0, :], op=ALU.mult), [r_bka, r_csb], [r_ta])
            VEC(lambda e, bkb_=bkb_, csb=csb: e.tensor_tensor(out=ta[RS, 512:1024], in0=bkb_[RS, :], in1=csb[RS, 1, :], op=ALU.mult), [r_bkb, r_csb], [r_ta])
            VEC(lambda e: e.tensor_tensor(out=ta[RS, 0:512], in0=ta[RS, 0:512], in1=ta[RS, 512:1024], op=ALU.add), [r_ta], [r_ta])
            kb, r_kb = kb_ring.next()
            POOL(lambda e, kb=kb: e.tensor_copy(out=kb[RS, :, :], in_=ta[RS, 0:512].rearrange("p (o t) -> p o t", o=1).to_broadcast([32, 8, 512])),
                 [r_ta], [r_kb])
            for h in range(8):
                yield
                bk, _, r_bk = next_bank()
                PE(lambda e, bk=bk, h=h, cT=cT: e.matmul(bk[0:64, :], lhsT=w_ukv_sb[:, h * 64:(h + 1) * 64], rhs=cT[:, 2, :], start=True, stop=True),
                   [r_cT, r_w], [r_bk])
                ACT(lambda e, bk=bk, h=h, kb=kb: e.copy(out=kb[0:64, h, :], in_=bk[0:64, :]), [r_bk], [r_kb])
            DMA(kT_d[:, :, T0:T0 + 512].rearrange("h p t -> p h t"), kb, [r_kb], [r_kd], q="scalar")
            for h in range(8):
                yield
                bka, _, r_bka = next_bank()
                bkb_, _, r_bkb = next_bank()
                for k in range(2):
                    PE(lambda e, k=k, bka=bka, h=h, cT=cT: e.matmul(bka[0:96, :], lhsT=w_uq_sb[:, k, h * 96:(h + 1) * 96], rhs=cT[:, k, :],
                                                                    start=(k == 0), stop=(k == 1)), [r_cT, r_w], [r_bka])
                for k in range(2):
                    PE(lambda e, k=k, bkb_=bkb_, h=h, cT=cT: e.matmul(bkb_[0:96, :], lhsT=w_uqs_sb[:, k, h * 96:(h + 1) * 96], rhs=cT[:, k, :],
                                                                      start=(k == 0), stop=(k == 1)), [r_cT, r_w], [r_bkb])
                ACT(lambda e, bka=bka, h=h: e.copy(out=qs[:, h, :], in_=bka[0:96, :]), [r_bka], [r_qs])
                ACT(lambda e, bkb_=bkb_, h=h: e.copy(out=qsw[RS, h, :], in_=bkb_[RS, :]), [r_bkb], [r_qsw])
            yield
            qb, r_qb = qb_ring.next()
            POOL(lambda e, qb=qb: e.tensor_copy(out=qb[0:64, :, :], in_=qs[0:64, :, :]), [r_qs], [r_qb])
            VEC(lambda e, csb=csb: e.tensor_tensor(out=qs[RS, :, :], in0=qs[RS, :, :],
                                                   in1=csb[RS, 0:1, :].to_broadcast([32, 8, 512]), op=ALU.mult), [r_qs, r_csb], [r_qs])
            VEC(lambda e, csb=csb: e.tensor_tensor(out=qsw[RS, :, :], in0=qsw[RS, :, :],
                                                   in1=csb[RS, 1:2, :].to_broadcast([32, 8, 512]), op=ALU.mult), [r_qsw, r_csb], [r_qsw])
            VEC(lambda e, qb=qb: e.tensor_tensor(out=qb[RS, :, :], in0=qs[RS, :, :], in1=qsw[RS, :, :], op=ALU.add), [r_qs, r_qsw], [r_qb])
            DMA(qT_d[:, :, T0:T0 + 512].rearrange("h p t -> p h t"), qb, [r_qb], [r_qd])
            yield

        def run_interleaved(gens):
            gens = list(gens)
            while gens:
                for g_ in list(gens):
                    try:
                        next(g_)
                    except StopIteration:
                        gens.remove(g_)

        for b in range(NB + 1):
            gl = []
            if b >= 1:
                gl.append(stageA2(b - 1))
            if b < NB:
                gl.append(stageA1(b))
            run_interleaved(gl)

        if stop_after == "A":
            return True

        P.barrier()
        areset()
        S_POOL = (0, 1, 2, 3)
        O_POOL = (4, 5)
        M_POOL = (6, 7)
        pT_ringB = ARing(6, [128, 512], BF16, "pTB")
        qh_ring = ARing(2, [96, S], BF16, "qh")
        kh_ring = ARing(2, [96, S], BF16, "kh")
        vh_ring = ARing(2, [128, 32, 65], BF16, "vh")
        oT_ring = ARing(3, [65, 1024], F32, "oT3")
        an_ring = ARing(3, [64, 512], F32, "an3")
        LA = 3

        def load_head(h):
            qh, r_qh = qh_ring.next()
            kh, r_kh = kh_ring.next()
            vh, r_vh = vh_ring.next()
            DMA(qh, qT_d[h], [r_qd], [r_qh])
            DMA(kh, kT_d[h], [r_kd], [r_kh])
            DMA(vh, V_d[h], [r_vd], [r_vh])
            return (qh, r_qh, kh, r_kh, vh, r_vh)

        heads = {0: load_head(0)}
        for h in range(8):
            qh, r_qh, kh, r_kh, vh, r_vh = heads[h]
            if h + 1 < 8:
                heads[h + 1] = load_head(h + 1)
            items = [(b, kt) for b in range(NB) for kt in range(4 * (b + 1))]
            pts = {}
            bos = {}
            deferred = []

            def stage1(i):
                b, kt = items[i]
                T0 = b * 512
                bs, _, r_bs = next_bank(S_POOL)
                PE(lambda e, bs=bs, kt=kt, kh=kh, qh=qh, T0=T0: e.matmul(bs[:, :], lhsT=kh[:, kt * 128:(kt + 1) * 128], rhs=qh[:, T0:T0 + 512],
                                                                         start=True, stop=True), [r_kh, r_qh], [r_bs])
                pT, r_pT = pT_ringB.next()
                ACT(lambda e, bs=bs, pT=pT: e.activation(out=pT[:], in_=bs[:, :], func=AF.Exp, scale=ATT_SCALE), [r_bs], [r_pT])
                if kt >= 4 * b:
                    base = T0 - kt * 128
                    POOL(lambda e, pT=pT, base=base: e.affine_select(out=pT[:], in_=pT[:], pattern=[[1, 512]], compare_op=ALU.is_ge,
                                                                     fill=0.0, base=base, channel_multiplier=-1), [r_pT], [r_pT])
                pts[i] = (pT, r_pT)

            def stage2(j, i_now):
                b, kt = items[j]
                nkt = 4 * (b + 1)
                if kt == 0:
                    bos[b] = next_bank(O_POOL)
                bo, _, r_bo = bos[b]
                pT, r_pT = pts.pop(j)
                PE(lambda e, bo=bo, pT=pT, kt=kt, nkt=nkt, vh=vh: e.matmul(bo[0:65, :], lhsT=vh[:, kt, :], rhs=pT[:],
                                                                           start=(kt == 0), stop=(kt == nkt - 1)), [r_vh, r_pT], [r_bo])
                if kt == nkt - 1:
                    oT, r_oT = oT_ring.next()
                    VEC(lambda e, bo=bo, oT=oT: e.tensor_copy(out=oT[:, 0:512], in_=bo[0:65, :]), [r_bo], [r_oT])

                    def epi(b=b, oT=oT, r_oT=r_oT):
                        bm, _, r_bm = next_bank(M_POOL)
                        PE(lambda e, bm=bm, oT=oT: e.matmul(bm[0:64, :], lhsT=sel65[:, :], rhs=oT[:, 0:512], start=True, stop=True), [r_oT, r_const], [r_bm])
                        VEC(lambda e, bm=bm, oT=oT: e.reciprocal(out=oT[0:64, 512:1024], in_=bm[0:64, :]), [r_bm, r_oT], [r_oT])
                        an, r_an = an_ring.next()
                        POOL(lambda e, oT=oT, an=an: e.tensor_tensor(out=an[:, :], in0=oT[0:64, 0:512], in1=oT[0:64, 512:1024], op=ALU.mult), [r_oT], [r_an])
                        DMA(aT_d[b, :, h, :], an, [r_an], [r_ad], q="gpsimd")
                    deferred.append((i_now + 3, epi))

            n_it = len(items)
            for i in range(n_it + LA):
                if i < n_it:
                    stage1(i)
                if i - LA >= 0:
                    stage2(i - LA, i)
                while deferred and deferred[0][0] <= i:
                    deferred.pop(0)[1]()
            while deferred:
                deferred.pop(0)[1]()
        if stop_after == "B1":
            return True
        P.barrier()
        areset()
        Wst, r_Wst = aalloc([128, 4, 8, 2, 128], BF16, "Wst")
        Wfir, r_Wfir = aalloc([128, 4, 8, 128], BF16, "Wfir")
        Wo_r, r_Wo = aalloc([128, 16, 8, 32], BF16, "Wo")
        Wo_i, _ = aalloc([128, 16, 8, 32], BF16)
        sm, r_sm = aalloc([128, 32, 16], F32, "sm")
        pw_r, _ = aalloc([128, 16, 9], F32)
        pw_i, _ = aalloc([128, 16, 9], F32)
        ph_r, _ = aalloc([128, 16, 9], F32)
        ph_i, _ = aalloc([128, 16, 9], F32)
        mark = arena_off[0]
        Bri, r_bc = aalloc([128, 2, 16, 16], F32, "Bri")
        Cri, _ = aalloc([128, 2, 16, 16], F32)
        Bb_r, r_T = aalloc([128, 16, 16], F32, "T")
        Bb_i, _ = aalloc([128, 16, 16], F32)
        T1, _ = aalloc([128, 16, 16], F32)
        T2, _ = aalloc([128, 16, 16], F32)
        T3, _ = aalloc([128, 16, 16], F32)
        T4, _ = aalloc([128, 16, 16], F32)
        ME_r, r_ME = aalloc([128, 8, 4, 128], F32, "ME")
        ME_i, _ = aalloc([128, 8, 4, 128], F32)
        MF_r, r_MF = aalloc([128, 4, 128], F32, "MF")
        MF_in, _ = aalloc([128, 4, 128], F32)
        tmpF, r_tmpF = aalloc([128, 128], F32, "tmpF")
        DMA(sm[:, 0:3, :], s5v_d[l], [], [r_sm])
        DMA(Bri, s5b_d[l], [], [r_bc])
        DMA(Cri, s5c_d[l], [], [r_bc])
        POOL(lambda e: e.memset(ME_r, 0.0), [], [r_ME])
        POOL(lambda e: e.memset(ME_i, 0.0), [], [r_ME])
        POOL(lambda e: e.memset(MF_r, 0.0), [], [r_MF])
        POOL(lambda e: e.memset(MF_in, 0.0), [], [r_MF])
        POOL(lambda e: e.memset(Wo_r, 0.0), [], [r_Wo])
        POOL(lambda e: e.memset(Wo_i, 0.0), [], [r_Wo])
        LRE, LIM, LDT, DT, TT_, ER, Y, RND, FR, SN_, CS_, AR, AI, ARM1, DEN, RDEN, CBR, CBI, U1, U2, RDEC, RR = range(22)

        def smtt(o, a, b, op):
            VEC(lambda e: e.tensor_tensor(out=sm[:, o, :], in0=sm[:, a, :], in1=sm[:, b, :], op=op), [r_sm], [r_sm])

        def smts(o, a, s1, op0, s2=None, op1=None):
            if op1 is None:
                VEC(lambda e: e.tensor_scalar(out=sm[:, o, :], in0=sm[:, a, :], scalar1=s1, scalar2=None, op0=op0), [r_sm], [r_sm])
            else:
                VEC(lambda e: e.tensor_scalar(out=sm[:, o, :], in0=sm[:, a, :], scalar1=s1, scalar2=s2, op0=op0, op1=op1), [r_sm], [r_sm])

        def smact(o, a, func, scale=1.0):
            ACT(lambda e: e.activation(out=sm[:, o, :], in_=sm[:, a, :], func=func, scale=scale), [r_sm], [r_sm])

        smact(DT, LDT, AF.Exp)
        smtt(TT_, LRE, DT, ALU.mult)
        smact(ER, TT_, AF.Exp)
        smact(RDEC, TT_, AF.Exp, 8.0)
        smtt(Y, LIM, DT, ALU.mult)
        smts(Y, Y, 1.0 / TWO_PI, ALU.mult)
        smts(RND, Y, MAGIC, ALU.add, MAGIC, ALU.subtract)
        smtt(FR, Y, RND, ALU.subtract)
        smact(SN_, FR, AF.Sin, SIN_SCALE)
        smts(Y, Y, 0.25, ALU.add)
        smts(RND, Y, MAGIC, ALU.add, MAGIC, ALU.subtract)
        smtt(FR, Y, RND, ALU.subtract)
        smact(CS_, FR, AF.Sin, SIN_SCALE)
        smtt(AR, ER, CS_, ALU.mult)
        smtt(AI, ER, SN_, ALU.mult)
        smts(ARM1, AR, -1.0, ALU.add)
        smtt(U1, LRE, LRE, ALU.mult)
        smtt(U2, LIM, LIM, ALU.mult)
        smtt(DEN, U1, U2, ALU.add)
        VEC(lambda e: e.reciprocal(out=sm[:, RDEN, :], in_=sm[:, DEN, :]), [r_sm], [r_sm])
        smtt(U1, ARM1, LRE, ALU.mult)
        smtt(U2, AI, LIM, ALU.mult)
        smtt(U1, U1, U2, ALU.add)
        smtt(CBR, U1, RDEN, ALU.mult)
        smtt(U1, AI, LRE, ALU.mult)
        smtt(U2, ARM1, LIM, ALU.mult)
        smtt(U1, U1, U2, ALU.subtract)
        smtt(CBI, U1, RDEN, ALU.mult)
        VEC(lambda e: e.memset(pw_r[:, :, 0:1], 1.0), [r_sm], [r_sm])
        VEC(lambda e: e.memset(pw_i[:, :, 0:1], 0.0), [r_sm], [r_sm])
        VEC(lambda e: e.tensor_copy(out=pw_r[:, :, 1], in_=sm[:, AR, :]), [r_sm], [r_sm])
        VEC(lambda e: e.tensor_copy(out=pw_i[:, :, 1], in_=sm[:, AI, :]), [r_sm], [r_sm])
        for k in range(1, 8):
            VEC(lambda e, k=k: e.tensor_tensor(out=sm[:, U1, :], in0=pw_r[:, :, k], in1=sm[:, AR, :], op=ALU.mult), [r_sm], [r_sm])
            VEC(lambda e, k=k: e.tensor_tensor(out=sm[:, U2, :], in0=pw_i[:, :, k], in1=sm[:, AI, :], op=ALU.mult), [r_sm], [r_sm])
            VEC(lambda e, k=k: e.tensor_tensor(out=pw_r[:, :, k + 1], in0=sm[:, U1, :], in1=sm[:, U2, :], op=ALU.subtract), [r_sm], [r_sm])
            VEC(lambda e, k=k: e.tensor_tensor(out=sm[:, U1, :], in0=pw_r[:, :, k], in1=sm[:, AI, :], op=ALU.mult), [r_sm], [r_sm])
            VEC(lambda e, k=k: e.tensor_tensor(out=sm[:, U2, :], in0=pw_i[:, :, k], in1=sm[:, AR, :], op=ALU.mult), [r_sm], [r_sm])
            VEC(lambda e, k=k: e.tensor_tensor(out=pw_i[:, :, k + 1], in0=sm[:, U1, :], in1=sm[:, U2, :], op=ALU.add), [r_sm], [r_sm])
        VEC(lambda e: e.reciprocal(out=sm[:, RR, :], in_=sm[:, RDEC, :]), [r_sm], [r_sm])
        VEC(lambda e: e.tensor_tensor(out=ph_r[:, :, 0], in0=pw_r[:, :, 8], in1=sm[:, RR, :], op=ALU.mult), [r_sm], [r_sm])
        VEC(lambda e: e.tensor_tensor(out=ph_i[:, :, 0], in0=pw_i[:, :, 8], in1=sm[:, RR, :], op=ALU.mult), [r_sm], [r_sm])
        for k in range(8):
            VEC(lambda e, k=k: e.tensor_tensor(out=sm[:, U1, :], in0=ph_r[:, :, k], in1=ph_r[:, :, k], op=ALU.mult), [r_sm], [r_sm])
            VEC(lambda e, k=k: e.tensor_tensor(out=sm[:, U2, :], in0=ph_i[:, :, k], in1=ph_i[:, :, k], op=ALU.mult), [r_sm], [r_sm])
            VEC(lambda e, k=k: e.tensor_tensor(out=ph_r[:, :, k + 1], in0=sm[:, U1, :], in1=sm[:, U2, :], op=ALU.subtract), [r_sm], [r_sm])
            VEC(lambda e, k=k: e.tensor_tensor(out=sm[:, U1, :], in0=ph_r[:, :, k], in1=ph_i[:, :, k], op=ALU.mult), [r_sm], [r_sm])
            VEC(lambda e, k=k: e.tensor_scalar(out=ph_i[:, :, k + 1], in0=sm[:, U1, :], scalar1=2.0, scalar2=None, op0=ALU.mult), [r_sm], [r_sm])

        def bc16(tile_idx_ap):
            return tile_idx_ap.rearrange("p (a o) -> p a o", o=1).to_broadcast([128, 16, 16])

        def cmul_bc(outr, outi, xr, xi, sr_ap, si_ap, rds, wrs):
            pass

        VEC(lambda e: e.tensor_tensor(out=T1, in0=Bri[:, 0], in1=bc16(sm[:, CBR, :]), op=ALU.mult), [r_sm, r_bc], [r_T])
        VEC(lambda e: e.tensor_tensor(out=T2, in0=Bri[:, 1], in1=bc16(sm[:, CBI, :]), op=ALU.mult), [r_sm, r_bc], [r_T])
        VEC(lambda e: e.tensor_tensor(out=Bb_r, in0=T1, in1=T2, op=ALU.subtract), [r_T], [r_T])
        VEC(lambda e: e.tensor_tensor(out=T1, in0=Bri[:, 1], in1=bc16(sm[:, CBR, :]), op=ALU.mult), [r_sm, r_bc, r_T], [r_T])
        VEC(lambda e: e.tensor_tensor(out=T2, in0=Bri[:, 0], in1=bc16(sm[:, CBI, :]), op=ALU.mult), [r_sm, r_bc], [r_T])
        VEC(lambda e: e.tensor_tensor(out=Bb_i, in0=T1, in1=T2, op=ALU.add), [r_T], [r_T])

        def blkME(M, lg, hf):
            return M[hf * 64:(hf + 1) * 64, lg, :, :].rearrange("p ct (q x) -> p ct q x", x=32)[:, :, :, hf * 16:(hf + 1) * 16]

        def halfT(T, hf):
            return T[hf * 64:(hf + 1) * 64, :, :].rearrange("p (ct q) c -> p ct q c", q=4)

        for lg in range(8):
            VEC(lambda e, lg=lg: e.tensor_tensor(out=T1, in0=Bb_r, in1=pw_r[:, :, lg:lg + 1].to_broadcast([128, 16, 16]), op=ALU.mult), [r_sm, r_T], [r_T])
            VEC(lambda e, lg=lg: e.tensor_tensor(out=T2, in0=Bb_i, in1=pw_i[:, :, lg:lg + 1].to_broadcast([128, 16, 16]), op=ALU.mult), [r_sm, r_T], [r_T])
            VEC(lambda e, lg=lg: e.tensor_tensor(out=T3, in0=Bb_i, in1=pw_r[:, :, lg:lg + 1].to_broadcast([128, 16, 16]), op=ALU.mult), [r_sm, r_T], [r_T])
            VEC(lambda e, lg=lg: e.tensor_tensor(out=T4, in0=Bb_r, in1=pw_i[:, :, lg:lg + 1].to_broadcast([128, 16, 16]), op=ALU.mult), [r_sm, r_T], [r_T])
            for hf in range(2):
                POOL(lambda e, lg=lg, hf=hf: e.tensor_tensor(out=blkME(ME_r, lg, hf), in0=halfT(T1, hf), in1=halfT(T2, hf), op=ALU.subtract), [r_T], [r_ME])
                POOL(lambda e, lg=lg, hf=hf: e.tensor_tensor(out=blkME(ME_i, lg, hf), in0=halfT(T3, hf), in1=halfT(T4, hf), op=ALU.add), [r_T], [r_ME])

        def blkMF(M, hf):
            return M[hf * 64:(hf + 1) * 64, :, :].rearrange("p ct (q x) -> p ct q x", x=32)[:, :, :, hf * 16:(hf + 1) * 16]

        for hf in range(2):
            POOL(lambda e, hf=hf: e.tensor_copy(out=blkMF(MF_r, hf), in_=halfT(Cri[:, 0], hf)), [r_bc], [r_MF])
            POOL(lambda e, hf=hf: e.tensor_scalar(out=blkMF(MF_in, hf), in0=halfT(Cri[:, 1], hf), scalar1=-1.0, scalar2=None, op0=ALU.mult), [r_bc], [r_MF])
        for ct in range(4):
            for ri, M in ((0, ME_r), (1, ME_i)):
                for j0 in (0, 4):
                    bk, _, r_bk = next_bank()
                    for jj in range(4):
                        j = j0 + jj
                        PE(lambda e, bk=bk, jj=jj, j=j, ct=ct, M=M: e.transpose(out=bk[:, jj * 128:(jj + 1) * 128], in_=M[:, 7 - j, ct, :], identity=identf[:]),
                           [r_ME, r_const], [r_bk])
                    ACT(lambda e, bk=bk, ct=ct, j0=j0, ri=ri: e.copy(out=Wst[:, ct, j0:j0 + 4, ri, :], in_=bk[:, :].rearrange("p (a b) -> p a b", b=128)),
                        [r_bk], [r_Wst])
        for ct in range(4):
            for l0 in (0, 4):
                bk, _, r_bk = next_bank()
                for ll in range(4):
                    lg = l0 + ll
                    PE(lambda e, bk=bk, ll=ll, lg=lg, ct=ct: e.matmul(bk[:, ll * 128:(ll + 1) * 128], lhsT=ME_r[:, lg, ct, :], rhs=MF_r[:, ct, :], start=True, stop=False),
                       [r_ME, r_MF], [r_bk])
                    PE(lambda e, bk=bk, ll=ll, lg=lg, ct=ct: e.matmul(bk[:, ll * 128:(ll + 1) * 128], lhsT=ME_i[:, lg, ct, :], rhs=MF_in[:, ct, :], start=False, stop=True),
                       [r_ME, r_MF], [r_bk])
                VEC(lambda e, bk=bk, ct=ct, l0=l0: e.tensor_tensor(out=Wfir[:, ct, l0:l0 + 4, :], in0=bk[:, :].rearrange("p (a b) -> p a b", b=128),
                                                                 in1=mask16[:, :].rearrange("p (o b) -> p o b", o=1).to_broadcast([128, 4, 128]), op=ALU.mult),
                    [r_bk, r_const], [r_Wfir])
                if l0 == 0:
                    VEC(lambda e, bk=bk: e.tensor_tensor(out=tmpF, in0=bk[:, 0:128], in1=mask16[:, :], op=ALU.mult), [r_bk, r_const], [r_tmpF])
                    VEC(lambda e, ct=ct: e.scalar_tensor_tensor(out=Wfir[:, ct, 0, :], in0=identf[:, :], scalar=col(l, C_D + ct), in1=tmpF, op0=ALU.mult, op1=ALU.add),
                        [r_tmpF, r_const], [r_Wfir])
        for i in range(8):
            VEC(lambda e, i=i: e.tensor_tensor(out=T1, in0=Cri[:, 0], in1=pw_r[:, :, i + 1:i + 2].to_broadcast([128, 16, 16]), op=ALU.mult), [r_sm, r_bc, r_T], [r_T])
            VEC(lambda e, i=i: e.tensor_tensor(out=T2, in0=Cri[:, 1], in1=pw_i[:, :, i + 1:i + 2].to_broadcast([128, 16, 16]), op=ALU.mult), [r_sm, r_bc, r_T], [r_T])
            VEC(lambda e, i=i: e.tensor_tensor(out=T3, in0=Cri[:, 1], in1=pw_r[:, :, i + 1:i + 2].to_broadcast([128, 16, 16]), op=ALU.mult), [r_sm, r_bc, r_T], [r_T])
            VEC(lambda e, i=i: e.tensor_tensor(out=T4, in0=Cri[:, 0], in1=pw_i[:, :, i + 1:i + 2].to_broadcast([128, 16, 16]), op=ALU.mult), [r_sm, r_bc, r_T], [r_T])
            for hf in range(2):
                hs = slice(hf * 64, (hf + 1) * 64)
                cs = slice(hf * 16, (hf + 1) * 16)
                POOL(lambda e, i=i, hs=hs, cs=cs: e.tensor_tensor(out=Wo_r[hs, :, i, cs], in0=T1[hs], in1=T2[hs], op=ALU.subtract), [r_T], [r_Wo])
                VEC(lambda e, i=i, hs=hs, cs=cs: e.scalar_tensor_tensor(out=Wo_i[hs, :, i, cs], in0=T3[hs], scalar=-1.0, in1=T4[hs], op0=ALU.mult, op1=ALU.subtract),
                    [r_T], [r_Wo])
        P.barrier()
        arena_off[0] = mark
        u_ring = ARing(2, [128, S], BF16, "uct")
        y_ring = ARing(1, [128, S], F32, "yct")
        tab_r, r_tab = aalloc([128, 4, 512], F32, "tab")
        tab_i, _ = aalloc([128, 4, 512], F32)
        tq1, r_tq = aalloc([128, 4, 256], F32, "tq")
        tq2, _ = aalloc([128, 4, 256], F32)
        xps = []
        for i_ in range(2):
            xr_, rx_ = aalloc([128, 4, 512], BF16, f"xpr{i_}")
            xi_, _ = aalloc([128, 4, 512], BF16)
            VEC(lambda e, xr_=xr_: e.memset(xr_[:, :, 0:1], 0.0), [], [rx_])
            VEC(lambda e, xi_=xi_: e.memset(xi_[:, :, 0:1], 0.0), [], [rx_])
            xps.append((xr_, xi_, rx_))

        class _AR:
            def __init__(self, bufs):
                self.bufs = bufs
                self.i = 0

            def next(self):
                b_ = self.bufs[self.i % len(self.bufs)]
                self.i += 1
                return b_
        A_ring = _AR([aalloc([128, 512], F32, f"A{i_}") for i_ in range(6)] + [(frB[0][:, i_ * 512:(i_ + 1) * 512], Res(f"AB{i_}")) for i_ in range(4)])
        ctxS = {}

        def stageS(ct):
            xp_r, xp_i, r_xp = xps[ct % 2]
            uct, r_uct = u_ring.next()
            DMA(uct, uT_d[:, ct, :], [r_ud], [r_uct])
            u8 = uct.rearrange("p (c j) -> p j c", j=8)
            ctxS[ct] = (uct, r_uct, u8)
            VEC(lambda e: e.memset(tab_r[:, :, 0:1], 1.0), [r_tab], [r_tab])
            VEC(lambda e: e.memset(tab_i[:, :, 0:1], 0.0), [r_tab], [r_tab])
            for k in range(9):
                yield
                s_ = 1 << k
                phr = ph_r[:, 4 * ct:4 * ct + 4, k:k + 1].to_broadcast([128, 4, s_])
                phi = ph_i[:, 4 * ct:4 * ct + 4, k:k + 1].to_broadcast([128, 4, s_])
                VEC(lambda e, s_=s_, phr=phr: e.tensor_tensor(out=tq1[:, :, 0:s_], in0=tab_r[:, :, 0:s_], in1=phr, op=ALU.mult), [r_tab, r_sm, r_tq], [r_tq])
                VEC(lambda e, s_=s_, phi=phi: e.tensor_tensor(out=tq2[:, :, 0:s_], in0=tab_i[:, :, 0:s_], in1=phi, op=ALU.mult), [r_tab, r_sm, r_tq], [r_tq])
                VEC(lambda e, s_=s_: e.tensor_tensor(out=tab_r[:, :, s_:2 * s_], in0=tq1[:, :, 0:s_], in1=tq2[:, :, 0:s_], op=ALU.subtract), [r_tq], [r_tab])
                VEC(lambda e, s_=s_, phi=phi: e.tensor_tensor(out=tq1[:, :, 0:s_], in0=tab_r[:, :, 0:s_], in1=phi, op=ALU.mult), [r_tab, r_sm, r_tq], [r_tq])
                VEC(lambda e, s_=s_, phr=phr: e.tensor_tensor(out=tq2[:, :, 0:s_], in0=tab_i[:, :, 0:s_], in1=phr, op=ALU.mult), [r_tab, r_sm, r_tq], [r_tq])
                VEC(lambda e, s_=s_: e.tensor_tensor(out=tab_i[:, :, s_:2 * s_], in0=tq1[:, :, 0:s_], in1=tq2[:, :, 0:s_], op=ALU.add), [r_tq], [r_tab])
            for q in range(4):
                yield
                pair = 4 * ct + q
                ps_ = slice(32 * q, 32 * q + 32)
                bks = []
                for ri in range(2):
                    bk, _, r_bk = next_bank()
                    for j in range(8):
                        PE(lambda e, bk=bk, j=j, ri=ri, ps_=ps_, q=q, ct=ct, u8=u8: e.matmul(bk[:, :], lhsT=Wst[ps_, ct, j, ri, :], rhs=u8[ps_, j, :],
                                                                                           start=(j == 0), stop=(j == 7), tile_position=(32 * q, 0)),
                           [r_Wst, r_uct], [r_bk])
                    bks.append((bk, r_bk))
                (Sr, r_Sr), (Si, r_Si) = bks
                tr, ti = tab_r[:, q, :], tab_i[:, q, :]
                a1, r_a1 = A_ring.next()
                a2, r_a2 = A_ring.next()
                a3, r_a3 = A_ring.next()
                a4, r_a4 = A_ring.next()
                VEC(lambda e, Sr=Sr, a1=a1, tr=tr: e.tensor_tensor(out=a1, in0=Sr[:, :], in1=tr, op=ALU.mult), [r_Sr, r_tab], [r_a1])
                VEC(lambda e, Si=Si, a2=a2, ti=ti: e.tensor_tensor(out=a2, in0=Si[:, :], in1=ti, op=ALU.mult), [r_Si, r_tab], [r_a2])
                VEC(lambda e, Si=Si, a3=a3, tr=tr: e.tensor_tensor(out=a3, in0=Si[:, :], in1=tr, op=ALU.mult), [r_Si, r_tab], [r_a3])
                VEC(lambda e, Sr=Sr, a4=a4, ti=ti: e.tensor_tensor(out=a4, in0=Sr[:, :], in1=ti, op=ALU.mult), [r_Sr, r_tab], [r_a4])
                POOL(lambda e, a1=a1, a2=a2: e.tensor_tensor(out=a1, in0=a1, in1=a2, op=ALU.add), [r_a1, r_a2], [r_a1])
                POOL(lambda e, a3=a3, a4=a4: e.tensor_tensor(out=a3, in0=a3, in1=a4, op=ALU.subtract), [r_a3, r_a4], [r_a3])
                yield
                rd = sm[:, RDEC, pair:pair + 1].to_broadcast([128, 512])
                VEC(lambda e, a1=a1, a2=a2, rd=rd: e.tensor_tensor_scan(out=a2, data0=rd, data1=a1, initial=0.0, op0=ALU.mult, op1=ALU.add), [r_a1, r_sm, r_a2], [r_a2])
                VEC(lambda e, a3=a3, a4=a4, rd=rd: e.tensor_tensor_scan(out=a4, data0=rd, data1=a3, initial=0.0, op0=ALU.mult, op1=ALU.add), [r_a3, r_sm, r_a4], [r_a4])
                yield
                b1, r_b1 = A_ring.next()
                b2, r_b2 = A_ring.next()
                POOL(lambda e, a2=a2, b1=b1, tr=tr: e.tensor_tensor(out=b1, in0=a2, in1=tr, op=ALU.mult), [r_a2, r_tab], [r_b1])
                POOL(lambda e, a4=a4, b2=b2, ti=ti: e.tensor_tensor(out=b2, in0=a4, in1=ti, op=ALU.mult), [r_a4, r_tab], [r_b2])
                VEC(lambda e, b1=b1, b2=b2, q=q: e.tensor_tensor(out=xp_r[:, q, 1:512], in0=b1[:, 0:511], in1=b2[:, 0:511], op=ALU.subtract), [r_b1, r_b2], [r_xp])
                POOL(lambda e, a2=a2, a1=a1, ti=ti: e.tensor_tensor(out=a1, in0=a2, in1=ti, op=ALU.mult), [r_a2, r_tab, r_a1], [r_a1])
                POOL(lambda e, a4=a4, a3=a3, tr=tr: e.tensor_tensor(out=a3, in0=a4, in1=tr, op=ALU.mult), [r_a4, r_tab, r_a3], [r_a3])
                VEC(lambda e, a1=a1, a3=a3, q=q: e.tensor_tensor(out=xp_i[:, q, 1:512], in0=a1[:, 0:511], in1=a3[:, 0:511], op=ALU.add), [r_a1, r_a3], [r_xp])

        def stageO(ct):
            xp_r, xp_i, r_xp = xps[ct % 2]
            uct, r_uct, u8 = ctxS.pop(ct)
            yct, r_yct = y_ring.next()
            y8 = yct.rearrange("p (c j) -> p j c", j=8)
            for i in range(8):
                yield
                bk, _, r_bk = next_bank()
                for lg in range(i + 1):
                    PE(lambda e, bk=bk, lg=lg, i=i, ct=ct, u8=u8: e.matmul(bk[:, :], lhsT=Wfir[:, ct, lg, :], rhs=u8[:, i - lg, :], start=(lg == 0), stop=False),
                       [r_Wfir, r_uct], [r_bk])
                for q in range(4):
                    pair = 4 * ct + q
                    PE(lambda e, bk=bk, q=q, pair=pair, i=i: e.matmul(bk[32 * q:32 * q + 32, :], lhsT=Wo_r[:, pair, i, :], rhs=xp_r[:, q, :], start=False, stop=False,
                                                                     tile_position=(0, 32 * q)), [r_Wo, r_xp], [r_bk])
                    PE(lambda e, bk=bk, q=q, pair=pair, i=i: e.matmul(bk[32 * q:32 * q + 32, :], lhsT=Wo_i[:, pair, i, :], rhs=xp_i[:, q, :], start=False, stop=(q == 3),
                                                                     tile_position=(0, 32 * q)), [r_Wo, r_xp], [r_bk])
                ACT(lambda e, bk=bk, y8=y8, i=i: e.copy(out=y8[:, i, :], in_=bk[:, :]), [r_bk], [r_yct])
            DMA(yT_d[:, ct, :], yct, [r_yct], [r_yd], q="scalar")

        for ct_ in range(5):
            gl = []
            if ct_ >= 1:
                gl.append(stageO(ct_ - 1))
            if ct_ < 4:
                gl.append(stageS(ct_))
            run_interleaved(gl)
        if stop_after == "B2a":
            return True
        P.barrier()
        arena_off[0] = mark
        wglu, r_wglu = aalloc([128, 4, 512], BF16, "wglu")
        DMAC(wglu, w_glu_d[l].rearrange("(k p) n -> p k n", p=128), [], [r_wglu])
        yb_ring = ARing(2, [128, 4, 512], F32, "yb")
        g_ring = ARing(2, [128, 4, 512], BF16, "gT")
        sg_ring = ARing(2, [128, 4, 512], F32, "sg")
        sq_ring = ARing(1, [128, 4, 512], F32, "sq")
        rs_ring = ARing(2, [128, 512], F32, "rs")
        sn_ring = ARing(2, [128, 4, 512], BF16, "sn")
        for b in range(NB):
            T0 = b * 512
            yb, r_yb = yb_ring.next()
            DMA(yb, yT_d[:, :, T0:T0 + 512], [r_yd], [r_yb])
            gT, r_gT = g_ring.next()
            ACT(lambda e, yb=yb, gT=gT: e.activation(out=gT, in_=yb, func=AF.Gelu_apprx_tanh), [r_yb], [r_gT])
            sg, r_sg = sg_ring.next()
            for co in range(4):
                bk, _, r_bk = next_bank()
                for ci in range(4):
                    PE(lambda e, bk=bk, ci=ci, co=co, gT=gT: e.matmul(bk[:, :], lhsT=wglu[:, ci, co * 128:(co + 1) * 128], rhs=gT[:, ci, :], start=(ci == 0), stop=(ci == 3)),
                       [r_wglu, r_gT], [r_bk])
                ACT(lambda e, bk=bk, co=co, sg=sg: e.activation(out=sg[:, co, :], in_=bk[:, :], func=AF.Sigmoid, bias=col(l, C_BGLU + co)), [r_bk, r_const], [r_sg])
            VEC(lambda e, sg=sg, yb=yb: e.tensor_tensor(out=sg, in0=sg, in1=yb, op=ALU.mult), [r_sg, r_yb], [r_sg])
            sq, r_sq = sq_ring.next()
            POOL(lambda e, sg=sg, sq=sq: e.tensor_tensor(out=sq, in0=sg, in1=sg, op=ALU.mult), [r_sg], [r_sq])
            bk, _, r_bk = next_bank()
            for co in range(4):
                PE(lambda e, bk=bk, co=co, sq=sq: e.matmul(bk[:, :], lhsT=onesf[:, :], rhs=sq[:, co, :], start=(co == 0), stop=(co == 3)), [r_sq, r_const], [r_bk])
            rs_, r_rs = rs_ring.next()
            ACT(lambda e, bk=bk, rs_=rs_: e.activation(out=rs_, in_=bk[:, :], func=AF.Sqrt, scale=1.0 / 512, bias=EPS), [r_bk], [r_rs])
            VEC(lambda e, rs_=rs_: e.reciprocal(out=rs_, in_=rs_), [r_rs], [r_rs])
            VEC(lambda e, sg=sg: e.tensor_tensor(out=sg, in0=sg, in1=col(l, C_GS, 4).rearrange("p (k o) -> p k o", o=1).to_broadcast([128, 4, 512]), op=ALU.mult),
                [r_sg, r_const], [r_sg])
            sn, r_sn = sn_ring.next()
            VEC(lambda e, sg=sg, sn=sn, rs_=rs_: e.tensor_tensor(out=sn, in0=sg, in1=rs_.rearrange("p (o t) -> p o t", o=1).to_broadcast([128, 4, 512]), op=ALU.mult),
                [r_sg, r_rs], [r_sn])
            DMA(sT_d[b], sn, [r_sn], [r_sd])
        if stop_after == "B2":
            return True
        P.barrier()
        areset()
        wo_a, r_wc = aalloc([128, 4, 1024], BF16, "wo_a")
        wo_s, _ = aalloc([128, 4, 1024], BF16)
        wxq, _ = aalloc([128, 8, 1024], BF16)
        wxo, _ = aalloc([128, 8, 1024], BF16)
        KxT, r_kx = aalloc([128, 8, 256], BF16, "KxT")
        Vx, _ = aalloc([128, 2, 1024], BF16)
        markc = arena_off[0]
        wxkv, r_wxkv = aalloc([128, 8, 2048], BF16, "wxkv")
        DMAC(wxkv, w_xkv_d[l].rearrange("(k p) n -> p k n", p=128), [], [r_wxkv])
        for par_ in range(2):
            DMAC(wo_a[par_ * 64:(par_ + 1) * 64, :, :], w_out_d[l, 0:512, :].rearrange("(hp par d) n -> par d hp n", par=2, d=64)[par_], [], [r_wc])
        VEC(lambda e: e.tensor_tensor(out=wo_a, in0=wo_a, in1=col(l, C_GA2, 4).rearrange("p (k o) -> p k o", o=1).to_broadcast([128, 4, 1024]), op=ALU.mult), [r_wc, r_const], [r_wc])
        DMAC(wo_s, w_out_d[l, 512:1024, :].rearrange("(k p) n -> p k n", p=128), [], [r_wc])
        DMAC(wxq, w_xq_d[l].rearrange("(k p) n -> p k n", p=128), [], [r_wc])
        DMAC(wxo, w_xo_d[l].rearrange("(k p) n -> p k n", p=128), [], [r_wc])
        memT, r_memT = xnT_ring.next()
        for mt in range(2):
            ht, r_ht = ht_ring.next()
            DMA(ht[:], mem_d[mt * 128:(mt + 1) * 128, :], [], [r_ht])
            norm_transpose(ht, r_ht, col(l, C_GMEM, 8), memT, r_memT, mt)
        for oc in range(8):
            bk, _, r_bk = next_bank()
            for k in range(8):
                PE(lambda e, bk=bk, k=k, oc=oc: e.matmul(bk[:, 0:256], lhsT=wxkv[:, k, oc * 128:(oc + 1) * 128], rhs=memT[:, k, 0:256], start=(k == 0), stop=(k == 7)),
                   [r_wxkv, r_memT], [r_bk])
            ACT(lambda e, bk=bk, oc=oc: e.copy(out=KxT[:, oc, :], in_=bk[:, 0:256]), [r_bk], [r_kx])
        for mt in range(2):
            for hf in range(2):
                bk, _, r_bk = next_bank()
                for k in range(8):
                    PE(lambda e, bk=bk, k=k, mt=mt, hf=hf: e.matmul(bk[:, :], lhsT=memT[:, k, mt * 128:(mt + 1) * 128], rhs=wxkv[:, k, 1024 + hf * 512:1024 + (hf + 1) * 512],
                                                                  start=(k == 0), stop=(k == 7)), [r_wxkv, r_memT], [r_bk])
                ACT(lambda e, bk=bk, mt=mt, hf=hf: e.copy(out=Vx[:, mt, hf * 512:(hf + 1) * 512], in_=bk[:, :]), [r_bk], [r_kx])
        P.barrier()
        arena_off[0] = markc
        an_ring2 = ARing(1, [128, 4, 512], BF16, "an2")
        sn_ring2 = ARing(1, [128, 4, 512], BF16, "sn2")
        h1_ring = ARing(8, [128, D], F32, "h1t")
        qx_ring = ARing(1, [128, 8, 512], BF16, "qxT")
        ox_ring = ARing(1, [128, 8, 512], BF16, "oxT")
        pX_ring = ARing(2, [128, 2, 512], BF16, "pX")
        rc_ring = ARing(2, [128, 512], F32, "rc")
        araw_v = frA[0][0:64, 0:4096].rearrange("p (h t) -> p h t", t=512)
        araw4 = frA[0][0:64, 0:4096].rearrange("p (hp par t) -> p hp par t", par=2, t=512)
        r_araw = frA[1]
        sqv = frB[0][0:64, 0:2048].rearrange("p (h t) -> p h t", t=512)
        r_sqv = frB[1]
        r_h1src = [Res() for _ in range(32)] if l == 0 else r_hd
        ctx1 = {}

        def stageC1a(b):
            T0 = b * 512
            DMA(araw_v, aT_d[b], [r_ad], [r_araw])
            sn2, r_sn2 = sn_ring2.next()
            DMA(sn2, sT_d[b], [r_sd], [r_sn2])
            bk, _, r_bk = next_bank()
            for hg in range(2):
                POOL(lambda e, hg=hg: e.tensor_tensor(out=sqv, in0=araw_v[:, hg * 4:(hg + 1) * 4, :], in1=araw_v[:, hg * 4:(hg + 1) * 4, :], op=ALU.mult), [r_araw], [r_sqv])
                for hh in range(4):
                    PE(lambda e, bk=bk, hg=hg, hh=hh: e.matmul(bk[0:64, :], lhsT=onesf[0:64, 0:64], rhs=sqv[:, hh, :], start=(hg == 0 and hh == 0), stop=(hg == 1 and hh == 3)),
                       [r_sqv, r_const], [r_bk])
                yield
            rc, r_rc = rc_ring.next()
            ACT(lambda e, bk=bk, rc=rc: e.activation(out=rc[0:64, :], in_=bk[0:64, :], func=AF.Sqrt, scale=1.0 / 512, bias=EPS), [r_bk], [r_rc])
            VEC(lambda e, rc=rc: e.reciprocal(out=rc[0:64, :], in_=rc[0:64, :]), [r_rc], [r_rc])
            yield
            an2, r_an2 = an_ring2.next()
            rcb = rc[0:64, :].rearrange("p (o t) -> p o t", o=1).to_broadcast([64, 4, 512])
            VEC(lambda e, an2=an2, rcb=rcb: e.tensor_tensor(out=an2[0:64, :, :], in0=araw4[:, :, 0, :], in1=rcb, op=ALU.mult), [r_araw, r_rc], [r_an2])
            yield
            VEC(lambda e, an2=an2, rcb=rcb: e.tensor_tensor(out=an2[64:128, :, :], in0=araw4[:, :, 1, :], in1=rcb, op=ALU.mult), [r_araw, r_rc], [r_an2])
            yield
            xnT, r_xnT = xnT_ring.next()
            h1s = []
            ctx1[b] = (xnT, r_xnT, h1s)
            for sub in range(4):
                ti = b * 4 + sub
                ts_ = slice(sub * 128, (sub + 1) * 128)
                ht, r_ht = ht_ring.next()
                DMA(ht[:], src_d[T0 + sub * 128:T0 + (sub + 1) * 128, :], [r_h1src[ti]], [r_ht])
                h1t, r_h1t = h1_ring.next()
                for hf in range(2):
                    bk, _, r_bk = next_bank()
                    cs_ = slice(hf * 512, (hf + 1) * 512)
                    for k in range(4):
                        PE(lambda e, bk=bk, k=k, an2=an2, ts_=ts_, cs_=cs_: e.matmul(bk[:, :], lhsT=an2[:, k, ts_], rhs=wo_a[:, k, cs_], start=(k == 0), stop=False),
                           [r_an2, r_wc], [r_bk])
                    for k in range(4):
                        PE(lambda e, bk=bk, k=k, sn2=sn2, ts_=ts_, cs_=cs_: e.matmul(bk[:, :], lhsT=sn2[:, k, ts_], rhs=wo_s[:, k, cs_], start=False, stop=(k == 3)),
                           [r_sn2, r_wc], [r_bk])
                    VEC(lambda e, bk=bk, ht=ht, h1t=h1t, cs_=cs_: e.tensor_tensor(out=h1t[:, cs_], in0=bk[:, :], in1=ht[:, cs_], op=ALU.add), [r_bk, r_ht], [r_h1t])
                    yield
                norm_transpose(h1t, r_h1t, col(l, C_GX, 8), xnT, r_xnT, sub)
                h1s.append((h1t, r_h1t))
                yield

        def stageC1b(b):
            T0 = b * 512
            xnT, r_xnT, h1s = ctx1.pop(b)
            qxT, r_qxT = qx_ring.next()
            for oc in range(8):
                bk, _, r_bk = next_bank()
                for k in range(8):
                    PE(lambda e, bk=bk, k=k, oc=oc, xnT=xnT: e.matmul(bk[:, :], lhsT=wxq[:, k, oc * 128:(oc + 1) * 128], rhs=xnT[:, k, :], start=(k == 0), stop=(k == 7)),
                       [r_wc, r_xnT], [r_bk])
                ACT(lambda e, bk=bk, oc=oc, qxT=qxT: e.copy(out=qxT[:, oc, :], in_=bk[:, :]), [r_bk], [r_qxT])
                if oc % 2 == 1:
                    yield
            oxT, r_oxT = ox_ring.next()
            for hx in range(4):
                pX, r_pX = pX_ring.next()
                for mt in range(2):
                    bk, _, r_bk = next_bank()
                    for dc in range(2):
                        PE(lambda e, bk=bk, dc=dc, mt=mt, hx=hx, qxT=qxT: e.matmul(bk[:, :], lhsT=KxT[:, hx * 2 + dc, mt * 128:(mt + 1) * 128], rhs=qxT[:, hx * 2 + dc, :],
                                                                              start=(dc == 0), stop=(dc == 1)), [r_kx, r_qxT], [r_bk])
                    ACT(lambda e, bk=bk, mt=mt, pX=pX: e.activation(out=pX[:, mt, :], in_=bk[:, :], func=AF.Exp, scale=X_SCALE), [r_bk], [r_pX])
                yield
                bk, _, r_bk = next_bank()
                for mt in range(2):
                    PE(lambda e, bk=bk, mt=mt, pX=pX: e.matmul(bk[:, :], lhsT=onesb[:, :], rhs=pX[:, mt, :], start=(mt == 0), stop=(mt == 1)), [r_pX, r_const], [r_bk])
                rc, r_rc = rc_ring.next()
                VEC(lambda e, bk=bk, rc=rc: e.reciprocal(out=rc, in_=bk[:, :]), [r_bk], [r_rc])
                yield
                for dc in range(2):
                    bk, _, r_bk = next_bank()
                    for mt in range(2):
                        PE(lambda e, bk=bk, mt=mt, dc=dc, hx=hx, pX=pX: e.matmul(bk[:, :], lhsT=Vx[:, mt, (hx * 2 + dc) * 128:(hx * 2 + dc + 1) * 128], rhs=pX[:, mt, :],
                                                                             start=(mt == 0), stop=(mt == 1)), [r_kx, r_pX], [r_bk])
                    VEC(lambda e, bk=bk, dc=dc, hx=hx, rc=rc, oxT=oxT: e.tensor_tensor(out=oxT[:, hx * 2 + dc, :], in0=bk[:, :], in1=rc, op=ALU.mult), [r_bk, r_rc], [r_oxT])
                yield
            for sub in range(4):
                ti = b * 4 + sub
                ts_ = slice(sub * 128, (sub + 1) * 128)
                h1t, r_h1t = h1s[sub]
                for hf in range(2):
                    bk, _, r_bk = next_bank()
                    cs_ = slice(hf * 512, (hf + 1) * 512)
                    for k in range(8):
                        PE(lambda e, bk=bk, k=k, oxT=oxT, ts_=ts_, cs_=cs_: e.matmul(bk[:, :], lhsT=oxT[:, k, ts_], rhs=wxo[:, k, cs_], start=(k == 0), stop=(k == 7)),
                           [r_oxT, r_wc], [r_bk])
                    VEC(lambda e, bk=bk, h1t=h1t, cs_=cs_: e.tensor_tensor(out=h1t[:, cs_], in0=bk[:, :], in1=h1t[:, cs_], op=ALU.add), [r_bk, r_h1t], [r_h1t])
                    yield
                DMA(h1_d[T0 + sub * 128:T0 + (sub + 1) * 128, :], h1t[:], [r_h1t], [r_h1d[ti]])

        for b in range(NB + 1):
            gl = []
            if b >= 1:
                gl.append(stageC1b(b - 1))
            if b < NB:
                gl.append(stageC1a(b))
            run_interleaved(gl)
        if stop_after == "C1":
            return True

        P.barrier()
        areset()
        wg, r_wf = aalloc([128, 8, DFF], BF16, "wg")
        wu, _ = aalloc([128, 8, DFF], BF16)
        wd, _ = aalloc([128, NFF, 1024], BF16)
        for k in range(8):
            DMAC(wg[:, k, :], w_gate_d[l, k * 128:(k + 1) * 128, :], [], [r_wf])
            DMAC(wu[:, k, :], w_up_d[l, k * 128:(k + 1) * 128, :], [], [r_wf])
        for k in range(0, NFF, 2):
            DMAC(wd[:, k:k + 2, :], w_down_d[l, k * 128:(k + 2) * 128, :].rearrange("(k p) n -> p k n", p=128), [], [r_wf])
        actT = frA[0].bitcast(BF16) if hasattr(frA[0], "bitcast") else None
        actT = actT[:, 0:NFF * 512].rearrange("p (f t) -> p f t", t=512)
        r_actT = frA[1]
        sgs = [(frB[0][:, i * 512:(i + 1) * 512], Res()) for i in range(4)]
        sg_i = [0]
        last = (l == L - 1)
        ctxC = {}
        nt_ring = Ring(P, f"ntl{l}", 1, [128, 4], F32) if False else None

        def stageC2a(b):
            T0 = b * 512
            xnT, r_xnT = xnT_ring.next()
            ctxC[b] = (xnT, r_xnT)
            for sub in range(4):
                ti = b * 4 + sub
                ht, r_ht = ht_ring.next()
                DMA(ht[:], h1_d[T0 + sub * 128:T0 + (sub + 1) * 128, :], [r_h1d[ti]], [r_ht])
                norm_transpose(ht, r_ht, col(l, C_GFFN, 8), xnT, r_xnT, sub)
                yield

        def stageC2b(b):
            T0 = b * 512
            xnT, r_xnT = ctxC.pop(b)
            for fc in range(NFF):
                if fc % 3 == 0:
                    yield
                bkg, _, r_bkg = next_bank()
                bku, _, r_bku = next_bank()
                for k in range(8):
                    PE(lambda e, bkg=bkg, k=k, fc=fc, xnT=xnT: e.matmul(bkg[:, :], lhsT=wg[:, k, fc * 128:(fc + 1) * 128], rhs=xnT[:, k, :], start=(k == 0), stop=(k == 7)),
                       [r_wf, r_xnT], [r_bkg])
                for k in range(8):
                    PE(lambda e, bku=bku, k=k, fc=fc, xnT=xnT: e.matmul(bku[:, :], lhsT=wu[:, k, fc * 128:(fc + 1) * 128], rhs=xnT[:, k, :], start=(k == 0), stop=(k == 7)),
                       [r_wf, r_xnT], [r_bku])
                sgt, r_sgt = sgs[sg_i[0] % 4]
                sg_i[0] += 1
                ACT(lambda e, bkg=bkg, sgt=sgt: e.activation(out=sgt, in_=bkg[:, :], func=AF.Silu), [r_bkg], [r_sgt])
                VEC(lambda e, bku=bku, sgt=sgt, fc=fc: e.tensor_tensor(out=actT[:, fc, :], in0=bku[:, :], in1=sgt, op=ALU.mult), [r_bku, r_sgt], [r_actT])
            for sub in range(4):
                ti = b * 4 + sub
                ts_ = slice(sub * 128, (sub + 1) * 128)
                ht, r_ht = ht_ring.next()
                DMA(ht[:], h1_d[T0 + sub * 128:T0 + (sub + 1) * 128, :], [r_h1d[ti]], [r_ht])
                for hf in range(2):
                    yield
                    bk, _, r_bk = next_bank()
                    cs_ = slice(hf * 512, (hf + 1) * 512)
                    for fc in range(NFF):
                        PE(lambda e, bk=bk, fc=fc, ts_=ts_, cs_=cs_: e.matmul(bk[:, :], lhsT=actT[:, fc, ts_], rhs=wd[:, fc, cs_], start=(fc == 0), stop=(fc == NFF - 1)),
                           [r_actT, r_wf], [r_bk])
                    VEC(lambda e, bk=bk, ht=ht, cs_=cs_: e.tensor_tensor(out=ht[:, cs_], in0=bk[:, :], in1=ht[:, cs_], op=ALU.add), [r_bk, r_ht], [r_ht])
                if not last:
                    DMA(h_d[T0 + sub * 128:T0 + (sub + 1) * 128, :], ht[:], [r_ht], [r_hd[ti]])
                else:
                    st, r_st = st_ring.next()
                    rms_stats(ht[:], r_ht, D, st, r_st, 0)
                    VEC(lambda e, ht=ht, st=st: e.scalar_tensor_tensor(out=ht[:], in0=ht[:], scalar=st[:, 0:1], in1=fing[:], op0=ALU.mult, op1=ALU.mult),
                        [r_ht, r_st, r_const], [r_ht])
                    final_ops.append(DMA(out_d[T0 + sub * 128:T0 + (sub + 1) * 128, :], ht[:], [r_ht], []))


        def run_il(gens):
            gens = list(gens)
            while gens:
                for g_ in list(gens):
                    try:
                        next(g_)
                    except StopIteration:
                        gens.remove(g_)

        for b in range(NB + 1):
            gl = []
            if b >= 1:
                gl.append(stageC2b(b - 1))
            if b < NB:
                gl.append(stageC2a(b))
            run_il(gl)
        return False

    for l_ in range(n_layers):
        if emit_layer(l_):
            break

    if debug:
        def dump(name, src, shape, dt, r):
            final_ops.append(DMA(dbg_out(name, shape, dt), src, [r], []))
        P.barrier()
        dump("qT", qT_d[:, :, 3584:4096], [8, 96, 512], BF16, r_qd)
        dump("kT", kT_d[:, :, 3584:4096], [8, 96, 512], BF16, r_kd)
        dump("V", V_d[:, :, 28:32, :], [8, 128, 4, 65], BF16, r_vd)
        dump("uT", uT_d[:, :, 3584:4096], [128, 4, 512], BF16, r_ud)
        dump("aT0", aT_d[0], [64, 8, 512], F32, r_ad)
        dump("aT7", aT_d[7], [64, 8, 512], F32, r_ad)
        dump("yT", yT_d[:, :, 3584:4096], [128, 4, 512], F32, r_yd)
        dump("yT0", yT_d[:, :, 0:512], [128, 4, 512], F32, r_yd)
        dump("sT7", sT_d[7], [128, 4, 512], BF16, r_sd)
        dump("sT0", sT_d[0], [128, 4, 512], BF16, r_sd)
        dump("h1", h1_d[3968:4096, :], [128, D], F32, r_h1d[31])
        dump("h", h_d[3968:4096, :], [128, D], F32, r_hd[31])
    P.barrier()
    return P.build(final_waits=final_ops), dbg


def prep_inputs(inp):
    f = np.float32
    g = lambda k: np.asarray(inp[k])
    cols = np.zeros((128, L, NCOL), f)
    for l in range(L):
        cols[:, l, C_GMIX:C_GMIX + 8] = g("norm_mix_g")[l].reshape(8, 128).T
        cols[:, l, C_GX:C_GX + 8] = g("norm_x_g")[l].reshape(8, 128).T
        cols[:, l, C_GFFN:C_GFFN + 8] = g("norm_ffn_g")[l].reshape(8, 128).T
        cols[:, l, C_GMEM:C_GMEM + 8] = g("mem_norm_g")[l].reshape(8, 128).T
        cols[:, l, C_GQ:C_GQ + 2] = g("q_norm_g")[l].reshape(2, 128).T
        cols[:, l, C_GKV] = g("kv_norm_g")[l]
        cols[:, l, C_GS:C_GS + 4] = g("ssm_out_g")[l].reshape(4, 128).T
        cols[:, l, C_D:C_D + 4] = g("ssm_d")[l].reshape(4, 128).T
        cols[:, l, C_BGLU:C_BGLU + 4] = g("ssm_b_glu")[l].reshape(4, 128).T
        cols[0:64, l, C_GA:C_GA + 8] = g("attn_out_g")[l].reshape(8, 64).T
        cols[:, l, C_GA2:C_GA2 + 4] = g("attn_out_g")[l].reshape(4, 2, 64).transpose(1, 2, 0).reshape(128, 4)
    consts = np.zeros((128, 2), f)
    freqs = (np.float32(10000.0) ** (-np.arange(0, 32, 2, dtype=np.float32) / np.float32(32))).astype(f)
    consts[64:80, 0] = freqs
    consts[80:96, 0] = freqs
    consts[64:80, 1] = -SIN_SCALE
    consts[80:96, 1] = SIN_SCALE
    idx = np.arange(128) // 16
    mask16 = (idx[:, None] == idx[None, :]).astype(f)
    w_in = g("w_in")
    w_kr = np.zeros((L, D, 192), f)
    w_kr[:, :, 64:96] = w_in[:, :, 384:416]
    w_kr[:, :, 160:176] = w_in[:, :, 400:416]
    w_kr[:, :, 176:192] = w_in[:, :, 384:400]
    w_uq = g("w_uq")
    w_uqs = np.zeros_like(w_uq)
    for h in range(8):
        w_uqs[:, :, h * 96 + 64:h * 96 + 80] = w_uq[:, :, h * 96 + 80:h * 96 + 96]
        w_uqs[:, :, h * 96 + 80:h * 96 + 96] = w_uq[:, :, h * 96 + 64:h * 96 + 80]
    w_ukv = g("w_ukv").reshape(L, 128, 8, 2, 64).transpose(0, 1, 3, 2, 4).reshape(L, 128, 1024)

    def pair_layout(a):
        sh = a.shape
        a = a.reshape(L, 16, 2, 64, *sh[3:])
        a = np.moveaxis(a, 1, 3)
        return a.reshape(L, 128, 16, *sh[3:])

    lam_re = pair_layout(g("ssm_lambda_re"))
    lam_im = pair_layout(g("ssm_lambda_im"))
    logdt = pair_layout(np.repeat(g("ssm_log_dt")[:, :, None], 64, axis=2))
    s5v = np.stack([lam_re, lam_im, logdt], axis=2)
    b_re = pair_layout(g("ssm_b_re"))
    b_im = pair_layout(g("ssm_b_im"))
    s5b = np.stack([b_re, b_im], axis=2)
    c_re = pair_layout(np.swapaxes(g("ssm_c_re"), 2, 3))
    c_im = pair_layout(np.swapaxes(g("ssm_c_im"), 2, 3))
    s5c = np.stack([c_re, c_im], axis=2)
    common = dict(
        cols=cols, consts=consts, mask16=mask16, fing=g("final_norm_g").reshape(1, D).astype(f),
        w_in=w_in, w_kr=w_kr, w_uq=w_uq, w_uqs=w_uqs, w_ukv=np.ascontiguousarray(w_ukv),
        s5v=np.ascontiguousarray(s5v), s5b=np.ascontiguousarray(s5b), s5c=np.ascontiguousarray(s5c),
        w_glu=g("ssm_w_glu"), w_out=g("w_out"), w_xq=g("w_xq"), w_xkv=g("w_xkv"), w_xo=g("w_xo"),
        w_gate=g("w_gate"), w_up=g("w_up"), w_down=g("w_down"),
    )
    common = {k: np.ascontiguousarray(v, dtype=f) for k, v in common.items()}
    x = g("x")
    mem = g("mem")
    pos = g("positions").astype(np.int32)
    per_core = []
    for c in range(x.shape[0]):
        d = dict(common)
        d["x"] = np.ascontiguousarray(x[c], dtype=f)
        d["mem"] = np.ascontiguousarray(mem[c], dtype=f)
        d["pos"] = np.ascontiguousarray(pos[c].reshape(1, S))
        per_core.append(d)
    return per_core


def kernel(**inputs):
    per_core = prep_inputs(inputs)
    nc, _ = build_program()
    res = run_bass_kernel_spmd(nc, per_core, core_ids=list(range(8)))
    return np.stack([np.asarray(r["out"], dtype=np.float32) for r in res.results], axis=0)
```

```python
import math
import numpy as np
import concourse.bass as bass
import concourse.mybir as mybir
from concourse.bass_utils import run_bass_kernel_spmd

F32 = mybir.dt.float32
BF16 = mybir.dt.bfloat16
I32 = mybir.dt.int32
AF = mybir.ActivationFunctionType
ALU = mybir.AluOpType

ENGS = ["tensor", "vector", "scalar", "gpsimd", "sync"]

L = 2
S = 4096
D = 1024
NB = 8
EPS = 1e-6
DFF = 2816
NFF = 22
TWO_PI = 2.0 * math.pi
SIN_SCALE = 6.2831845
MAGIC = 12582912.0
ATT_SCALE = 96.0 ** -0.5
X_SCALE = 256.0 ** -0.5
NCOL = 59
C_GMIX, C_GX, C_GFFN, C_GMEM, C_GQ, C_GKV, C_GS, C_D, C_BGLU, C_GA, C_GA2 = 0, 8, 16, 24, 32, 34, 35, 39, 43, 47, 55


class Res:
    __slots__ = ("name", "last_w", "readers")

    def __init__(self, name=""):
        self.name = name
        self.last_w = None
        self.readers = []


class Op:
    __slots__ = ("eng", "fn", "waits", "dma", "sem", "val")


class Prog:
    def __init__(self, n_dma_sems=24):
        self.nc = bass.Bass("TRN2", target_bir_lowering=False)
        self.ops = {e: [] for e in ENGS}
        self.n_dma_sems = n_dma_sems
        self.wm = {e: {} for e in ENGS}
        self.dma_rr = {e: 0 for e in ENGS}
        self.dma_cnt = {}
        self.dma_last = {}
        self.eng_cnt = {}
        self.pending = {e: {} for e in ENGS}
        self._ctx = []

    def sbuf(self, name, shape, dtype):
        g = self.nc.sbuf_tensor("sb_" + name, list(shape), dtype)
        h = g.__enter__()
        self._ctx.append(g)
        return h

    def psum(self, name, shape, dtype):
        g = self.nc.psum_tensor("ps_" + name, list(shape), dtype)
        h = g.__enter__()
        self._ctx.append(g)
        return h

    def _need(self, op, dep):
        if dep is None:
            return
        if dep.eng == "tensor" and op.eng == "tensor" and not dep.dma and not op.dma:
            return
        if dep.val > op.waits.get(dep.sem, 0):
            op.waits[dep.sem] = dep.val

    def barrier(self):
        cur = {}
        for e, c in self.eng_cnt.items():
            cur[("eng", e)] = c
        for k, c in self.dma_cnt.items():
            cur[k] = 16 * c
        for e in ENGS:
            pe = self.pending[e]
            for k, v in cur.items():
                if v > pe.get(k, 0):
                    pe[k] = v

    def op(self, eng, fn, reads=(), writes=(), dma=False):
        o = Op()
        o.eng = eng
        o.fn = fn
        o.dma = dma
        o.waits = {}
        if self.pending[eng]:
            o.waits.update(self.pending[eng])
            self.pending[eng] = {}
        if dma:
            slot = self.dma_rr[eng] % self.n_dma_sems
            self.dma_rr[eng] += 1
            key = ("dma", eng, slot)
            prev = self.dma_last.get(key)
            cnt = self.dma_cnt.get(key, 0) + 1
            self.dma_cnt[key] = cnt
            o.sem = key
            o.val = 16 * cnt
            if prev is not None:
                self._need(o, prev)
            self.dma_last[key] = o
        else:
            o.sem = ("eng", eng)
            self.eng_cnt[eng] = self.eng_cnt.get(eng, 0) + 1
            o.val = self.eng_cnt[eng]
        for r in reads:
            self._need(o, r.last_w)
        for w in writes:
            self._need(o, w.last_w)
            for rd in w.readers:
                self._need(o, rd)
        for r in reads:
            r.readers.append(o)
        for w in writes:
            w.last_w = o
            w.readers = []
        wm = self.wm[eng]
        for k in list(o.waits):
            if wm.get(k, 0) >= o.waits[k]:
                del o.waits[k]
            else:
                wm[k] = o.waits[k]
        self.ops[eng].append(o)
        return o

    def build(self, final_waits=()):
        nc = self.nc
        sems = {}
        for e in ENGS:
            for o in self.ops[e]:
                if o.sem not in sems:
                    g = nc.semaphore("s_" + "_".join(str(x) for x in o.sem))
                    sems[o.sem] = g.__enter__()
                    self._ctx.append(g)
        fin = {}
        for o in final_waits:
            fin[o.sem] = max(fin.get(o.sem, 0), o.val)
        with nc.Block() as block:
            def make(e):
                def body(engobj):
                    for o in self.ops[e]:
                        for k, v in o.waits.items():
                            engobj.wait_ge(sems[k], v)
                        ins = o.fn(engobj)
                        ins.then_inc(sems[o.sem], 16 if o.dma else 1)
                    if e == "sync":
                        for k, v in fin.items():
                            engobj.wait_ge(sems[k], v)
                return body
            for e in ENGS:
                if self.ops[e] or e == "sync":
                    getattr(block, e)(make(e))
        return nc


class Ring:
    def __init__(self, P, name, n, shape, dtype):
        self.bufs = [(P.sbuf(f"{name}{i}", shape, dtype), Res(f"{name}{i}")) for i in range(n)]
        self.i = 0

    def next(self):
        b = self.bufs[self.i % len(self.bufs)]
        self.i += 1
        return b


def build_program(debug=False, n_layers=L, stop_after=None):
    P = Prog()
    nc = P.nc

    def din(name, shape, dt=F32):
        return nc.dram_tensor(name, list(shape), dt, kind="ExternalInput").ap()

    def dscr(name, shape, dt):
        return nc.dram_tensor(name, list(shape), dt, kind="Internal").ap()

    x_d = din("x", [S, D])
    mem_d = din("mem", [256, D])
    pos_d = din("pos", [1, S], I32)
    cols_d = din("cols", [128, L, NCOL])
    consts_d = din("consts", [128, 2])
    mask16_d = din("mask16", [128, 128])
    fing_d = din("fing", [1, D])
    w_in_d = din("w_in", [L, D, 928])
    w_kr_d = din("w_kr", [L, D, 192])
    w_uq_d = din("w_uq", [L, 256, 768])
    w_uqs_d = din("w_uqs", [L, 256, 768])
    w_ukv_d = din("w_ukv", [L, 128, 1024])
    s5v_d = din("s5v", [L, 128, 3, 16])
    s5b_d = din("s5b", [L, 128, 2, 16, 16])
    s5c_d = din("s5c", [L, 128, 2, 16, 16])
    w_glu_d = din("w_glu", [L, 512, 512])
    w_out_d = din("w_out", [L, D, D])
    w_xq_d = din("w_xq", [L, D, D])
    w_xkv_d = din("w_xkv", [L, D, 2 * D])
    w_xo_d = din("w_xo", [L, D, D])
    w_gate_d = din("w_gate", [L, D, DFF])
    w_up_d = din("w_up", [L, D, DFF])
    w_down_d = din("w_down", [L, DFF, D])
    out_d = nc.dram_tensor("out", [S, D], F32, kind="ExternalOutput").ap()

    h_d = dscr("h_scr", [S, D], F32)
    h1_d = dscr("h1_scr", [S, D], F32)
    qT_d = dscr("qT_scr", [8, 96, S], BF16)
    kT_d = dscr("kT_scr", [8, 96, S], BF16)
    V_d = dscr("V_scr", [8, 128, 32, 65], BF16)
    uT_d = dscr("uT_scr", [128, 4, S], BF16)
    yT_d = dscr("yT_scr", [128, 4, S], F32)
    aT_d = dscr("aT_scr", [NB, 64, 8, 512], F32)
    sT_d = dscr("sT_scr", [NB, 128, 4, 512], BF16)
    cos_d = dscr("cos_scr", [32, S], F32)
    sin_d = dscr("sin_scr", [32, S], F32)
    r_hd = [Res() for _ in range(32)]
    r_h1d = [Res() for _ in range(32)]
    r_qd, r_kd, r_vd, r_ud, r_yd = Res(), Res(), Res(), Res(), Res()
    r_ad, r_sd, r_csd = Res(), Res(), Res()

    dbg = {}

    def dbg_out(name, shape, dt=F32):
        t = nc.dram_tensor("dbg_" + name, list(shape), dt, kind="ExternalOutput").ap()
        dbg[name] = t
        return t

    final_ops = []

    def VEC(fn, reads, writes):
        return P.op("vector", fn, reads, writes)

    def ACT(fn, reads, writes):
        return P.op("scalar", fn, reads, writes)

    def POOL(fn, reads, writes):
        return P.op("gpsimd", fn, reads, writes)

    def PE(fn, reads, writes):
        return P.op("tensor", fn, reads, writes)

    def DMA(out, in_, reads, writes, q="sync"):
        return P.op(q, lambda e: e.dma_start(out=out, in_=in_), reads, writes, dma=True)

    def DMAC(out, in_, reads, writes):
        return P.op("gpsimd", lambda e: e.dma_start(out=out, in_=in_), reads, writes, dma=True)

    banks = []
    for i in range(8):
        t = P.psum(f"bank{i}", [128, 512], F32)
        banks.append((t, t.bitcast(BF16), Res(f"bank{i}")))
    bank_rr = [0]

    def next_bank(pool=(0, 1, 2, 3, 4, 5, 6, 7)):
        i = pool[bank_rr[0] % len(pool)]
        bank_rr[0] += 1
        return banks[i]

    identf = P.sbuf("identf", [128, 128], F32)
    ident = P.sbuf("ident", [128, 128], BF16)
    onesf = P.sbuf("onesf", [128, 128], F32)
    sel65 = P.sbuf("sel65", [65, 64], F32)
    onesb = P.sbuf("onesb", [128, 128], BF16)
    fing = P.sbuf("fing", [128, D], F32)
    mask16 = P.sbuf("mask16", [128, 128], F32)
    cols = P.sbuf("cols", [128, L, NCOL], F32)
    consts = P.sbuf("consts", [128, 2], F32)
    r_const = Res("const")
    POOL(lambda e: e.memset(identf[:], 1.0), [], [r_const])
    POOL(lambda e: e.affine_select(out=identf[:], in_=identf[:], pattern=[[-1, 128]], compare_op=ALU.is_equal,
                                   fill=0.0, base=0, channel_multiplier=1), [r_const], [r_const])
    VEC(lambda e: e.tensor_copy(out=ident[:], in_=identf[:]), [r_const], [r_const])
    VEC(lambda e: e.memset(onesf[:], 1.0), [], [r_const])
    VEC(lambda e: e.memset(onesb[:], 1.0), [], [r_const])
    DMA(fing[:], fing_d[0:1, :].to_broadcast([128, D]), [], [r_const])
    VEC(lambda e: e.memset(sel65[:], 0.0), [], [r_const])
    VEC(lambda e: e.memset(sel65[64:65, :], 1.0), [r_const], [r_const])
    DMA(mask16[:], mask16_d[:, :], [], [r_const])
    DMA(cols[:], cols_d[:, :, :], [], [r_const])
    DMA(consts[:], consts_d[:, :], [], [r_const])

    def col(l, c, n=1, p0=0, p1=128):
        return cols[p0:p1, l, c:c + n]

    ARENA_F32 = 33792
    arena_f = P.sbuf("arena", [128, ARENA_F32], F32)
    arena_b = arena_f.bitcast(BF16)
    arena_i = arena_f.bitcast(I32)
    arena_off = [0]

    def areset():
        arena_off[0] = 0

    def aalloc(shape, dt, name=""):
        n = 1
        for s_ in shape[1:]:
            n *= s_
        esz = 2 if dt == BF16 else 4
        off = (arena_off[0] + 3) // 4 * 4
        arena_off[0] = off + n * esz
        assert arena_off[0] <= ARENA_F32 * 4, (name, arena_off[0])
        base = {BF16: arena_b, F32: arena_f, I32: arena_i}[dt]
        e0 = off // esz
        ap = base[0:shape[0], e0:e0 + n]
        if len(shape) == 3:
            ap = ap.rearrange("p (a b) -> p a b", b=shape[2])
        elif len(shape) == 4:
            ap = ap.rearrange("p (a b c) -> p a b c", b=shape[2], c=shape[3])
        elif len(shape) == 5:
            ap = ap.rearrange("p (a b c d) -> p a b c d", b=shape[2], c=shape[3], d=shape[4])
        return ap, Res(name)

    class ARing:
        def __init__(self, n, shape, dt, name=""):
            self.bufs = [aalloc(shape, dt, f"{name}{i}") for i in range(n)]
            self.i = 0

        def next(self):
            b = self.bufs[self.i % len(self.bufs)]
            self.i += 1
            return b

    ht_ring = Ring(P, "ht", 2, [128, D], F32)
    junk_ring = Ring(P, "junk", 1, [128, D], BF16)
    xn_ring = Ring(P, "xn", 2, [128, D], BF16)
    xnT_ring = Ring(P, "xnT", 2, [128, 8, 512], BF16)
    st_ring = Ring(P, "st", 8, [128, 4], F32)
    pT_ring = Ring(P, "pT", 4, [128, 512], BF16)
    frA = (P.sbuf("frA", [128, 5632], F32), Res("frA"))
    frB = (P.sbuf("frB", [128, 2048], F32), Res("frB"))

    def rms_stats(src_ap, r_src, nfeat, st, r_st, c0):
        junk, r_junk = junk_ring.next()
        n = src_ap.shape[-1]
        ACT(lambda e: e.activation(out=junk[:, 0:n], in_=src_ap, func=AF.Square, accum_out=st[:, c0:c0 + 1]), [r_src], [r_junk, r_st])
        ACT(lambda e: e.activation(out=st[:, c0:c0 + 1], in_=st[:, c0:c0 + 1], func=AF.Sqrt, scale=1.0 / nfeat, bias=EPS), [r_st], [r_st])
        VEC(lambda e: e.reciprocal(out=st[:, c0:c0 + 1], in_=st[:, c0:c0 + 1]), [r_st], [r_st])

    def norm_transpose(ht, r_ht, gcol, xnT, r_xnT, sub):
        st, r_st = st_ring.next()
        rms_stats(ht[:], r_ht, D, st, r_st, 0)
        xn, r_xn = xn_ring.next()
        VEC(lambda e: e.tensor_scalar(out=xn[:], in0=ht[:], scalar1=st[:, 0:1], scalar2=None, op0=ALU.mult), [r_ht, r_st], [r_xn])
        bk, bkb, r_bk = next_bank()
        bkv = bkb[:, :].rearrange("p (a b) -> p a b", b=128)
        for k in range(8):
            PE(lambda e, k=k: e.transpose(out=bkv[:, k, :], in_=xn[:, k * 128:(k + 1) * 128], identity=ident[:]), [r_xn, r_const], [r_bk])
        VEC(lambda e: e.tensor_tensor(out=xnT[:, :, sub * 128:(sub + 1) * 128], in0=bkv, in1=gcol.to_broadcast([128, 8, 128]), op=ALU.mult),
            [r_bk, r_const], [r_xnT])

    RS = slice(64, 96)

    areset()
    posi, r_ra = aalloc([96, S], I32, "posi")
    tmpa, _ = aalloc([96, S], F32, "tmpa")
    tmpb, _ = aalloc([96, S], F32, "tmpb")
    cs_t, r_cs = aalloc([96, S], F32, "cs")
    sn_t, _ = aalloc([96, S], F32, "sn")
    DMA(posi[RS, :], pos_d[0:1, :].to_broadcast([32, S]), [], [r_ra])
    VEC(lambda e: e.tensor_copy(out=tmpa[RS, :], in_=posi[RS, :]), [r_ra], [r_ra])
    VEC(lambda e: e.tensor_scalar(out=tmpa[RS, :], in0=tmpa[RS, :], scalar1=consts[RS, 0:1], scalar2=1.0 / TWO_PI,
                                  op0=ALU.mult, op1=ALU.mult), [r_ra, r_const], [r_ra])
    VEC(lambda e: e.tensor_scalar(out=tmpb[RS, :], in0=tmpa[RS, :], scalar1=MAGIC, scalar2=MAGIC, op0=ALU.add, op1=ALU.subtract), [r_ra], [r_ra])
    VEC(lambda e: e.tensor_tensor(out=tmpb[RS, :], in0=tmpa[RS, :], in1=tmpb[RS, :], op=ALU.subtract), [r_ra], [r_ra])
    ACT(lambda e: e.activation(out=sn_t[RS, :], in_=tmpb[RS, :], func=AF.Sin, scale=consts[RS, 1:2]), [r_ra, r_const], [r_cs])
    VEC(lambda e: e.tensor_scalar(out=tmpa[RS, :], in0=tmpa[RS, :], scalar1=0.25, scalar2=None, op0=ALU.add), [r_ra, r_cs], [r_ra])
    VEC(lambda e: e.tensor_scalar(out=tmpb[RS, :], in0=tmpa[RS, :], scalar1=MAGIC, scalar2=MAGIC, op0=ALU.add, op1=ALU.subtract), [r_ra], [r_ra])
    VEC(lambda e: e.tensor_tensor(out=tmpb[RS, :], in0=tmpa[RS, :], in1=tmpb[RS, :], op=ALU.subtract), [r_ra], [r_ra])
    ACT(lambda e: e.activation(out=cs_t[RS, :], in_=tmpb[RS, :], func=AF.Sin, scale=SIN_SCALE), [r_ra], [r_cs])
    DMA(cos_d[:, :], cs_t[RS, :], [r_cs], [r_csd])
    DMA(sin_d[:, :], sn_t[RS, :], [r_cs], [r_csd])

    def emit_layer(l):
        src_d = x_d if l == 0 else h_d
        r_src = [Res() for _ in range(32)] if l == 0 else r_hd
        P.barrier()
        areset()
        w_in_sb, r_w = aalloc([128, 8, 928], BF16, "w_in")
        w_kr_sb, _ = aalloc([128, 8, 192], BF16)
        w_uq_sb, _ = aalloc([128, 2, 768], BF16)
        w_uqs_sb, _ = aalloc([128, 2, 768], BF16)
        w_ukv_sb, _ = aalloc([128, 1024], BF16)
        DMAC(w_in_sb, w_in_d[l].rearrange("(k p) n -> p k n", p=128), [], [r_w])
        DMAC(w_kr_sb, w_kr_d[l].rearrange("(k p) n -> p k n", p=128), [], [r_w])
        DMAC(w_uq_sb, w_uq_d[l].rearrange("(k p) n -> p k n", p=128), [], [r_w])
        DMAC(w_uqs_sb, w_uqs_d[l].rearrange("(k p) n -> p k n", p=128), [], [r_w])
        DMAC(w_ukv_sb, w_ukv_d[l], [], [r_w])
        qs, r_qs = aalloc([96, 8, 512], F32, "qs")
        qsw, r_qsw = aalloc([96, 8, 512], F32, "qsw")
        qb_ring = ARing(2, [96, 8, 512], BF16, "qb")
        kb_ring = ARing(2, [96, 8, 512], BF16, "kb")
        vb_ring = ARing(2, [128, 8, 4, 65], BF16, "vb")
        ub_ring = ARing(2, [128, 4, 512], BF16, "ub")
        cT_ring = ARing(2, [128, 3, 512], BF16, "cT")
        cs_ring = ARing(2, [96, 2, 512], F32, "csb")
        ta, r_ta = aalloc([96, 1024], F32, "ta")
        for vb, r_vb in vb_ring.bufs:
            VEC(lambda e, vb=vb: e.memset(vb[:, :, :, 64:65], 1.0), [], [r_vb])

        ctxA = {}

        htx = [(frB[0][:, 0:1024], Res("htx0")), (frB[0][:, 1024:2048], Res("htx1"))]
        frA_b = frA[0].bitcast(BF16)
        xnx = [(frA_b[:, 0:1024], Res("xnx0")), (frA_b[:, 1024:2048], Res("xnx1"))]
        jkx = [(frA_b[:, 2048 + i * 1024:2048 + (i + 1) * 1024], Res(f"jkx{i}")) for i in range(4)]

        def stageA1(b):
            T0 = b * 512
            xnT, r_xnT = xnT_ring.next()
            cT, r_cT = cT_ring.next()
            vb, r_vb = vb_ring.next()
            csb, r_csb = cs_ring.next()
            ctxA[b] = (xnT, r_xnT, cT, r_cT, csb, r_csb)
            DMA(csb[RS, 0, :], cos_d[:, T0:T0 + 512], [r_csd], [r_csb])
            DMA(csb[RS, 1, :], sin_d[:, T0:T0 + 512], [r_csd], [r_csb])
            hts = [ht_ring.next(), ht_ring.next(), htx[0], htx[1]]
            xns = [xn_ring.next(), xn_ring.next(), xnx[0], xnx[1]]
            sts = [st_ring.next() for _ in range(4)]
            gcol = col(l, C_GMIX, 8)
            gq = col(l, C_GQ, 3)
            for sub in range(4):
                ht, r_ht = hts[sub]
                DMA(ht[:, :], src_d[T0 + sub * 128:T0 + (sub + 1) * 128, :], [r_src[b * 4 + sub]], [r_ht])
            yield
            for sub in range(4):
                (ht, r_ht), (st, r_st), (jk, r_jk) = hts[sub], sts[sub], jkx[sub]
                ACT(lambda e, ht=ht, st=st, jk=jk: e.activation(out=jk, in_=ht[:, :], func=AF.Square, accum_out=st[:, 0:1]), [r_ht], [r_jk, r_st])
            for sub in range(4):
                st, r_st = sts[sub]
                ACT(lambda e, st=st: e.activation(out=st[:, 0:1], in_=st[:, 0:1], func=AF.Sqrt, scale=1.0 / D, bias=EPS), [r_st], [r_st])
            yield
            for sub in range(4):
                st, r_st = sts[sub]
                VEC(lambda e, st=st: e.reciprocal(out=st[:, 0:1], in_=st[:, 0:1]), [r_st], [r_st])
            for sub in range(4):
                (ht, r_ht), (st, r_st), (xn, r_xn) = hts[sub], sts[sub], xns[sub]
                VEC(lambda e, ht=ht, st=st, xn=xn: e.tensor_scalar(out=xn[:, :], in0=ht[:, :], scalar1=st[:, 0:1], scalar2=None, op0=ALU.mult),
                    [r_ht, r_st], [r_xn])
            yield
            bks = []
            for sub in range(4):
                xn, r_xn = xns[sub]
                bk, bkb, r_bk = next_bank()
                bkv = bkb[:, :].rearrange("p (a b) -> p a b", b=128)
                for k in range(8):
                    PE(lambda e, k=k, bkv=bkv, xn=xn: e.transpose(out=bkv[:, k, :], in_=xn[:, k * 128:(k + 1) * 128], identity=ident[:]), [r_xn, r_const], [r_bk])
                bks.append((bkv, r_bk))
                if sub % 2 == 1:
                    yield
            for sub in range(4):
                bkv, r_bk = bks[sub]
                VEC(lambda e, bkv=bkv, sub=sub: e.tensor_tensor(out=xnT[:, :, sub * 128:(sub + 1) * 128], in0=bkv, in1=gcol.to_broadcast([128, 8, 128]), op=ALU.mult),
                    [r_bk, r_const], [r_xnT])
            yield
            cbk = []
            for sub in range(4):
                bk, bkb, r_bk = next_bank()
                for k in range(8):
                    PE(lambda e, k=k, bk=bk, sub=sub: e.matmul(bk[:, 0:384], lhsT=xnT[:, k, sub * 128:(sub + 1) * 128], rhs=w_in_sb[:, k, 0:384],
                                                              start=(k == 0), stop=(k == 7)), [r_xnT, r_w], [r_bk])
                cbk.append((bk, r_bk))
                if sub % 2 == 1:
                    yield
            for sub in range(4):
                (bk, r_bk), (st, r_st), (jk, r_jk) = cbk[sub], sts[sub], jkx[sub]
                ACT(lambda e, bk=bk, st=st, jk=jk: e.activation(out=jk[:, 0:256], in_=bk[:, 0:256], func=AF.Square, accum_out=st[:, 1:2]), [r_bk], [r_jk, r_st])
                ACT(lambda e, bk=bk, st=st, jk=jk: e.activation(out=jk[:, 256:384], in_=bk[:, 256:384], func=AF.Square, accum_out=st[:, 2:3]), [r_bk], [r_jk, r_st])
            yield
            for sub in range(4):
                st, r_st = sts[sub]
                ACT(lambda e, st=st: e.activation(out=st[:, 1:2], in_=st[:, 1:2], func=AF.Sqrt, scale=1.0 / 256, bias=EPS), [r_st], [r_st])
                ACT(lambda e, st=st: e.activation(out=st[:, 2:3], in_=st[:, 2:3], func=AF.Sqrt, scale=1.0 / 128, bias=EPS), [r_st], [r_st])
            for sub in range(4):
                st, r_st = sts[sub]
                VEC(lambda e, st=st: e.reciprocal(out=st[:, 1:3], in_=st[:, 1:3]), [r_st], [r_st])
            yield
            for sub in range(4):
                (bk, r_bk), (st, r_st), (cn, r_cn) = cbk[sub], sts[sub], xns[sub]
                VEC(lambda e, bk=bk, cn=cn, st=st: e.tensor_scalar(out=cn[:, 0:256], in0=bk[:, 0:256], scalar1=st[:, 1:2], scalar2=None, op0=ALU.mult),
                    [r_bk, r_st], [r_cn])
                VEC(lambda e, bk=bk, cn=cn, st=st: e.tensor_scalar(out=cn[:, 256:384], in0=bk[:, 256:384], scalar1=st[:, 2:3], scalar2=None, op0=ALU.mult),
                    [r_bk, r_st], [r_cn])
            yield
            bk2s = []
            for sub in range(4):
                cn, r_cn = xns[sub]
                bk2, bk2b, r_bk2 = next_bank()
                bk2v = bk2b[:, :].rearrange("p (a b) -> p a b", b=128)
                for k in range(3):
                    PE(lambda e, k=k, bk2v=bk2v, cn=cn: e.transpose(out=bk2v[:, k, :], in_=cn[:, k * 128:(k + 1) * 128], identity=ident[:]), [r_cn, r_const], [r_bk2])
                bk2s.append((bk2v, r_bk2))
            yield
            for sub in range(4):
                bk2v, r_bk2 = bk2s[sub]
                VEC(lambda e, bk2v=bk2v, sub=sub: e.tensor_tensor(out=cT[:, :, sub * 128:(sub + 1) * 128], in0=bk2v[:, 0:3, :],
                                                                 in1=gq.to_broadcast([128, 3, 128]), op=ALU.mult), [r_bk2, r_const], [r_cT])
            yield
            for sub in range(4):
                bk3, _, r_bk3 = next_bank()
                PE(lambda e, bk3=bk3, sub=sub: e.matmul(bk3[:, :], lhsT=cT[:, 2, sub * 128:(sub + 1) * 128], rhs=w_ukv_sb[:, 512:1024],
                                                       start=True, stop=True), [r_cT, r_w], [r_bk3])
                ACT(lambda e, bk3=bk3, sub=sub: e.copy(out=vb[:, :, sub, 0:64], in_=bk3[:, :].rearrange("p (h d) -> p h d", d=64)), [r_bk3], [r_vb])
                if sub % 2 == 1:
                    yield
            DMA(V_d[:, :, b * 4:(b + 1) * 4, :].rearrange("h p t d -> p h t d"), vb, [r_vb], [r_vd], q="scalar")
            yield

        def stageA2(b):
            T0 = b * 512
            xnT, r_xnT, cT, r_cT, csb, r_csb = ctxA.pop(b)
            ub, r_ub = ub_ring.next()
            for ct in range(4):
                yield
                bk, _, r_bk = next_bank()
                for k in range(8):
                    PE(lambda e, k=k, bk=bk, xnT=xnT, ct=ct: e.matmul(bk[:, :], lhsT=w_in_sb[:, k, 416 + ct * 128:416 + (ct + 1) * 128],
                                                                      rhs=xnT[:, k, :], start=(k == 0), stop=(k == 7)), [r_xnT, r_w], [r_bk])
                ACT(lambda e, bk=bk, ct=ct, ub=ub: e.copy(out=ub[:, ct, :], in_=bk[:, :]), [r_bk], [r_ub])
            DMA(uT_d[:, :, T0:T0 + 512], ub, [r_ub], [r_ud], q="scalar")
            yield
            bka, _, r_bka = next_bank()
            bkb_, _, r_bkb = next_bank()
            for k in range(8):
                PE(lambda e, k=k, bka=bka, xnT=xnT: e.matmul(bka[0:96, :], lhsT=w_kr_sb[:, k, 0:96], rhs=xnT[:, k, :], start=(k == 0), stop=(k == 7)),
                   [r_xnT, r_w], [r_bka])
            for k in range(8):
                PE(lambda e, k=k, bkb_=bkb_, xnT=xnT: e.matmul(bkb_[0:96, :], lhsT=w_kr_sb[:, k, 96:192], rhs=xnT[:, k, :], start=(k == 0), stop=(k == 7)),
                   [r_xnT, r_w], [r_bkb])
            yield
            VEC(lambda e, bka=bka, csb=csb: e.tensor_tensor(out=ta[RS, 0:512], in0=bka[RS, :], in1=csb[RS, 0, :], op=ALU.mult), [r_bka, r_csb], [r_ta])
            VEC(lambda e, bkb_=bkb_, csb=csb: e.tensor_tensor(out=ta[RS, 512:1024], in0=bkb_[RS, :], in1=csb[RS, 1, :], op=ALU.mult), [r_bkb, r_csb], [r_ta])
            VEC(lambda e: e.tensor_tensor(out=ta[RS, 0:512], in0=ta[RS, 0:512], in1=ta[RS, 512:1024], op=ALU.add), [r_ta], [r_ta])
            kb, r_kb = kb_ring.next()
            POOL(lambda e, kb=kb: e.tensor_copy(out=kb[RS, :, :], in_=ta[RS, 0:512].rearrange("p (o t) -> p o t", o=1).to_broadcast([32, 8, 512])),
                 [r_ta], [r_kb])
            for h in range(8):
                yield
                bk, _, r_bk = next_bank()
                PE(lambda e, bk=bk, h=h, cT=cT: e.matmul(bk[0:64, :], lhsT=w_ukv_sb[:, h * 64:(h + 1) * 64], rhs=cT[:, 2, :], start=True, stop=True),
                   [r_cT, r_w], [r_bk])
                ACT(lambda e, bk=bk, h=h, kb=kb: e.copy(out=kb[0:64, h, :], in_=bk[0:64, :]), [r_bk], [r_kb])
            DMA(kT_d[:, :, T0:T0 + 512].rearrange("h p t -> p h t"), kb, [r_kb], [r_kd], q="scalar")
            for h in range(8):
                yield
                bka, _, r_bka = next_bank()
                bkb_, _, r_bkb = next_bank()
                for k in range(2):
                    PE(lambda e, k=k, bka=bka, h=h, cT=cT: e.matmul(bka[0:96, :], lhsT=w_uq_sb[:, k, h * 96:(h + 1) * 96], rhs=cT[:, k, :],
                                                                    start=(k == 0), stop=(k == 1)), [r_cT, r_w], [r_bka])
                for k in range(2):
                    PE(lambda e, k=k, bkb_=bkb_, h=h, cT=cT: e.matmul(bkb_[0:96, :], lhsT=w_uqs_sb[:, k, h * 96:(h + 1) * 96], rhs=cT[:, k, :],
                                                                      start=(k == 0), stop=(k == 1)), [r_cT, r_w], [r_bkb])
                ACT(lambda e, bka=bka, h=h: e.copy(out=qs[:, h, :], in_=bka[0:96, :]), [r_bka], [r_qs])
                ACT(lambda e, bkb_=bkb_, h=h: e.copy(out=qsw[RS, h, :], in_=bkb_[RS, :]), [r_bkb], [r_qsw])
            yield
            qb, r_qb = qb_ring.next()
            POOL(lambda e, qb=qb: e.tensor_copy(out=qb[0:64, :, :], in_=qs[0:64, :, :]), [r_qs], [r_qb])
            VEC(lambda e, csb=csb: e.tensor_tensor(out=qs[RS, :, :], in0=qs[RS, :, :],
                                                   in1=csb[RS, 0:1, :].to_broadcast([32, 8, 512]), op=ALU.mult), [r_qs, r_csb], [r_qs])
            VEC(lambda e, csb=csb: e.tensor_tensor(out=qsw[RS, :, :], in0=qsw[RS, :, :],
                                                   in1=csb[RS, 1:2, :].to_broadcast([32, 8, 512]), op=ALU.mult), [r_qsw, r_csb], [r_qsw])
            VEC(lambda e, qb=qb: e.tensor_tensor(out=qb[RS, :, :], in0=qs[RS, :, :], in1=qsw[RS, :, :], op=ALU.add), [r_qs, r_qsw], [r_qb])
            DMA(qT_d[:, :, T0:T0 + 512].rearrange("h p t -> p h t"), qb, [r_qb], [r_qd])
            yield

        def run_interleaved(gens):
            gens = list(gens)
            while gens:
                for g_ in list(gens):
                    try:
                        next(g_)
                    except StopIteration:
                        gens.remove(g_)

        for b in range(NB + 1):
            gl = []
            if b >= 1:
                gl.append(stageA2(b - 1))
            if b < NB:
                gl.append(stageA1(b))
            run_interleaved(gl)

        if stop_after == "A":
            return True

        P.barrier()
        areset()
        S_POOL = (0, 1, 2, 3)
        O_POOL = (4, 5)
        M_POOL = (6, 7)
        pT_ringB = ARing(6, [128, 512], BF16, "pTB")
        qh_ring = ARing(2, [96, S], BF16, "qh")
        kh_ring = ARing(2, [96, S], BF16, "kh")
        vh_ring = ARing(2, [128, 32, 65], BF16, "vh")
        oT_ring = ARing(3, [65, 1024], F32, "oT3")
        an_ring = ARing(3, [64, 512], F32, "an3")
        LA = 3

        def load_head(h):
            qh, r_qh = qh_ring.next()
            kh, r_kh = kh_ring.next()
            vh, r_vh = vh_ring.next()
            DMA(qh, qT_d[h], [r_qd], [r_qh])
            DMA(kh, kT_d[h], [r_kd], [r_kh])
            DMA(vh, V_d[h], [r_vd], [r_vh])
            return (qh, r_qh, kh, r_kh, vh, r_vh)

        heads = {0: load_head(0)}
        for h in range(8):
            qh, r_qh, kh, r_kh, vh, r_vh = heads[h]
            if h + 1 < 8:
                heads[h + 1] = load_head(h + 1)
            items = [(b, kt) for b in range(NB) for kt in range(4 * (b + 1))]
            pts = {}
            bos = {}
            deferred = []

            def stage1(i):
                b, kt = items[i]
                T0 = b * 512
                bs, _, r_bs = next_bank(S_POOL)
                PE(lambda e, bs=bs, kt=kt, kh=kh, qh=qh, T0=T0: e.matmul(bs[:, :], lhsT=kh[:, kt * 128:(kt + 1) * 128], rhs=qh[:, T0:T0 + 512],
                                                                         start=True, stop=True), [r_kh, r_qh], [r_bs])
                pT, r_pT = pT_ringB.next()
                ACT(lambda e, bs=bs, pT=pT: e.activation(out=pT[:], in_=bs[:, :], func=AF.Exp, scale=ATT_SCALE), [r_bs], [r_pT])
                if kt >= 4 * b:
                    base = T0 - kt * 128
                    POOL(lambda e, pT=pT, base=base: e.affine_select(out=pT[:], in_=pT[:], pattern=[[1, 512]], compare_op=ALU.is_ge,
                                                                     fill=0.0, base=base, channel_multiplier=-1), [r_pT], [r_pT])
                pts[i] = (pT, r_pT)

            def stage2(j, i_now):
                b, kt = items[j]
                nkt = 4 * (b + 1)
                if kt == 0:
                    bos[b] = next_bank(O_POOL)
                bo, _, r_bo = bos[b]
                pT, r_pT = pts.pop(j)
                PE(lambda e, bo=bo, pT=pT, kt=kt, nkt=nkt, vh=vh: e.matmul(bo[0:65, :], lhsT=vh[:, kt, :], rhs=pT[:],
                                                                           start=(kt == 0), stop=(kt == nkt - 1)), [r_vh, r_pT], [r_bo])
                if kt == nkt - 1:
                    oT, r_oT = oT_ring.next()
                    VEC(lambda e, bo=bo, oT=oT: e.tensor_copy(out=oT[:, 0:512], in_=bo[0:65, :]), [r_bo], [r_oT])

                    def epi(b=b, oT=oT, r_oT=r_oT):
                        bm, _, r_bm = next_bank(M_POOL)
                        PE(lambda e, bm=bm, oT=oT: e.matmul(bm[0:64, :], lhsT=sel65[:, :], rhs=oT[:, 0:512], start=True, stop=True), [r_oT, r_const], [r_bm])
                        VEC(lambda e, bm=bm, oT=oT: e.reciprocal(out=oT[0:64, 512:1024], in_=bm[0:64, :]), [r_bm, r_oT], [r_oT])
                        an, r_an = an_ring.next()
                        POOL(lambda e, oT=oT, an=an: e.tensor_tensor(out=an[:, :], in0=oT[0:64, 0:512], in1=oT[0:64, 512:1024], op=ALU.mult), [r_oT], [r_an])
                        DMA(aT_d[b, :, h, :], an, [r_an], [r_ad], q="gpsimd")
                    deferred.append((i_now + 3, epi))

            n_it = len(items)
            for i in range(n_it + LA):
                if i < n_it:
                    stage1(i)
                if i - LA >= 0:
                    stage2(i - LA, i)
                while deferred and deferred[0][0] <= i:
                    deferred.pop(0)[1]()
            while deferred:
                deferred.pop(0)[1]()
        if stop_after == "B1":
            return True
        P.barrier()
        areset()
        Wst, r_Wst = aalloc([128, 4, 8, 2, 128], BF16, "Wst")
        Wfir, r_Wfir = aalloc([128, 4, 8, 128], BF16, "Wfir")
        Wo_r, r_Wo = aalloc([128, 16, 8, 32], BF16, "Wo")
        Wo_i, _ = aalloc([128, 16, 8, 32], BF16)
        sm, r_sm = aalloc([128, 32, 16], F32, "sm")
        pw_r, _ = aalloc([128, 16, 9], F32)
        pw_i, _ = aalloc([128, 16, 9], F32)
        ph_r, _ = aalloc([128, 16, 9], F32)
        ph_i, _ = aalloc([128, 16, 9], F32)
        mark = arena_off[0]
        Bri, r_bc = aalloc([128, 2, 16, 16], F32, "Bri")
        Cri, _ = aalloc([128, 2, 16, 16], F32)
        Bb_r, r_T = aalloc([128, 16, 16], F32, "T")
        Bb_i, _ = aalloc([128, 16, 16], F32)
        T1, _ = aalloc([128, 16, 16], F32)
        T2, _ = aalloc([128, 16, 16], F32)
        T3, _ = aalloc([128, 16, 16], F32)
        T4, _ = aalloc([128, 16, 16], F32)
        ME_r, r_ME = aalloc([128, 8, 4, 128], F32, "ME")
        ME_i, _ = aalloc([128, 8, 4, 128], F32)
        MF_r, r_MF = aalloc([128, 4, 128], F32, "MF")
        MF_in, _ = aalloc([128, 4, 128], F32)
        tmpF, r_tmpF = aalloc([128, 128], F32, "tmpF")
        DMA(sm[:, 0:3, :], s5v_d[l], [], [r_sm])
        DMA(Bri, s5b_d[l], [], [r_bc])
        DMA(Cri, s5c_d[l], [], [r_bc])
        POOL(lambda e: e.memset(ME_r, 0.0), [], [r_ME])
        POOL(lambda e: e.memset(ME_i, 0.0), [], [r_ME])
        POOL(lambda e: e.memset(MF_r, 0.0), [], [r_MF])
        POOL(lambda e: e.memset(MF_in, 0.0), [], [r_MF])
        POOL(lambda e: e.memset(Wo_r, 0.0), [], [r_Wo])
        POOL(lambda e: e.memset(Wo_i, 0.0), [], [r_Wo])
        LRE, LIM, LDT, DT, TT_, ER, Y, RND, FR, SN_, CS_, AR, AI, ARM1, DEN, RDEN, CBR, CBI, U1, U2, RDEC, RR = range(22)

        def smtt(o, a, b, op):
            VEC(lambda e: e.tensor_tensor(out=sm[:, o, :], in0=sm[:, a, :], in1=sm[:, b, :], op=op), [r_sm], [r_sm])

        def smts(o, a, s1, op0, s2=None, op1=None):
            if op1 is None:
                VEC(lambda e: e.tensor_scalar(out=sm[:, o, :], in0=sm[:, a, :], scalar1=s1, scalar2=None, op0=op0), [r_sm], [r_sm])
            else:
                VEC(lambda e: e.tensor_scalar(out=sm[:, o, :], in0=sm[:, a, :], scalar1=s1, scalar2=s2, op0=op0, op1=op1), [r_sm], [r_sm])

        def smact(o, a, func, scale=1.0):
            ACT(lambda e: e.activation(out=sm[:, o, :], in_=sm[:, a, :], func=func, scale=scale), [r_sm], [r_sm])

        smact(DT, LDT, AF.Exp)
        smtt(TT_, LRE, DT, ALU.mult)
        smact(ER, TT_, AF.Exp)
        smact(RDEC, TT_, AF.Exp, 8.0)
        smtt(Y, LIM, DT, ALU.mult)
        smts(Y, Y, 1.0 / TWO_PI, ALU.mult)
        smts(RND, Y, MAGIC, ALU.add, MAGIC, ALU.subtract)
        smtt(FR, Y, RND, ALU.subtract)
        smact(SN_, FR, AF.Sin, SIN_SCALE)
        smts(Y, Y, 0.25, ALU.add)
        smts(RND, Y, MAGIC, ALU.add, MAGIC, ALU.subtract)
        smtt(FR, Y, RND, ALU.subtract)
        smact(CS_, FR, AF.Sin, SIN_SCALE)
        smtt(AR, ER, CS_, ALU.mult)
        smtt(AI, ER, SN_, ALU.mult)
        smts(ARM1, AR, -1.0, ALU.add)
        smtt(U1, LRE, LRE, ALU.mult)
        smtt(U2, LIM, LIM, ALU.mult)
        smtt(DEN, U1, U2, ALU.add)
        VEC(lambda e: e.reciprocal(out=sm[:, RDEN, :], in_=sm[:, DEN, :]), [r_sm], [r_sm])
        smtt(U1, ARM1, LRE, ALU.mult)
        smtt(U2, AI, LIM, ALU.mult)
        smtt(U1, U1, U2, ALU.add)
        smtt(CBR, U1, RDEN, ALU.mult)
        smtt(U1, AI, LRE, ALU.mult)
        smtt(U2, ARM1, LIM, ALU.mult)
        smtt(U1, U1, U2, ALU.subtract)
        smtt(CBI, U1, RDEN, ALU.mult)
        VEC(lambda e: e.memset(pw_r[:, :, 0:1], 1.0), [r_sm], [r_sm])
        VEC(lambda e: e.memset(pw_i[:, :, 0:1], 0.0), [r_sm], [r_sm])
        VEC(lambda e: e.tensor_copy(out=pw_r[:, :, 1], in_=sm[:, AR, :]), [r_sm], [r_sm])
        VEC(lambda e: e.tensor_copy(out=pw_i[:, :, 1], in_=sm[:, AI, :]), [r_sm], [r_sm])
        for k in range(1, 8):
            VEC(lambda e, k=k: e.tensor_tensor(out=sm[:, U1, :], in0=pw_r[:, :, k], in1=sm[:, AR, :], op=ALU.mult), [r_sm], [r_sm])
            VEC(lambda e, k=k: e.tensor_tensor(out=sm[:, U2, :], in0=pw_i[:, :, k], in1=sm[:, AI, :], op=ALU.mult), [r_sm], [r_sm])
            VEC(lambda e, k=k: e.tensor_tensor(out=pw_r[:, :, k + 1], in0=sm[:, U1, :], in1=sm[:, U2, :], op=ALU.subtract), [r_sm], [r_sm])
            VEC(lambda e, k=k: e.tensor_tensor(out=sm[:, U1, :], in0=pw_r[:, :, k], in1=sm[:, AI, :], op=ALU.mult), [r_sm], [r_sm])
            VEC(lambda e, k=k: e.tensor_tensor(out=sm[:, U2, :], in0=pw_i[:, :, k], in1=sm[:, AR, :], op=ALU.mult), [r_sm], [r_sm])
            VEC(lambda e, k=k: e.tensor_tensor(out=pw_i[:, :, k + 1], in0=sm[:, U1, :], in1=sm[:, U2, :], op=ALU.add), [r_sm], [r_sm])
        VEC(lambda e: e.reciprocal(out=sm[:, RR, :], in_=sm[:, RDEC, :]), [r_sm], [r_sm])
        VEC(lambda e: e.tensor_tensor(out=ph_r[:, :, 0], in0=pw_r[:, :, 8], in1=sm[:, RR, :], op=ALU.mult), [r_sm], [r_sm])
        VEC(lambda e: e.tensor_tensor(out=ph_i[:, :, 0], in0=pw_i[:, :, 8], in1=sm[:, RR, :], op=ALU.mult), [r_sm], [r_sm])
        for k in range(8):
            VEC(lambda e, k=k: e.tensor_tensor(out=sm[:, U1, :], in0=ph_r[:, :, k], in1=ph_r[:, :, k], op=ALU.mult), [r_sm], [r_sm])
            VEC(lambda e, k=k: e.tensor_tensor(out=sm[:, U2, :], in0=ph_i[:, :, k], in1=ph_i[:, :, k], op=ALU.mult), [r_sm], [r_sm])
            VEC(lambda e, k=k: e.tensor_tensor(out=ph_r[:, :, k + 1], in0=sm[:, U1, :], in1=sm[:, U2, :], op=ALU.subtract), [r_sm], [r_sm])
            VEC(lambda e, k=k: e.tensor_tensor(out=sm[:, U1, :], in0=ph_r[:, :, k], in1=ph_i[:, :, k], op=ALU.mult), [r_sm], [r_sm])
            VEC(lambda e, k=k: e.tensor_scalar(out=ph_i[:, :, k + 1], in0=sm[:, U1, :], scalar1=2.0, scalar2=None, op0=ALU.mult), [r_sm], [r_sm])

        def bc16(tile_idx_ap):
            return tile_idx_ap.rearrange("p (a o) -> p a o", o=1).to_broadcast([128, 16, 16])

        def cmul_bc(outr, outi, xr, xi, sr_ap, si_ap, rds, wrs):
            pass

        VEC(lambda e: e.tensor_tensor(out=T1, in0=Bri[:, 0], in1=bc16(sm[:, CBR, :]), op=ALU.mult), [r_sm, r_bc], [r_T])
        VEC(lambda e: e.tensor_tensor(out=T2, in0=Bri[:, 1], in1=bc16(sm[:, CBI, :]), op=ALU.mult), [r_sm, r_bc], [r_T])
        VEC(lambda e: e.tensor_tensor(out=Bb_r, in0=T1, in1=T2, op=ALU.subtract), [r_T], [r_T])
        VEC(lambda e: e.tensor_tensor(out=T1, in0=Bri[:, 1], in1=bc16(sm[:, CBR, :]), op=ALU.mult), [r_sm, r_bc, r_T], [r_T])
        VEC(lambda e: e.tensor_tensor(out=T2, in0=Bri[:, 0], in1=bc16(sm[:, CBI, :]), op=ALU.mult), [r_sm, r_bc], [r_T])
        VEC(lambda e: e.tensor_tensor(out=Bb_i, in0=T1, in1=T2, op=ALU.add), [r_T], [r_T])

        def blkME(M, lg, hf):
            return M[hf * 64:(hf + 1) * 64, lg, :, :].rearrange("p ct (q x) -> p ct q x", x=32)[:, :, :, hf * 16:(hf + 1) * 16]

        def halfT(T, hf):
            return T[hf * 64:(hf + 1) * 64, :, :].rearrange("p (ct q) c -> p ct q c", q=4)

        for lg in range(8):
            VEC(lambda e, lg=lg: e.tensor_tensor(out=T1, in0=Bb_r, in1=pw_r[:, :, lg:lg + 1].to_broadcast([128, 16, 16]), op=ALU.mult), [r_sm, r_T], [r_T])
            VEC(lambda e, lg=lg: e.tensor_tensor(out=T2, in0=Bb_i, in1=pw_i[:, :, lg:lg + 1].to_broadcast([128, 16, 16]), op=ALU.mult), [r_sm, r_T], [r_T])
            VEC(lambda e, lg=lg: e.tensor_tensor(out=T3, in0=Bb_i, in1=pw_r[:, :, lg:lg + 1].to_broadcast([128, 16, 16]), op=ALU.mult), [r_sm, r_T], [r_T])
            VEC(lambda e, lg=lg: e.tensor_tensor(out=T4, in0=Bb_r, in1=pw_i[:, :, lg:lg + 1].to_broadcast([128, 16, 16]), op=ALU.mult), [r_sm, r_T], [r_T])
            for hf in range(2):
                POOL(lambda e, lg=lg, hf=hf: e.tensor_tensor(out=blkME(ME_r, lg, hf), in0=halfT(T1, hf), in1=halfT(T2, hf), op=ALU.subtract), [r_T], [r_ME])
                POOL(lambda e, lg=lg, hf=hf: e.tensor_tensor(out=blkME(ME_i, lg, hf), in0=halfT(T3, hf), in1=halfT(T4, hf), op=ALU.add), [r_T], [r_ME])

        def blkMF(M, hf):
            return M[hf * 64:(hf + 1) * 64, :, :].rearrange("p ct (q x) -> p ct q x", x=32)[:, :, :, hf * 16:(hf + 1) * 16]

        for hf in range(2):
            POOL(lambda e, hf=hf: e.tensor_copy(out=blkMF(MF_r, hf), in_=halfT(Cri[:, 0], hf)), [r_bc], [r_MF])
            POOL(lambda e, hf=hf: e.tensor_scalar(out=blkMF(MF_in, hf), in0=halfT(Cri[:, 1], hf), scalar1=-1.0, scalar2=None, op0=ALU.mult), [r_bc], [r_MF])
        for ct in range(4):
            for ri, M in ((0, ME_r), (1, ME_i)):
                for j0 in (0, 4):
                    bk, _, r_bk = next_bank()
                    for jj in range(4):
                        j = j0 + jj
                        PE(lambda e, bk=bk, jj=jj, j=j, ct=ct, M=M: e.transpose(out=bk[:, jj * 128:(jj + 1) * 128], in_=M[:, 7 - j, ct, :], identity=identf[:]),
                           [r_ME, r_const], [r_bk])
                    ACT(lambda e, bk=bk, ct=ct, j0=j0, ri=ri: e.copy(out=Wst[:, ct, j0:j0 + 4, ri, :], in_=bk[:, :].rearrange("p (a b) -> p a b", b=128)),
                        [r_bk], [r_Wst])
        for ct in range(4):
            for l0 in (0, 4):
                bk, _, r_bk = next_bank()
                for ll in range(4):
                    lg = l0 + ll
                    PE(lambda e, bk=bk, ll=ll, lg=lg, ct=ct: e.matmul(bk[:, ll * 128:(ll + 1) * 128], lhsT=ME_r[:, lg, ct, :], rhs=MF_r[:, ct, :], start=True, stop=False),
                       [r_ME, r_MF], [r_bk])
                    PE(lambda e, bk=bk, ll=ll, lg=lg, ct=ct: e.matmul(bk[:, ll * 128:(ll + 1) * 128], lhsT=ME_i[:, lg, ct, :], rhs=MF_in[:, ct, :], start=False, stop=True),
                       [r_ME, r_MF], [r_bk])
                VEC(lambda e, bk=bk, ct=ct, l0=l0: e.tensor_tensor(out=Wfir[:, ct, l0:l0 + 4, :], in0=bk[:, :].rearrange("p (a b) -> p a b", b=128),
                                                                 in1=mask16[:, :].rearrange("p (o b) -> p o b", o=1).to_broadcast([128, 4, 128]), op=ALU.mult),
                    [r_bk, r_const], [r_Wfir])
                if l0 == 0:
                    VEC(lambda e, bk=bk: e.tensor_tensor(out=tmpF, in0=bk[:, 0:128], in1=mask16[:, :], op=ALU.mult), [r_bk, r_const], [r_tmpF])
                    VEC(lambda e, ct=ct: e.scalar_tensor_tensor(out=Wfir[:, ct, 0, :], in0=identf[:, :], scalar=col(l, C_D + ct), in1=tmpF, op0=ALU.mult, op1=ALU.add),
                        [r_tmpF, r_const], [r_Wfir])
        for i in range(8):
            VEC(lambda e, i=i: e.tensor_tensor(out=T1, in0=Cri[:, 0], in1=pw_r[:, :, i + 1:i + 2].to_broadcast([128, 16, 16]), op=ALU.mult), [r_sm, r_bc, r_T], [r_T])
            VEC(lambda e, i=i: e.tensor_tensor(out=T2, in0=Cri[:, 1], in1=pw_i[:, :, i + 1:i + 2].to_broadcast([128, 16, 16]), op=ALU.mult), [r_sm, r_bc, r_T], [r_T])
            VEC(lambda e, i=i: e.tensor_tensor(out=T3, in0=Cri[:, 1], in1=pw_r[:, :, i + 1:i + 2].to_broadcast([128, 16, 16]), op=ALU.mult), [r_sm, r_bc, r_T], [r_T])
            VEC(lambda e, i=i: e.tensor_tensor(out=T4, in0=Cri[:, 0], in1=pw_i[:, :, i + 1:i + 2].to_broadcast([128, 16, 16]), op=ALU.mult), [r_sm, r_bc, r_T], [r_T])
            for hf in range(2):
                hs = slice(hf * 64, (hf + 1) * 64)
                cs = slice(hf * 16, (hf + 1) * 16)
                POOL(lambda e, i=i, hs=hs, cs=cs: e.tensor_tensor(out=Wo_r[hs, :, i, cs], in0=T1[hs], in1=T2[hs], op=ALU.subtract), [r_T], [r_Wo])
                VEC(lambda e, i=i, hs=hs, cs=cs: e.scalar_tensor_tensor(out=Wo_i[hs, :, i, cs], in0=T3[hs], scalar=-1.0, in1=T4[hs], op0=ALU.mult, op1=ALU.subtract),
                    [r_T], [r_Wo])
        P.barrier()
        arena_off[0] = mark
        u_ring = ARing(2, [128, S], BF16, "uct")
        y_ring = ARing(1, [128, S], F32, "yct")
        tab_r, r_tab = aalloc([128, 4, 512], F32, "tab")
        tab_i, _ = aalloc([128, 4, 512], F32)
        tq1, r_tq = aalloc([128, 4, 256], F32, "tq")
        tq2, _ = aalloc([128, 4, 256], F32)
        xps = []
        for i_ in range(2):
            xr_, rx_ = aalloc([128, 4, 512], BF16, f"xpr{i_}")
            xi_, _ = aalloc([128, 4, 512], BF16)
            VEC(lambda e, xr_=xr_: e.memset(xr_[:, :, 0:1], 0.0), [], [rx_])
            VEC(lambda e, xi_=xi_: e.memset(xi_[:, :, 0:1], 0.0), [], [rx_])
            xps.append((xr_, xi_, rx_))

        class _AR:
            def __init__(self, bufs):
                self.bufs = bufs
                self.i = 0

            def next(self):
                b_ = self.bufs[self.i % len(self.bufs)]
                self.i += 1
                return b_
        A_ring = _AR([aalloc([128, 512], F32, f"A{i_}") for i_ in range(6)] + [(frB[0][:, i_ * 512:(i_ + 1) * 512], Res(f"AB{i_}")) for i_ in range(4)])
        ctxS = {}

        def stageS(ct):
            xp_r, xp_i, r_xp = xps[ct % 2]
            uct, r_uct = u_ring.next()
            DMA(uct, uT_d[:, ct, :], [r_ud], [r_uct])
            u8 = uct.rearrange("p (c j) -> p j c", j=8)
            ctxS[ct] = (uct, r_uct, u8)
            VEC(lambda e: e.memset(tab_r[:, :, 0:1], 1.0), [r_tab], [r_tab])
            VEC(lambda e: e.memset(tab_i[:, :, 0:1], 0.0), [r_tab], [r_tab])
            for k in range(9):
                yield
                s_ = 1 << k
                phr = ph_r[:, 4 * ct:4 * ct + 4, k:k + 1].to_broadcast([128, 4, s_])
                phi = ph_i[:, 4 * ct:4 * ct + 4, k:k + 1].to_broadcast([128, 4, s_])
                VEC(lambda e, s_=s_, phr=phr: e.tensor_tensor(out=tq1[:, :, 0:s_], in0=tab_r[:, :, 0:s_], in1=phr, op=ALU.mult), [r_tab, r_sm, r_tq], [r_tq])
                VEC(lambda e, s_=s_, phi=phi: e.tensor_tensor(out=tq2[:, :, 0:s_], in0=tab_i[:, :, 0:s_], in1=phi, op=ALU.mult), [r_tab, r_sm, r_tq], [r_tq])
                VEC(lambda e, s_=s_: e.tensor_tensor(out=tab_r[:, :, s_:2 * s_], in0=tq1[:, :, 0:s_], in1=tq2[:, :, 0:s_], op=ALU.subtract), [r_tq], [r_tab])
                VEC(lambda e, s_=s_, phi=phi: e.tensor_tensor(out=tq1[:, :, 0:s_], in0=tab_r[:, :, 0:s_], in1=phi, op=ALU.mult), [r_tab, r_sm, r_tq], [r_tq])
                VEC(lambda e, s_=s_, phr=phr: e.tensor_tensor(out=tq2[:, :, 0:s_], in0=tab_i[:, :, 0:s_], in1=phr, op=ALU.mult), [r_tab, r_sm, r_tq], [r_tq])
                VEC(lambda e, s_=s_: e.tensor_tensor(out=tab_i[:, :, s_:2 * s_], in0=tq1[:, :, 0:s_], in1=tq2[:, :, 0:s_], op=ALU.add), [r_tq], [r_tab])
            for q in range(4):
                yield
                pair = 4 * ct + q
                ps_ = slice(32 * q, 32 * q + 32)
                bks = []
                for ri in range(2):
                    bk, _, r_bk = next_bank()
                    for j in range(8):
                        PE(lambda e, bk=bk, j=j, ri=ri, ps_=ps_, q=q, ct=ct, u8=u8: e.matmul(bk[:, :], lhsT=Wst[ps_, ct, j, ri, :], rhs=u8[ps_, j, :],
                                                                                           start=(j == 0), stop=(j == 7), tile_position=(32 * q, 0)),
                           [r_Wst, r_uct], [r_bk])
                    bks.append((bk, r_bk))
                (Sr, r_Sr), (Si, r_Si) = bks
                tr, ti = tab_r[:, q, :], tab_i[:, q, :]
                a1, r_a1 = A_ring.next()
                a2, r_a2 = A_ring.next()
                a3, r_a3 = A_ring.next()
                a4, r_a4 = A_ring.next()
                VEC(lambda e, Sr=Sr, a1=a1, tr=tr: e.tensor_tensor(out=a1, in0=Sr[:, :], in1=tr, op=ALU.mult), [r_Sr, r_tab], [r_a1])
                VEC(lambda e, Si=Si, a2=a2, ti=ti: e.tensor_tensor(out=a2, in0=Si[:, :], in1=ti, op=ALU.mult), [r_Si, r_tab], [r_a2])
                VEC(lambda e, Si=Si, a3=a3, tr=tr: e.tensor_tensor(out=a3, in0=Si[:, :], in1=tr, op=ALU.mult), [r_Si, r_tab], [r_a3])
                VEC(lambda e, Sr=Sr, a4=a4, ti=ti: e.tensor_tensor(out=a4, in0=Sr[:, :], in1=ti, op=ALU.mult), [r_Sr, r_tab], [r_a4])
                POOL(lambda e, a1=a1, a2=a2: e.tensor_tensor(out=a1, in0=a1, in1=a2, op=ALU.add), [r_a1, r_a2], [r_a1])
                POOL(lambda e, a3=a3, a4=a4: e.tensor_tensor(out=a3, in0=a3, in1=a4, op=ALU.subtract), [r_a3, r_a4], [r_a3])
                yield
                rd = sm[:, RDEC, pair:pair + 1].to_broadcast([128, 512])
                VEC(lambda e, a1=a1, a2=a2, rd=rd: e.tensor_tensor_scan(out=a2, data0=rd, data1=a1, initial=0.0, op0=ALU.mult, op1=ALU.add), [r_a1, r_sm, r_a2], [r_a2])
                VEC(lambda e, a3=a3, a4=a4, rd=rd: e.tensor_tensor_scan(out=a4, data0=rd, data1=a3, initial=0.0, op0=ALU.mult, op1=ALU.add), [r_a3, r_sm, r_a4], [r_a4])
                yield
                b1, r_b1 = A_ring.next()
                b2, r_b2 = A_ring.next()
                POOL(lambda e, a2=a2, b1=b1, tr=tr: e.tensor_tensor(out=b1, in0=a2, in1=tr, op=ALU.mult), [r_a2, r_tab], [r_b1])
                POOL(lambda e, a4=a4, b2=b2, ti=ti: e.tensor_tensor(out=b2, in0=a4, in1=ti, op=ALU.mult), [r_a4, r_tab], [r_b2])
                VEC(lambda e, b1=b1, b2=b2, q=q: e.tensor_tensor(out=xp_r[:, q, 1:512], in0=b1[:, 0:511], in1=b2[:, 0:511], op=ALU.subtract), [r_b1, r_b2], [r_xp])
                POOL(lambda e, a2=a2, a1=a1, ti=ti: e.tensor_tensor(out=a1, in0=a2, in1=ti, op=ALU.mult), [r_a2, r_tab, r_a1], [r_a1])
                POOL(lambda e, a4=a4, a3=a3, tr=tr: e.tensor_tensor(out=a3, in0=a4, in1=tr, op=ALU.mult), [r_a4, r_tab, r_a3], [r_a3])
                VEC(lambda e, a1=a1, a3=a3, q=q: e.tensor_tensor(out=xp_i[:, q, 1:512], in0=a1[:, 0:511], in1=a3[:, 0:511], op=ALU.add), [r_a1, r_a3], [r_xp])

        def stageO(ct):
            xp_r, xp_i, r_xp = xps[ct % 2]
            uct, r_uct, u8 = ctxS.pop(ct)
            yct, r_yct = y_ring.next()
            y8 = yct.rearrange("p (c j) -> p j c", j=8)
            for i in range(8):
                yield
                bk, _, r_bk = next_bank()
                for lg in range(i + 1):
                    PE(lambda e, bk=bk, lg=lg, i=i, ct=ct, u8=u8: e.matmul(bk[:, :], lhsT=Wfir[:, ct, lg, :], rhs=u8[:, i - lg, :], start=(lg == 0), stop=False),
                       [r_Wfir, r_uct], [r_bk])
                for q in range(4):
                    pair = 4 * ct + q
                    PE(lambda e, bk=bk, q=q, pair=pair, i=i: e.matmul(bk[32 * q:32 * q + 32, :], lhsT=Wo_r[:, pair, i, :], rhs=xp_r[:, q, :], start=False, stop=False,
                                                                     tile_position=(0, 32 * q)), [r_Wo, r_xp], [r_bk])
                    PE(lambda e, bk=bk, q=q, pair=pair, i=i: e.matmul(bk[32 * q:32 * q + 32, :], lhsT=Wo_i[:, pair, i, :], rhs=xp_i[:, q, :], start=False, stop=(q == 3),
                                                                     tile_position=(0, 32 * q)), [r_Wo, r_xp], [r_bk])
                ACT(lambda e, bk=bk, y8=y8, i=i: e.copy(out=y8[:, i, :], in_=bk[:, :]), [r_bk], [r_yct])
            DMA(yT_d[:, ct, :], yct, [r_yct], [r_yd], q="scalar")

        for ct_ in range(5):
            gl = []
            if ct_ >= 1:
                gl.append(stageO(ct_ - 1))
            if ct_ < 4:
                gl.append(stageS(ct_))
            run_interleaved(gl)
        if stop_after == "B2a":
            return True
        P.barrier()
        areset()
        wglu, r_wglu = aalloc([128, 4, 512], BF16, "wglu")
        DMAC(wglu, w_glu_d[l].rearrange("(k p) n -> p k n", p=128), [], [r_wglu])
        yb_ring = ARing(1, [128, 4, 512], F32, "yb")
        g_ring = ARing(2, [128, 4, 512], BF16, "gT")
        sg_ring = ARing(1, [128, 4, 512], F32, "sg")
        sq_ring = ARing(1, [128, 4, 512], F32, "sq")
        rs_ring = ARing(2, [128, 512], F32, "rs")
        sn_ring = ARing(2, [128, 4, 512], BF16, "sn")
        glu_end = arena_off[0]
        wo_a, r_wc = aalloc([128, 4, 1024], BF16, "wo_a")
        wo_s, _ = aalloc([128, 4, 1024], BF16)
        wxq, _ = aalloc([128, 8, 1024], BF16)
        wxo, _ = aalloc([128, 8, 1024], BF16)
        KxT, r_kx = aalloc([128, 8, 256], BF16, "KxT")
        Vx, _ = aalloc([128, 2, 1024], BF16)
        c1w_end = arena_off[0]
        for par_ in range(2):
            DMAC(wo_a[par_ * 64:(par_ + 1) * 64, :, :], w_out_d[l, 0:512, :].rearrange("(hp par d) n -> par d hp n", par=2, d=64)[par_], [], [r_wc])
        VEC(lambda e: e.tensor_tensor(out=wo_a, in0=wo_a, in1=col(l, C_GA2, 4).rearrange("p (k o) -> p k o", o=1).to_broadcast([128, 4, 1024]), op=ALU.mult), [r_wc, r_const], [r_wc])
        DMAC(wo_s, w_out_d[l, 512:1024, :].rearrange("(k p) n -> p k n", p=128), [], [r_wc])
        DMAC(wxq, w_xq_d[l].rearrange("(k p) n -> p k n", p=128), [], [r_wc])
        DMAC(wxo, w_xo_d[l].rearrange("(k p) n -> p k n", p=128), [], [r_wc])
        for b in range(NB):
            T0 = b * 512
            yb, r_yb = yb_ring.next()
            DMA(yb, yT_d[:, :, T0:T0 + 512], [r_yd], [r_yb])
            gT, r_gT = g_ring.next()
            ACT(lambda e, yb=yb, gT=gT: e.activation(out=gT, in_=yb, func=AF.Gelu_apprx_tanh), [r_yb], [r_gT])
            sg, r_sg = sg_ring.next()
            for co in range(4):
                bk, _, r_bk = next_bank()
                for ci in range(4):
                    PE(lambda e, bk=bk, ci=ci, co=co, gT=gT: e.matmul(bk[:, :], lhsT=wglu[:, ci, co * 128:(co + 1) * 128], rhs=gT[:, ci, :], start=(ci == 0), stop=(ci == 3)),
                       [r_wglu, r_gT], [r_bk])
                ACT(lambda e, bk=bk, co=co, sg=sg: e.activation(out=sg[:, co, :], in_=bk[:, :], func=AF.Sigmoid, bias=col(l, C_BGLU + co)), [r_bk, r_const], [r_sg])
            VEC(lambda e, sg=sg, yb=yb: e.tensor_tensor(out=sg, in0=sg, in1=yb, op=ALU.mult), [r_sg, r_yb], [r_sg])
            sq, r_sq = sq_ring.next()
            POOL(lambda e, sg=sg, sq=sq: e.tensor_tensor(out=sq, in0=sg, in1=sg, op=ALU.mult), [r_sg], [r_sq])
            bk, _, r_bk = next_bank()
            for co in range(4):
                PE(lambda e, bk=bk, co=co, sq=sq: e.matmul(bk[:, :], lhsT=onesf[:, :], rhs=sq[:, co, :], start=(co == 0), stop=(co == 3)), [r_sq, r_const], [r_bk])
            rs_, r_rs = rs_ring.next()
            ACT(lambda e, bk=bk, rs_=rs_: e.activation(out=rs_, in_=bk[:, :], func=AF.Sqrt, scale=1.0 / 512, bias=EPS), [r_bk], [r_rs])
            VEC(lambda e, rs_=rs_: e.reciprocal(out=rs_, in_=rs_), [r_rs], [r_rs])
            VEC(lambda e, sg=sg: e.tensor_tensor(out=sg, in0=sg, in1=col(l, C_GS, 4).rearrange("p (k o) -> p k o", o=1).to_broadcast([128, 4, 512]), op=ALU.mult),
                [r_sg, r_const], [r_sg])
            sn, r_sn = sn_ring.next()
            VEC(lambda e, sg=sg, sn=sn, rs_=rs_: e.tensor_tensor(out=sn, in0=sg, in1=rs_.rearrange("p (o t) -> p o t", o=1).to_broadcast([128, 4, 512]), op=ALU.mult),
                [r_sg, r_rs], [r_sn])
            DMA(sT_d[b], sn, [r_sn], [r_sd])
        if stop_after == "B2":
            return True
        P.barrier()
        arena_off[0] = 0
        wxkv, r_wxkv = aalloc([128, 8, 2048], BF16, "wxkv")
        assert arena_off[0] <= glu_end
        DMAC(wxkv, w_xkv_d[l].rearrange("(k p) n -> p k n", p=128), [], [r_wxkv])
        memT, r_memT = xnT_ring.next()
        for mt in range(2):
            ht, r_ht = ht_ring.next()
            DMA(ht[:], mem_d[mt * 128:(mt + 1) * 128, :], [], [r_ht])
            norm_transpose(ht, r_ht, col(l, C_GMEM, 8), memT, r_memT, mt)
        for oc in range(8):
            bk, _, r_bk = next_bank()
            for k in range(8):
                PE(lambda e, bk=bk, k=k, oc=oc: e.matmul(bk[:, 0:256], lhsT=wxkv[:, k, oc * 128:(oc + 1) * 128], rhs=memT[:, k, 0:256], start=(k == 0), stop=(k == 7)),
                   [r_wxkv, r_memT], [r_bk])
            ACT(lambda e, bk=bk, oc=oc: e.copy(out=KxT[:, oc, :], in_=bk[:, 0:256]), [r_bk], [r_kx])
        for mt in range(2):
            for hf in range(2):
                bk, _, r_bk = next_bank()
                for k in range(8):
                    PE(lambda e, bk=bk, k=k, mt=mt, hf=hf: e.matmul(bk[:, :], lhsT=memT[:, k, mt * 128:(mt + 1) * 128], rhs=wxkv[:, k, 1024 + hf * 512:1024 + (hf + 1) * 512],
                                                                  start=(k == 0), stop=(k == 7)), [r_wxkv, r_memT], [r_bk])
                ACT(lambda e, bk=bk, mt=mt, hf=hf: e.copy(out=Vx[:, mt, hf * 512:(hf + 1) * 512], in_=bk[:, :]), [r_bk], [r_kx])
        P.barrier()
        arena_off[0] = 0
        h1_ring = ARing(8, [128, D], F32, "h1t")
        qx_ring = ARing(1, [128, 8, 512], BF16, "qxT")
        ox_ring = ARing(1, [128, 8, 512], BF16, "oxT")
        assert arena_off[0] <= glu_end, (arena_off[0], glu_end)
        arena_off[0] = c1w_end
        an_ring2 = ARing(1, [128, 4, 512], BF16, "an2")
        sn_ring2 = ARing(1, [128, 4, 512], BF16, "sn2")
        pX_ring = ARing(2, [128, 2, 512], BF16, "pX")
        rc_ring = ARing(2, [128, 512], F32, "rc")
        araw_v = frA[0][0:64, 0:4096].rearrange("p (h t) -> p h t", t=512)
        araw4 = frA[0][0:64, 0:4096].rearrange("p (hp par t) -> p hp par t", par=2, t=512)
        r_araw = frA[1]
        sqv = frB[0][0:64, 0:2048].rearrange("p (h t) -> p h t", t=512)
        r_sqv = frB[1]
        r_h1src = [Res() for _ in range(32)] if l == 0 else r_hd
        ctx1 = {}

        def stageC1a(b):
            T0 = b * 512
            DMA(araw_v, aT_d[b], [r_ad], [r_araw])
            sn2, r_sn2 = sn_ring2.next()
            DMA(sn2, sT_d[b], [r_sd], [r_sn2])
            bk, _, r_bk = next_bank()
            for hg in range(2):
                POOL(lambda e, hg=hg: e.tensor_tensor(out=sqv, in0=araw_v[:, hg * 4:(hg + 1) * 4, :], in1=araw_v[:, hg * 4:(hg + 1) * 4, :], op=ALU.mult), [r_araw], [r_sqv])
                for hh in range(4):
                    PE(lambda e, bk=bk, hg=hg, hh=hh: e.matmul(bk[0:64, :], lhsT=onesf[0:64, 0:64], rhs=sqv[:, hh, :], start=(hg == 0 and hh == 0), stop=(hg == 1 and hh == 3)),
                       [r_sqv, r_const], [r_bk])
                yield
            rc, r_rc = rc_ring.next()
            ACT(lambda e, bk=bk, rc=rc: e.activation(out=rc[0:64, :], in_=bk[0:64, :], func=AF.Sqrt, scale=1.0 / 512, bias=EPS), [r_bk], [r_rc])
            VEC(lambda e, rc=rc: e.reciprocal(out=rc[0:64, :], in_=rc[0:64, :]), [r_rc], [r_rc])
            yield
            an2, r_an2 = an_ring2.next()
            rcb = rc[0:64, :].rearrange("p (o t) -> p o t", o=1).to_broadcast([64, 4, 512])
            VEC(lambda e, an2=an2, rcb=rcb: e.tensor_tensor(out=an2[0:64, :, :], in0=araw4[:, :, 0, :], in1=rcb, op=ALU.mult), [r_araw, r_rc], [r_an2])
            yield
            VEC(lambda e, an2=an2, rcb=rcb: e.tensor_tensor(out=an2[64:128, :, :], in0=araw4[:, :, 1, :], in1=rcb, op=ALU.mult), [r_araw, r_rc], [r_an2])
            yield
            xnT, r_xnT = xnT_ring.next()
            h1s = []
            ctx1[b] = (xnT, r_xnT, h1s)
            for sub in range(4):
                ti = b * 4 + sub
                ts_ = slice(sub * 128, (sub + 1) * 128)
                ht, r_ht = ht_ring.next()
                DMA(ht[:], src_d[T0 + sub * 128:T0 + (sub + 1) * 128, :], [r_h1src[ti]], [r_ht])
                h1t, r_h1t = h1_ring.next()
                for hf in range(2):
                    bk, _, r_bk = next_bank()
                    cs_ = slice(hf * 512, (hf + 1) * 512)
                    for k in range(4):
                        PE(lambda e, bk=bk, k=k, an2=an2, ts_=ts_, cs_=cs_: e.matmul(bk[:, :], lhsT=an2[:, k, ts_], rhs=wo_a[:, k, cs_], start=(k == 0), stop=False),
                           [r_an2, r_wc], [r_bk])
                    for k in range(4):
                        PE(lambda e, bk=bk, k=k, sn2=sn2, ts_=ts_, cs_=cs_: e.matmul(bk[:, :], lhsT=sn2[:, k, ts_], rhs=wo_s[:, k, cs_], start=False, stop=(k == 3)),
                           [r_sn2, r_wc], [r_bk])
                    VEC(lambda e, bk=bk, ht=ht, h1t=h1t, cs_=cs_: e.tensor_tensor(out=h1t[:, cs_], in0=bk[:, :], in1=ht[:, cs_], op=ALU.add), [r_bk, r_ht], [r_h1t])
                    yield
                norm_transpose(h1t, r_h1t, col(l, C_GX, 8), xnT, r_xnT, sub)
                h1s.append((h1t, r_h1t))
                yield

        def stageC1b(b):
            T0 = b * 512
            xnT, r_xnT, h1s = ctx1.pop(b)
            qxT, r_qxT = qx_ring.next()
            for oc in range(8):
                bk, _, r_bk = next_bank()
                for k in range(8):
                    PE(lambda e, bk=bk, k=k, oc=oc, xnT=xnT: e.matmul(bk[:, :], lhsT=wxq[:, k, oc * 128:(oc + 1) * 128], rhs=xnT[:, k, :], start=(k == 0), stop=(k == 7)),
                       [r_wc, r_xnT], [r_bk])
                ACT(lambda e, bk=bk, oc=oc, qxT=qxT: e.copy(out=qxT[:, oc, :], in_=bk[:, :]), [r_bk], [r_qxT])
                if oc % 2 == 1:
                    yield
            oxT, r_oxT = ox_ring.next()
            for hx in range(4):
                pX, r_pX = pX_ring.next()
                for mt in range(2):
                    bk, _, r_bk = next_bank()
                    for dc in range(2):
                        PE(lambda e, bk=bk, dc=dc, mt=mt, hx=hx, qxT=qxT: e.matmul(bk[:, :], lhsT=KxT[:, hx * 2 + dc, mt * 128:(mt + 1) * 128], rhs=qxT[:, hx * 2 + dc, :],
                                                                              start=(dc == 0), stop=(dc == 1)), [r_kx, r_qxT], [r_bk])
                    ACT(lambda e, bk=bk, mt=mt, pX=pX: e.activation(out=pX[:, mt, :], in_=bk[:, :], func=AF.Exp, scale=X_SCALE), [r_bk], [r_pX])
                yield
                bk, _, r_bk = next_bank()
                for mt in range(2):
                    PE(lambda e, bk=bk, mt=mt, pX=pX: e.matmul(bk[:, :], lhsT=onesb[:, :], rhs=pX[:, mt, :], start=(mt == 0), stop=(mt == 1)), [r_pX, r_const], [r_bk])
                rc, r_rc = rc_ring.next()
                VEC(lambda e, bk=bk, rc=rc: e.reciprocal(out=rc, in_=bk[:, :]), [r_bk], [r_rc])
                yield
                for dc in range(2):
                    bk, _, r_bk = next_bank()
                    for mt in range(2):
                        PE(lambda e, bk=bk, mt=mt, dc=dc, hx=hx, pX=pX: e.matmul(bk[:, :], lhsT=Vx[:, mt, (hx * 2 + dc) * 128:(hx * 2 + dc + 1) * 128], rhs=pX[:, mt, :],
                                                                             start=(mt == 0), stop=(mt == 1)), [r_kx, r_pX], [r_bk])
                    VEC(lambda e, bk=bk, dc=dc, hx=hx, rc=rc, oxT=oxT: e.tensor_tensor(out=oxT[:, hx * 2 + dc, :], in0=bk[:, :], in1=rc, op=ALU.mult), [r_bk, r_rc], [r_oxT])
                yield
            for sub in range(4):
                ti = b * 4 + sub
                ts_ = slice(sub * 128, (sub + 1) * 128)
                h1t, r_h1t = h1s[sub]
                for hf in range(2):
                    bk, _, r_bk = next_bank()
                    cs_ = slice(hf * 512, (hf + 1) * 512)
                    for k in range(8):
                        PE(lambda e, bk=bk, k=k, oxT=oxT, ts_=ts_, cs_=cs_: e.matmul(bk[:, :], lhsT=oxT[:, k, ts_], rhs=wxo[:, k, cs_], start=(k == 0), stop=(k == 7)),
                           [r_oxT, r_wc], [r_bk])
                    VEC(lambda e, bk=bk, h1t=h1t, cs_=cs_: e.tensor_tensor(out=h1t[:, cs_], in0=bk[:, :], in1=h1t[:, cs_], op=ALU.add), [r_bk, r_h1t], [r_h1t])
                    yield
                DMA(h1_d[T0 + sub * 128:T0 + (sub + 1) * 128, :], h1t[:], [r_h1t], [r_h1d[ti]])

        for b in range(NB + 1):
            gl = []
            if b >= 1:
                gl.append(stageC1b(b - 1))
            if b < NB:
                gl.append(stageC1a(b))
            run_interleaved(gl)
        if stop_after == "C1":
            return True

        P.barrier()
        areset()
        wg, r_wf = aalloc([128, 8, DFF], BF16, "wg")
        wu, _ = aalloc([128, 8, DFF], BF16)
        wd, _ = aalloc([128, NFF, 1024], BF16)
        for k in range(8):
            DMAC(wg[:, k, :], w_gate_d[l, k * 128:(k + 1) * 128, :], [], [r_wf])
            DMAC(wu[:, k, :], w_up_d[l, k * 128:(k + 1) * 128, :], [], [r_wf])
        for k in range(0, NFF, 2):
            DMAC(wd[:, k:k + 2, :], w_down_d[l, k * 128:(k + 2) * 128, :].rearrange("(k p) n -> p k n", p=128), [], [r_wf])
        actT = frA[0].bitcast(BF16) if hasattr(frA[0], "bitcast") else None
        actT = actT[:, 0:NFF * 512].rearrange("p (f t) -> p f t", t=512)
        r_actT = frA[1]
        sgs = [(frB[0][:, i * 512:(i + 1) * 512], Res()) for i in range(4)]
        sg_i = [0]
        last = (l == L - 1)
        ctxC = {}
        nt_ring = Ring(P, f"ntl{l}", 1, [128, 4], F32) if False else None

        def stageC2a(b):
            T0 = b * 512
            xnT, r_xnT = xnT_ring.next()
            ctxC[b] = (xnT, r_xnT)
            for sub in range(4):
                ti = b * 4 + sub
                ht, r_ht = ht_ring.next()
                DMA(ht[:], h1_d[T0 + sub * 128:T0 + (sub + 1) * 128, :], [r_h1d[ti]], [r_ht])
                norm_transpose(ht, r_ht, col(l, C_GFFN, 8), xnT, r_xnT, sub)
                yield

        def stageC2b(b):
            T0 = b * 512
            xnT, r_xnT = ctxC.pop(b)
            for fc in range(NFF):
                if fc % 3 == 0:
                    yield
                bkg, _, r_bkg = next_bank()
                bku, _, r_bku = next_bank()
                for k in range(8):
                    PE(lambda e, bkg=bkg, k=k, fc=fc, xnT=xnT: e.matmul(bkg[:, :], lhsT=wg[:, k, fc * 128:(fc + 1) * 128], rhs=xnT[:, k, :], start=(k == 0), stop=(k == 7)),
                       [r_wf, r_xnT], [r_bkg])
                for k in range(8):
                    PE(lambda e, bku=bku, k=k, fc=fc, xnT=xnT: e.matmul(bku[:, :], lhsT=wu[:, k, fc * 128:(fc + 1) * 128], rhs=xnT[:, k, :], start=(k == 0), stop=(k == 7)),
                       [r_wf, r_xnT], [r_bku])
                sgt, r_sgt = sgs[sg_i[0] % 4]
                sg_i[0] += 1
                ACT(lambda e, bkg=bkg, sgt=sgt: e.activation(out=sgt, in_=bkg[:, :], func=AF.Silu), [r_bkg], [r_sgt])
                VEC(lambda e, bku=bku, sgt=sgt, fc=fc: e.tensor_tensor(out=actT[:, fc, :], in0=bku[:, :], in1=sgt, op=ALU.mult), [r_bku, r_sgt], [r_actT])
            for sub in range(4):
                ti = b * 4 + sub
                ts_ = slice(sub * 128, (sub + 1) * 128)
                ht, r_ht = ht_ring.next()
                DMA(ht[:], h1_d[T0 + sub * 128:T0 + (sub + 1) * 128, :], [r_h1d[ti]], [r_ht])
                for hf in range(2):
                    yield
                    bk, _, r_bk = next_bank()
                    cs_ = slice(hf * 512, (hf + 1) * 512)
                    for fc in range(NFF):
                        PE(lambda e, bk=bk, fc=fc, ts_=ts_, cs_=cs_: e.matmul(bk[:, :], lhsT=actT[:, fc, ts_], rhs=wd[:, fc, cs_], start=(fc == 0), stop=(fc == NFF - 1)),
                           [r_actT, r_wf], [r_bk])
                    VEC(lambda e, bk=bk, ht=ht, cs_=cs_: e.tensor_tensor(out=ht[:, cs_], in0=bk[:, :], in1=ht[:, cs_], op=ALU.add), [r_bk, r_ht], [r_ht])
                if not last:
                    DMA(h_d[T0 + sub * 128:T0 + (sub + 1) * 128, :], ht[:], [r_ht], [r_hd[ti]])
                else:
                    st, r_st = st_ring.next()
                    rms_stats(ht[:], r_ht, D, st, r_st, 0)
                    VEC(lambda e, ht=ht, st=st: e.scalar_tensor_tensor(out=ht[:], in0=ht[:], scalar=st[:, 0:1], in1=fing[:], op0=ALU.mult, op1=ALU.mult),
                        [r_ht, r_st, r_const], [r_ht])
                    final_ops.append(DMA(out_d[T0 + sub * 128:T0 + (sub + 1) * 128, :], ht[:], [r_ht], []))


        def run_il(gens):
            gens = list(gens)
            while gens:
                for g_ in list(gens):
                    try:
                        next(g_)
                    except StopIteration:
                        gens.remove(g_)

        for b in range(NB + 1):
            gl = []
            if b >= 1:
                gl.append(stageC2b(b - 1))
            if b < NB:
                gl.append(stageC2a(b))
            run_il(gl)
        return False

    for l_ in range(n_layers):
        if emit_layer(l_):
            break

    if debug:
        def dump(name, src, shape, dt, r):
            final_ops.append(DMA(dbg_out(name, shape, dt), src, [r], []))
        P.barrier()
        dump("qT", qT_d[:, :, 3584:4096], [8, 96, 512], BF16, r_qd)
        dump("kT", kT_d[:, :, 3584:4096], [8, 96, 512], BF16, r_kd)
        dump("V", V_d[:, :, 28:32, :], [8, 128, 4, 65], BF16, r_vd)
        dump("uT", uT_d[:, :, 3584:4096], [128, 4, 512], BF16, r_ud)
        dump("aT0", aT_d[0], [64, 8, 512], F32, r_ad)
        dump("aT7", aT_d[7], [64, 8, 512], F32, r_ad)
        dump("yT", yT_d[:, :, 3584:4096], [128, 4, 512], F32, r_yd)
        dump("yT0", yT_d[:, :, 0:512], [128, 4, 512], F32, r_yd)
        dump("sT7", sT_d[7], [128, 4, 512], BF16, r_sd)
        dump("sT0", sT_d[0], [128, 4, 512], BF16, r_sd)
        dump("h1", h1_d[3968:4096, :], [128, D], F32, r_h1d[31])
        dump("h", h_d[3968:4096, :], [128, D], F32, r_hd[31])
    P.barrier()
    return P.build(final_waits=final_ops), dbg


def prep_inputs(inp):
    f = np.float32
    g = lambda k: np.asarray(inp[k])
    cols = np.zeros((128, L, NCOL), f)
    for l in range(L):
        cols[:, l, C_GMIX:C_GMIX + 8] = g("norm_mix_g")[l].reshape(8, 128).T
        cols[:, l, C_GX:C_GX + 8] = g("norm_x_g")[l].reshape(8, 128).T
        cols[:, l, C_GFFN:C_GFFN + 8] = g("norm_ffn_g")[l].reshape(8, 128).T
        cols[:, l, C_GMEM:C_GMEM + 8] = g("mem_norm_g")[l].reshape(8, 128).T
        cols[:, l, C_GQ:C_GQ + 2] = g("q_norm_g")[l].reshape(2, 128).T
        cols[:, l, C_GKV] = g("kv_norm_g")[l]
        cols[:, l, C_GS:C_GS + 4] = g("ssm_out_g")[l].reshape(4, 128).T
        cols[:, l, C_D:C_D + 4] = g("ssm_d")[l].reshape(4, 128).T
        cols[:, l, C_BGLU:C_BGLU + 4] = g("ssm_b_glu")[l].reshape(4, 128).T
        cols[0:64, l, C_GA:C_GA + 8] = g("attn_out_g")[l].reshape(8, 64).T
        cols[:, l, C_GA2:C_GA2 + 4] = g("attn_out_g")[l].reshape(4, 2, 64).transpose(1, 2, 0).reshape(128, 4)
    consts = np.zeros((128, 2), f)
    freqs = (np.float32(10000.0) ** (-np.arange(0, 32, 2, dtype=np.float32) / np.float32(32))).astype(f)
    consts[64:80, 0] = freqs
    consts[80:96, 0] = freqs
    consts[64:80, 1] = -SIN_SCALE
    consts[80:96, 1] = SIN_SCALE
    idx = np.arange(128) // 16
    mask16 = (idx[:, None] == idx[None, :]).astype(f)
    w_in = g("w_in")
    w_kr = np.zeros((L, D, 192), f)
    w_kr[:, :, 64:96] = w_in[:, :, 384:416]
    w_kr[:, :, 160:176] = w_in[:, :, 400:416]
    w_kr[:, :, 176:192] = w_in[:, :, 384:400]
    w_uq = g("w_uq")
    w_uqs = np.zeros_like(w_uq)
    for h in range(8):
        w_uqs[:, :, h * 96 + 64:h * 96 + 80] = w_uq[:, :, h * 96 + 80:h * 96 + 96]
        w_uqs[:, :, h * 96 + 80:h * 96 + 96] = w_uq[:, :, h * 96 + 64:h * 96 + 80]
    w_ukv = g("w_ukv").reshape(L, 128, 8, 2, 64).transpose(0, 1, 3, 2, 4).reshape(L, 128, 1024)

    def pair_layout(a):
        sh = a.shape
        a = a.reshape(L, 16, 2, 64, *sh[3:])
        a = np.moveaxis(a, 1, 3)
        return a.reshape(L, 128, 16, *sh[3:])

    lam_re = pair_layout(g("ssm_lambda_re"))
    lam_im = pair_layout(g("ssm_lambda_im"))
    logdt = pair_layout(np.repeat(g("ssm_log_dt")[:, :, None], 64, axis=2))
    s5v = np.stack([lam_re, lam_im, logdt], axis=2)
    b_re = pair_layout(g("ssm_b_re"))
    b_im = pair_layout(g("ssm_b_im"))
    s5b = np.stack([b_re, b_im], axis=2)
    c_re = pair_layout(np.swapaxes(g("ssm_c_re"), 2, 3))
    c_im = pair_layout(np.swapaxes(g("ssm_c_im"), 2, 3))
    s5c = np.stack([c_re, c_im], axis=2)
    common = dict(
        cols=cols, consts=consts, mask16=mask16, fing=g("final_norm_g").reshape(1, D).astype(f),
        w_in=w_in, w_kr=w_kr, w_uq=w_uq, w_uqs=w_uqs, w_ukv=np.ascontiguousarray(w_ukv),
        s5v=np.ascontiguousarray(s5v), s5b=np.ascontiguousarray(s5b), s5c=np.ascontiguousarray(s5c),
        w_glu=g("ssm_w_glu"), w_out=g("w_out"), w_xq=g("w_xq"), w_xkv=g("w_xkv"), w_xo=g("w_xo"),
        w_gate=g("w_gate"), w_up=g("w_up"), w_down=g("w_down"),
    )
    common = {k: np.ascontiguousarray(v, dtype=f) for k, v in common.items()}
    x = g("x")
    mem = g("mem")
    pos = g("positions").astype(np.int32)
    per_core = []
    for c in range(x.shape[0]):
        d = dict(common)
        d["x"] = np.ascontiguousarray(x[c], dtype=f)
        d["mem"] = np.ascontiguousarray(mem[c], dtype=f)
        d["pos"] = np.ascontiguousarray(pos[c].reshape(1, S))
        per_core.append(d)
    return per_core


def kernel(**inputs):
    per_core = prep_inputs(inputs)
    nc, _ = build_program()
    res = run_bass_kernel_spmd(nc, per_core, core_ids=list(range(8)))
    return np.stack([np.asarray(r["out"], dtype=np.float32) for r in res.results], axis=0)
```

```python
import math
import numpy as np
import concourse.bass as bass
import concourse.mybir as mybir
from concourse.bass_utils import run_bass_kernel_spmd

F32 = mybir.dt.float32
BF16 = mybir.dt.bfloat16
I32 = mybir.dt.int32
AF = mybir.ActivationFunctionType
ALU = mybir.AluOpType

ENGS = ["tensor", "vector", "scalar", "gpsimd", "sync"]

L = 2
S = 4096
D = 1024
NB = 8
EPS = 1e-6
DFF = 2816
NFF = 22
TWO_PI = 2.0 * math.pi
SIN_SCALE = 6.2831845
MAGIC = 12582912.0
ATT_SCALE = 96.0 ** -0.5
X_SCALE = 256.0 ** -0.5
NCOL = 59
C_GMIX, C_GX, C_GFFN, C_GMEM, C_GQ, C_GKV, C_GS, C_D, C_BGLU, C_GA, C_GA2 = 0, 8, 16, 24, 32, 34, 35, 39, 43, 47, 55


class Res:
    __slots__ = ("name", "last_w", "readers")

    def __init__(self, name=""):
        self.name = name
        self.last_w = None
        self.readers = []


class Op:
    __slots__ = ("eng", "fn", "waits", "dma", "sem", "val")


class Prog:
    def __init__(self, n_dma_sems=24):
        self.nc = bass.Bass("TRN2", target_bir_lowering=False)
        self.ops = {e: [] for e in ENGS}
        self.n_dma_sems = n_dma_sems
        self.wm = {e: {} for e in ENGS}
        self.dma_rr = {e: 0 for e in ENGS}
        self.dma_cnt = {}
        self.dma_last = {}
        self.eng_cnt = {}
        self.pending = {e: {} for e in ENGS}
        self._ctx = []

    def sbuf(self, name, shape, dtype):
        g = self.nc.sbuf_tensor("sb_" + name, list(shape), dtype)
        h = g.__enter__()
        self._ctx.append(g)
        return h

    def psum(self, name, shape, dtype):
        g = self.nc.psum_tensor("ps_" + name, list(shape), dtype)
        h = g.__enter__()
        self._ctx.append(g)
        return h

    def _need(self, op, dep):
        if dep is None:
            return
        if dep.eng == "tensor" and op.eng == "tensor" and not dep.dma and not op.dma:
            return
        if dep.val > op.waits.get(dep.sem, 0):
            op.waits[dep.sem] = dep.val

    def barrier(self):
        cur = {}
        for e, c in self.eng_cnt.items():
            cur[("eng", e)] = c
        for k, c in self.dma_cnt.items():
            cur[k] = 16 * c
        for e in ENGS:
            pe = self.pending[e]
            for k, v in cur.items():
                if v > pe.get(k, 0):
                    pe[k] = v

    def op(self, eng, fn, reads=(), writes=(), dma=False):
        o = Op()
        o.eng = eng
        o.fn = fn
        o.dma = dma
        o.waits = {}
        if self.pending[eng]:
            o.waits.update(self.pending[eng])
            self.pending[eng] = {}
        if dma:
            slot = self.dma_rr[eng] % self.n_dma_sems
            self.dma_rr[eng] += 1
            key = ("dma", eng, slot)
            prev = self.dma_last.get(key)
            cnt = self.dma_cnt.get(key, 0) + 1
            self.dma_cnt[key] = cnt
            o.sem = key
            o.val = 16 * cnt
            if prev is not None:
                self._need(o, prev)
            self.dma_last[key] = o
        else:
            o.sem = ("eng", eng)
            self.eng_cnt[eng] = self.eng_cnt.get(eng, 0) + 1
            o.val = self.eng_cnt[eng]
        for r in reads:
            self._need(o, r.last_w)
        for w in writes:
            self._need(o, w.last_w)
            for rd in w.readers:
                self._need(o, rd)
        for r in reads:
            r.readers.append(o)
        for w in writes:
            w.last_w = o
            w.readers = []
        wm = self.wm[eng]
        for k in list(o.waits):
            if wm.get(k, 0) >= o.waits[k]:
                del o.waits[k]
            else:
                wm[k] = o.waits[k]
        self.ops[eng].append(o)
        return o

    def build(self, final_waits=()):
        nc = self.nc
        sems = {}
        for e in ENGS:
            for o in self.ops[e]:
                if o.sem not in sems:
                    g = nc.semaphore("s_" + "_".join(str(x) for x in o.sem))
                    sems[o.sem] = g.__enter__()
                    self._ctx.append(g)
        fin = {}
        for o in final_waits:
            fin[o.sem] = max(fin.get(o.sem, 0), o.val)
        with nc.Block() as block:
            def make(e):
                def body(engobj):
                    for o in self.ops[e]:
                        for k, v in o.waits.items():
                            engobj.wait_ge(sems[k], v)
                        ins = o.fn(engobj)
                        ins.then_inc(sems[o.sem], 16 if o.dma else 1)
                    if e == "sync":
                        for k, v in fin.items():
                            engobj.wait_ge(sems[k], v)
                return body
            for e in ENGS:
                if self.ops[e] or e == "sync":
                    getattr(block, e)(make(e))
        return nc


class Ring:
    def __init__(self, P, name, n, shape, dtype):
        self.bufs = [(P.sbuf(f"{name}{i}", shape, dtype), Res(f"{name}{i}")) for i in range(n)]
        self.i = 0

    def next(self):
        b = self.bufs[self.i % len(self.bufs)]
        self.i += 1
        return b


def build_program(debug=False, n_layers=L, stop_after=None):
    P = Prog()
    nc = P.nc

    def din(name, shape, dt=F32):
        return nc.dram_tensor(name, list(shape), dt, kind="ExternalInput").ap()

    def dscr(name, shape, dt):
        return nc.dram_tensor(name, list(shape), dt, kind="Internal").ap()

    x_d = din("x", [S, D])
    mem_d = din("mem", [256, D])
    pos_d = din("pos", [1, S], I32)
    cols_d = din("cols", [128, L, NCOL])
    consts_d = din("consts", [128, 2])
    mask16_d = din("mask16", [128, 128])
    fing_d = din("fing", [1, D])
    w_in_d = din("w_in", [L, D, 928])
    w_kr_d = din("w_kr", [L, D, 192])
    w_uq_d = din("w_uq", [L, 256, 768])
    w_uqs_d = din("w_uqs", [L, 256, 768])
    w_ukv_d = din("w_ukv", [L, 128, 1024])
    s5v_d = din("s5v", [L, 128, 3, 16])
    s5b_d = din("s5b", [L, 128, 2, 16, 16])
    s5c_d = din("s5c", [L, 128, 2, 16, 16])
    w_glu_d = din("w_glu", [L, 512, 512])
    w_out_d = din("w_out", [L, D, D])
    w_xq_d = din("w_xq", [L, D, D])
    w_xkv_d = din("w_xkv", [L, D, 2 * D])
    w_xo_d = din("w_xo", [L, D, D])
    w_gate_d = din("w_gate", [L, D, DFF])
    w_up_d = din("w_up", [L, D, DFF])
    w_down_d = din("w_down", [L, DFF, D])
    out_d = nc.dram_tensor("out", [S, D], F32, kind="ExternalOutput").ap()

    h_d = dscr("h_scr", [S, D], F32)
    h1_d = dscr("h1_scr", [S, D], F32)
    qT_d = dscr("qT_scr", [8, 96, S], BF16)
    kT_d = dscr("kT_scr", [8, 96, S], BF16)
    V_d = dscr("V_scr", [8, 128, 32, 65], BF16)
    uT_d = dscr("uT_scr", [128, 4, S], BF16)
    yT_d = dscr("yT_scr", [128, 4, S], F32)
    aT_d = dscr("aT_scr", [NB, 64, 8, 512], F32)
    sT_d = dscr("sT_scr", [NB, 128, 4, 512], BF16)
    cos_d = dscr("cos_scr", [32, S], F32)
    sin_d = dscr("sin_scr", [32, S], F32)
    r_hd = [Res() for _ in range(32)]
    r_h1d = [Res() for _ in range(32)]
    r_qd, r_kd, r_vd, r_ud, r_yd = Res(), Res(), Res(), Res(), Res()
    r_ad, r_sd, r_csd = Res(), Res(), Res()

    dbg = {}

    def dbg_out(name, shape, dt=F32):
        t = nc.dram_tensor("dbg_" + name, list(shape), dt, kind="ExternalOutput").ap()
        dbg[name] = t
        return t

    final_ops = []

    def VEC(fn, reads, writes):
        return P.op("vector", fn, reads, writes)

    def ACT(fn, reads, writes):
        return P.op("scalar", fn, reads, writes)

    def POOL(fn, reads, writes):
        return P.op("gpsimd", fn, reads, writes)

    def PE(fn, reads, writes):
        return P.op("tensor", fn, reads, writes)

    def DMA(out, in_, reads, writes, q="sync"):
        return P.op(q, lambda e: e.dma_start(out=out, in_=in_), reads, writes, dma=True)

    def DMAC(out, in_, reads, writes):
        return P.op("gpsimd", lambda e: e.dma_start(out=out, in_=in_), reads, writes, dma=True)

    banks = []
    for i in range(8):
        t = P.psum(f"bank{i}", [128, 512], F32)
        banks.append((t, t.bitcast(BF16), Res(f"bank{i}")))
    bank_rr = [0]

    def next_bank(pool=(0, 1, 2, 3, 4, 5, 6, 7)):
        i = pool[bank_rr[0] % len(pool)]
        bank_rr[0] += 1
        return banks[i]

    identf = P.sbuf("identf", [128, 128], F32)
    ident = P.sbuf("ident", [128, 128], BF16)
    onesf = P.sbuf("onesf", [128, 128], F32)
    sel65 = P.sbuf("sel65", [65, 64], F32)
    onesb = P.sbuf("onesb", [128, 128], BF16)
    fing = P.sbuf("fing", [128, D], F32)
    mask16 = P.sbuf("mask16", [128, 128], F32)
    cols = P.sbuf("cols", [128, L, NCOL], F32)
    consts = P.sbuf("consts", [128, 2], F32)
    r_const = Res("const")
    POOL(lambda e: e.memset(identf[:], 1.0), [], [r_const])
    POOL(lambda e: e.affine_select(out=identf[:], in_=identf[:], pattern=[[-1, 128]], compare_op=ALU.is_equal,
                                   fill=0.0, base=0, channel_multiplier=1), [r_const], [r_const])
    VEC(lambda e: e.tensor_copy(out=ident[:], in_=identf[:]), [r_const], [r_const])
    VEC(lambda e: e.memset(onesf[:], 1.0), [], [r_const])
    VEC(lambda e: e.memset(onesb[:], 1.0), [], [r_const])
    DMA(fing[:], fing_d[0:1, :].to_broadcast([128, D]), [], [r_const])
    VEC(lambda e: e.memset(sel65[:], 0.0), [], [r_const])
    VEC(lambda e: e.memset(sel65[64:65, :], 1.0), [r_const], [r_const])
    DMA(mask16[:], mask16_d[:, :], [], [r_const])
    DMA(cols[:], cols_d[:, :, :], [], [r_const])
    DMA(consts[:], consts_d[:, :], [], [r_const])

    def col(l, c, n=1, p0=0, p1=128):
        return cols[p0:p1, l, c:c + n]

    ARENA_F32 = 33792
    arena_f = P.sbuf("arena", [128, ARENA_F32], F32)
    arena_b = arena_f.bitcast(BF16)
    arena_i = arena_f.bitcast(I32)
    arena_off = [0]

    def areset():
        arena_off[0] = 0

    def aalloc(shape, dt, name=""):
        n = 1
        for s_ in shape[1:]:
            n *= s_
        esz = 2 if dt == BF16 else 4
        off = (arena_off[0] + 3) // 4 * 4
        arena_off[0] = off + n * esz
        assert arena_off[0] <= ARENA_F32 * 4, (name, arena_off[0])
        base = {BF16: arena_b, F32: arena_f, I32: arena_i}[dt]
        e0 = off // esz
        ap = base[0:shape[0], e0:e0 + n]
        if len(shape) == 3:
            ap = ap.rearrange("p (a b) -> p a b", b=shape[2])
        elif len(shape) == 4:
            ap = ap.rearrange("p (a b c) -> p a b c", b=shape[2], c=shape[3])
        elif len(shape) == 5:
            ap = ap.rearrange("p (a b c d) -> p a b c d", b=shape[2], c=shape[3], d=shape[4])
        return ap, Res(name)

    class ARing:
        def __init__(self, n, shape, dt, name=""):
            self.bufs = [aalloc(shape, dt, f"{name}{i}") for i in range(n)]
            self.i = 0

        def next(self):
            b = self.bufs[self.i % len(self.bufs)]
            self.i += 1
            return b

    ht_ring = Ring(P, "ht", 2, [128, D], F32)
    junk_ring = Ring(P, "junk", 1, [128, D], BF16)
    xn_ring = Ring(P, "xn", 2, [128, D], BF16)
    xnT_ring = Ring(P, "xnT", 2, [128, 8, 512], BF16)
    st_ring = Ring(P, "st", 8, [128, 4], F32)
    pT_ring = Ring(P, "pT", 4, [128, 512], BF16)
    frA = (P.sbuf("frA", [128, 5632], F32), Res("frA"))
    frB = (P.sbuf("frB", [128, 2048], F32), Res("frB"))

    def rms_stats(src_ap, r_src, nfeat, st, r_st, c0):
        junk, r_junk = junk_ring.next()
        n = src_ap.shape[-1]
        ACT(lambda e: e.activation(out=junk[:, 0:n], in_=src_ap, func=AF.Square, accum_out=st[:, c0:c0 + 1]), [r_src], [r_junk, r_st])
        ACT(lambda e: e.activation(out=st[:, c0:c0 + 1], in_=st[:, c0:c0 + 1], func=AF.Sqrt, scale=1.0 / nfeat, bias=EPS), [r_st], [r_st])
        VEC(lambda e: e.reciprocal(out=st[:, c0:c0 + 1], in_=st[:, c0:c0 + 1]), [r_st], [r_st])

    def norm_transpose(ht, r_ht, gcol, xnT, r_xnT, sub):
        st, r_st = st_ring.next()
        rms_stats(ht[:], r_ht, D, st, r_st, 0)
        xn, r_xn = xn_ring.next()
        VEC(lambda e: e.tensor_scalar(out=xn[:], in0=ht[:], scalar1=st[:, 0:1], scalar2=None, op0=ALU.mult), [r_ht, r_st], [r_xn])
        bk, bkb, r_bk = next_bank()
        bkv = bkb[:, :].rearrange("p (a b) -> p a b", b=128)
        for k in range(8):
            PE(lambda e, k=k: e.transpose(out=bkv[:, k, :], in_=xn[:, k * 128:(k + 1) * 128], identity=ident[:]), [r_xn, r_const], [r_bk])
        VEC(lambda e: e.tensor_tensor(out=xnT[:, :, sub * 128:(sub + 1) * 128], in0=bkv, in1=gcol.to_broadcast([128, 8, 128]), op=ALU.mult),
            [r_bk, r_const], [r_xnT])

    RS = slice(64, 96)

    areset()
    posi, r_ra = aalloc([96, S], I32, "posi")
    tmpa, _ = aalloc([96, S], F32, "tmpa")
    tmpb, _ = aalloc([96, S], F32, "tmpb")
    cs_t, r_cs = aalloc([96, S], F32, "cs")
    sn_t, _ = aalloc([96, S], F32, "sn")
    DMA(posi[RS, :], pos_d[0:1, :].to_broadcast([32, S]), [], [r_ra])
    VEC(lambda e: e.tensor_copy(out=tmpa[RS, :], in_=posi[RS, :]), [r_ra], [r_ra])
    VEC(lambda e: e.tensor_scalar(out=tmpa[RS, :], in0=tmpa[RS, :], scalar1=consts[RS, 0:1], scalar2=1.0 / TWO_PI,
                                  op0=ALU.mult, op1=ALU.mult), [r_ra, r_const], [r_ra])
    VEC(lambda e: e.tensor_scalar(out=tmpb[RS, :], in0=tmpa[RS, :], scalar1=MAGIC, scalar2=MAGIC, op0=ALU.add, op1=ALU.subtract), [r_ra], [r_ra])
    VEC(lambda e: e.tensor_tensor(out=tmpb[RS, :], in0=tmpa[RS, :], in1=tmpb[RS, :], op=ALU.subtract), [r_ra], [r_ra])
    ACT(lambda e: e.activation(out=sn_t[RS, :], in_=tmpb[RS, :], func=AF.Sin, scale=consts[RS, 1:2]), [r_ra, r_const], [r_cs])
    VEC(lambda e: e.tensor_scalar(out=tmpa[RS, :], in0=tmpa[RS, :], scalar1=0.25, scalar2=None, op0=ALU.add), [r_ra, r_cs], [r_ra])
    VEC(lambda e: e.tensor_scalar(out=tmpb[RS, :], in0=tmpa[RS, :], scalar1=MAGIC, scalar2=MAGIC, op0=ALU.add, op1=ALU.subtract), [r_ra], [r_ra])
    VEC(lambda e: e.tensor_tensor(out=tmpb[RS, :], in0=tmpa[RS, :], in1=tmpb[RS, :], op=ALU.subtract), [r_ra], [r_ra])
    ACT(lambda e: e.activation(out=cs_t[RS, :], in_=tmpb[RS, :], func=AF.Sin, scale=SIN_SCALE), [r_ra], [r_cs])
    DMA(cos_d[:, :], cs_t[RS, :], [r_cs], [r_csd])
    DMA(sin_d[:, :], sn_t[RS, :], [r_cs], [r_csd])

    def emit_layer(l):
        src_d = x_d if l == 0 else h_d
        r_src = [Res() for _ in range(32)] if l == 0 else r_hd
        P.barrier()
        areset()
        w_in_sb, r_w = aalloc([128, 8, 928], BF16, "w_in")
        w_kr_sb, _ = aalloc([128, 8, 192], BF16)
        w_uq_sb, _ = aalloc([128, 2, 768], BF16)
        w_uqs_sb, _ = aalloc([128, 2, 768], BF16)
        w_ukv_sb, _ = aalloc([128, 1024], BF16)
        DMAC(w_in_sb, w_in_d[l].rearrange("(k p) n -> p k n", p=128), [], [r_w])
        DMAC(w_kr_sb, w_kr_d[l].rearrange("(k p) n -> p k n", p=128), [], [r_w])
        DMAC(w_uq_sb, w_uq_d[l].rearrange("(k p) n -> p k n", p=128), [], [r_w])
        DMAC(w_uqs_sb, w_uqs_d[l].rearrange("(k p) n -> p k n", p=128), [], [r_w])
        DMAC(w_ukv_sb, w_ukv_d[l], [], [r_w])
        qs, r_qs = aalloc([96, 8, 512], F32, "qs")
        qsw, r_qsw = aalloc([96, 8, 512], F32, "qsw")
        qb_ring = ARing(2, [96, 8, 512], BF16, "qb")
        kb_ring = ARing(2, [96, 8, 512], BF16, "kb")
        vb_ring = ARing(2, [128, 8, 4, 65], BF16, "vb")
        ub_ring = ARing(2, [128, 4, 512], BF16, "ub")
        cT_ring = ARing(2, [128, 3, 512], BF16, "cT")
        cs_ring = ARing(2, [96, 2, 512], F32, "csb")
        ta, r_ta = aalloc([96, 1024], F32, "ta")
        for vb, r_vb in vb_ring.bufs:
            VEC(lambda e, vb=vb: e.memset(vb[:, :, :, 64:65], 1.0), [], [r_vb])

        ctxA = {}

        htx = [(frB[0][:, 0:1024], Res("htx0")), (frB[0][:, 1024:2048], Res("htx1"))]
        frA_b = frA[0].bitcast(BF16)
        xnx = [(frA_b[:, 0:1024], Res("xnx0")), (frA_b[:, 1024:2048], Res("xnx1"))]
        jkx = [(frA_b[:, 2048 + i * 1024:2048 + (i + 1) * 1024], Res(f"jkx{i}")) for i in range(4)]

        def stageA1(b):
            T0 = b * 512
            xnT, r_xnT = xnT_ring.next()
            cT, r_cT = cT_ring.next()
            vb, r_vb = vb_ring.next()
            csb, r_csb = cs_ring.next()
            ctxA[b] = (xnT, r_xnT, cT, r_cT, csb, r_csb)
            DMA(csb[RS, 0, :], cos_d[:, T0:T0 + 512], [r_csd], [r_csb])
            DMA(csb[RS, 1, :], sin_d[:, T0:T0 + 512], [r_csd], [r_csb])
            hts = [ht_ring.next(), ht_ring.next(), htx[0], htx[1]]
            xns = [xn_ring.next(), xn_ring.next(), xnx[0], xnx[1]]
            sts = [st_ring.next() for _ in range(4)]
            gcol = col(l, C_GMIX, 8)
            gq = col(l, C_GQ, 3)
            for sub in range(4):
                ht, r_ht = hts[sub]
                DMA(ht[:, :], src_d[T0 + sub * 128:T0 + (sub + 1) * 128, :], [r_src[b * 4 + sub]], [r_ht])
            yield
            for sub in range(4):
                (ht, r_ht), (st, r_st), (jk, r_jk) = hts[sub], sts[sub], jkx[sub]
                ACT(lambda e, ht=ht, st=st, jk=jk: e.activation(out=jk, in_=ht[:, :], func=AF.Square, accum_out=st[:, 0:1]), [r_ht], [r_jk, r_st])
            for sub in range(4):
                st, r_st = sts[sub]
                ACT(lambda e, st=st: e.activation(out=st[:, 0:1], in_=st[:, 0:1], func=AF.Sqrt, scale=1.0 / D, bias=EPS), [r_st], [r_st])
            yield
            for sub in range(4):
                st, r_st = sts[sub]
                VEC(lambda e, st=st: e.reciprocal(out=st[:, 0:1], in_=st[:, 0:1]), [r_st], [r_st])
            for sub in range(4):
                (ht, r_ht), (st, r_st), (xn, r_xn) = hts[sub], sts[sub], xns[sub]
                VEC(lambda e, ht=ht, st=st, xn=xn: e.tensor_scalar(out=xn[:, :], in0=ht[:, :], scalar1=st[:, 0:1], scalar2=None, op0=ALU.mult),
                    [r_ht, r_st], [r_xn])
            yield
            bks = []
            for sub in range(4):
                xn, r_xn = xns[sub]
                bk, bkb, r_bk = next_bank()
                bkv = bkb[:, :].rearrange("p (a b) -> p a b", b=128)
                for k in range(8):
                    PE(lambda e, k=k, bkv=bkv, xn=xn: e.transpose(out=bkv[:, k, :], in_=xn[:, k * 128:(k + 1) * 128], identity=ident[:]), [r_xn, r_const], [r_bk])
                bks.append((bkv, r_bk))
                if sub % 2 == 1:
                    yield
            for sub in range(4):
                bkv, r_bk = bks[sub]
                VEC(lambda e, bkv=bkv, sub=sub: e.tensor_tensor(out=xnT[:, :, sub * 128:(sub + 1) * 128], in0=bkv, in1=gcol.to_broadcast([128, 8, 128]), op=ALU.mult),
                    [r_bk, r_const], [r_xnT])
            yield
            cbk = []
            for sub in range(4):
                bk, bkb, r_bk = next_bank()
                for k in range(8):
                    PE(lambda e, k=k, bk=bk, sub=sub: e.matmul(bk[:, 0:384], lhsT=xnT[:, k, sub * 128:(sub + 1) * 128], rhs=w_in_sb[:, k, 0:384],
                                                              start=(k == 0), stop=(k == 7)), [r_xnT, r_w], [r_bk])
                cbk.append((bk, r_bk))
                if sub % 2 == 1:
                    yield
            for sub in range(4):
                (bk, r_bk), (st, r_st), (jk, r_jk) = cbk[sub], sts[sub], jkx[sub]
                ACT(lambda e, bk=bk, st=st, jk=jk: e.activation(out=jk[:, 0:256], in_=bk[:, 0:256], func=AF.Square, accum_out=st[:, 1:2]), [r_bk], [r_jk, r_st])
                ACT(lambda e, bk=bk, st=st, jk=jk: e.activation(out=jk[:, 256:384], in_=bk[:, 256:384], func=AF.Square, accum_out=st[:, 2:3]), [r_bk], [r_jk, r_st])
            yield
            for sub in range(4):
                st, r_st = sts[sub]
                ACT(lambda e, st=st: e.activation(out=st[:, 1:2], in_=st[:, 1:2], func=AF.Sqrt, scale=1.0 / 256, bias=EPS), [r_st], [r_st])
                ACT(lambda e, st=st: e.activation(out=st[:, 2:3], in_=st[:, 2:3], func=AF.Sqrt, scale=1.0 / 128, bias=EPS), [r_st], [r_st])
            for sub in range(4):
                st, r_st = sts[sub]
                VEC(lambda e, st=st: e.reciprocal(out=st[:, 1:3], in_=st[:, 1:3]), [r_st], [r_st])
            yield
            for sub in range(4):
                (bk, r_bk), (st, r_st), (cn, r_cn) = cbk[sub], sts[sub], xns[sub]
                VEC(lambda e, bk=bk, cn=cn, st=st: e.tensor_scalar(out=cn[:, 0:256], in0=bk[:, 0:256], scalar1=st[:, 1:2], scalar2=None, op0=ALU.mult),
                    [r_bk, r_st], [r_cn])
                VEC(lambda e, bk=bk, cn=cn, st=st: e.tensor_scalar(out=cn[:, 256:384], in0=bk[:, 256:384], scalar1=st[:, 2:3], scalar2=None, op0=ALU.mult),
                    [r_bk, r_st], [r_cn])
            yield
            bk2s = []
            for sub in range(4):
                cn, r_cn = xns[sub]
                bk2, bk2b, r_bk2 = next_bank()
                bk2v = bk2b[:, :].rearrange("p (a b) -> p a b", b=128)
                for k in range(3):
                    PE(lambda e, k=k, bk2v=bk2v, cn=cn: e.transpose(out=bk2v[:, k, :], in_=cn[:, k * 128:(k + 1) * 128], identity=ident[:]), [r_cn, r_const], [r_bk2])
                bk2s.append((bk2v, r_bk2))
            yield
            for sub in range(4):
                bk2v, r_bk2 = bk2s[sub]
                VEC(lambda e, bk2v=bk2v, sub=sub: e.tensor_tensor(out=cT[:, :, sub * 128:(sub + 1) * 128], in0=bk2v[:, 0:3, :],
                                                                 in1=gq.to_broadcast([128, 3, 128]), op=ALU.mult), [r_bk2, r_const], [r_cT])
            yield
            for sub in range(4):
                bk3, _, r_bk3 = next_bank()
                PE(lambda e, bk3=bk3, sub=sub: e.matmul(bk3[:, :], lhsT=cT[:, 2, sub * 128:(sub + 1) * 128], rhs=w_ukv_sb[:, 512:1024],
                                                       start=True, stop=True), [r_cT, r_w], [r_bk3])
                ACT(lambda e, bk3=bk3, sub=sub: e.copy(out=vb[:, :, sub, 0:64], in_=bk3[:, :].rearrange("p (h d) -> p h d", d=64)), [r_bk3], [r_vb])
                if sub % 2 == 1:
                    yield
            DMA(V_d[:, :, b * 4:(b + 1) * 4, :].rearrange("h p t d -> p h t d"), vb, [r_vb], [r_vd], q="scalar")
            yield

        def stageA2(b):
            T0 = b * 512
            xnT, r_xnT, cT, r_cT, csb, r_csb = ctxA.pop(b)
            ub, r_ub = ub_ring.next()
            for ct in range(4):
                yield
                bk, _, r_bk = next_bank()
                for k in range(8):
                    PE(lambda e, k=k, bk=bk, xnT=xnT, ct=ct: e.matmul(bk[:, :], lhsT=w_in_sb[:, k, 416 + ct * 128:416 + (ct + 1) * 128],
                                                                      rhs=xnT[:, k, :], start=(k == 0), stop=(k == 7)), [r_xnT, r_w], [r_bk])
                ACT(lambda e, bk=bk, ct=ct, ub=ub: e.copy(out=ub[:, ct, :], in_=bk[:, :]), [r_bk], [r_ub])
            DMA(uT_d[:, :, T0:T0 + 512], ub, [r_ub], [r_ud], q="scalar")
            yield
            bka, _, r_bka = next_bank()
            bkb_, _, r_bkb = next_bank()
            for k in range(8):
                PE(lambda e, k=k, bka=bka, xnT=xnT: e.matmul(bka[0:96, :], lhsT=w_kr_sb[:, k, 0:96], rhs=xnT[:, k, :], start=(k == 0), stop=(k == 7)),
                   [r_xnT, r_w], [r_bka])
            for k in range(8):
                PE(lambda e, k=k, bkb_=bkb_, xnT=xnT: e.matmul(bkb_[0:96, :], lhsT=w_kr_sb[:, k, 96:192], rhs=xnT[:, k, :], start=(k == 0), stop=(k == 7)),
                   [r_xnT, r_w], [r_bkb])
            yield
            VEC(lambda e, bka=bka, csb=csb: e.tensor_tensor(out=ta[RS, 0:512], in0=bka[RS, :], in1=csb[RS, 0, :], op=ALU.mult), [r_bka, r_csb], [r_ta])
            VEC(lambda e, bkb_=bkb_, csb=csb: e.tensor_tensor(out=ta[RS, 512:1024], in0=bkb_[RS, :], in1=csb[RS, 1, :], op=ALU.mult), [r_bkb, r_csb], [r_ta])
            VEC(lambda e: e.tensor_tensor(out=ta[RS, 0:512], in0=ta[RS, 0:512], in1=ta[RS, 512:1024], op=ALU.add), [r_ta], [r_ta])
            kb, r_kb = kb_ring.next()
            POOL(lambda e, kb=kb: e.tensor_copy(out=kb[RS, :, :], in_=ta[RS, 0:512].rearrange("p (o t) -> p o t", o=1).to_broadcast([32, 8, 512])),
                 [r_ta], [r_kb])
            for h in range(8):
                yield
                bk, _, r_bk = next_bank()
                PE(lambda e, bk=bk, h=h, cT=cT: e.matmul(bk[0:64, :], lhsT=w_ukv_sb[:, h * 64:(h + 1) * 64], rhs=cT[:, 2, :], start=True, stop=True),
                   [r_cT, r_w], [r_bk])
                ACT(lambda e, bk=bk, h=h, kb=kb: e.copy(out=kb[0:64, h, :], in_=bk[0:64, :]), [r_bk], [r_kb])
            DMA(kT_d[:, :, T0:T0 + 512].rearrange("h p t -> p h t"), kb, [r_kb], [r_kd], q="scalar")
            for h in range(8):
                yield
                bka, _, r_bka = next_bank()
                bkb_, _, r_bkb = next_bank()
                for k in range(2):
                    PE(lambda e, k=k, bka=bka, h=h, cT=cT: e.matmul(bka[0:96, :], lhsT=w_uq_sb[:, k, h * 96:(h + 1) * 96], rhs=cT[:, k, :],
                                                                    start=(k == 0), stop=(k == 1)), [r_cT, r_w], [r_bka])
                for k in range(2):
                    PE(lambda e, k=k, bkb_=bkb_, h=h, cT=cT: e.matmul(bkb_[0:96, :], lhsT=w_uqs_sb[:, k, h * 96:(h + 1) * 96], rhs=cT[:, k, :],
                                                                      start=(k == 0), stop=(k == 1)), [r_cT, r_w], [r_bkb])
                ACT(lambda e, bka=bka, h=h: e.copy(out=qs[:, h, :], in_=bka[0:96, :]), [r_bka], [r_qs])
                ACT(lambda e, bkb_=bkb_, h=h: e.copy(out=qsw[RS, h, :], in_=bkb_[RS, :]), [r_bkb], [r_qsw])
            yield
            qb, r_qb = qb_ring.next()
            POOL(lambda e, qb=qb: e.tensor_copy(out=qb[0:64, :, :], in_=qs[0:64, :, :]), [r_qs], [r_qb])
            VEC(lambda e, csb=csb: e.tensor_tensor(out=qs[RS, :, :], in0=qs[RS, :, :],
                                                   in1=csb[RS, 0:1, :].to_broadcast([32, 8, 512]), op=ALU.mult), [r_qs, r_csb], [r_qs])
            VEC(lambda e, csb=csb: e.tensor_tensor(out=qsw[RS, :, :], in0=qsw[RS, :, :],
                                                   in1=csb[RS, 1:2, :].to_broadcast([32, 8, 512]), op=ALU.mult), [r_qsw, r_csb], [r_qsw])
            VEC(lambda e, qb=qb: e.tensor_tensor(out=qb[RS, :, :], in0=qs[RS, :, :], in1=qsw[RS, :, :], op=ALU.add), [r_qs, r_qsw], [r_qb])
            DMA(qT_d[:, :, T0:T0 + 512].rearrange("h p t -> p h t"), qb, [r_qb], [r_qd])
            yield

        def run_interleaved(gens):
            gens = list(gens)
            while gens:
                for g_ in list(gens):
                    try:
                        next(g_)
                    except StopIteration:
                        gens.remove(g_)

        for b in range(NB + 1):
            gl = []
            if b >= 1:
                gl.append(stageA2(b - 1))
            if b < NB:
                gl.append(stageA1(b))
            run_interleaved(gl)

        if stop_after == "A":
            return True

        P.barrier()
        areset()
        S_POOL = (0, 1, 2, 3)
        O_POOL = (4, 5)
        M_POOL = (6, 7)
        pT_ringB = ARing(6, [128, 512], BF16, "pTB")
        qh_ring = ARing(2, [96, S], BF16, "qh")
        kh_ring = ARing(2, [96, S], BF16, "kh")
        vh_ring = ARing(2, [128, 32, 65], BF16, "vh")
        oT_ring = ARing(3, [65, 1024], F32, "oT3")
        an_ring = ARing(3, [64, 512], F32, "an3")
        LA = 3

        def load_head(h):
            qh, r_qh = qh_ring.next()
            kh, r_kh = kh_ring.next()
            vh, r_vh = vh_ring.next()
            DMA(qh, qT_d[h], [r_qd], [r_qh])
            DMA(kh, kT_d[h], [r_kd], [r_kh])
            DMA(vh, V_d[h], [r_vd], [r_vh])
            return (qh, r_qh, kh, r_kh, vh, r_vh)

        heads = {0: load_head(0)}
        for h in range(8):
            qh, r_qh, kh, r_kh, vh, r_vh = heads[h]
            if h + 1 < 8:
                heads[h + 1] = load_head(h + 1)
            items = [(b, kt) for b in range(NB) for kt in range(4 * (b + 1))]
            pts = {}
            bos = {}
            deferred = []

            def stage1(i):
                b, kt = items[i]
                T0 = b * 512
                bs, _, r_bs = next_bank(S_POOL)
                PE(lambda e, bs=bs, kt=kt, kh=kh, qh=qh, T0=T0: e.matmul(bs[:, :], lhsT=kh[:, kt * 128:(kt + 1) * 128], rhs=qh[:, T0:T0 + 512],
                                                                         start=True, stop=True), [r_kh, r_qh], [r_bs])
                pT, r_pT = pT_ringB.next()
                ACT(lambda e, bs=bs, pT=pT: e.activation(out=pT[:], in_=bs[:, :], func=AF.Exp, scale=ATT_SCALE), [r_bs], [r_pT])
                if kt >= 4 * b:
                    base = T0 - kt * 128
                    POOL(lambda e, pT=pT, base=base: e.affine_select(out=pT[:], in_=pT[:], pattern=[[1, 512]], compare_op=ALU.is_ge,
                                                                     fill=0.0, base=base, channel_multiplier=-1), [r_pT], [r_pT])
                pts[i] = (pT, r_pT)

            def stage2(j, i_now):
                b, kt = items[j]
                nkt = 4 * (b + 1)
                if kt == 0:
                    bos[b] = next_bank(O_POOL)
                bo, _, r_bo = bos[b]
                pT, r_pT = pts.pop(j)
                PE(lambda e, bo=bo, pT=pT, kt=kt, nkt=nkt, vh=vh: e.matmul(bo[0:65, :], lhsT=vh[:, kt, :], rhs=pT[:],
                                                                           start=(kt == 0), stop=(kt == nkt - 1)), [r_vh, r_pT], [r_bo])
                if kt == nkt - 1:
                    oT, r_oT = oT_ring.next()
                    VEC(lambda e, bo=bo, oT=oT: e.tensor_copy(out=oT[:, 0:512], in_=bo[0:65, :]), [r_bo], [r_oT])

                    def epi(b=b, oT=oT, r_oT=r_oT):
                        bm, _, r_bm = next_bank(M_POOL)
                        PE(lambda e, bm=bm, oT=oT: e.matmul(bm[0:64, :], lhsT=sel65[:, :], rhs=oT[:, 0:512], start=True, stop=True), [r_oT, r_const], [r_bm])
                        VEC(lambda e, bm=bm, oT=oT: e.reciprocal(out=oT[0:64, 512:1024], in_=bm[0:64, :]), [r_bm, r_oT], [r_oT])
                        an, r_an = an_ring.next()
                        POOL(lambda e, oT=oT, an=an: e.tensor_tensor(out=an[:, :], in0=oT[0:64, 0:512], in1=oT[0:64, 512:1024], op=ALU.mult), [r_oT], [r_an])
                        DMA(aT_d[b, :, h, :], an, [r_an], [r_ad], q="gpsimd")
                    deferred.append((i_now + 3, epi))

            n_it = len(items)
            for i in range(n_it + LA):
                if i < n_it:
                    stage1(i)
                if i - LA >= 0:
                    stage2(i - LA, i)
                while deferred and deferred[0][0] <= i:
                    deferred.pop(0)[1]()
            while deferred:
                deferred.pop(0)[1]()
        if stop_after == "B1":
            return True
        P.barrier()
        areset()
        Wst, r_Wst = aalloc([128, 4, 8, 2, 128], BF16, "Wst")
        Wfir, r_Wfir = aalloc([128, 4, 8, 128], BF16, "Wfir")
        Wo_r, r_Wo = aalloc([128, 16, 8, 32], BF16, "Wo")
        Wo_i, _ = aalloc([128, 16, 8, 32], BF16)
        sm, r_sm = aalloc([128, 32, 16], F32, "sm")
        pw_r, _ = aalloc([128, 16, 9], F32)
        pw_i, _ = aalloc([128, 16, 9], F32)
        ph_r, _ = aalloc([128, 16, 9], F32)
        ph_i, _ = aalloc([128, 16, 9], F32)
        mark = arena_off[0]
        Bri, r_bc = aalloc([128, 2, 16, 16], F32, "Bri")
        Cri, _ = aalloc([128, 2, 16, 16], F32)
        Bb_r, r_T = aalloc([128, 16, 16], F32, "T")
        Bb_i, _ = aalloc([128, 16, 16], F32)
        T1, _ = aalloc([128, 16, 16], F32)
        T2, _ = aalloc([128, 16, 16], F32)
        T3, _ = aalloc([128, 16, 16], F32)
        T4, _ = aalloc([128, 16, 16], F32)
        ME_r, r_ME = aalloc([128, 8, 4, 128], F32, "ME")
        ME_i, _ = aalloc([128, 8, 4, 128], F32)
        MF_r, r_MF = aalloc([128, 4, 128], F32, "MF")
        MF_in, _ = aalloc([128, 4, 128], F32)
        tmpF, r_tmpF = aalloc([128, 128], F32, "tmpF")
        DMA(sm[:, 0:3, :], s5v_d[l], [], [r_sm])
        DMA(Bri, s5b_d[l], [], [r_bc])
        DMA(Cri, s5c_d[l], [], [r_bc])
        POOL(lambda e: e.memset(ME_r, 0.0), [], [r_ME])
        POOL(lambda e: e.memset(ME_i, 0.0), [], [r_ME])
        POOL(lambda e: e.memset(MF_r, 0.0), [], [r_MF])
        POOL(lambda e: e.memset(MF_in, 0.0), [], [r_MF])
        POOL(lambda e: e.memset(Wo_r, 0.0), [], [r_Wo])
        POOL(lambda e: e.memset(Wo_i, 0.0), [], [r_Wo])
        LRE, LIM, LDT, DT, TT_, ER, Y, RND, FR, SN_, CS_, AR, AI, ARM1, DEN, RDEN, CBR, CBI, U1, U2, RDEC, RR = range(22)

        def smtt(o, a, b, op):
            VEC(lambda e: e.tensor_tensor(out=sm[:, o, :], in0=sm[:, a, :], in1=sm[:, b, :], op=op), [r_sm], [r_sm])

        def smts(o, a, s1, op0, s2=None, op1=None):
            if op1 is None:
                VEC(lambda e: e.tensor_scalar(out=sm[:, o, :], in0=sm[:, a, :], scalar1=s1, scalar2=None, op0=op0), [r_sm], [r_sm])
            else:
                VEC(lambda e: e.tensor_scalar(out=sm[:, o, :], in0=sm[:, a, :], scalar1=s1, scalar2=s2, op0=op0, op1=op1), [r_sm], [r_sm])

        def smact(o, a, func, scale=1.0):
            ACT(lambda e: e.activation(out=sm[:, o, :], in_=sm[:, a, :], func=func, scale=scale), [r_sm], [r_sm])

        smact(DT, LDT, AF.Exp)
        smtt(TT_, LRE, DT, ALU.mult)
        smact(ER, TT_, AF.Exp)
        smact(RDEC, TT_, AF.Exp, 8.0)
        smtt(Y, LIM, DT, ALU.mult)
        smts(Y, Y, 1.0 / TWO_PI, ALU.mult)
        smts(RND, Y, MAGIC, ALU.add, MAGIC, ALU.subtract)
        smtt(FR, Y, RND, ALU.subtract)
        smact(SN_, FR, AF.Sin, SIN_SCALE)
        smts(Y, Y, 0.25, ALU.add)
        smts(RND, Y, MAGIC, ALU.add, MAGIC, ALU.subtract)
        smtt(FR, Y, RND, ALU.subtract)
        smact(CS_, FR, AF.Sin, SIN_SCALE)
        smtt(AR, ER, CS_, ALU.mult)
        smtt(AI, ER, SN_, ALU.mult)
        smts(ARM1, AR, -1.0, ALU.add)
        smtt(U1, LRE, LRE, ALU.mult)
        smtt(U2, LIM, LIM, ALU.mult)
        smtt(DEN, U1, U2, ALU.add)
        VEC(lambda e: e.reciprocal(out=sm[:, RDEN, :], in_=sm[:, DEN, :]), [r_sm], [r_sm])
        smtt(U1, ARM1, LRE, ALU.mult)
        smtt(U2, AI, LIM, ALU.mult)
        smtt(U1, U1, U2, ALU.add)
        smtt(CBR, U1, RDEN, ALU.mult)
        smtt(U1, AI, LRE, ALU.mult)
        smtt(U2, ARM1, LIM, ALU.mult)
        smtt(U1, U1, U2, ALU.subtract)
        smtt(CBI, U1, RDEN, ALU.mult)
        VEC(lambda e: e.memset(pw_r[:, :, 0:1], 1.0), [r_sm], [r_sm])
        VEC(lambda e: e.memset(pw_i[:, :, 0:1], 0.0), [r_sm], [r_sm])
        VEC(lambda e: e.tensor_copy(out=pw_r[:, :, 1], in_=sm[:, AR, :]), [r_sm], [r_sm])
        VEC(lambda e: e.tensor_copy(out=pw_i[:, :, 1], in_=sm[:, AI, :]), [r_sm], [r_sm])
        for k in range(1, 8):
            VEC(lambda e, k=k: e.tensor_tensor(out=sm[:, U1, :], in0=pw_r[:, :, k], in1=sm[:, AR, :], op=ALU.mult), [r_sm], [r_sm])
            VEC(lambda e, k=k: e.tensor_tensor(out=sm[:, U2, :], in0=pw_i[:, :, k], in1=sm[:, AI, :], op=ALU.mult), [r_sm], [r_sm])
            VEC(lambda e, k=k: e.tensor_tensor(out=pw_r[:, :, k + 1], in0=sm[:, U1, :], in1=sm[:, U2, :], op=ALU.subtract), [r_sm], [r_sm])
            VEC(lambda e, k=k: e.tensor_tensor(out=sm[:, U1, :], in0=pw_r[:, :, k], in1=sm[:, AI, :], op=ALU.mult), [r_sm], [r_sm])
            VEC(lambda e, k=k: e.tensor_tensor(out=sm[:, U2, :], in0=pw_i[:, :, k], in1=sm[:, AR, :], op=ALU.mult), [r_sm], [r_sm])
            VEC(lambda e, k=k: e.tensor_tensor(out=pw_i[:, :, k + 1], in0=sm[:, U1, :], in1=sm[:, U2, :], op=ALU.add), [r_sm], [r_sm])
        VEC(lambda e: e.reciprocal(out=sm[:, RR, :], in_=sm[:, RDEC, :]), [r_sm], [r_sm])
        VEC(lambda e: e.tensor_tensor(out=ph_r[:, :, 0], in0=pw_r[:, :, 8], in1=sm[:, RR, :], op=ALU.mult), [r_sm], [r_sm])
        VEC(lambda e: e.tensor_tensor(out=ph_i[:, :, 0], in0=pw_i[:, :, 8], in1=sm[:, RR, :], op=ALU.mult), [r_sm], [r_sm])
        for k in range(8):
            VEC(lambda e, k=k: e.tensor_tensor(out=sm[:, U1, :], in0=ph_r[:, :, k], in1=ph_r[:, :, k], op=ALU.mult), [r_sm], [r_sm])
            VEC(lambda e, k=k: e.tensor_tensor(out=sm[:, U2, :], in0=ph_i[:, :, k], in1=ph_i[:, :, k], op=ALU.mult), [r_sm], [r_sm])
            VEC(lambda e, k=k: e.tensor_tensor(out=ph_r[:, :, k + 1], in0=sm[:, U1, :], in1=sm[:, U2, :], op=ALU.subtract), [r_sm], [r_sm])
            VEC(lambda e, k=k: e.tensor_tensor(out=sm[:, U1, :], in0=ph_r[:, :, k], in1=ph_i[:, :, k], op=ALU.mult), [r_sm], [r_sm])
            VEC(lambda e, k=k: e.tensor_scalar(out=ph_i[:, :, k + 1], in0=sm[:, U1, :], scalar1=2.0, scalar2=None, op0=ALU.mult), [r_sm], [r_sm])

        def bc16(tile_idx_ap):
            return tile_idx_ap.rearrange("p (a o) -> p a o", o=1).to_broadcast([128, 16, 16])

        def cmul_bc(outr, outi, xr, xi, sr_ap, si_ap, rds, wrs):
            pass

        VEC(lambda e: e.tensor_tensor(out=T1, in0=Bri[:, 0], in1=bc16(sm[:, CBR, :]), op=ALU.mult), [r_sm, r_bc], [r_T])
        VEC(lambda e: e.tensor_tensor(out=T2, in0=Bri[:, 1], in1=bc16(sm[:, CBI, :]), op=ALU.mult), [r_sm, r_bc], [r_T])
        VEC(lambda e: e.tensor_tensor(out=Bb_r, in0=T1, in1=T2, op=ALU.subtract), [r_T], [r_T])
        VEC(lambda e: e.tensor_tensor(out=T1, in0=Bri[:, 1], in1=bc16(sm[:, CBR, :]), op=ALU.mult), [r_sm, r_bc, r_T], [r_T])
        VEC(lambda e: e.tensor_tensor(out=T2, in0=Bri[:, 0], in1=bc16(sm[:, CBI, :]), op=ALU.mult), [r_sm, r_bc], [r_T])
        VEC(lambda e: e.tensor_tensor(out=Bb_i, in0=T1, in1=T2, op=ALU.add), [r_T], [r_T])

        def blkME(M, lg, hf):
            return M[hf * 64:(hf + 1) * 64, lg, :, :].rearrange("p ct (q x) -> p ct q x", x=32)[:, :, :, hf * 16:(hf + 1) * 16]

        def halfT(T, hf):
            return T[hf * 64:(hf + 1) * 64, :, :].rearrange("p (ct q) c -> p ct q c", q=4)

        for lg in range(8):
            VEC(lambda e, lg=lg: e.tensor_tensor(out=T1, in0=Bb_r, in1=pw_r[:, :, lg:lg + 1].to_broadcast([128, 16, 16]), op=ALU.mult), [r_sm, r_T], [r_T])
            VEC(lambda e, lg=lg: e.tensor_tensor(out=T2, in0=Bb_i, in1=pw_i[:, :, lg:lg + 1].to_broadcast([128, 16, 16]), op=ALU.mult), [r_sm, r_T], [r_T])
            VEC(lambda e, lg=lg: e.tensor_tensor(out=T3, in0=Bb_i, in1=pw_r[:, :, lg:lg + 1].to_broadcast([128, 16, 16]), op=ALU.mult), [r_sm, r_T], [r_T])
            VEC(lambda e, lg=lg: e.tensor_tensor(out=T4, in0=Bb_r, in1=pw_i[:, :, lg:lg + 1].to_broadcast([128, 16, 16]), op=ALU.mult), [r_sm, r_T], [r_T])
            for hf in range(2):
                POOL(lambda e, lg=lg, hf=hf: e.tensor_tensor(out=blkME(ME_r, lg, hf), in0=halfT(T1, hf), in1=halfT(T2, hf), op=ALU.subtract), [r_T], [r_ME])
                POOL(lambda e, lg=lg, hf=hf: e.tensor_tensor(out=blkME(ME_i, lg, hf), in0=halfT(T3, hf), in1=halfT(T4, hf), op=ALU.add), [r_T], [r_ME])

        def blkMF(M, hf):
            return M[hf * 64:(hf + 1) * 64, :, :].rearrange("p ct (q x) -> p ct q x", x=32)[:, :, :, hf * 16:(hf + 1) * 16]

        for hf in range(2):
            POOL(lambda e, hf=hf: e.tensor_copy(out=blkMF(MF_r, hf), in_=halfT(Cri[:, 0], hf)), [r_bc], [r_MF])
            POOL(lambda e, hf=hf: e.tensor_scalar(out=blkMF(MF_in, hf), in0=halfT(Cri[:, 1], hf), scalar1=-1.0, scalar2=None, op0=ALU.mult), [r_bc], [r_MF])
        for ct in range(4):
            for ri, M in ((0, ME_r), (1, ME_i)):
                for j0 in (0, 4):
                    bk, _, r_bk = next_bank()
                    for jj in range(4):
                        j = j0 + jj
                        PE(lambda e, bk=bk, jj=jj, j=j, ct=ct, M=M: e.transpose(out=bk[:, jj * 128:(jj + 1) * 128], in_=M[:, 7 - j, ct, :], identity=identf[:]),
                           [r_ME, r_const], [r_bk])
                    ACT(lambda e, bk=bk, ct=ct, j0=j0, ri=ri: e.copy(out=Wst[:, ct, j0:j0 + 4, ri, :], in_=bk[:, :].rearrange("p (a b) -> p a b", b=128)),
                        [r_bk], [r_Wst])
        for ct in range(4):
            for l0 in (0, 4):
                bk, _, r_bk = next_bank()
                for ll in range(4):
                    lg = l0 + ll
                    PE(lambda e, bk=bk, ll=ll, lg=lg, ct=ct: e.matmul(bk[:, ll * 128:(ll + 1) * 128], lhsT=ME_r[:, lg, ct, :], rhs=MF_r[:, ct, :], start=True, stop=False),
                       [r_ME, r_MF], [r_bk])
                    PE(lambda e, bk=bk, ll=ll, lg=lg, ct=ct: e.matmul(bk[:, ll * 128:(ll + 1) * 128], lhsT=ME_i[:, lg, ct, :], rhs=MF_in[:, ct, :], start=False, stop=True),
                       [r_ME, r_MF], [r_bk])
                VEC(lambda e, bk=bk, ct=ct, l0=l0: e.tensor_tensor(out=Wfir[:, ct, l0:l0 + 4, :], in0=bk[:, :].rearrange("p (a b) -> p a b", b=128),
                                                                 in1=mask16[:, :].rearrange("p (o b) -> p o b", o=1).to_broadcast([128, 4, 128]), op=ALU.mult),
                    [r_bk, r_const], [r_Wfir])
                if l0 == 0:
                    VEC(lambda e, bk=bk: e.tensor_tensor(out=tmpF, in0=bk[:, 0:128], in1=mask16[:, :], op=ALU.mult), [r_bk, r_const], [r_tmpF])
                    VEC(lambda e, ct=ct: e.scalar_tensor_tensor(out=Wfir[:, ct, 0, :], in0=identf[:, :], scalar=col(l, C_D + ct), in1=tmpF, op0=ALU.mult, op1=ALU.add),
                        [r_tmpF, r_const], [r_Wfir])
        for i in range(8):
            VEC(lambda e, i=i: e.tensor_tensor(out=T1, in0=Cri[:, 0], in1=pw_r[:, :, i + 1:i + 2].to_broadcast([128, 16, 16]), op=ALU.mult), [r_sm, r_bc, r_T], [r_T])
            VEC(lambda e, i=i: e.tensor_tensor(out=T2, in0=Cri[:, 1], in1=pw_i[:, :, i + 1:i + 2].to_broadcast([128, 16, 16]), op=ALU.mult), [r_sm, r_bc, r_T], [r_T])
            VEC(lambda e, i=i: e.tensor_tensor(out=T3, in0=Cri[:, 1], in1=pw_r[:, :, i + 1:i + 2].to_broadcast([128, 16, 16]), op=ALU.mult), [r_sm, r_bc, r_T], [r_T])
            VEC(lambda e, i=i: e.tensor_tensor(out=T4, in0=Cri[:, 0], in1=pw_i[:, :, i + 1:i + 2].to_broadcast([128, 16, 16]), op=ALU.mult), [r_sm, r_bc, r_T], [r_T])
            for hf in range(2):
                hs = slice(hf * 64, (hf + 1) * 64)
                cs = slice(hf * 16, (hf + 1) * 16)
                POOL(lambda e, i=i, hs=hs, cs=cs: e.tensor_tensor(out=Wo_r[hs, :, i, cs], in0=T1[hs], in1=T2[hs], op=ALU.subtract), [r_T], [r_Wo])
                VEC(lambda e, i=i, hs=hs, cs=cs: e.scalar_tensor_tensor(out=Wo_i[hs, :, i, cs], in0=T3[hs], scalar=-1.0, in1=T4[hs], op0=ALU.mult, op1=ALU.subtract),
                    [r_T], [r_Wo])
        P.barrier()
        arena_off[0] = mark
        u_ring = ARing(2, [128, S], BF16, "uct")
        y_ring = ARing(1, [128, S], F32, "yct")
        tab_r, r_tab = aalloc([128, 4, 512], F32, "tab")
        tab_i, _ = aalloc([128, 4, 512], F32)
        tq1, r_tq = aalloc([128, 4, 256], F32, "tq")
        tq2, _ = aalloc([128, 4, 256], F32)
        xps = []
        for i_ in range(2):
            xr_, rx_ = aalloc([128, 4, 512], BF16, f"xpr{i_}")
            xi_, _ = aalloc([128, 4, 512], BF16)
            VEC(lambda e, xr_=xr_: e.memset(xr_[:, :, 0:1], 0.0), [], [rx_])
            VEC(lambda e, xi_=xi_: e.memset(xi_[:, :, 0:1], 0.0), [], [rx_])
            xps.append((xr_, xi_, rx_))

        class _AR:
            def __init__(self, bufs):
                self.bufs = bufs
                self.i = 0

            def next(self):
                b_ = self.bufs[self.i % len(self.bufs)]
                self.i += 1
                return b_
        A_ring = _AR([aalloc([128, 512], F32, f"A{i_}") for i_ in range(6)] + [(frB[0][:, i_ * 512:(i_ + 1) * 512], Res(f"AB{i_}")) for i_ in range(4)])
        ctxS = {}

        def stageS(ct):
            xp_r, xp_i, r_xp = xps[ct % 2]
            uct, r_uct = u_ring.next()
            DMA(uct, uT_d[:, ct, :], [r_ud], [r_uct])
            u8 = uct.rearrange("p (c j) -> p j c", j=8)
            ctxS[ct] = (uct, r_uct, u8)
            VEC(lambda e: e.memset(tab_r[:, :, 0:1], 1.0), [r_tab], [r_tab])
            VEC(lambda e: e.memset(tab_i[:, :, 0:1], 0.0), [r_tab], [r_tab])
            for k in range(9):
                yield
                s_ = 1 << k
                phr = ph_r[:, 4 * ct:4 * ct + 4, k:k + 1].to_broadcast([128, 4, s_])
                phi = ph_i[:, 4 * ct:4 * ct + 4, k:k + 1].to_broadcast([128, 4, s_])
                VEC(lambda e, s_=s_, phr=phr: e.tensor_tensor(out=tq1[:, :, 0:s_], in0=tab_r[:, :, 0:s_], in1=phr, op=ALU.mult), [r_tab, r_sm, r_tq], [r_tq])
                VEC(lambda e, s_=s_, phi=phi: e.tensor_tensor(out=tq2[:, :, 0:s_], in0=tab_i[:, :, 0:s_], in1=phi, op=ALU.mult), [r_tab, r_sm, r_tq], [r_tq])
                VEC(lambda e, s_=s_: e.tensor_tensor(out=tab_r[:, :, s_:2 * s_], in0=tq1[:, :, 0:s_], in1=tq2[:, :, 0:s_], op=ALU.subtract), [r_tq], [r_tab])
                VEC(lambda e, s_=s_, phi=phi: e.tensor_tensor(out=tq1[:, :, 0:s_], in0=tab_r[:, :, 0:s_], in1=phi, op=ALU.mult), [r_tab, r_sm, r_tq], [r_tq])
                VEC(lambda e, s_=s_, phr=phr: e.tensor_tensor(out=tq2[:, :, 0:s_], in0=tab_i[:, :, 0:s_], in1=phr, op=ALU.mult), [r_tab, r_sm, r_tq], [r_tq])
                VEC(lambda e, s_=s_: e.tensor_tensor(out=tab_i[:, :, s_:2 * s_], in0=tq1[:, :, 0:s_], in1=tq2[:, :, 0:s_], op=ALU.add), [r_tq], [r_tab])
            for q in range(4):
                yield
                pair = 4 * ct + q
                ps_ = slice(32 * q, 32 * q + 32)
                bks = []
                for ri in range(2):
                    bk, _, r_bk = next_bank()
                    for j in range(8):
                        PE(lambda e, bk=bk, j=j, ri=ri, ps_=ps_, q=q, ct=ct, u8=u8: e.matmul(bk[:, :], lhsT=Wst[ps_, ct, j, ri, :], rhs=u8[ps_, j, :],
                                                                                           start=(j == 0), stop=(j == 7), tile_position=(32 * q, 0)),
                           [r_Wst, r_uct], [r_bk])
                    bks.append((bk, r_bk))
                (Sr, r_Sr), (Si, r_Si) = bks
                tr, ti = tab_r[:, q, :], tab_i[:, q, :]
                a1, r_a1 = A_ring.next()
                a2, r_a2 = A_ring.next()
                a3, r_a3 = A_ring.next()
                a4, r_a4 = A_ring.next()
                VEC(lambda e, Sr=Sr, a1=a1, tr=tr: e.tensor_tensor(out=a1, in0=Sr[:, :], in1=tr, op=ALU.mult), [r_Sr, r_tab], [r_a1])
                VEC(lambda e, Si=Si, a2=a2, ti=ti: e.tensor_tensor(out=a2, in0=Si[:, :], in1=ti, op=ALU.mult), [r_Si, r_tab], [r_a2])
                VEC(lambda e, Si=Si, a3=a3, tr=tr: e.tensor_tensor(out=a3, in0=Si[:, :], in1=tr, op=ALU.mult), [r_Si, r_tab], [r_a3])
                VEC(lambda e, Sr=Sr, a4=a4, ti=ti: e.tensor_tensor(out=a4, in0=Sr[:, :], in1=ti, op=ALU.mult), [r_Sr, r_tab], [r_a4])
                POOL(lambda e, a1=a1, a2=a2: e.tensor_tensor(out=a1, in0=a1, in1=a2, op=ALU.add), [r_a1, r_a2], [r_a1])
                POOL(lambda e, a3=a3, a4=a4: e.tensor_tensor(out=a3, in0=a3, in1=a4, op=ALU.subtract), [r_a3, r_a4], [r_a3])
                yield
                rd = sm[:, RDEC, pair:pair + 1].to_broadcast([128, 512])
                VEC(lambda e, a1=a1, a2=a2, rd=rd: e.tensor_tensor_scan(out=a2, data0=rd, data1=a1, initial=0.0, op0=ALU.mult, op1=ALU.add), [r_a1, r_sm, r_a2], [r_a2])
                VEC(lambda e, a3=a3, a4=a4, rd=rd: e.tensor_tensor_scan(out=a4, data0=rd, data1=a3, initial=0.0, op0=ALU.mult, op1=ALU.add), [r_a3, r_sm, r_a4], [r_a4])
                yield
                b1, r_b1 = A_ring.next()
                b2, r_b2 = A_ring.next()
                POOL(lambda e, a2=a2, b1=b1, tr=tr: e.tensor_tensor(out=b1, in0=a2, in1=tr, op=ALU.mult), [r_a2, r_tab], [r_b1])
                POOL(lambda e, a4=a4, b2=b2, ti=ti: e.tensor_tensor(out=b2, in0=a4, in1=ti, op=ALU.mult), [r_a4, r_tab], [r_b2])
                VEC(lambda e, b1=b1, b2=b2, q=q: e.tensor_tensor(out=xp_r[:, q, 1:512], in0=b1[:, 0:511], in1=b2[:, 0:511], op=ALU.subtract), [r_b1, r_b2], [r_xp])
                POOL(lambda e, a2=a2, a1=a1, ti=ti: e.tensor_tensor(out=a1, in0=a2, in1=ti, op=ALU.mult), [r_a2, r_tab, r_a1], [r_a1])
                POOL(lambda e, a4=a4, a3=a3, tr=tr: e.tensor_tensor(out=a3, in0=a4, in1=tr, op=ALU.mult), [r_a4, r_tab, r_a3], [r_a3])
                VEC(lambda e, a1=a1, a3=a3, q=q: e.tensor_tensor(out=xp_i[:, q, 1:512], in0=a1[:, 0:511], in1=a3[:, 0:511], op=ALU.add), [r_a1, r_a3], [r_xp])

        def stageO(ct):
            xp_r, xp_i, r_xp = xps[ct % 2]
            uct, r_uct, u8 = ctxS.pop(ct)
            yct, r_yct = y_ring.next()
            y8 = yct.rearrange("p (c j) -> p j c", j=8)
            for i in range(8):
                yield
                bk, _, r_bk = next_bank()
                for lg in range(i + 1):
                    PE(lambda e, bk=bk, lg=lg, i=i, ct=ct, u8=u8: e.matmul(bk[:, :], lhsT=Wfir[:, ct, lg, :], rhs=u8[:, i - lg, :], start=(lg == 0), stop=False),
                       [r_Wfir, r_uct], [r_bk])
                for q in range(4):
                    pair = 4 * ct + q
                    PE(lambda e, bk=bk, q=q, pair=pair, i=i: e.matmul(bk[32 * q:32 * q + 32, :], lhsT=Wo_r[:, pair, i, :], rhs=xp_r[:, q, :], start=False, stop=False,
                                                                     tile_position=(0, 32 * q)), [r_Wo, r_xp], [r_bk])
                    PE(lambda e, bk=bk, q=q, pair=pair, i=i: e.matmul(bk[32 * q:32 * q + 32, :], lhsT=Wo_i[:, pair, i, :], rhs=xp_i[:, q, :], start=False, stop=True,
                                                                     tile_position=(0, 32 * q)), [r_Wo, r_xp], [r_bk])
                ACT(lambda e, bk=bk, y8=y8, i=i: e.copy(out=y8[:, i, :], in_=bk[:, :]), [r_bk], [r_yct])
            DMA(yT_d[:, ct, :], yct, [r_yct], [r_yd], q="scalar")

        for ct_ in range(5):
            gl = []
            if ct_ >= 1:
                gl.append(stageO(ct_ - 1))
            if ct_ < 4:
                gl.append(stageS(ct_))
            run_interleaved(gl)
        if stop_after == "B2a":
            return True
        P.barrier()
        areset()
        wglu, r_wglu = aalloc([128, 4, 512], BF16, "wglu")
        DMAC(wglu, w_glu_d[l].rearrange("(k p) n -> p k n", p=128), [], [r_wglu])
        yb_ring = ARing(2, [128, 4, 512], F32, "yb")
        g_ring = ARing(2, [128, 4, 512], BF16, "gT")
        sg_ring = ARing(2, [128, 4, 512], F32, "sg")
        sq_ring = ARing(2, [128, 4, 512], F32, "sq")
        rs_ring = ARing(2, [128, 512], F32, "rs")
        sn_ring = ARing(2, [128, 4, 512], BF16, "sn")
        glu_end = arena_off[0]
        wo_a, r_wc = aalloc([128, 4, 1024], BF16, "wo_a")
        wo_s, _ = aalloc([128, 4, 1024], BF16)
        wxq, _ = aalloc([128, 8, 1024], BF16)
        wxo, _ = aalloc([128, 8, 1024], BF16)
        KxT, r_kx = aalloc([128, 8, 256], BF16, "KxT")
        Vx, _ = aalloc([128, 2, 1024], BF16)
        c1w_end = arena_off[0]
        for par_ in range(2):
            DMAC(wo_a[par_ * 64:(par_ + 1) * 64, :, :], w_out_d[l, 0:512, :].rearrange("(hp par d) n -> par d hp n", par=2, d=64)[par_], [], [r_wc])
        VEC(lambda e: e.tensor_tensor(out=wo_a, in0=wo_a, in1=col(l, C_GA2, 4).rearrange("p (k o) -> p k o", o=1).to_broadcast([128, 4, 1024]), op=ALU.mult), [r_wc, r_const], [r_wc])
        DMAC(wo_s, w_out_d[l, 512:1024, :].rearrange("(k p) n -> p k n", p=128), [], [r_wc])
        DMAC(wxq, w_xq_d[l].rearrange("(k p) n -> p k n", p=128), [], [r_wc])
        DMAC(wxo, w_xo_d[l].rearrange("(k p) n -> p k n", p=128), [], [r_wc])
        def stageG(b):
            T0 = b * 512
            yb, r_yb = yb_ring.next()
            DMA(yb, yT_d[:, :, T0:T0 + 512], [r_yd], [r_yb])
            yield
            gT, r_gT = g_ring.next()
            ACT(lambda e, yb=yb, gT=gT: e.activation(out=gT, in_=yb, func=AF.Gelu_apprx_tanh), [r_yb], [r_gT])
            sg, r_sg = sg_ring.next()
            for co in range(4):
                yield
                bk, _, r_bk = next_bank()
                for ci in range(4):
                    PE(lambda e, bk=bk, ci=ci, co=co, gT=gT: e.matmul(bk[:, :], lhsT=wglu[:, ci, co * 128:(co + 1) * 128], rhs=gT[:, ci, :], start=(ci == 0), stop=(ci == 3)),
                       [r_wglu, r_gT], [r_bk])
                ACT(lambda e, bk=bk, co=co, sg=sg: e.activation(out=sg[:, co, :], in_=bk[:, :], func=AF.Sigmoid, bias=col(l, C_BGLU + co)), [r_bk, r_const], [r_sg])
            yield
            VEC(lambda e, sg=sg, yb=yb: e.tensor_tensor(out=sg, in0=sg, in1=yb, op=ALU.mult), [r_sg, r_yb], [r_sg])
            yield
            sq, r_sq = sq_ring.next()
            POOL(lambda e, sg=sg, sq=sq: e.tensor_tensor(out=sq, in0=sg, in1=sg, op=ALU.mult), [r_sg], [r_sq])
            yield
            bk, _, r_bk = next_bank()
            for co in range(4):
                PE(lambda e, bk=bk, co=co, sq=sq: e.matmul(bk[:, :], lhsT=onesf[:, :], rhs=sq[:, co, :], start=(co == 0), stop=(co == 3)), [r_sq, r_const], [r_bk])
            yield
            rs_, r_rs = rs_ring.next()
            ACT(lambda e, bk=bk, rs_=rs_: e.activation(out=rs_, in_=bk[:, :], func=AF.Sqrt, scale=1.0 / 512, bias=EPS), [r_bk], [r_rs])
            yield
            VEC(lambda e, rs_=rs_: e.reciprocal(out=rs_, in_=rs_), [r_rs], [r_rs])
            VEC(lambda e, sg=sg: e.tensor_tensor(out=sg, in0=sg, in1=col(l, C_GS, 4).rearrange("p (k o) -> p k o", o=1).to_broadcast([128, 4, 512]), op=ALU.mult),
                [r_sg, r_const], [r_sg])
            yield
            sn, r_sn = sn_ring.next()
            VEC(lambda e, sg=sg, sn=sn, rs_=rs_: e.tensor_tensor(out=sn, in0=sg, in1=rs_.rearrange("p (o t) -> p o t", o=1).to_broadcast([128, 4, 512]), op=ALU.mult),
                [r_sg, r_rs], [r_sn])
            DMA(sT_d[b], sn, [r_sn], [r_sd])

        for b_ in range(0, NB, 2):
            run_interleaved([stageG(b_), stageG(b_ + 1)])
        if stop_after == "B2":
            return True
        P.barrier()
        arena_off[0] = 0
        wxkv, r_wxkv = aalloc([128, 8, 2048], BF16, "wxkv")
        assert arena_off[0] <= glu_end
        DMAC(wxkv, w_xkv_d[l].rearrange("(k p) n -> p k n", p=128), [], [r_wxkv])
        memT, r_memT = xnT_ring.next()
        for mt in range(2):
            ht, r_ht = ht_ring.next()
            DMA(ht[:], mem_d[mt * 128:(mt + 1) * 128, :], [], [r_ht])
            norm_transpose(ht, r_ht, col(l, C_GMEM, 8), memT, r_memT, mt)
        for oc in range(8):
            bk, _, r_bk = next_bank()
            for k in range(8):
                PE(lambda e, bk=bk, k=k, oc=oc: e.matmul(bk[:, 0:256], lhsT=wxkv[:, k, oc * 128:(oc + 1) * 128], rhs=memT[:, k, 0:256], start=(k == 0), stop=(k == 7)),
                   [r_wxkv, r_memT], [r_bk])
            ACT(lambda e, bk=bk, oc=oc: e.copy(out=KxT[:, oc, :], in_=bk[:, 0:256]), [r_bk], [r_kx])
        for mt in range(2):
            for hf in range(2):
                bk, _, r_bk = next_bank()
                for k in range(8):
                    PE(lambda e, bk=bk, k=k, mt=mt, hf=hf: e.matmul(bk[:, :], lhsT=memT[:, k, mt * 128:(mt + 1) * 128], rhs=wxkv[:, k, 1024 + hf * 512:1024 + (hf + 1) * 512],
                                                                  start=(k == 0), stop=(k == 7)), [r_wxkv, r_memT], [r_bk])
                ACT(lambda e, bk=bk, mt=mt, hf=hf: e.copy(out=Vx[:, mt, hf * 512:(hf + 1) * 512], in_=bk[:, :]), [r_bk], [r_kx])
        P.barrier()
        arena_off[0] = 0
        h1_ring = ARing(8, [128, D], F32, "h1t")
        qx_ring = ARing(1, [128, 8, 512], BF16, "qxT")
        ox_ring = ARing(1, [128, 8, 512], BF16, "oxT")
        an_ring2 = ARing(1, [128, 4, 512], BF16, "an2")
        sn_ring2 = ARing(1, [128, 4, 512], BF16, "sn2")
        pX_ring = ARing(2, [128, 2, 512], BF16, "pX")
        rc_ring = ARing(2, [128, 512], F32, "rc")
        assert arena_off[0] <= glu_end, (arena_off[0], glu_end)
        araw_v = frA[0][0:64, 0:4096].rearrange("p (h t) -> p h t", t=512)
        araw4 = frA[0][0:64, 0:4096].rearrange("p (hp par t) -> p hp par t", par=2, t=512)
        r_araw = frA[1]
        sqv = frB[0][0:64, 0:2048].rearrange("p (h t) -> p h t", t=512)
        r_sqv = frB[1]
        r_h1src = [Res() for _ in range(32)] if l == 0 else r_hd
        ctx1 = {}

        def stageC1a(b):
            T0 = b * 512
            DMA(araw_v, aT_d[b], [r_ad], [r_araw])
            sn2, r_sn2 = sn_ring2.next()
            DMA(sn2, sT_d[b], [r_sd], [r_sn2])
            bk, _, r_bk = next_bank()
            for hg in range(2):
                POOL(lambda e, hg=hg: e.tensor_tensor(out=sqv, in0=araw_v[:, hg * 4:(hg + 1) * 4, :], in1=araw_v[:, hg * 4:(hg + 1) * 4, :], op=ALU.mult), [r_araw], [r_sqv])
                for hh in range(4):
                    PE(lambda e, bk=bk, hg=hg, hh=hh: e.matmul(bk[0:64, :], lhsT=onesf[0:64, 0:64], rhs=sqv[:, hh, :], start=(hg == 0 and hh == 0), stop=(hg == 1 and hh == 3)),
                       [r_sqv, r_const], [r_bk])
                yield
            rc, r_rc = rc_ring.next()
            ACT(lambda e, bk=bk, rc=rc: e.activation(out=rc[0:64, :], in_=bk[0:64, :], func=AF.Sqrt, scale=1.0 / 512, bias=EPS), [r_bk], [r_rc])
            VEC(lambda e, rc=rc: e.reciprocal(out=rc[0:64, :], in_=rc[0:64, :]), [r_rc], [r_rc])
            yield
            an2, r_an2 = an_ring2.next()
            rcb = rc[0:64, :].rearrange("p (o t) -> p o t", o=1).to_broadcast([64, 4, 512])
            VEC(lambda e, an2=an2, rcb=rcb: e.tensor_tensor(out=an2[0:64, :, :], in0=araw4[:, :, 0, :], in1=rcb, op=ALU.mult), [r_araw, r_rc], [r_an2])
            yield
            VEC(lambda e, an2=an2, rcb=rcb: e.tensor_tensor(out=an2[64:128, :, :], in0=araw4[:, :, 1, :], in1=rcb, op=ALU.mult), [r_araw, r_rc], [r_an2])
            yield
            xnT, r_xnT = xnT_ring.next()
            h1s = []
            ctx1[b] = (xnT, r_xnT, h1s)
            for sub in range(4):
                ti = b * 4 + sub
                ts_ = slice(sub * 128, (sub + 1) * 128)
                ht, r_ht = ht_ring.next()
                DMA(ht[:], src_d[T0 + sub * 128:T0 + (sub + 1) * 128, :], [r_h1src[ti]], [r_ht])
                h1t, r_h1t = h1_ring.next()
                for hf in range(2):
                    bk, _, r_bk = next_bank()
                    cs_ = slice(hf * 512, (hf + 1) * 512)
                    for k in range(4):
                        PE(lambda e, bk=bk, k=k, an2=an2, ts_=ts_, cs_=cs_: e.matmul(bk[:, :], lhsT=an2[:, k, ts_], rhs=wo_a[:, k, cs_], start=(k == 0), stop=False),
                           [r_an2, r_wc], [r_bk])
                    for k in range(4):
                        PE(lambda e, bk=bk, k=k, sn2=sn2, ts_=ts_, cs_=cs_: e.matmul(bk[:, :], lhsT=sn2[:, k, ts_], rhs=wo_s[:, k, cs_], start=False, stop=(k == 3)),
                           [r_sn2, r_wc], [r_bk])
                    VEC(lambda e, bk=bk, ht=ht, h1t=h1t, cs_=cs_: e.tensor_tensor(out=h1t[:, cs_], in0=bk[:, :], in1=ht[:, cs_], op=ALU.add), [r_bk, r_ht], [r_h1t])
                    yield
                norm_transpose(h1t, r_h1t, col(l, C_GX, 8), xnT, r_xnT, sub)
                h1s.append((h1t, r_h1t))
                yield

        def stageC1b(b):
            T0 = b * 512
            xnT, r_xnT, h1s = ctx1.pop(b)
            qxT, r_qxT = qx_ring.next()
            for oc in range(8):
                bk, _, r_bk = next_bank()
                for k in range(8):
                    PE(lambda e, bk=bk, k=k, oc=oc, xnT=xnT: e.matmul(bk[:, :], lhsT=wxq[:, k, oc * 128:(oc + 1) * 128], rhs=xnT[:, k, :], start=(k == 0), stop=(k == 7)),
                       [r_wc, r_xnT], [r_bk])
                ACT(lambda e, bk=bk, oc=oc, qxT=qxT: e.copy(out=qxT[:, oc, :], in_=bk[:, :]), [r_bk], [r_qxT])
                if oc % 2 == 1:
                    yield
            oxT, r_oxT = ox_ring.next()
            for hx in range(4):
                pX, r_pX = pX_ring.next()
                for mt in range(2):
                    bk, _, r_bk = next_bank()
                    for dc in range(2):
                        PE(lambda e, bk=bk, dc=dc, mt=mt, hx=hx, qxT=qxT: e.matmul(bk[:, :], lhsT=KxT[:, hx * 2 + dc, mt * 128:(mt + 1) * 128], rhs=qxT[:, hx * 2 + dc, :],
                                                                              start=(dc == 0), stop=(dc == 1)), [r_kx, r_qxT], [r_bk])
                    ACT(lambda e, bk=bk, mt=mt, pX=pX: e.activation(out=pX[:, mt, :], in_=bk[:, :], func=AF.Exp, scale=X_SCALE), [r_bk], [r_pX])
                yield
                bk, _, r_bk = next_bank()
                for mt in range(2):
                    PE(lambda e, bk=bk, mt=mt, pX=pX: e.matmul(bk[:, :], lhsT=onesb[:, :], rhs=pX[:, mt, :], start=(mt == 0), stop=(mt == 1)), [r_pX, r_const], [r_bk])
                rc, r_rc = rc_ring.next()
                VEC(lambda e, bk=bk, rc=rc: e.reciprocal(out=rc, in_=bk[:, :]), [r_bk], [r_rc])
                yield
                for dc in range(2):
                    bk, _, r_bk = next_bank()
                    for mt in range(2):
                        PE(lambda e, bk=bk, mt=mt, dc=dc, hx=hx, pX=pX: e.matmul(bk[:, :], lhsT=Vx[:, mt, (hx * 2 + dc) * 128:(hx * 2 + dc + 1) * 128], rhs=pX[:, mt, :],
                                                                             start=(mt == 0), stop=(mt == 1)), [r_kx, r_pX], [r_bk])
                    VEC(lambda e, bk=bk, dc=dc, hx=hx, rc=rc, oxT=oxT: e.tensor_tensor(out=oxT[:, hx * 2 + dc, :], in0=bk[:, :], in1=rc, op=ALU.mult), [r_bk, r_rc], [r_oxT])
                yield
            for sub in range(4):
                ti = b * 4 + sub
                ts_ = slice(sub * 128, (sub + 1) * 128)
                h1t, r_h1t = h1s[sub]
                for hf in range(2):
                    bk, _, r_bk = next_bank()
                    cs_ = slice(hf * 512, (hf + 1) * 512)
                    for k in range(8):
                        PE(lambda e, bk=bk, k=k, oxT=oxT, ts_=ts_, cs_=cs_: e.matmul(bk[:, :], lhsT=oxT[:, k, ts_], rhs=wxo[:, k, cs_], start=(k == 0), stop=(k == 7)),
                           [r_oxT, r_wc], [r_bk])
                    VEC(lambda e, bk=bk, h1t=h1t, cs_=cs_: e.tensor_tensor(out=h1t[:, cs_], in0=bk[:, :], in1=h1t[:, cs_], op=ALU.add), [r_bk, r_h1t], [r_h1t])
                    yield
                DMA(h1_d[T0 + sub * 128:T0 + (sub + 1) * 128, :], h1t[:], [r_h1t], [r_h1d[ti]])

        for b in range(NB + 1):
            gl = []
            if b >= 1:
                gl.append(stageC1b(b - 1))
            if b < NB:
                gl.append(stageC1a(b))
            run_interleaved(gl)
        if stop_after == "C1":
            return True

        P.barrier()
        areset()
        wg, r_wf = aalloc([128, 8, DFF], BF16, "wg")
        wu, _ = aalloc([128, 8, DFF], BF16)
        wd, _ = aalloc([128, NFF, 1024], BF16)
        for k in range(8):
            DMAC(wg[:, k, :], w_gate_d[l, k * 128:(k + 1) * 128, :], [], [r_wf])
            DMAC(wu[:, k, :], w_up_d[l, k * 128:(k + 1) * 128, :], [], [r_wf])
        for k in range(0, NFF, 2):
            DMAC(wd[:, k:k + 2, :], w_down_d[l, k * 128:(k + 2) * 128, :].rearrange("(k p) n -> p k n", p=128), [], [r_wf])
        actT = frA[0].bitcast(BF16) if hasattr(frA[0], "bitcast") else None
        actT = actT[:, 0:NFF * 512].rearrange("p (f t) -> p f t", t=512)
        r_actT = frA[1]
        sgs = [(frB[0][:, i * 512:(i + 1) * 512], Res()) for i in range(4)]
        sg_i = [0]
        last = (l == L - 1)
        ctxC = {}
        nt_ring = Ring(P, f"ntl{l}", 1, [128, 4], F32) if False else None

        def stageC2a(b):
            T0 = b * 512
            xnT, r_xnT = xnT_ring.next()
            ctxC[b] = (xnT, r_xnT)
            for sub in range(4):
                ti = b * 4 + sub
                ht, r_ht = ht_ring.next()
                DMA(ht[:], h1_d[T0 + sub * 128:T0 + (sub + 1) * 128, :], [r_h1d[ti]], [r_ht])
                norm_transpose(ht, r_ht, col(l, C_GFFN, 8), xnT, r_xnT, sub)
                yield

        def stageC2b(b):
            T0 = b * 512
            xnT, r_xnT = ctxC.pop(b)
            for fc in range(NFF):
                if fc % 3 == 0:
                    yield
                bkg, _, r_bkg = next_bank()
                bku, _, r_bku = next_bank()
                for k in range(8):
                    PE(lambda e, bkg=bkg, k=k, fc=fc, xnT=xnT: e.matmul(bkg[:, :], lhsT=wg[:, k, fc * 128:(fc + 1) * 128], rhs=xnT[:, k, :], start=(k == 0), stop=(k == 7)),
                       [r_wf, r_xnT], [r_bkg])
                for k in range(8):
                    PE(lambda e, bku=bku, k=k, fc=fc, xnT=xnT: e.matmul(bku[:, :], lhsT=wu[:, k, fc * 128:(fc + 1) * 128], rhs=xnT[:, k, :], start=(k == 0), stop=(k == 7)),
                       [r_wf, r_xnT], [r_bku])
                sgt, r_sgt = sgs[sg_i[0] % 4]
                sg_i[0] += 1
                ACT(lambda e, bkg=bkg, sgt=sgt: e.activation(out=sgt, in_=bkg[:, :], func=AF.Silu), [r_bkg], [r_sgt])
                VEC(lambda e, bku=bku, sgt=sgt, fc=fc: e.tensor_tensor(out=actT[:, fc, :], in0=bku[:, :], in1=sgt, op=ALU.mult), [r_bku, r_sgt], [r_actT])
            for sub in range(4):
                ti = b * 4 + sub
                ts_ = slice(sub * 128, (sub + 1) * 128)
                ht, r_ht = ht_ring.next()
                DMA(ht[:], h1_d[T0 + sub * 128:T0 + (sub + 1) * 128, :], [r_h1d[ti]], [r_ht])
                for hf in range(2):
                    yield
                    bk, _, r_bk = next_bank()
                    cs_ = slice(hf * 512, (hf + 1) * 512)
                    for fc in range(NFF):
                        PE(lambda e, bk=bk, fc=fc, ts_=ts_, cs_=cs_: e.matmul(bk[:, :], lhsT=actT[:, fc, ts_], rhs=wd[:, fc, cs_], start=(fc == 0), stop=(fc == NFF - 1)),
                           [r_actT, r_wf], [r_bk])
                    VEC(lambda e, bk=bk, ht=ht, cs_=cs_: e.tensor_tensor(out=ht[:, cs_], in0=bk[:, :], in1=ht[:, cs_], op=ALU.add), [r_bk, r_ht], [r_ht])
                if not last:
                    DMA(h_d[T0 + sub * 128:T0 + (sub + 1) * 128, :], ht[:], [r_ht], [r_hd[ti]])
                else:
                    st, r_st = st_ring.next()
                    rms_stats(ht[:], r_ht, D, st, r_st, 0)
                    VEC(lambda e, ht=ht, st=st: e.scalar_tensor_tensor(out=ht[:], in0=ht[:], scalar=st[:, 0:1], in1=fing[:], op0=ALU.mult, op1=ALU.mult),
                        [r_ht, r_st, r_const], [r_ht])
                    final_ops.append(DMA(out_d[T0 + sub * 128:T0 + (sub + 1) * 128, :], ht[:], [r_ht], []))


        def run_il(gens):
            gens = list(gens)
            while gens:
                for g_ in list(gens):
                    try:
                        next(g_)
                    except StopIteration:
                        gens.remove(g_)

        for b in range(NB + 1):
            gl = []
            if b >= 1:
                gl.append(stageC2b(b - 1))
            if b < NB:
                gl.append(stageC2a(b))
            run_il(gl)
        return False

    for l_ in range(n_layers):
        if emit_layer(l_):
            break

    if debug:
        def dump(name, src, shape, dt, r):
            final_ops.append(DMA(dbg_out(name, shape, dt), src, [r], []))
        P.barrier()
        dump("qT", qT_d[:, :, 3584:4096], [8, 96, 512], BF16, r_qd)
        dump("kT", kT_d[:, :, 3584:4096], [8, 96, 512], BF16, r_kd)
        dump("V", V_d[:, :, 28:32, :], [8, 128, 4, 65], BF16, r_vd)
        dump("uT", uT_d[:, :, 3584:4096], [128, 4, 512], BF16, r_ud)
        dump("aT0", aT_d[0], [64, 8, 512], F32, r_ad)
        dump("aT7", aT_d[7], [64, 8, 512], F32, r_ad)
        dump("yT", yT_d[:, :, 3584:4096], [128, 4, 512], F32, r_yd)
        dump("yT0", yT_d[:, :, 0:512], [128, 4, 512], F32, r_yd)
        dump("sT7", sT_d[7], [128, 4, 512], BF16, r_sd)
        dump("sT0", sT_d[0], [128, 4, 512], BF16, r_sd)
        dump("h1", h1_d[3968:4096, :], [128, D], F32, r_h1d[31])
        dump("h", h_d[3968:4096, :], [128, D], F32, r_hd[31])
    P.barrier()
    return P.build(final_waits=final_ops), dbg


def prep_inputs(inp):
    f = np.float32
    g = lambda k: np.asarray(inp[k])
    cols = np.zeros((128, L, NCOL), f)
    for l in range(L):
        cols[:, l, C_GMIX:C_GMIX + 8] = g("norm_mix_g")[l].reshape(8, 128).T
        cols[:, l, C_GX:C_GX + 8] = g("norm_x_g")[l].reshape(8, 128).T
        cols[:, l, C_GFFN:C_GFFN + 8] = g("norm_ffn_g")[l].reshape(8, 128).T
        cols[:, l, C_GMEM:C_GMEM + 8] = g("mem_norm_g")[l].reshape(8, 128).T
        cols[:, l, C_GQ:C_GQ + 2] = g("q_norm_g")[l].reshape(2, 128).T
        cols[:, l, C_GKV] = g("kv_norm_g")[l]
        cols[:, l, C_GS:C_GS + 4] = g("ssm_out_g")[l].reshape(4, 128).T
        cols[:, l, C_D:C_D + 4] = g("ssm_d")[l].reshape(4, 128).T
        cols[:, l, C_BGLU:C_BGLU + 4] = g("ssm_b_glu")[l].reshape(4, 128).T
        cols[0:64, l, C_GA:C_GA + 8] = g("attn_out_g")[l].reshape(8, 64).T
        cols[:, l, C_GA2:C_GA2 + 4] = g("attn_out_g")[l].reshape(4, 2, 64).transpose(1, 2, 0).reshape(128, 4)
    consts = np.zeros((128, 2), f)
    freqs = (np.float32(10000.0) ** (-np.arange(0, 32, 2, dtype=np.float32) / np.float32(32))).astype(f)
    consts[64:80, 0] = freqs
    consts[80:96, 0] = freqs
    consts[64:80, 1] = -SIN_SCALE
    consts[80:96, 1] = SIN_SCALE
    idx = np.arange(128) // 16
    mask16 = (idx[:, None] == idx[None, :]).astype(f)
    w_in = g("w_in")
    w_kr = np.zeros((L, D, 192), f)
    w_kr[:, :, 64:96] = w_in[:, :, 384:416]
    w_kr[:, :, 160:176] = w_in[:, :, 400:416]
    w_kr[:, :, 176:192] = w_in[:, :, 384:400]
    w_uq = g("w_uq")
    w_uqs = np.zeros_like(w_uq)
    for h in range(8):
        w_uqs[:, :, h * 96 + 64:h * 96 + 80] = w_uq[:, :, h * 96 + 80:h * 96 + 96]
        w_uqs[:, :, h * 96 + 80:h * 96 + 96] = w_uq[:, :, h * 96 + 64:h * 96 + 80]
    w_ukv = g("w_ukv").reshape(L, 128, 8, 2, 64).transpose(0, 1, 3, 2, 4).reshape(L, 128, 1024)

    def pair_layout(a):
        sh = a.shape
        a = a.reshape(L, 16, 2, 64, *sh[3:])
        a = np.moveaxis(a, 1, 3)
        return a.reshape(L, 128, 16, *sh[3:])

    lam_re = pair_layout(g("ssm_lambda_re"))
    lam_im = pair_layout(g("ssm_lambda_im"))
    logdt = pair_layout(np.repeat(g("ssm_log_dt")[:, :, None], 64, axis=2))
    s5v = np.stack([lam_re, lam_im, logdt], axis=2)
    b_re = pair_layout(g("ssm_b_re"))
    b_im = pair_layout(g("ssm_b_im"))
    s5b = np.stack([b_re, b_im], axis=2)
    c_re = pair_layout(np.swapaxes(g("ssm_c_re"), 2, 3))
    c_im = pair_layout(np.swapaxes(g("ssm_c_im"), 2, 3))
    s5c = np.stack([c_re, c_im], axis=2)
    common = dict(
        cols=cols, consts=consts, mask16=mask16, fing=g("final_norm_g").reshape(1, D).astype(f),
        w_in=w_in, w_kr=w_kr, w_uq=w_uq, w_uqs=w_uqs, w_ukv=np.ascontiguousarray(w_ukv),
        s5v=np.ascontiguousarray(s5v), s5b=np.ascontiguousarray(s5b), s5c=np.ascontiguousarray(s5c),
        w_glu=g("ssm_w_glu"), w_out=g("w_out"), w_xq=g("w_xq"), w_xkv=g("w_xkv"), w_xo=g("w_xo"),
        w_gate=g("w_gate"), w_up=g("w_up"), w_down=g("w_down"),
    )
    common = {k: np.ascontiguousarray(v, dtype=f) for k, v in common.items()}
    x = g("x")
    mem = g("mem")
    pos = g("positions").astype(np.int32)
    per_core = []
    for c in range(x.shape[0]):
        d = dict(common)
        d["x"] = np.ascontiguousarray(x[c], dtype=f)
        d["mem"] = np.ascontiguousarray(mem[c], dtype=f)
        d["pos"] = np.ascontiguousarray(pos[c].reshape(1, S))
        per_core.append(d)
    return per_core


def kernel(**inputs):
    per_core = prep_inputs(inputs)
    nc, _ = build_program()
    res = run_bass_kernel_spmd(nc, per_core, core_ids=list(range(8)))
    return np.stack([np.asarray(r["out"], dtype=np.float32) for r in res.results], axis=0)
```

```python
import math
import numpy as np
import concourse.bass as bass
import concourse.mybir as mybir
from concourse.bass_utils import run_bass_kernel_spmd

F32 = mybir.dt.float32
BF16 = mybir.dt.bfloat16
I32 = mybir.dt.int32
AF = mybir.ActivationFunctionType
ALU = mybir.AluOpType

ENGS = ["tensor", "vector", "scalar", "gpsimd", "sync"]

L = 2
S = 4096
D = 1024
NB = 8
EPS = 1e-6
DFF = 2816
NFF = 22
TWO_PI = 2.0 * math.pi
SIN_SCALE = 6.2831845
MAGIC = 12582912.0
ATT_SCALE = 96.0 ** -0.5
X_SCALE = 256.0 ** -0.5
NCOL = 59
C_GMIX, C_GX, C_GFFN, C_GMEM, C_GQ, C_GKV, C_GS, C_D, C_BGLU, C_GA, C_GA2 = 0, 8, 16, 24, 32, 34, 35, 39, 43, 47, 55


class Res:
    __slots__ = ("name", "last_w", "readers")

    def __init__(self, name=""):
        self.name = name
        self.last_w = None
        self.readers = []


class Op:
    __slots__ = ("eng", "fn", "waits", "dma", "sem", "val")


class Prog:
    def __init__(self, n_dma_sems=24):
        self.nc = bass.Bass("TRN2", target_bir_lowering=False)
        self.ops = {e: [] for e in ENGS}
        self.n_dma_sems = n_dma_sems
        self.wm = {e: {} for e in ENGS}
        self.dma_rr = {e: 0 for e in ENGS}
        self.dma_cnt = {}
        self.dma_last = {}
        self.eng_cnt = {}
        self.pending = {e: {} for e in ENGS}
        self._ctx = []

    def sbuf(self, name, shape, dtype):
        g = self.nc.sbuf_tensor("sb_" + name, list(shape), dtype)
        h = g.__enter__()
        self._ctx.append(g)
        return h

    def psum(self, name, shape, dtype):
        g = self.nc.psum_tensor("ps_" + name, list(shape), dtype)
        h = g.__enter__()
        self._ctx.append(g)
        return h

    def _need(self, op, dep):
        if dep is None:
            return
        if dep.eng == "tensor" and op.eng == "tensor" and not dep.dma and not op.dma:
            return
        if dep.val > op.waits.get(dep.sem, 0):
            op.waits[dep.sem] = dep.val

    def barrier(self):
        cur = {}
        for e, c in self.eng_cnt.items():
            cur[("eng", e)] = c
        for k, c in self.dma_cnt.items():
            cur[k] = 16 * c
        for e in ENGS:
            pe = self.pending[e]
            for k, v in cur.items():
                if v > pe.get(k, 0):
                    pe[k] = v

    def op(self, eng, fn, reads=(), writes=(), dma=False):
        o = Op()
        o.eng = eng
        o.fn = fn
        o.dma = dma
        o.waits = {}
        if self.pending[eng]:
            o.waits.update(self.pending[eng])
            self.pending[eng] = {}
        if dma:
            slot = self.dma_rr[eng] % self.n_dma_sems
            self.dma_rr[eng] += 1
            key = ("dma", eng, slot)
            prev = self.dma_last.get(key)
            cnt = self.dma_cnt.get(key, 0) + 1
            self.dma_cnt[key] = cnt
            o.sem = key
            o.val = 16 * cnt
            if prev is not None:
                self._need(o, prev)
            self.dma_last[key] = o
        else:
            o.sem = ("eng", eng)
            self.eng_cnt[eng] = self.eng_cnt.get(eng, 0) + 1
            o.val = self.eng_cnt[eng]
        for r in reads:
            self._need(o, r.last_w)
        for w in writes:
            self._need(o, w.last_w)
            for rd in w.readers:
                self._need(o, rd)
        for r in reads:
            r.readers.append(o)
        for w in writes:
            w.last_w = o
            w.readers = []
        wm = self.wm[eng]
        for k in list(o.waits):
            if wm.get(k, 0) >= o.waits[k]:
                del o.waits[k]
            else:
                wm[k] = o.waits[k]
        self.ops[eng].append(o)
        return o

    def build(self, final_waits=()):
        nc = self.nc
        sems = {}
        for e in ENGS:
            for o in self.ops[e]:
                if o.sem not in sems:
                    g = nc.semaphore("s_" + "_".join(str(x) for x in o.sem))
                    sems[o.sem] = g.__enter__()
                    self._ctx.append(g)
        fin = {}
        for o in final_waits:
            fin[o.sem] = max(fin.get(o.sem, 0), o.val)
        with nc.Block() as block:
            def make(e):
                def body(engobj):
                    for o in self.ops[e]:
                        for k, v in o.waits.items():
                            engobj.wait_ge(sems[k], v)
                        ins = o.fn(engobj)
                        ins.then_inc(sems[o.sem], 16 if o.dma else 1)
                    if e == "sync":
                        for k, v in fin.items():
                            engobj.wait_ge(sems[k], v)
                return body
            for e in ENGS:
                if self.ops[e] or e == "sync":
                    getattr(block, e)(make(e))
        return nc


class Ring:
    def __init__(self, P, name, n, shape, dtype):
        self.bufs = [(P.sbuf(f"{name}{i}", shape, dtype), Res(f"{name}{i}")) for i in range(n)]
        self.i = 0

    def next(self):
        b = self.bufs[self.i % len(self.bufs)]
        self.i += 1
        return b


def build_program(debug=False, n_layers=L, stop_after=None):
    P = Prog()
    nc = P.nc

    def din(name, shape, dt=F32):
        return nc.dram_tensor(name, list(shape), dt, kind="ExternalInput").ap()

    def dscr(name, shape, dt):
        return nc.dram_tensor(name, list(shape), dt, kind="Internal").ap()

    x_d = din("x", [S, D])
    mem_d = din("mem", [256, D])
    pos_d = din("pos", [1, S], I32)
    cols_d = din("cols", [128, L, NCOL])
    consts_d = din("consts", [128, 2])
    mask16_d = din("mask16", [128, 128])
    fing_d = din("fing", [1, D])
    w_in_d = din("w_in", [L, D, 928])
    w_kr_d = din("w_kr", [L, D, 192])
    w_uq_d = din("w_uq", [L, 256, 768])
    w_uqs_d = din("w_uqs", [L, 256, 768])
    w_ukv_d = din("w_ukv", [L, 128, 1024])
    s5v_d = din("s5v", [L, 128, 3, 16])
    s5b_d = din("s5b", [L, 128, 2, 16, 16])
    s5c_d = din("s5c", [L, 128, 2, 16, 16])
    w_glu_d = din("w_glu", [L, 512, 512])
    w_out_d = din("w_out", [L, D, D])
    w_xq_d = din("w_xq", [L, D, D])
    w_xkv_d = din("w_xkv", [L, D, 2 * D])
    w_xo_d = din("w_xo", [L, D, D])
    w_gate_d = din("w_gate", [L, D, DFF])
    w_up_d = din("w_up", [L, D, DFF])
    w_down_d = din("w_down", [L, DFF, D])
    out_d = nc.dram_tensor("out", [S, D], F32, kind="ExternalOutput").ap()

    h_d = dscr("h_scr", [S, D], F32)
    h1_d = dscr("h1_scr", [S, D], F32)
    qT_d = dscr("qT_scr", [8, 96, S], BF16)
    kT_d = dscr("kT_scr", [8, 96, S], BF16)
    V_d = dscr("V_scr", [8, 128, 32, 65], BF16)
    uT_d = dscr("uT_scr", [128, 4, S], BF16)
    yT_d = dscr("yT_scr", [128, 4, S], F32)
    aT_d = dscr("aT_scr", [NB, 64, 8, 512], F32)
    sT_d = dscr("sT_scr", [NB, 128, 4, 512], BF16)
    cos_d = dscr("cos_scr", [32, S], F32)
    sin_d = dscr("sin_scr", [32, S], F32)
    r_hd = [Res() for _ in range(32)]
    r_h1d = [Res() for _ in range(32)]
    r_qd, r_kd, r_vd, r_ud, r_yd = Res(), Res(), Res(), Res(), Res()
    r_ad, r_sd, r_csd = Res(), Res(), Res()

    dbg = {}

    def dbg_out(name, shape, dt=F32):
        t = nc.dram_tensor("dbg_" + name, list(shape), dt, kind="ExternalOutput").ap()
        dbg[name] = t
        return t

    final_ops = []

    def VEC(fn, reads, writes):
        return P.op("vector", fn, reads, writes)

    def ACT(fn, reads, writes):
        return P.op("scalar", fn, reads, writes)

    def POOL(fn, reads, writes):
        return P.op("gpsimd", fn, reads, writes)

    def PE(fn, reads, writes):
        return P.op("tensor", fn, reads, writes)

    def DMA(out, in_, reads, writes, q="sync"):
        return P.op(q, lambda e: e.dma_start(out=out, in_=in_), reads, writes, dma=True)

    def DMAC(out, in_, reads, writes):
        return P.op("gpsimd", lambda e: e.dma_start(out=out, in_=in_), reads, writes, dma=True)

    banks = []
    for i in range(8):
        t = P.psum(f"bank{i}", [128, 512], F32)
        banks.append((t, t.bitcast(BF16), Res(f"bank{i}")))
    bank_rr = [0]

    def next_bank(pool=(0, 1, 2, 3, 4, 5, 6, 7)):
        i = pool[bank_rr[0] % len(pool)]
        bank_rr[0] += 1
        return banks[i]

    identf = P.sbuf("identf", [128, 128], F32)
    ident = P.sbuf("ident", [128, 128], BF16)
    onesf = P.sbuf("onesf", [128, 128], F32)
    sel65 = P.sbuf("sel65", [65, 64], F32)
    onesb = P.sbuf("onesb", [128, 128], BF16)
    fing = P.sbuf("fing", [128, D], F32)
    mask16 = P.sbuf("mask16", [128, 128], F32)
    cols = P.sbuf("cols", [128, L, NCOL], F32)
    consts = P.sbuf("consts", [128, 2], F32)
    r_const = Res("const")
    POOL(lambda e: e.memset(identf[:], 1.0), [], [r_const])
    POOL(lambda e: e.affine_select(out=identf[:], in_=identf[:], pattern=[[-1, 128]], compare_op=ALU.is_equal,
                                   fill=0.0, base=0, channel_multiplier=1), [r_const], [r_const])
    VEC(lambda e: e.tensor_copy(out=ident[:], in_=identf[:]), [r_const], [r_const])
    VEC(lambda e: e.memset(onesf[:], 1.0), [], [r_const])
    VEC(lambda e: e.memset(onesb[:], 1.0), [], [r_const])
    DMA(fing[:], fing_d[0:1, :].to_broadcast([128, D]), [], [r_const])
    VEC(lambda e: e.memset(sel65[:], 0.0), [], [r_const])
    VEC(lambda e: e.memset(sel65[64:65, :], 1.0), [r_const], [r_const])
    DMA(mask16[:], mask16_d[:, :], [], [r_const])
    DMA(cols[:], cols_d[:, :, :], [], [r_const])
    DMA(consts[:], consts_d[:, :], [], [r_const])

    def col(l, c, n=1, p0=0, p1=128):
        return cols[p0:p1, l, c:c + n]

    ARENA_F32 = 33792
    arena_f = P.sbuf("arena", [128, ARENA_F32], F32)
    arena_b = arena_f.bitcast(BF16)
    arena_i = arena_f.bitcast(I32)
    arena_off = [0]

    def areset():
        arena_off[0] = 0

    def aalloc(shape, dt, name=""):
        n = 1
        for s_ in shape[1:]:
            n *= s_
        esz = 2 if dt == BF16 else 4
        off = (arena_off[0] + 3) // 4 * 4
        arena_off[0] = off + n * esz
        assert arena_off[0] <= ARENA_F32 * 4, (name, arena_off[0])
        base = {BF16: arena_b, F32: arena_f, I32: arena_i}[dt]
        e0 = off // esz
        ap = base[0:shape[0], e0:e0 + n]
        if len(shape) == 3:
            ap = ap.rearrange("p (a b) -> p a b", b=shape[2])
        elif len(shape) == 4:
            ap = ap.rearrange("p (a b c) -> p a b c", b=shape[2], c=shape[3])
        elif len(shape) == 5:
            ap = ap.rearrange("p (a b c d) -> p a b c d", b=shape[2], c=shape[3], d=shape[4])
        return ap, Res(name)

    class ARing:
        def __init__(self, n, shape, dt, name=""):
            self.bufs = [aalloc(shape, dt, f"{name}{i}") for i in range(n)]
            self.i = 0

        def next(self):
            b = self.bufs[self.i % len(self.bufs)]
            self.i += 1
            return b

    ht_ring = Ring(P, "ht", 2, [128, D], F32)
    junk_ring = Ring(P, "junk", 1, [128, D], BF16)
    xn_ring = Ring(P, "xn", 2, [128, D], BF16)
    xnT_ring = Ring(P, "xnT", 2, [128, 8, 512], BF16)
    st_ring = Ring(P, "st", 8, [128, 4], F32)
    pT_ring = Ring(P, "pT", 4, [128, 512], BF16)
    frA = (P.sbuf("frA", [128, 5632], F32), Res("frA"))
    frB = (P.sbuf("frB", [128, 2048], F32), Res("frB"))

    def rms_stats(src_ap, r_src, nfeat, st, r_st, c0):
        junk, r_junk = junk_ring.next()
        n = src_ap.shape[-1]
        ACT(lambda e: e.activation(out=junk[:, 0:n], in_=src_ap, func=AF.Square, accum_out=st[:, c0:c0 + 1]), [r_src], [r_junk, r_st])
        ACT(lambda e: e.activation(out=st[:, c0:c0 + 1], in_=st[:, c0:c0 + 1], func=AF.Sqrt, scale=1.0 / nfeat, bias=EPS), [r_st], [r_st])
        VEC(lambda e: e.reciprocal(out=st[:, c0:c0 + 1], in_=st[:, c0:c0 + 1]), [r_st], [r_st])

    def norm_transpose(ht, r_ht, gcol, xnT, r_xnT, sub):
        st, r_st = st_ring.next()
        rms_stats(ht[:], r_ht, D, st, r_st, 0)
        xn, r_xn = xn_ring.next()
        VEC(lambda e: e.tensor_scalar(out=xn[:], in0=ht[:], scalar1=st[:, 0:1], scalar2=None, op0=ALU.mult), [r_ht, r_st], [r_xn])
        bk, bkb, r_bk = next_bank()
        bkv = bkb[:, :].rearrange("p (a b) -> p a b", b=128)
        for k in range(8):
            PE(lambda e, k=k: e.transpose(out=bkv[:, k, :], in_=xn[:, k * 128:(k + 1) * 128], identity=ident[:]), [r_xn, r_const], [r_bk])
        VEC(lambda e: e.tensor_tensor(out=xnT[:, :, sub * 128:(sub + 1) * 128], in0=bkv, in1=gcol.to_broadcast([128, 8, 128]), op=ALU.mult),
            [r_bk, r_const], [r_xnT])

    RS = slice(64, 96)

    areset()
    posi, r_ra = aalloc([96, S], I32, "posi")
    tmpa, _ = aalloc([96, S], F32, "tmpa")
    tmpb, _ = aalloc([96, S], F32, "tmpb")
    cs_t, r_cs = aalloc([96, S], F32, "cs")
    sn_t, _ = aalloc([96, S], F32, "sn")
    DMA(posi[RS, :], pos_d[0:1, :].to_broadcast([32, S]), [], [r_ra])
    VEC(lambda e: e.tensor_copy(out=tmpa[RS, :], in_=posi[RS, :]), [r_ra], [r_ra])
    VEC(lambda e: e.tensor_scalar(out=tmpa[RS, :], in0=tmpa[RS, :], scalar1=consts[RS, 0:1], scalar2=1.0 / TWO_PI,
                                  op0=ALU.mult, op1=ALU.mult), [r_ra, r_const], [r_ra])
    VEC(lambda e: e.tensor_scalar(out=tmpb[RS, :], in0=tmpa[RS, :], scalar1=MAGIC, scalar2=MAGIC, op0=ALU.add, op1=ALU.subtract), [r_ra], [r_ra])
    VEC(lambda e: e.tensor_tensor(out=tmpb[RS, :], in0=tmpa[RS, :], in1=tmpb[RS, :], op=ALU.subtract), [r_ra], [r_ra])
    ACT(lambda e: e.activation(out=sn_t[RS, :], in_=tmpb[RS, :], func=AF.Sin, scale=consts[RS, 1:2]), [r_ra, r_const], [r_cs])
    VEC(lambda e: e.tensor_scalar(out=tmpa[RS, :], in0=tmpa[RS, :], scalar1=0.25, scalar2=None, op0=ALU.add), [r_ra, r_cs], [r_ra])
    VEC(lambda e: e.tensor_scalar(out=tmpb[RS, :], in0=tmpa[RS, :], scalar1=MAGIC, scalar2=MAGIC, op0=ALU.add, op1=ALU.subtract), [r_ra], [r_ra])
    VEC(lambda e: e.tensor_tensor(out=tmpb[RS, :], in0=tmpa[RS, :], in1=tmpb[RS, :], op=ALU.subtract), [r_ra], [r_ra])
    ACT(lambda e: e.activation(out=cs_t[RS, :], in_=tmpb[RS, :], func=AF.Sin, scale=SIN_SCALE), [r_ra], [r_cs])
    DMA(cos_d[:, :], cs_t[RS, :], [r_cs], [r_csd])
    DMA(sin_d[:, :], sn_t[RS, :], [r_cs], [r_csd])

    def emit_layer(l):
        src_d = x_d if l == 0 else h_d
        r_src = [Res() for _ in range(32)] if l == 0 else r_hd
        P.barrier()
        areset()
        w_in_sb, r_w = aalloc([128, 8, 928], BF16, "w_in")
        w_kr_sb, _ = aalloc([128, 8, 192], BF16)
        w_uq_sb, _ = aalloc([128, 2, 768], BF16)
        w_uqs_sb, _ = aalloc([128, 2, 768], BF16)
        w_ukv_sb, _ = aalloc([128, 1024], BF16)
        DMAC(w_in_sb, w_in_d[l].rearrange("(k p) n -> p k n", p=128), [], [r_w])
        DMAC(w_kr_sb, w_kr_d[l].rearrange("(k p) n -> p k n", p=128), [], [r_w])
        DMAC(w_uq_sb, w_uq_d[l].rearrange("(k p) n -> p k n", p=128), [], [r_w])
        DMAC(w_uqs_sb, w_uqs_d[l].rearrange("(k p) n -> p k n", p=128), [], [r_w])
        DMAC(w_ukv_sb, w_ukv_d[l], [], [r_w])
        qs, r_qs = aalloc([96, 8, 512], F32, "qs")
        qsw, r_qsw = aalloc([96, 8, 512], F32, "qsw")
        qb_ring = ARing(2, [96, 8, 512], BF16, "qb")
        kb_ring = ARing(2, [96, 8, 512], BF16, "kb")
        vb_ring = ARing(2, [128, 8, 4, 65], BF16, "vb")
        ub_ring = ARing(2, [128, 4, 512], BF16, "ub")
        cT_ring = ARing(2, [128, 3, 512], BF16, "cT")
        cs_ring = ARing(2, [96, 2, 512], F32, "csb")
        ta, r_ta = aalloc([96, 1024], F32, "ta")
        for vb, r_vb in vb_ring.bufs:
            VEC(lambda e, vb=vb: e.memset(vb[:, :, :, 64:65], 1.0), [], [r_vb])

        ctxA = {}

        htx = [(frB[0][:, 0:1024], Res("htx0")), (frB[0][:, 1024:2048], Res("htx1"))]
        frA_b = frA[0].bitcast(BF16)
        xnx = [(frA_b[:, 0:1024], Res("xnx0")), (frA_b[:, 1024:2048], Res("xnx1"))]
        jkx = [(frA_b[:, 2048 + i * 1024:2048 + (i + 1) * 1024], Res(f"jkx{i}")) for i in range(4)]

        def stageA1(b):
            T0 = b * 512
            xnT, r_xnT = xnT_ring.next()
            cT, r_cT = cT_ring.next()
            vb, r_vb = vb_ring.next()
            csb, r_csb = cs_ring.next()
            ctxA[b] = (xnT, r_xnT, cT, r_cT, csb, r_csb)
            DMA(csb[RS, 0, :], cos_d[:, T0:T0 + 512], [r_csd], [r_csb])
            DMA(csb[RS, 1, :], sin_d[:, T0:T0 + 512], [r_csd], [r_csb])
            hts = [ht_ring.next(), ht_ring.next(), htx[0], htx[1]]
            xns = [xn_ring.next(), xn_ring.next(), xnx[0], xnx[1]]
            sts = [st_ring.next() for _ in range(4)]
            gcol = col(l, C_GMIX, 8)
            gq = col(l, C_GQ, 3)
            for sub in range(4):
                ht, r_ht = hts[sub]
                DMA(ht[:, :], src_d[T0 + sub * 128:T0 + (sub + 1) * 128, :], [r_src[b * 4 + sub]], [r_ht])
            yield
            for sub in range(4):
                (ht, r_ht), (st, r_st), (jk, r_jk) = hts[sub], sts[sub], jkx[sub]
                ACT(lambda e, ht=ht, st=st, jk=jk: e.activation(out=jk, in_=ht[:, :], func=AF.Square, accum_out=st[:, 0:1]), [r_ht], [r_jk, r_st])
            for sub in range(4):
                st, r_st = sts[sub]
                ACT(lambda e, st=st: e.activation(out=st[:, 0:1], in_=st[:, 0:1], func=AF.Sqrt, scale=1.0 / D, bias=EPS), [r_st], [r_st])
            yield
            for sub in range(4):
                st, r_st = sts[sub]
                VEC(lambda e, st=st: e.reciprocal(out=st[:, 0:1], in_=st[:, 0:1]), [r_st], [r_st])
            for sub in range(4):
                (ht, r_ht), (st, r_st), (xn, r_xn) = hts[sub], sts[sub], xns[sub]
                VEC(lambda e, ht=ht, st=st, xn=xn: e.tensor_scalar(out=xn[:, :], in0=ht[:, :], scalar1=st[:, 0:1], scalar2=None, op0=ALU.mult),
                    [r_ht, r_st], [r_xn])
            yield
            bks = []
            for sub in range(4):
                xn, r_xn = xns[sub]
                bk, bkb, r_bk = next_bank()
                bkv = bkb[:, :].rearrange("p (a b) -> p a b", b=128)
                for k in range(8):
                    PE(lambda e, k=k, bkv=bkv, xn=xn: e.transpose(out=bkv[:, k, :], in_=xn[:, k * 128:(k + 1) * 128], identity=ident[:]), [r_xn, r_const], [r_bk])
                bks.append((bkv, r_bk))
                if sub % 2 == 1:
                    yield
            for sub in range(4):
                bkv, r_bk = bks[sub]
                VEC(lambda e, bkv=bkv, sub=sub: e.tensor_tensor(out=xnT[:, :, sub * 128:(sub + 1) * 128], in0=bkv, in1=gcol.to_broadcast([128, 8, 128]), op=ALU.mult),
                    [r_bk, r_const], [r_xnT])
            yield
            cbk = []
            for sub in range(4):
                bk, bkb, r_bk = next_bank()
                for k in range(8):
                    PE(lambda e, k=k, bk=bk, sub=sub: e.matmul(bk[:, 0:384], lhsT=xnT[:, k, sub * 128:(sub + 1) * 128], rhs=w_in_sb[:, k, 0:384],
                                                              start=(k == 0), stop=(k == 7)), [r_xnT, r_w], [r_bk])
                cbk.append((bk, r_bk))
                if sub % 2 == 1:
                    yield
            for sub in range(4):
                (bk, r_bk), (st, r_st), (jk, r_jk) = cbk[sub], sts[sub], jkx[sub]
                ACT(lambda e, bk=bk, st=st, jk=jk: e.activation(out=jk[:, 0:256], in_=bk[:, 0:256], func=AF.Square, accum_out=st[:, 1:2]), [r_bk], [r_jk, r_st])
                ACT(lambda e, bk=bk, st=st, jk=jk: e.activation(out=jk[:, 256:384], in_=bk[:, 256:384], func=AF.Square, accum_out=st[:, 2:3]), [r_bk], [r_jk, r_st])
            yield
            for sub in range(4):
                st, r_st = sts[sub]
                ACT(lambda e, st=st: e.activation(out=st[:, 1:2], in_=st[:, 1:2], func=AF.Sqrt, scale=1.0 / 256, bias=EPS), [r_st], [r_st])
                ACT(lambda e, st=st: e.activation(out=st[:, 2:3], in_=st[:, 2:3], func=AF.Sqrt, scale=1.0 / 128, bias=EPS), [r_st], [r_st])
            for sub in range(4):
                st, r_st = sts[sub]
                VEC(lambda e, st=st: e.reciprocal(out=st[:, 1:3], in_=st[:, 1:3]), [r_st], [r_st])
            yield
            for sub in range(4):
                (bk, r_bk), (st, r_st), (cn, r_cn) = cbk[sub], sts[sub], xns[sub]
                VEC(lambda e, bk=bk, cn=cn, st=st: e.tensor_scalar(out=cn[:, 0:256], in0=bk[:, 0:256], scalar1=st[:, 1:2], scalar2=None, op0=ALU.mult),
                    [r_bk, r_st], [r_cn])
                VEC(lambda e, bk=bk, cn=cn, st=st: e.tensor_scalar(out=cn[:, 256:384], in0=bk[:, 256:384], scalar1=st[:, 2:3], scalar2=None, op0=ALU.mult),
                    [r_bk, r_st], [r_cn])
            yield
            bk2s = []
            for sub in range(4):
                cn, r_cn = xns[sub]
                bk2, bk2b, r_bk2 = next_bank()
                bk2v = bk2b[:, :].rearrange("p (a b) -> p a b", b=128)
                for k in range(3):
                    PE(lambda e, k=k, bk2v=bk2v, cn=cn: e.transpose(out=bk2v[:, k, :], in_=cn[:, k * 128:(k + 1) * 128], identity=ident[:]), [r_cn, r_const], [r_bk2])
                bk2s.append((bk2v, r_bk2))
            yield
            for sub in range(4):
                bk2v, r_bk2 = bk2s[sub]
                VEC(lambda e, bk2v=bk2v, sub=sub: e.tensor_tensor(out=cT[:, :, sub * 128:(sub + 1) * 128], in0=bk2v[:, 0:3, :],
                                                                 in1=gq.to_broadcast([128, 3, 128]), op=ALU.mult), [r_bk2, r_const], [r_cT])
            yield
            for sub in range(4):
                bk3, _, r_bk3 = next_bank()
                PE(lambda e, bk3=bk3, sub=sub: e.matmul(bk3[:, :], lhsT=cT[:, 2, sub * 128:(sub + 1) * 128], rhs=w_ukv_sb[:, 512:1024],
                                                       start=True, stop=True), [r_cT, r_w], [r_bk3])
                ACT(lambda e, bk3=bk3, sub=sub: e.copy(out=vb[:, :, sub, 0:64], in_=bk3[:, :].rearrange("p (h d) -> p h d", d=64)), [r_bk3], [r_vb])
                if sub % 2 == 1:
                    yield
            DMA(V_d[:, :, b * 4:(b + 1) * 4, :].rearrange("h p t d -> p h t d"), vb, [r_vb], [r_vd], q="scalar")
            yield

        def stageA2(b):
            T0 = b * 512
            xnT, r_xnT, cT, r_cT, csb, r_csb = ctxA.pop(b)
            ub, r_ub = ub_ring.next()
            for ct in range(4):
                yield
                bk, _, r_bk = next_bank()
                for k in range(8):
                    PE(lambda e, k=k, bk=bk, xnT=xnT, ct=ct: e.matmul(bk[:, :], lhsT=w_in_sb[:, k, 416 + ct * 128:416 + (ct + 1) * 128],
                                                                      rhs=xnT[:, k, :], start=(k == 0), stop=(k == 7)), [r_xnT, r_w], [r_bk])
                ACT(lambda e, bk=bk, ct=ct, ub=ub: e.copy(out=ub[:, ct, :], in_=bk[:, :]), [r_bk], [r_ub])
            DMA(uT_d[:, :, T0:T0 + 512], ub, [r_ub], [r_ud], q="scalar")
            yield
            bka, _, r_bka = next_bank()
            bkb_, _, r_bkb = next_bank()
            for k in range(8):
                PE(lambda e, k=k, bka=bka, xnT=xnT: e.matmul(bka[0:96, :], lhsT=w_kr_sb[:, k, 0:96], rhs=xnT[:, k, :], start=(k == 0), stop=(k == 7)),
                   [r_xnT, r_w], [r_bka])
            for k in range(8):
                PE(lambda e, k=k, bkb_=bkb_, xnT=xnT: e.matmul(bkb_[0:96, :], lhsT=w_kr_sb[:, k, 96:192], rhs=xnT[:, k, :], start=(k == 0), stop=(k == 7)),
                   [r_xnT, r_w], [r_bkb])
            yield
            VEC(lambda e, bka=bka, csb=csb: e.tensor_tensor(out=ta[RS, 0:512], in0=bka[RS, :], in1=csb[RS, 0, :], op=ALU.mult), [r_bka, r_csb], [r_ta])
            VEC(lambda e, bkb_=bkb_, csb=csb: e.tensor_tensor(out=ta[RS, 512:1024], in0=bkb_[RS, :], in1=csb[RS, 1, :], op=ALU.mult), [r_bkb, r_csb], [r_ta])
            VEC(lambda e: e.tensor_tensor(out=ta[RS, 0:512], in0=ta[RS, 0:512], in1=ta[RS, 512:1024], op=ALU.add), [r_ta], [r_ta])
            kb, r_kb = kb_ring.next()
            POOL(lambda e, kb=kb: e.tensor_copy(out=kb[RS, :, :], in_=ta[RS, 0:512].rearrange("p (o t) -> p o t", o=1).to_broadcast([32, 8, 512])),
                 [r_ta], [r_kb])
            for h in range(8):
                yield
                bk, _, r_bk = next_bank()
                PE(lambda e, bk=bk, h=h, cT=cT: e.matmul(bk[0:64, :], lhsT=w_ukv_sb[:, h * 64:(h + 1) * 64], rhs=cT[:, 2, :], start=True, stop=True),
                   [r_cT, r_w], [r_bk])
                ACT(lambda e, bk=bk, h=h, kb=kb: e.copy(out=kb[0:64, h, :], in_=bk[0:64, :]), [r_bk], [r_kb])
            DMA(kT_d[:, :, T0:T0 + 512].rearrange("h p t -> p h t"), kb, [r_kb], [r_kd], q="scalar")
            for h in range(8):
                yield
                bka, _, r_bka = next_bank()
                bkb_, _, r_bkb = next_bank()
                for k in range(2):
                    PE(lambda e, k=k, bka=bka, h=h, cT=cT: e.matmul(bka[0:96, :], lhsT=w_uq_sb[:, k, h * 96:(h + 1) * 96], rhs=cT[:, k, :],
                                                                    start=(k == 0), stop=(k == 1)), [r_cT, r_w], [r_bka])
                for k in range(2):
                    PE(lambda e, k=k, bkb_=bkb_, h=h, cT=cT: e.matmul(bkb_[0:96, :], lhsT=w_uqs_sb[:, k, h * 96:(h + 1) * 96], rhs=cT[:, k, :],
                                                                      start=(k == 0), stop=(k == 1)), [r_cT, r_w], [r_bkb])
                ACT(lambda e, bka=bka, h=h: e.copy(out=qs[:, h, :], in_=bka[0:96, :]), [r_bka], [r_qs])
                ACT(lambda e, bkb_=bkb_, h=h: e.copy(out=qsw[RS, h, :], in_=bkb_[RS, :]), [r_bkb], [r_qsw])
            yield
            qb, r_qb = qb_ring.next()
            POOL(lambda e, qb=qb: e.tensor_copy(out=qb[0:64, :, :], in_=qs[0:64, :, :]), [r_qs], [r_qb])
            VEC(lambda e, csb=csb: e.tensor_tensor(out=qs[RS, :, :], in0=qs[RS, :, :],
                                                   in1=csb[RS, 0:1, :].to_broadcast([32, 8, 512]), op=ALU.mult), [r_qs, r_csb], [r_qs])
            VEC(lambda e, csb=csb: e.tensor_tensor(out=qsw[RS, :, :], in0=qsw[RS, :, :],
                                                   in1=csb[RS, 1:2, :].to_broadcast([32, 8, 512]), op=ALU.mult), [r_qsw, r_csb], [r_qsw])
            VEC(lambda e, qb=qb: e.tensor_tensor(out=qb[RS, :, :], in0=qs[RS, :, :], in1=qsw[RS, :, :], op=ALU.add), [r_qs, r_qsw], [r_qb])
            DMA(qT_d[:, :, T0:T0 + 512].rearrange("h p t -> p h t"), qb, [r_qb], [r_qd])
            yield

        def run_interleaved(gens):
            gens = list(gens)
            while gens:
                for g_ in list(gens):
                    try:
                        next(g_)
                    except StopIteration:
                        gens.remove(g_)

        for b in range(NB + 1):
            gl = []
            if b >= 1:
                gl.append(stageA2(b - 1))
            if b < NB:
                gl.append(stageA1(b))
            run_interleaved(gl)

        if stop_after == "A":
            return True

        P.barrier()
        areset()
        S_POOL = (0, 1, 2, 3)
        O_POOL = (4, 5)
        M_POOL = (6, 7)
        pT_ringB = ARing(6, [128, 512], BF16, "pTB")
        qh_ring = ARing(2, [96, S], BF16, "qh")
        kh_ring = ARing(2, [96, S], BF16, "kh")
        vh_ring = ARing(2, [128, 32, 65], BF16, "vh")
        oT_ring = ARing(3, [65, 1024], F32, "oT3")
        an_ring = ARing(3, [64, 512], F32, "an3")
        LA = 3

        def load_head(h):
            qh, r_qh = qh_ring.next()
            kh, r_kh = kh_ring.next()
            vh, r_vh = vh_ring.next()
            DMA(qh, qT_d[h], [r_qd], [r_qh])
            DMA(kh, kT_d[h], [r_kd], [r_kh])
            DMA(vh, V_d[h], [r_vd], [r_vh])
            return (qh, r_qh, kh, r_kh, vh, r_vh)

        heads = {0: load_head(0)}
        for h in range(8):
            qh, r_qh, kh, r_kh, vh, r_vh = heads[h]
            if h + 1 < 8:
                heads[h + 1] = load_head(h + 1)
            items = [(b, kt) for b in range(NB) for kt in range(4 * (b + 1))]
            pts = {}
            bos = {}
            deferred = []

            def stage1(i):
                b, kt = items[i]
                T0 = b * 512
                bs, _, r_bs = next_bank(S_POOL)
                PE(lambda e, bs=bs, kt=kt, kh=kh, qh=qh, T0=T0: e.matmul(bs[:, :], lhsT=kh[:, kt * 128:(kt + 1) * 128], rhs=qh[:, T0:T0 + 512],
                                                                         start=True, stop=True), [r_kh, r_qh], [r_bs])
                pT, r_pT = pT_ringB.next()
                ACT(lambda e, bs=bs, pT=pT: e.activation(out=pT[:], in_=bs[:, :], func=AF.Exp, scale=ATT_SCALE), [r_bs], [r_pT])
                if kt >= 4 * b:
                    base = T0 - kt * 128
                    POOL(lambda e, pT=pT, base=base: e.affine_select(out=pT[:], in_=pT[:], pattern=[[1, 512]], compare_op=ALU.is_ge,
                                                                     fill=0.0, base=base, channel_multiplier=-1), [r_pT], [r_pT])
                pts[i] = (pT, r_pT)

            def stage2(j, i_now):
                b, kt = items[j]
                nkt = 4 * (b + 1)
                if kt == 0:
                    bos[b] = next_bank(O_POOL)
                bo, _, r_bo = bos[b]
                pT, r_pT = pts.pop(j)
                PE(lambda e, bo=bo, pT=pT, kt=kt, nkt=nkt, vh=vh: e.matmul(bo[0:65, :], lhsT=vh[:, kt, :], rhs=pT[:],
                                                                           start=(kt == 0), stop=(kt == nkt - 1)), [r_vh, r_pT], [r_bo])
                if kt == nkt - 1:
                    oT, r_oT = oT_ring.next()
                    VEC(lambda e, bo=bo, oT=oT: e.tensor_copy(out=oT[:, 0:512], in_=bo[0:65, :]), [r_bo], [r_oT])

                    def epi(b=b, oT=oT, r_oT=r_oT):
                        bm, _, r_bm = next_bank(M_POOL)
                        PE(lambda e, bm=bm, oT=oT: e.matmul(bm[0:64, :], lhsT=sel65[:, :], rhs=oT[:, 0:512], start=True, stop=True), [r_oT, r_const], [r_bm])
                        VEC(lambda e, bm=bm, oT=oT: e.reciprocal(out=oT[0:64, 512:1024], in_=bm[0:64, :]), [r_bm, r_oT], [r_oT])
                        an, r_an = an_ring.next()
                        POOL(lambda e, oT=oT, an=an: e.tensor_tensor(out=an[:, :], in0=oT[0:64, 0:512], in1=oT[0:64, 512:1024], op=ALU.mult), [r_oT], [r_an])
                        DMA(aT_d[b, :, h, :], an, [r_an], [r_ad], q="gpsimd")
                    deferred.append((i_now + 3, epi))

            n_it = len(items)
            for i in range(n_it + LA):
                if i < n_it:
                    stage1(i)
                if i - LA >= 0:
                    stage2(i - LA, i)
                while deferred and deferred[0][0] <= i:
                    deferred.pop(0)[1]()
            while deferred:
                deferred.pop(0)[1]()
        if stop_after == "B1":
            return True
        P.barrier()
        areset()
        Wst, r_Wst = aalloc([128, 4, 8, 2, 128], BF16, "Wst")
        Wfir, r_Wfir = aalloc([128, 4, 8, 128], BF16, "Wfir")
        Wo_r, r_Wo = aalloc([128, 16, 8, 32], BF16, "Wo")
        Wo_i, _ = aalloc([128, 16, 8, 32], BF16)
        sm, r_sm = aalloc([128, 32, 16], F32, "sm")
        pw_r, _ = aalloc([128, 16, 9], F32)
        pw_i, _ = aalloc([128, 16, 9], F32)
        ph_r, _ = aalloc([128, 16, 9], F32)
        ph_i, _ = aalloc([128, 16, 9], F32)
        mark = arena_off[0]
        Bri, r_bc = aalloc([128, 2, 16, 16], F32, "Bri")
        Cri, _ = aalloc([128, 2, 16, 16], F32)
        Bb_r, r_T = aalloc([128, 16, 16], F32, "T")
        Bb_i, _ = aalloc([128, 16, 16], F32)
        T1, _ = aalloc([128, 16, 16], F32)
        T2, _ = aalloc([128, 16, 16], F32)
        T3, _ = aalloc([128, 16, 16], F32)
        T4, _ = aalloc([128, 16, 16], F32)
        ME_r, r_ME = aalloc([128, 8, 4, 128], F32, "ME")
        ME_i, _ = aalloc([128, 8, 4, 128], F32)
        MF_r, r_MF = aalloc([128, 4, 128], F32, "MF")
        MF_in, _ = aalloc([128, 4, 128], F32)
        tmpF, r_tmpF = aalloc([128, 128], F32, "tmpF")
        DMA(sm[:, 0:3, :], s5v_d[l], [], [r_sm])
        DMA(Bri, s5b_d[l], [], [r_bc])
        DMA(Cri, s5c_d[l], [], [r_bc])
        POOL(lambda e: e.memset(ME_r, 0.0), [], [r_ME])
        POOL(lambda e: e.memset(ME_i, 0.0), [], [r_ME])
        POOL(lambda e: e.memset(MF_r, 0.0), [], [r_MF])
        POOL(lambda e: e.memset(MF_in, 0.0), [], [r_MF])
        POOL(lambda e: e.memset(Wo_r, 0.0), [], [r_Wo])
        POOL(lambda e: e.memset(Wo_i, 0.0), [], [r_Wo])
        LRE, LIM, LDT, DT, TT_, ER, Y, RND, FR, SN_, CS_, AR, AI, ARM1, DEN, RDEN, CBR, CBI, U1, U2, RDEC, RR = range(22)

        def smtt(o, a, b, op):
            VEC(lambda e: e.tensor_tensor(out=sm[:, o, :], in0=sm[:, a, :], in1=sm[:, b, :], op=op), [r_sm], [r_sm])

        def smts(o, a, s1, op0, s2=None, op1=None):
            if op1 is None:
                VEC(lambda e: e.tensor_scalar(out=sm[:, o, :], in0=sm[:, a, :], scalar1=s1, scalar2=None, op0=op0), [r_sm], [r_sm])
            else:
                VEC(lambda e: e.tensor_scalar(out=sm[:, o, :], in0=sm[:, a, :], scalar1=s1, scalar2=s2, op0=op0, op1=op1), [r_sm], [r_sm])

        def smact(o, a, func, scale=1.0):
            ACT(lambda e: e.activation(out=sm[:, o, :], in_=sm[:, a, :], func=func, scale=scale), [r_sm], [r_sm])

        smact(DT, LDT, AF.Exp)
        smtt(TT_, LRE, DT, ALU.mult)
        smact(ER, TT_, AF.Exp)
        smact(RDEC, TT_, AF.Exp, 8.0)
        smtt(Y, LIM, DT, ALU.mult)
        smts(Y, Y, 1.0 / TWO_PI, ALU.mult)
        smts(RND, Y, MAGIC, ALU.add, MAGIC, ALU.subtract)
        smtt(FR, Y, RND, ALU.subtract)
        smact(SN_, FR, AF.Sin, SIN_SCALE)
        smts(Y, Y, 0.25, ALU.add)
        smts(RND, Y, MAGIC, ALU.add, MAGIC, ALU.subtract)
        smtt(FR, Y, RND, ALU.subtract)
        smact(CS_, FR, AF.Sin, SIN_SCALE)
        smtt(AR, ER, CS_, ALU.mult)
        smtt(AI, ER, SN_, ALU.mult)
        smts(ARM1, AR, -1.0, ALU.add)
        smtt(U1, LRE, LRE, ALU.mult)
        smtt(U2, LIM, LIM, ALU.mult)
        smtt(DEN, U1, U2, ALU.add)
        VEC(lambda e: e.reciprocal(out=sm[:, RDEN, :], in_=sm[:, DEN, :]), [r_sm], [r_sm])
        smtt(U1, ARM1, LRE, ALU.mult)
        smtt(U2, AI, LIM, ALU.mult)
        smtt(U1, U1, U2, ALU.add)
        smtt(CBR, U1, RDEN, ALU.mult)
        smtt(U1, AI, LRE, ALU.mult)
        smtt(U2, ARM1, LIM, ALU.mult)
        smtt(U1, U1, U2, ALU.subtract)
        smtt(CBI, U1, RDEN, ALU.mult)
        VEC(lambda e: e.memset(pw_r[:, :, 0:1], 1.0), [r_sm], [r_sm])
        VEC(lambda e: e.memset(pw_i[:, :, 0:1], 0.0), [r_sm], [r_sm])
        VEC(lambda e: e.tensor_copy(out=pw_r[:, :, 1], in_=sm[:, AR, :]), [r_sm], [r_sm])
        VEC(lambda e: e.tensor_copy(out=pw_i[:, :, 1], in_=sm[:, AI, :]), [r_sm], [r_sm])
        for k in range(1, 8):
            VEC(lambda e, k=k: e.tensor_tensor(out=sm[:, U1, :], in0=pw_r[:, :, k], in1=sm[:, AR, :], op=ALU.mult), [r_sm], [r_sm])
            VEC(lambda e, k=k: e.tensor_tensor(out=sm[:, U2, :], in0=pw_i[:, :, k], in1=sm[:, AI, :], op=ALU.mult), [r_sm], [r_sm])
            VEC(lambda e, k=k: e.tensor_tensor(out=pw_r[:, :, k + 1], in0=sm[:, U1, :], in1=sm[:, U2, :], op=ALU.subtract), [r_sm], [r_sm])
            VEC(lambda e, k=k: e.tensor_tensor(out=sm[:, U1, :], in0=pw_r[:, :, k], in1=sm[:, AI, :], op=ALU.mult), [r_sm], [r_sm])
            VEC(lambda e, k=k: e.tensor_tensor(out=sm[:, U2, :], in0=pw_i[:, :, k], in1=sm[:, AR, :], op=ALU.mult), [r_sm], [r_sm])
            VEC(lambda e, k=k: e.tensor_tensor(out=pw_i[:, :, k + 1], in0=sm[:, U1, :], in1=sm[:, U2, :], op=ALU.add), [r_sm], [r_sm])
        VEC(lambda e: e.reciprocal(out=sm[:, RR, :], in_=sm[:, RDEC, :]), [r_sm], [r_sm])
        VEC(lambda e: e.tensor_tensor(out=ph_r[:, :, 0], in0=pw_r[:, :, 8], in1=sm[:, RR, :], op=ALU.mult), [r_sm], [r_sm])
        VEC(lambda e: e.tensor_tensor(out=ph_i[:, :, 0], in0=pw_i[:, :, 8], in1=sm[:, RR, :], op=ALU.mult), [r_sm], [r_sm])
        for k in range(8):
            VEC(lambda e, k=k: e.tensor_tensor(out=sm[:, U1, :], in0=ph_r[:, :, k], in1=ph_r[:, :, k], op=ALU.mult), [r_sm], [r_sm])
            VEC(lambda e, k=k: e.tensor_tensor(out=sm[:, U2, :], in0=ph_i[:, :, k], in1=ph_i[:, :, k], op=ALU.mult), [r_sm], [r_sm])
            VEC(lambda e, k=k: e.tensor_tensor(out=ph_r[:, :, k + 1], in0=sm[:, U1, :], in1=sm[:, U2, :], op=ALU.subtract), [r_sm], [r_sm])
            VEC(lambda e, k=k: e.tensor_tensor(out=sm[:, U1, :], in0=ph_r[:, :, k], in1=ph_i[:, :, k], op=ALU.mult), [r_sm], [r_sm])
            VEC(lambda e, k=k: e.tensor_scalar(out=ph_i[:, :, k + 1], in0=sm[:, U1, :], scalar1=2.0, scalar2=None, op0=ALU.mult), [r_sm], [r_sm])

        def bc16(tile_idx_ap):
            return tile_idx_ap.rearrange("p (a o) -> p a o", o=1).to_broadcast([128, 16, 16])

        def cmul_bc(outr, outi, xr, xi, sr_ap, si_ap, rds, wrs):
            pass

        VEC(lambda e: e.tensor_tensor(out=T1, in0=Bri[:, 0], in1=bc16(sm[:, CBR, :]), op=ALU.mult), [r_sm, r_bc], [r_T])
        VEC(lambda e: e.tensor_tensor(out=T2, in0=Bri[:, 1], in1=bc16(sm[:, CBI, :]), op=ALU.mult), [r_sm, r_bc], [r_T])
        VEC(lambda e: e.tensor_tensor(out=Bb_r, in0=T1, in1=T2, op=ALU.subtract), [r_T], [r_T])
        VEC(lambda e: e.tensor_tensor(out=T1, in0=Bri[:, 1], in1=bc16(sm[:, CBR, :]), op=ALU.mult), [r_sm, r_bc, r_T], [r_T])
        VEC(lambda e: e.tensor_tensor(out=T2, in0=Bri[:, 0], in1=bc16(sm[:, CBI, :]), op=ALU.mult), [r_sm, r_bc], [r_T])
        VEC(lambda e: e.tensor_tensor(out=Bb_i, in0=T1, in1=T2, op=ALU.add), [r_T], [r_T])

        def blkME(M, lg, hf):
            return M[hf * 64:(hf + 1) * 64, lg, :, :].rearrange("p ct (q x) -> p ct q x", x=32)[:, :, :, hf * 16:(hf + 1) * 16]

        def halfT(T, hf):
            return T[hf * 64:(hf + 1) * 64, :, :].rearrange("p (ct q) c -> p ct q c", q=4)

        for lg in range(8):
            VEC(lambda e, lg=lg: e.tensor_tensor(out=T1, in0=Bb_r, in1=pw_r[:, :, lg:lg + 1].to_broadcast([128, 16, 16]), op=ALU.mult), [r_sm, r_T], [r_T])
            VEC(lambda e, lg=lg: e.tensor_tensor(out=T2, in0=Bb_i, in1=pw_i[:, :, lg:lg + 1].to_broadcast([128, 16, 16]), op=ALU.mult), [r_sm, r_T], [r_T])
            VEC(lambda e, lg=lg: e.tensor_tensor(out=T3, in0=Bb_i, in1=pw_r[:, :, lg:lg + 1].to_broadcast([128, 16, 16]), op=ALU.mult), [r_sm, r_T], [r_T])
            VEC(lambda e, lg=lg: e.tensor_tensor(out=T4, in0=Bb_r, in1=pw_i[:, :, lg:lg + 1].to_broadcast([128, 16, 16]), op=ALU.mult), [r_sm, r_T], [r_T])
            for hf in range(2):
                POOL(lambda e, lg=lg, hf=hf: e.tensor_tensor(out=blkME(ME_r, lg, hf), in0=halfT(T1, hf), in1=halfT(T2, hf), op=ALU.subtract), [r_T], [r_ME])
                POOL(lambda e, lg=lg, hf=hf: e.tensor_tensor(out=blkME(ME_i, lg, hf), in0=halfT(T3, hf), in1=halfT(T4, hf), op=ALU.add), [r_T], [r_ME])

        def blkMF(M, hf):
            return M[hf * 64:(hf + 1) * 64, :, :].rearrange("p ct (q x) -> p ct q x", x=32)[:, :, :, hf * 16:(hf + 1) * 16]

        for hf in range(2):
            POOL(lambda e, hf=hf: e.tensor_copy(out=blkMF(MF_r, hf), in_=halfT(Cri[:, 0], hf)), [r_bc], [r_MF])
            POOL(lambda e, hf=hf: e.tensor_scalar(out=blkMF(MF_in, hf), in0=halfT(Cri[:, 1], hf), scalar1=-1.0, scalar2=None, op0=ALU.mult), [r_bc], [r_MF])
        for ct in range(4):
            for ri, M in ((0, ME_r), (1, ME_i)):
                for j0 in (0, 4):
                    bk, _, r_bk = next_bank()
                    for jj in range(4):
                        j = j0 + jj
                        PE(lambda e, bk=bk, jj=jj, j=j, ct=ct, M=M: e.transpose(out=bk[:, jj * 128:(jj + 1) * 128], in_=M[:, 7 - j, ct, :], identity=identf[:]),
                           [r_ME, r_const], [r_bk])
                    ACT(lambda e, bk=bk, ct=ct, j0=j0, ri=ri: e.copy(out=Wst[:, ct, j0:j0 + 4, ri, :], in_=bk[:, :].rearrange("p (a b) -> p a b", b=128)),
                        [r_bk], [r_Wst])
        for ct in range(4):
            for l0 in (0, 4):
                bk, _, r_bk = next_bank()
                for ll in range(4):
                    lg = l0 + ll
                    PE(lambda e, bk=bk, ll=ll, lg=lg, ct=ct: e.matmul(bk[:, ll * 128:(ll + 1) * 128], lhsT=ME_r[:, lg, ct, :], rhs=MF_r[:, ct, :], start=True, stop=False),
                       [r_ME, r_MF], [r_bk])
                    PE(lambda e, bk=bk, ll=ll, lg=lg, ct=ct: e.matmul(bk[:, ll * 128:(ll + 1) * 128], lhsT=ME_i[:, lg, ct, :], rhs=MF_in[:, ct, :], start=False, stop=True),
                       [r_ME, r_MF], [r_bk])
                VEC(lambda e, bk=bk, ct=ct, l0=l0: e.tensor_tensor(out=Wfir[:, ct, l0:l0 + 4, :], in0=bk[:, :].rearrange("p (a b) -> p a b", b=128),
                                                                 in1=mask16[:, :].rearrange("p (o b) -> p o b", o=1).to_broadcast([128, 4, 128]), op=ALU.mult),
                    [r_bk, r_const], [r_Wfir])
                if l0 == 0:
                    VEC(lambda e, bk=bk: e.tensor_tensor(out=tmpF, in0=bk[:, 0:128], in1=mask16[:, :], op=ALU.mult), [r_bk, r_const], [r_tmpF])
                    VEC(lambda e, ct=ct: e.scalar_tensor_tensor(out=Wfir[:, ct, 0, :], in0=identf[:, :], scalar=col(l, C_D + ct), in1=tmpF, op0=ALU.mult, op1=ALU.add),
                        [r_tmpF, r_const], [r_Wfir])
        for i in range(8):
            VEC(lambda e, i=i: e.tensor_tensor(out=T1, in0=Cri[:, 0], in1=pw_r[:, :, i + 1:i + 2].to_broadcast([128, 16, 16]), op=ALU.mult), [r_sm, r_bc, r_T], [r_T])
            VEC(lambda e, i=i: e.tensor_tensor(out=T2, in0=Cri[:, 1], in1=pw_i[:, :, i + 1:i + 2].to_broadcast([128, 16, 16]), op=ALU.mult), [r_sm, r_bc, r_T], [r_T])
            VEC(lambda e, i=i: e.tensor_tensor(out=T3, in0=Cri[:, 1], in1=pw_r[:, :, i + 1:i + 2].to_broadcast([128, 16, 16]), op=ALU.mult), [r_sm, r_bc, r_T], [r_T])
            VEC(lambda e, i=i: e.tensor_tensor(out=T4, in0=Cri[:, 0], in1=pw_i[:, :, i + 1:i + 2].to_broadcast([128, 16, 16]), op=ALU.mult), [r_sm, r_bc, r_T], [r_T])
            for hf in range(2):
                hs = slice(hf * 64, (hf + 1) * 64)
                cs = slice(hf * 16, (hf + 1) * 16)
                POOL(lambda e, i=i, hs=hs, cs=cs: e.tensor_tensor(out=Wo_r[hs, :, i, cs], in0=T1[hs], in1=T2[hs], op=ALU.subtract), [r_T], [r_Wo])
                VEC(lambda e, i=i, hs=hs, cs=cs: e.scalar_tensor_tensor(out=Wo_i[hs, :, i, cs], in0=T3[hs], scalar=-1.0, in1=T4[hs], op0=ALU.mult, op1=ALU.subtract),
                    [r_T], [r_Wo])
        P.barrier()
        arena_off[0] = mark
        u_ring = ARing(2, [128, S], BF16, "uct")
        y_ring = ARing(1, [128, S], F32, "yct")
        tab_r, r_tab = aalloc([128, 4, 512], F32, "tab")
        tab_i, _ = aalloc([128, 4, 512], F32)
        tq1, r_tq = aalloc([128, 4, 256], F32, "tq")
        tq2, _ = aalloc([128, 4, 256], F32)
        xps = []
        for i_ in range(2):
            xr_, rx_ = aalloc([128, 4, 512], BF16, f"xpr{i_}")
            xi_, _ = aalloc([128, 4, 512], BF16)
            VEC(lambda e, xr_=xr_: e.memset(xr_[:, :, 0:1], 0.0), [], [rx_])
            VEC(lambda e, xi_=xi_: e.memset(xi_[:, :, 0:1], 0.0), [], [rx_])
            xps.append((xr_, xi_, rx_))

        class _AR:
            def __init__(self, bufs):
                self.bufs = bufs
                self.i = 0

            def next(self):
                b_ = self.bufs[self.i % len(self.bufs)]
                self.i += 1
                return b_
        A_ring = _AR([aalloc([128, 512], F32, f"A{i_}") for i_ in range(6)] + [(frB[0][:, i_ * 512:(i_ + 1) * 512], Res(f"AB{i_}")) for i_ in range(4)])
        ctxS = {}

        def stageS(ct):
            xp_r, xp_i, r_xp = xps[ct % 2]
            uct, r_uct = u_ring.next()
            DMA(uct, uT_d[:, ct, :], [r_ud], [r_uct])
            u8 = uct.rearrange("p (c j) -> p j c", j=8)
            ctxS[ct] = (uct, r_uct, u8)
            VEC(lambda e: e.memset(tab_r[:, :, 0:1], 1.0), [r_tab], [r_tab])
            VEC(lambda e: e.memset(tab_i[:, :, 0:1], 0.0), [r_tab], [r_tab])
            for k in range(9):
                yield
                s_ = 1 << k
                phr = ph_r[:, 4 * ct:4 * ct + 4, k:k + 1].to_broadcast([128, 4, s_])
                phi = ph_i[:, 4 * ct:4 * ct + 4, k:k + 1].to_broadcast([128, 4, s_])
                VEC(lambda e, s_=s_, phr=phr: e.tensor_tensor(out=tq1[:, :, 0:s_], in0=tab_r[:, :, 0:s_], in1=phr, op=ALU.mult), [r_tab, r_sm, r_tq], [r_tq])
                VEC(lambda e, s_=s_, phi=phi: e.tensor_tensor(out=tq2[:, :, 0:s_], in0=tab_i[:, :, 0:s_], in1=phi, op=ALU.mult), [r_tab, r_sm, r_tq], [r_tq])
                VEC(lambda e, s_=s_: e.tensor_tensor(out=tab_r[:, :, s_:2 * s_], in0=tq1[:, :, 0:s_], in1=tq2[:, :, 0:s_], op=ALU.subtract), [r_tq], [r_tab])
                VEC(lambda e, s_=s_, phi=phi: e.tensor_tensor(out=tq1[:, :, 0:s_], in0=tab_r[:, :, 0:s_], in1=phi, op=ALU.mult), [r_tab, r_sm, r_tq], [r_tq])
                VEC(lambda e, s_=s_, phr=phr: e.tensor_tensor(out=tq2[:, :, 0:s_], in0=tab_i[:, :, 0:s_], in1=phr, op=ALU.mult), [r_tab, r_sm, r_tq], [r_tq])
                VEC(lambda e, s_=s_: e.tensor_tensor(out=tab_i[:, :, s_:2 * s_], in0=tq1[:, :, 0:s_], in1=tq2[:, :, 0:s_], op=ALU.add), [r_tq], [r_tab])
            for q in range(4):
                yield
                pair = 4 * ct + q
                ps_ = slice(32 * q, 32 * q + 32)
                bks = []
                for ri in range(2):
                    bk, _, r_bk = next_bank()
                    for j in range(8):
                        PE(lambda e, bk=bk, j=j, ri=ri, ps_=ps_, q=q, ct=ct, u8=u8: e.matmul(bk[:, :], lhsT=Wst[ps_, ct, j, ri, :], rhs=u8[ps_, j, :],
                                                                                           start=(j == 0), stop=(j == 7), tile_position=(32 * q, 0)),
                           [r_Wst, r_uct], [r_bk])
                    bks.append((bk, r_bk))
                (Sr, r_Sr), (Si, r_Si) = bks
                tr, ti = tab_r[:, q, :], tab_i[:, q, :]
                a1, r_a1 = A_ring.next()
                a2, r_a2 = A_ring.next()
                a3, r_a3 = A_ring.next()
                a4, r_a4 = A_ring.next()
                VEC(lambda e, Sr=Sr, a1=a1, tr=tr: e.tensor_tensor(out=a1, in0=Sr[:, :], in1=tr, op=ALU.mult), [r_Sr, r_tab], [r_a1])
                VEC(lambda e, Si=Si, a2=a2, ti=ti: e.tensor_tensor(out=a2, in0=Si[:, :], in1=ti, op=ALU.mult), [r_Si, r_tab], [r_a2])
                VEC(lambda e, Si=Si, a3=a3, tr=tr: e.tensor_tensor(out=a3, in0=Si[:, :], in1=tr, op=ALU.mult), [r_Si, r_tab], [r_a3])
                VEC(lambda e, Sr=Sr, a4=a4, ti=ti: e.tensor_tensor(out=a4, in0=Sr[:, :], in1=ti, op=ALU.mult), [r_Sr, r_tab], [r_a4])
                POOL(lambda e, a1=a1, a2=a2: e.tensor_tensor(out=a1, in0=a1, in1=a2, op=ALU.add), [r_a1, r_a2], [r_a1])
                POOL(lambda e, a3=a3, a4=a4: e.tensor_tensor(out=a3, in0=a3, in1=a4, op=ALU.subtract), [r_a3, r_a4], [r_a3])
                yield
                rd = sm[:, RDEC, pair:pair + 1].to_broadcast([128, 512])
                VEC(lambda e, a1=a1, a2=a2, rd=rd: e.tensor_tensor_scan(out=a2, data0=rd, data1=a1, initial=0.0, op0=ALU.mult, op1=ALU.add), [r_a1, r_sm, r_a2], [r_a2])
                VEC(lambda e, a3=a3, a4=a4, rd=rd: e.tensor_tensor_scan(out=a4, data0=rd, data1=a3, initial=0.0, op0=ALU.mult, op1=ALU.add), [r_a3, r_sm, r_a4], [r_a4])
                yield
                b1, r_b1 = A_ring.next()
                b2, r_b2 = A_ring.next()
                POOL(lambda e, a2=a2, b1=b1, tr=tr: e.tensor_tensor(out=b1, in0=a2, in1=tr, op=ALU.mult), [r_a2, r_tab], [r_b1])
                POOL(lambda e, a4=a4, b2=b2, ti=ti: e.tensor_tensor(out=b2, in0=a4, in1=ti, op=ALU.mult), [r_a4, r_tab], [r_b2])
                VEC(lambda e, b1=b1, b2=b2, q=q: e.tensor_tensor(out=xp_r[:, q, 1:512], in0=b1[:, 0:511], in1=b2[:, 0:511], op=ALU.subtract), [r_b1, r_b2], [r_xp])
                POOL(lambda e, a2=a2, a1=a1, ti=ti: e.tensor_tensor(out=a1, in0=a2, in1=ti, op=ALU.mult), [r_a2, r_tab, r_a1], [r_a1])
                POOL(lambda e, a4=a4, a3=a3, tr=tr: e.tensor_tensor(out=a3, in0=a4, in1=tr, op=ALU.mult), [r_a4, r_tab, r_a3], [r_a3])
                VEC(lambda e, a1=a1, a3=a3, q=q: e.tensor_tensor(out=xp_i[:, q, 1:512], in0=a1[:, 0:511], in1=a3[:, 0:511], op=ALU.add), [r_a1, r_a3], [r_xp])

        def stageO(ct):
            xp_r, xp_i, r_xp = xps[ct % 2]
            uct, r_uct, u8 = ctxS.pop(ct)
            yct, r_yct = y_ring.next()
            y8 = yct.rearrange("p (c j) -> p j c", j=8)
            for i in range(8):
                yield
                bk, _, r_bk = next_bank()
                for lg in range(i + 1):
                    PE(lambda e, bk=bk, lg=lg, i=i, ct=ct, u8=u8: e.matmul(bk[:, :], lhsT=Wfir[:, ct, lg, :], rhs=u8[:, i - lg, :], start=(lg == 0), stop=False),
                       [r_Wfir, r_uct], [r_bk])
                for q in range(4):
                    pair = 4 * ct + q
                    PE(lambda e, bk=bk, q=q, pair=pair, i=i: e.matmul(bk[32 * q:32 * q + 32, :], lhsT=Wo_r[:, pair, i, :], rhs=xp_r[:, q, :], start=False, stop=False,
                                                                     tile_position=(0, 32 * q)), [r_Wo, r_xp], [r_bk])
                    PE(lambda e, bk=bk, q=q, pair=pair, i=i: e.matmul(bk[32 * q:32 * q + 32, :], lhsT=Wo_i[:, pair, i, :], rhs=xp_i[:, q, :], start=False, stop=True,
                                                                     tile_position=(0, 32 * q)), [r_Wo, r_xp], [r_bk])
                ACT(lambda e, bk=bk, y8=y8, i=i: e.copy(out=y8[:, i, :], in_=bk[:, :]), [r_bk], [r_yct])
            DMA(yT_d[:, ct, :], yct, [r_yct], [r_yd], q="scalar")

        for ct_ in range(5):
            gl = []
            if ct_ >= 1:
                gl.append(stageO(ct_ - 1))
            if ct_ < 4:
                gl.append(stageS(ct_))
            run_interleaved(gl)
        if stop_after == "B2a":
            return True
        P.barrier()
        areset()
        wglu, r_wglu = aalloc([128, 4, 512], BF16, "wglu")
        DMAC(wglu, w_glu_d[l].rearrange("(k p) n -> p k n", p=128), [], [r_wglu])
        yb_ring = ARing(2, [128, 4, 512], F32, "yb")
        g_ring = ARing(2, [128, 4, 512], BF16, "gT")
        sg_ring = ARing(2, [128, 4, 512], F32, "sg")
        sq_ring = ARing(2, [128, 4, 512], F32, "sq")
        rs_ring = ARing(2, [128, 512], F32, "rs")
        sn_ring = ARing(2, [128, 4, 512], BF16, "sn")
        glu_end = arena_off[0]
        wo_a, r_wc = aalloc([128, 4, 1024], BF16, "wo_a")
        wo_s, _ = aalloc([128, 4, 1024], BF16)
        wxq, _ = aalloc([128, 8, 1024], BF16)
        wxo, _ = aalloc([128, 8, 1024], BF16)
        KxT, r_kx = aalloc([128, 8, 256], BF16, "KxT")
        Vx, _ = aalloc([128, 2, 1024], BF16)
        c1w_end = arena_off[0]
        for par_ in range(2):
            DMAC(wo_a[par_ * 64:(par_ + 1) * 64, :, :], w_out_d[l, 0:512, :].rearrange("(hp par d) n -> par d hp n", par=2, d=64)[par_], [], [r_wc])
        VEC(lambda e: e.tensor_tensor(out=wo_a, in0=wo_a, in1=col(l, C_GA2, 4).rearrange("p (k o) -> p k o", o=1).to_broadcast([128, 4, 1024]), op=ALU.mult), [r_wc, r_const], [r_wc])
        DMAC(wo_s, w_out_d[l, 512:1024, :].rearrange("(k p) n -> p k n", p=128), [], [r_wc])
        DMAC(wxq, w_xq_d[l].rearrange("(k p) n -> p k n", p=128), [], [r_wc])
        DMAC(wxo, w_xo_d[l].rearrange("(k p) n -> p k n", p=128), [], [r_wc])
        def stageG(b):
            T0 = b * 512
            yb, r_yb = yb_ring.next()
            DMA(yb, yT_d[:, :, T0:T0 + 512], [r_yd], [r_yb])
            yield
            gT, r_gT = g_ring.next()
            ACT(lambda e, yb=yb, gT=gT: e.activation(out=gT, in_=yb, func=AF.Gelu_apprx_tanh), [r_yb], [r_gT])
            sg, r_sg = sg_ring.next()
            for co in range(4):
                yield
                bk, _, r_bk = next_bank()
                for ci in range(4):
                    PE(lambda e, bk=bk, ci=ci, co=co, gT=gT: e.matmul(bk[:, :], lhsT=wglu[:, ci, co * 128:(co + 1) * 128], rhs=gT[:, ci, :], start=(ci == 0), stop=(ci == 3)),
                       [r_wglu, r_gT], [r_bk])
                ACT(lambda e, bk=bk, co=co, sg=sg: e.activation(out=sg[:, co, :], in_=bk[:, :], func=AF.Sigmoid, bias=col(l, C_BGLU + co)), [r_bk, r_const], [r_sg])
            yield
            VEC(lambda e, sg=sg, yb=yb: e.tensor_tensor(out=sg, in0=sg, in1=yb, op=ALU.mult), [r_sg, r_yb], [r_sg])
            yield
            sq, r_sq = sq_ring.next()
            POOL(lambda e, sg=sg, sq=sq: e.tensor_tensor(out=sq, in0=sg, in1=sg, op=ALU.mult), [r_sg], [r_sq])
            yield
            bk, _, r_bk = next_bank()
            for co in range(4):
                PE(lambda e, bk=bk, co=co, sq=sq: e.matmul(bk[:, :], lhsT=onesf[:, :], rhs=sq[:, co, :], start=(co == 0), stop=(co == 3)), [r_sq, r_const], [r_bk])
            yield
            rs_, r_rs = rs_ring.next()
            ACT(lambda e, bk=bk, rs_=rs_: e.activation(out=rs_, in_=bk[:, :], func=AF.Sqrt, scale=1.0 / 512, bias=EPS), [r_bk], [r_rs])
            yield
            VEC(lambda e, rs_=rs_: e.reciprocal(out=rs_, in_=rs_), [r_rs], [r_rs])
            VEC(lambda e, sg=sg: e.tensor_tensor(out=sg, in0=sg, in1=col(l, C_GS, 4).rearrange("p (k o) -> p k o", o=1).to_broadcast([128, 4, 512]), op=ALU.mult),
                [r_sg, r_const], [r_sg])
            yield
            sn, r_sn = sn_ring.next()
            VEC(lambda e, sg=sg, sn=sn, rs_=rs_: e.tensor_tensor(out=sn, in0=sg, in1=rs_.rearrange("p (o t) -> p o t", o=1).to_broadcast([128, 4, 512]), op=ALU.mult),
                [r_sg, r_rs], [r_sn])
            DMA(sT_d[b], sn, [r_sn], [r_sd])

        for b_ in range(0, NB, 2):
            run_interleaved([stageG(b_), stageG(b_ + 1)])
        if stop_after == "B2":
            return True
        P.barrier()
        arena_off[0] = 0
        wxkv, r_wxkv = aalloc([128, 8, 2048], BF16, "wxkv")
        assert arena_off[0] <= glu_end
        DMAC(wxkv, w_xkv_d[l].rearrange("(k p) n -> p k n", p=128), [], [r_wxkv])
        memT, r_memT = xnT_ring.next()
        for mt in range(2):
            ht, r_ht = ht_ring.next()
            DMA(ht[:], mem_d[mt * 128:(mt + 1) * 128, :], [], [r_ht])
            norm_transpose(ht, r_ht, col(l, C_GMEM, 8), memT, r_memT, mt)
        for oc in range(8):
            bk, _, r_bk = next_bank()
            for k in range(8):
                PE(lambda e, bk=bk, k=k, oc=oc: e.matmul(bk[:, 0:256], lhsT=wxkv[:, k, oc * 128:(oc + 1) * 128], rhs=memT[:, k, 0:256], start=(k == 0), stop=(k == 7)),
                   [r_wxkv, r_memT], [r_bk])
            ACT(lambda e, bk=bk, oc=oc: e.copy(out=KxT[:, oc, :], in_=bk[:, 0:256]), [r_bk], [r_kx])
        for mt in range(2):
            for hf in range(2):
                bk, _, r_bk = next_bank()
                for k in range(8):
                    PE(lambda e, bk=bk, k=k, mt=mt, hf=hf: e.matmul(bk[:, :], lhsT=memT[:, k, mt * 128:(mt + 1) * 128], rhs=wxkv[:, k, 1024 + hf * 512:1024 + (hf + 1) * 512],
                                                                  start=(k == 0), stop=(k == 7)), [r_wxkv, r_memT], [r_bk])
                ACT(lambda e, bk=bk, mt=mt, hf=hf: e.copy(out=Vx[:, mt, hf * 512:(hf + 1) * 512], in_=bk[:, :]), [r_bk], [r_kx])
        P.barrier()
        arena_off[0] = 0
        h1_ring = ARing(8, [128, D], F32, "h1t")
        qx_ring = ARing(1, [128, 8, 512], BF16, "qxT")
        ox_ring = ARing(1, [128, 8, 512], BF16, "oxT")
        an_ring2 = ARing(1, [128, 4, 512], BF16, "an2")
        sn_ring2 = ARing(1, [128, 4, 512], BF16, "sn2")
        pX_ring = ARing(2, [128, 2, 512], BF16, "pX")
        rc_ring = ARing(2, [128, 512], F32, "rc")
        assert arena_off[0] <= glu_end, (arena_off[0], glu_end)
        araw_v = frA[0][0:64, 0:4096].rearrange("p (h t) -> p h t", t=512)
        araw4 = frA[0][0:64, 0:4096].rearrange("p (hp par t) -> p hp par t", par=2, t=512)
        r_araw = frA[1]
        sqv = frB[0][0:64, 0:2048].rearrange("p (h t) -> p h t", t=512)
        r_sqv = frB[1]
        r_h1src = [Res() for _ in range(32)] if l == 0 else r_hd
        ctx1 = {}

        def stageC1a(b):
            T0 = b * 512
            DMA(araw_v, aT_d[b], [r_ad], [r_araw])
            sn2, r_sn2 = sn_ring2.next()
            DMA(sn2, sT_d[b], [r_sd], [r_sn2])
            bk, _, r_bk = next_bank()
            for hg in range(2):
                POOL(lambda e, hg=hg: e.tensor_tensor(out=sqv, in0=araw_v[:, hg * 4:(hg + 1) * 4, :], in1=araw_v[:, hg * 4:(hg + 1) * 4, :], op=ALU.mult), [r_araw], [r_sqv])
                for hh in range(4):
                    PE(lambda e, bk=bk, hg=hg, hh=hh: e.matmul(bk[0:64, :], lhsT=onesf[0:64, 0:64], rhs=sqv[:, hh, :], start=(hg == 0 and hh == 0), stop=(hg == 1 and hh == 3)),
                       [r_sqv, r_const], [r_bk])
                yield
            rc, r_rc = rc_ring.next()
            ACT(lambda e, bk=bk, rc=rc: e.activation(out=rc[0:64, :], in_=bk[0:64, :], func=AF.Sqrt, scale=1.0 / 512, bias=EPS), [r_bk], [r_rc])
            VEC(lambda e, rc=rc: e.reciprocal(out=rc[0:64, :], in_=rc[0:64, :]), [r_rc], [r_rc])
            yield
            an2, r_an2 = an_ring2.next()
            rcb = rc[0:64, :].rearrange("p (o t) -> p o t", o=1).to_broadcast([64, 4, 512])
            VEC(lambda e, an2=an2, rcb=rcb: e.tensor_tensor(out=an2[0:64, :, :], in0=araw4[:, :, 0, :], in1=rcb, op=ALU.mult), [r_araw, r_rc], [r_an2])
            yield
            VEC(lambda e, an2=an2, rcb=rcb: e.tensor_tensor(out=an2[64:128, :, :], in0=araw4[:, :, 1, :], in1=rcb, op=ALU.mult), [r_araw, r_rc], [r_an2])
            yield
            xnT, r_xnT = xnT_ring.next()
            h1s = []
            ctx1[b] = (xnT, r_xnT, h1s)
            for sub in range(4):
                ti = b * 4 + sub
                ts_ = slice(sub * 128, (sub + 1) * 128)
                ht, r_ht = ht_ring.next()
                DMA(ht[:], src_d[T0 + sub * 128:T0 + (sub + 1) * 128, :], [r_h1src[ti]], [r_ht])
                h1t, r_h1t = h1_ring.next()
                for hf in range(2):
                    bk, _, r_bk = next_bank()
                    cs_ = slice(hf * 512, (hf + 1) * 512)
                    for k in range(4):
                        PE(lambda e, bk=bk, k=k, an2=an2, ts_=ts_, cs_=cs_: e.matmul(bk[:, :], lhsT=an2[:, k, ts_], rhs=wo_a[:, k, cs_], start=(k == 0), stop=False),
                           [r_an2, r_wc], [r_bk])
                    for k in range(4):
                        PE(lambda e, bk=bk, k=k, sn2=sn2, ts_=ts_, cs_=cs_: e.matmul(bk[:, :], lhsT=sn2[:, k, ts_], rhs=wo_s[:, k, cs_], start=False, stop=(k == 3)),
                           [r_sn2, r_wc], [r_bk])
                    VEC(lambda e, bk=bk, ht=ht, h1t=h1t, cs_=cs_: e.tensor_tensor(out=h1t[:, cs_], in0=bk[:, :], in1=ht[:, cs_], op=ALU.add), [r_bk, r_ht], [r_h1t])
                    yield
                norm_transpose(h1t, r_h1t, col(l, C_GX, 8), xnT, r_xnT, sub)
                h1s.append((h1t, r_h1t))
                yield

        def stageC1b(b):
            T0 = b * 512
            xnT, r_xnT, h1s = ctx1.pop(b)
            qxT, r_qxT = qx_ring.next()
            for oc in range(8):
                bk, _, r_bk = next_bank()
                for k in range(8):
                    PE(lambda e, bk=bk, k=k, oc=oc, xnT=xnT: e.matmul(bk[:, :], lhsT=wxq[:, k, oc * 128:(oc + 1) * 128], rhs=xnT[:, k, :], start=(k == 0), stop=(k == 7)),
                       [r_wc, r_xnT], [r_bk])
                ACT(lambda e, bk=bk, oc=oc, qxT=qxT: e.copy(out=qxT[:, oc, :], in_=bk[:, :]), [r_bk], [r_qxT])
                if oc % 2 == 1:
                    yield
            oxT, r_oxT = ox_ring.next()
            for hx in range(4):
                pX, r_pX = pX_ring.next()
                for mt in range(2):
                    bk, _, r_bk = next_bank()
                    for dc in range(2):
                        PE(lambda e, bk=bk, dc=dc, mt=mt, hx=hx, qxT=qxT: e.matmul(bk[:, :], lhsT=KxT[:, hx * 2 + dc, mt * 128:(mt + 1) * 128], rhs=qxT[:, hx * 2 + dc, :],
                                                                              start=(dc == 0), stop=(dc == 1)), [r_kx, r_qxT], [r_bk])
                    ACT(lambda e, bk=bk, mt=mt, pX=pX: e.activation(out=pX[:, mt, :], in_=bk[:, :], func=AF.Exp, scale=X_SCALE), [r_bk], [r_pX])
                yield
                bk, _, r_bk = next_bank()
                for mt in range(2):
                    PE(lambda e, bk=bk, mt=mt, pX=pX: e.matmul(bk[:, :], lhsT=onesb[:, :], rhs=pX[:, mt, :], start=(mt == 0), stop=(mt == 1)), [r_pX, r_const], [r_bk])
                rc, r_rc = rc_ring.next()
                VEC(lambda e, bk=bk, rc=rc: e.reciprocal(out=rc, in_=bk[:, :]), [r_bk], [r_rc])
                yield
                for dc in range(2):
                    bk, _, r_bk = next_bank()
                    for mt in range(2):
                        PE(lambda e, bk=bk, mt=mt, dc=dc, hx=hx, pX=pX: e.matmul(bk[:, :], lhsT=Vx[:, mt, (hx * 2 + dc) * 128:(hx * 2 + dc + 1) * 128], rhs=pX[:, mt, :],
                                                                             start=(mt == 0), stop=(mt == 1)), [r_kx, r_pX], [r_bk])
                    VEC(lambda e, bk=bk, dc=dc, hx=hx, rc=rc, oxT=oxT: e.tensor_tensor(out=oxT[:, hx * 2 + dc, :], in0=bk[:, :], in1=rc, op=ALU.mult), [r_bk, r_rc], [r_oxT])
                yield
            for sub in range(4):
                ti = b * 4 + sub
                ts_ = slice(sub * 128, (sub + 1) * 128)
                h1t, r_h1t = h1s[sub]
                for hf in range(2):
                    bk, _, r_bk = next_bank()
                    cs_ = slice(hf * 512, (hf + 1) * 512)
                    for k in range(8):
                        PE(lambda e, bk=bk, k=k, oxT=oxT, ts_=ts_, cs_=cs_: e.matmul(bk[:, :], lhsT=oxT[:, k, ts_], rhs=wxo[:, k, cs_], start=(k == 0), stop=(k == 7)),
                           [r_oxT, r_wc], [r_bk])
                    VEC(lambda e, bk=bk, h1t=h1t, cs_=cs_: e.tensor_tensor(out=h1t[:, cs_], in0=bk[:, :], in1=h1t[:, cs_], op=ALU.add), [r_bk, r_h1t], [r_h1t])
                    yield
                DMA(h1_d[T0 + sub * 128:T0 + (sub + 1) * 128, :], h1t[:], [r_h1t], [r_h1d[ti]])

        for b in range(NB + 1):
            gl = []
            if b >= 1:
                gl.append(stageC1b(b - 1))
            if b < NB:
                gl.append(stageC1a(b))
            run_interleaved(gl)
        if stop_after == "C1":
            return True

        P.barrier()
        areset()
        wg, r_wf = aalloc([128, 8, DFF], BF16, "wg")
        wu, _ = aalloc([128, 8, DFF], BF16)
        wd, _ = aalloc([128, NFF, 1024], BF16)
        FG = [(0, 768), (768, 1536), (1536, 2176), (2176, 2816)]
        r_wgu = [Res(f"wgu{g_}") for g_ in range(4)]
        r_wdk = [Res(f"wd{k_}") for k_ in range(NFF // 2)]
        for g_, (c0_, c1_) in enumerate(FG):
            DMAC(wg[:, :, c0_:c1_], w_gate_d[l][:, c0_:c1_].rearrange("(k p) n -> p k n", p=128), [], [r_wgu[g_]])
            DMAC(wu[:, :, c0_:c1_], w_up_d[l][:, c0_:c1_].rearrange("(k p) n -> p k n", p=128), [], [r_wgu[g_]])
        for k in range(0, NFF, 2):
            DMAC(wd[:, k:k + 2, :], w_down_d[l, k * 128:(k + 2) * 128, :].rearrange("(k p) n -> p k n", p=128), [], [r_wdk[k // 2]])

        def fgrp(fc):
            for g_, (c0_, c1_) in enumerate(FG):
                if c0_ <= fc * 128 < c1_:
                    return r_wgu[g_]
        actT = frA[0].bitcast(BF16) if hasattr(frA[0], "bitcast") else None
        actT = actT[:, 0:NFF * 512].rearrange("p (f t) -> p f t", t=512)
        r_actT = frA[1]
        sgs = [(frB[0][:, i * 512:(i + 1) * 512], Res()) for i in range(4)]
        sg_i = [0]
        last = (l == L - 1)
        ctxC = {}
        nt_ring = Ring(P, f"ntl{l}", 1, [128, 4], F32) if False else None

        def stageC2a(b):
            T0 = b * 512
            xnT, r_xnT = xnT_ring.next()
            ctxC[b] = (xnT, r_xnT)
            for sub in range(4):
                ti = b * 4 + sub
                ht, r_ht = ht_ring.next()
                DMA(ht[:], h1_d[T0 + sub * 128:T0 + (sub + 1) * 128, :], [r_h1d[ti]], [r_ht])
                norm_transpose(ht, r_ht, col(l, C_GFFN, 8), xnT, r_xnT, sub)
                yield

        def stageC2b(b):
            T0 = b * 512
            xnT, r_xnT = ctxC.pop(b)
            for fc in range(NFF):
                if fc % 3 == 0:
                    yield
                bkg, _, r_bkg = next_bank()
                bku, _, r_bku = next_bank()
                for k in range(8):
                    PE(lambda e, bkg=bkg, k=k, fc=fc, xnT=xnT: e.matmul(bkg[:, :], lhsT=wg[:, k, fc * 128:(fc + 1) * 128], rhs=xnT[:, k, :], start=(k == 0), stop=(k == 7)),
                       [fgrp(fc), r_xnT], [r_bkg])
                for k in range(8):
                    PE(lambda e, bku=bku, k=k, fc=fc, xnT=xnT: e.matmul(bku[:, :], lhsT=wu[:, k, fc * 128:(fc + 1) * 128], rhs=xnT[:, k, :], start=(k == 0), stop=(k == 7)),
                       [fgrp(fc), r_xnT], [r_bku])
                sgt, r_sgt = sgs[sg_i[0] % 4]
                sg_i[0] += 1
                ACT(lambda e, bkg=bkg, sgt=sgt: e.activation(out=sgt, in_=bkg[:, :], func=AF.Silu), [r_bkg], [r_sgt])
                VEC(lambda e, bku=bku, sgt=sgt, fc=fc: e.tensor_tensor(out=actT[:, fc, :], in0=bku[:, :], in1=sgt, op=ALU.mult), [r_bku, r_sgt], [r_actT])
            for sub in range(4):
                ti = b * 4 + sub
                ts_ = slice(sub * 128, (sub + 1) * 128)
                ht, r_ht = ht_ring.next()
                DMA(ht[:], h1_d[T0 + sub * 128:T0 + (sub + 1) * 128, :], [r_h1d[ti]], [r_ht])
                for hf in range(2):
                    yield
                    bk, _, r_bk = next_bank()
                    cs_ = slice(hf * 512, (hf + 1) * 512)
                    for fc in range(NFF):
                        PE(lambda e, bk=bk, fc=fc, ts_=ts_, cs_=cs_: e.matmul(bk[:, :], lhsT=actT[:, fc, ts_], rhs=wd[:, fc, cs_], start=(fc == 0), stop=(fc == NFF - 1)),
                           [r_actT, r_wdk[fc // 2]], [r_bk])
                    VEC(lambda e, bk=bk, ht=ht, cs_=cs_: e.tensor_tensor(out=ht[:, cs_], in0=bk[:, :], in1=ht[:, cs_], op=ALU.add), [r_bk, r_ht], [r_ht])
                if not last:
                    DMA(h_d[T0 + sub * 128:T0 + (sub + 1) * 128, :], ht[:], [r_ht], [r_hd[ti]])
                else:
                    st, r_st = st_ring.next()
                    rms_stats(ht[:], r_ht, D, st, r_st, 0)
                    VEC(lambda e, ht=ht, st=st: e.scalar_tensor_tensor(out=ht[:], in0=ht[:], scalar=st[:, 0:1], in1=fing[:], op0=ALU.mult, op1=ALU.mult),
                        [r_ht, r_st, r_const], [r_ht])
                    final_ops.append(DMA(out_d[T0 + sub * 128:T0 + (sub + 1) * 128, :], ht[:], [r_ht], []))


        def run_il(gens):
            gens = list(gens)
            while gens:
                for g_ in list(gens):
                    try:
                        next(g_)
                    except StopIteration:
                        gens.remove(g_)

        for b in range(NB + 1):
            gl = []
            if b >= 1:
                gl.append(stageC2b(b - 1))
            if b < NB:
                gl.append(stageC2a(b))
            run_il(gl)
        return False

    for l_ in range(n_layers):
        if emit_layer(l_):
            break

    if debug:
        def dump(name, src, shape, dt, r):
            final_ops.append(DMA(dbg_out(name, shape, dt), src, [r], []))
        P.barrier()
        dump("qT", qT_d[:, :, 3584:4096], [8, 96, 512], BF16, r_qd)
        dump("kT", kT_d[:, :, 3584:4096], [8, 96, 512], BF16, r_kd)
        dump("V", V_d[:, :, 28:32, :], [8, 128, 4, 65], BF16, r_vd)
        dump("uT", uT_d[:, :, 3584:4096], [128, 4, 512], BF16, r_ud)
        dump("aT0", aT_d[0], [64, 8, 512], F32, r_ad)
        dump("aT7", aT_d[7], [64, 8, 512], F32, r_ad)
        dump("yT", yT_d[:, :, 3584:4096], [128, 4, 512], F32, r_yd)
        dump("yT0", yT_d[:, :, 0:512], [128, 4, 512], F32, r_yd)
        dump("sT7", sT_d[7], [128, 4, 512], BF16, r_sd)
        dump("sT0", sT_d[0], [128, 4, 512], BF16, r_sd)
        dump("h1", h1_d[3968:4096, :], [128, D], F32, r_h1d[31])
        dump("h", h_d[3968:4096, :], [128, D], F32, r_hd[31])
    P.barrier()
    return P.build(final_waits=final_ops), dbg


def prep_inputs(inp):
    f = np.float32
    g = lambda k: np.asarray(inp[k])
    cols = np.zeros((128, L, NCOL), f)
    for l in range(L):
        cols[:, l, C_GMIX:C_GMIX + 8] = g("norm_mix_g")[l].reshape(8, 128).T
        cols[:, l, C_GX:C_GX + 8] = g("norm_x_g")[l].reshape(8, 128).T
        cols[:, l, C_GFFN:C_GFFN + 8] = g("norm_ffn_g")[l].reshape(8, 128).T
        cols[:, l, C_GMEM:C_GMEM + 8] = g("mem_norm_g")[l].reshape(8, 128).T
        cols[:, l, C_GQ:C_GQ + 2] = g("q_norm_g")[l].reshape(2, 128).T
        cols[:, l, C_GKV] = g("kv_norm_g")[l]
        cols[:, l, C_GS:C_GS + 4] = g("ssm_out_g")[l].reshape(4, 128).T
        cols[:, l, C_D:C_D + 4] = g("ssm_d")[l].reshape(4, 128).T
        cols[:, l, C_BGLU:C_BGLU + 4] = g("ssm_b_glu")[l].reshape(4, 128).T
        cols[0:64, l, C_GA:C_GA + 8] = g("attn_out_g")[l].reshape(8, 64).T
        cols[:, l, C_GA2:C_GA2 + 4] = g("attn_out_g")[l].reshape(4, 2, 64).transpose(1, 2, 0).reshape(128, 4)
    consts = np.zeros((128, 2), f)
    freqs = (np.float32(10000.0) ** (-np.arange(0, 32, 2, dtype=np.float32) / np.float32(32))).astype(f)
    consts[64:80, 0] = freqs
    consts[80:96, 0] = freqs
    consts[64:80, 1] = -SIN_SCALE
    consts[80:96, 1] = SIN_SCALE
    idx = np.arange(128) // 16
    mask16 = (idx[:, None] == idx[None, :]).astype(f)
    w_in = g("w_in")
    w_kr = np.zeros((L, D, 192), f)
    w_kr[:, :, 64:96] = w_in[:, :, 384:416]
    w_kr[:, :, 160:176] = w_in[:, :, 400:416]
    w_kr[:, :, 176:192] = w_in[:, :, 384:400]
    w_uq = g("w_uq")
    w_uqs = np.zeros_like(w_uq)
    for h in range(8):
        w_uqs[:, :, h * 96 + 64:h * 96 + 80] = w_uq[:, :, h * 96 + 80:h * 96 + 96]
        w_uqs[:, :, h * 96 + 80:h * 96 + 96] = w_uq[:, :, h * 96 + 64:h * 96 + 80]
    w_ukv = g("w_ukv").reshape(L, 128, 8, 2, 64).transpose(0, 1, 3, 2, 4).reshape(L, 128, 1024)

    def pair_layout(a):
        sh = a.shape
        a = a.reshape(L, 16, 2, 64, *sh[3:])
        a = np.moveaxis(a, 1, 3)
        return a.reshape(L, 128, 16, *sh[3:])

    lam_re = pair_layout(g("ssm_lambda_re"))
    lam_im = pair_layout(g("ssm_lambda_im"))
    logdt = pair_layout(np.repeat(g("ssm_log_dt")[:, :, None], 64, axis=2))
    s5v = np.stack([lam_re, lam_im, logdt], axis=2)
    b_re = pair_layout(g("ssm_b_re"))
    b_im = pair_layout(g("ssm_b_im"))
    s5b = np.stack([b_re, b_im], axis=2)
    c_re = pair_layout(np.swapaxes(g("ssm_c_re"), 2, 3))
    c_im = pair_layout(np.swapaxes(g("ssm_c_im"), 2, 3))
    s5c = np.stack([c_re, c_im], axis=2)
    common = dict(
        cols=cols, consts=consts, mask16=mask16, fing=g("final_norm_g").reshape(1, D).astype(f),
        w_in=w_in, w_kr=w_kr, w_uq=w_uq, w_uqs=w_uqs, w_ukv=np.ascontiguousarray(w_ukv),
        s5v=np.ascontiguousarray(s5v), s5b=np.ascontiguousarray(s5b), s5c=np.ascontiguousarray(s5c),
        w_glu=g("ssm_w_glu"), w_out=g("w_out"), w_xq=g("w_xq"), w_xkv=g("w_xkv"), w_xo=g("w_xo"),
        w_gate=g("w_gate"), w_up=g("w_up"), w_down=g("w_down"),
    )
    common = {k: np.ascontiguousarray(v, dtype=f) for k, v in common.items()}
    x = g("x")
    mem = g("mem")
    pos = g("positions").astype(np.int32)
    per_core = []
    for c in range(x.shape[0]):
        d = dict(common)
        d["x"] = np.ascontiguousarray(x[c], dtype=f)
        d["mem"] = np.ascontiguousarray(mem[c], dtype=f)
        d["pos"] = np.ascontiguousarray(pos[c].reshape(1, S))
        per_core.append(d)
    return per_core


def kernel(**inputs):
    per_core = prep_inputs(inputs)
    nc, _ = build_program()
    res = run_bass_kernel_spmd(nc, per_core, core_ids=list(range(8)))
    return np.stack([np.asarray(r["out"], dtype=np.float32) for r in res.results], axis=0)
```

```python
import math
import numpy as np
import concourse.bass as bass
import concourse.mybir as mybir
from concourse.bass_utils import run_bass_kernel_spmd

F32 = mybir.dt.float32
BF16 = mybir.dt.bfloat16
I32 = mybir.dt.int32
AF = mybir.ActivationFunctionType
ALU = mybir.AluOpType

ENGS = ["tensor", "vector", "scalar", "gpsimd", "sync"]

L = 2
S = 4096
D = 1024
NB = 8
EPS = 1e-6
DFF = 2816
NFF = 22
TWO_PI = 2.0 * math.pi
SIN_SCALE = 6.2831845
MAGIC = 12582912.0
ATT_SCALE = 96.0 ** -0.5
X_SCALE = 256.0 ** -0.5
NCOL = 59
C_GMIX, C_GX, C_GFFN, C_GMEM, C_GQ, C_GKV, C_GS, C_D, C_BGLU, C_GA, C_GA2 = 0, 8, 16, 24, 32, 34, 35, 39, 43, 47, 55


class Res:
    __slots__ = ("name", "last_w", "readers")

    def __init__(self, name=""):
        self.name = name
        self.last_w = None
        self.readers = []


class Op:
    __slots__ = ("eng", "fn", "deps", "dma", "sem", "val", "target", "waits")


class Prog:
    def __init__(self, n_dma_sems=24):
        self.nc = bass.Bass("TRN2", target_bir_lowering=False)
        self.ops = {e: [] for e in ENGS}
        self.n_dma_sems = n_dma_sems
        self.dma_rr = {e: 0 for e in ENGS}
        self.dma_cnt = {}
        self.dma_last = {}
        self.last_op = {}
        self.pending = {e: [] for e in ENGS}
        self._ctx = []

    def sbuf(self, name, shape, dtype):
        g = self.nc.sbuf_tensor("sb_" + name, list(shape), dtype)
        h = g.__enter__()
        self._ctx.append(g)
        return h

    def psum(self, name, shape, dtype):
        g = self.nc.psum_tensor("ps_" + name, list(shape), dtype)
        h = g.__enter__()
        self._ctx.append(g)
        return h

    def _need(self, op, dep):
        if dep is None or dep is op:
            return
        if dep.eng == "tensor" and op.eng == "tensor" and not dep.dma and not op.dma:
            return
        op.deps.append(dep)

    def barrier(self):
        cur = list(self.last_op.values()) + list(self.dma_last.values())
        for e in ENGS:
            self.pending[e] = list(cur)

    def op(self, eng, fn, reads=(), writes=(), dma=False):
        o = Op()
        o.eng = eng
        o.fn = fn
        o.dma = dma
        o.deps = []
        o.target = False
        o.val = None
        o.waits = None
        if self.pending[eng]:
            for d in self.pending[eng]:
                if d is not None:
                    o.deps.append(d)
            self.pending[eng] = []
        if dma:
            slot = self.dma_rr[eng] % self.n_dma_sems
            self.dma_rr[eng] += 1
            key = ("dma", eng, slot)
            prev = self.dma_last.get(key)
            cnt = self.dma_cnt.get(key, 0) + 1
            self.dma_cnt[key] = cnt
            o.sem = key
            o.val = 16 * cnt
            if prev is not None:
                self._need(o, prev)
            self.dma_last[key] = o
        else:
            o.sem = ("eng", eng)
            self.last_op[eng] = o
        for r in reads:
            self._need(o, r.last_w)
        for w in writes:
            self._need(o, w.last_w)
            for rd in w.readers:
                self._need(o, rd)
        for r in reads:
            r.readers.append(o)
        for w in writes:
            w.last_w = o
            w.readers = []
        self.ops[eng].append(o)
        return o

    def build(self, final_waits=()):
        nc = self.nc
        for e in ENGS:
            for o in self.ops[e]:
                for d in o.deps:
                    d.target = True
        for o in final_waits:
            o.target = True
        for e in ENGS:
            c = 0
            for o in self.ops[e]:
                if not o.dma:
                    if o.target:
                        c += 1
                        o.val = c
        sems = {}
        for e in ENGS:
            wm = {}
            for o in self.ops[e]:
                need = {}
                for d in o.deps:
                    if d.val > need.get(d.sem, 0):
                        need[d.sem] = d.val
                o.waits = {}
                for k, v in need.items():
                    if wm.get(k, 0) < v:
                        wm[k] = v
                        o.waits[k] = v
                if (o.dma or o.target) and o.sem not in sems:
                    g = nc.semaphore("s_" + "_".join(str(x) for x in o.sem))
                    sems[o.sem] = g.__enter__()
                    self._ctx.append(g)
        fin = {}
        for o in final_waits:
            fin[o.sem] = max(fin.get(o.sem, 0), o.val)
        with nc.Block() as block:
            def make(e):
                def body(engobj):
                    for o in self.ops[e]:
                        for k, v in o.waits.items():
                            engobj.wait_ge(sems[k], v)
                        ins = o.fn(engobj)
                        if o.dma:
                            ins.then_inc(sems[o.sem], 16)
                        elif o.target:
                            ins.then_inc(sems[o.sem], 1)
                    if e == "sync":
                        for k, v in fin.items():
                            engobj.wait_ge(sems[k], v)
                return body
            for e in ENGS:
                if self.ops[e] or e == "sync":
                    getattr(block, e)(make(e))
        return nc


class Ring:
    def __init__(self, P, name, n, shape, dtype):
        self.bufs = [(P.sbuf(f"{name}{i}", shape, dtype), Res(f"{name}{i}")) for i in range(n)]
        self.i = 0

    def next(self):
        b = self.bufs[self.i % len(self.bufs)]
        self.i += 1
        return b


def build_program(debug=False, n_layers=L, stop_after=None):
    P = Prog()
    nc = P.nc

    def din(name, shape, dt=F32):
        return nc.dram_tensor(name, list(shape), dt, kind="ExternalInput").ap()

    def dscr(name, shape, dt):
        return nc.dram_tensor(name, list(shape), dt, kind="Internal").ap()

    x_d = din("x", [S, D])
    mem_d = din("mem", [256, D])
    pos_d = din("pos", [1, S], I32)
    cols_d = din("cols", [128, L, NCOL])
    consts_d = din("consts", [128, 2])
    mask16_d = din("mask16", [128, 128])
    fing_d = din("fing", [1, D])
    w_in_d = din("w_in", [L, D, 928])
    w_kr_d = din("w_kr", [L, D, 192])
    w_uq_d = din("w_uq", [L, 256, 768])
    w_uqs_d = din("w_uqs", [L, 256, 768])
    w_ukv_d = din("w_ukv", [L, 128, 1024])
    s5v_d = din("s5v", [L, 128, 3, 16])
    s5b_d = din("s5b", [L, 128, 2, 16, 16])
    s5c_d = din("s5c", [L, 128, 2, 16, 16])
    w_glu_d = din("w_glu", [L, 512, 512])
    w_out_d = din("w_out", [L, D, D])
    w_xq_d = din("w_xq", [L, D, D])
    w_xkv_d = din("w_xkv", [L, D, 2 * D])
    w_xo_d = din("w_xo", [L, D, D])
    w_gate_d = din("w_gate", [L, D, DFF])
    w_up_d = din("w_up", [L, D, DFF])
    w_down_d = din("w_down", [L, DFF, D])
    out_d = nc.dram_tensor("out", [S, D], F32, kind="ExternalOutput").ap()

    h_d = dscr("h_scr", [S, D], F32)
    h1_d = dscr("h1_scr", [S, D], F32)
    qT_d = dscr("qT_scr", [8, 96, S], BF16)
    kT_d = dscr("kT_scr", [8, 96, S], BF16)
    V_d = dscr("V_scr", [8, 128, 32, 65], BF16)
    uT_d = dscr("uT_scr", [128, 4, S], BF16)
    yT_d = dscr("yT_scr", [128, 4, S], F32)
    aT_d = dscr("aT_scr", [NB, 64, 8, 512], F32)
    sT_d = dscr("sT_scr", [NB, 128, 4, 512], BF16)
    cos_d = dscr("cos_scr", [32, S], F32)
    sin_d = dscr("sin_scr", [32, S], F32)
    r_hd = [Res() for _ in range(32)]
    r_h1d = [Res() for _ in range(32)]
    r_qd, r_kd, r_vd, r_ud, r_yd = Res(), Res(), Res(), Res(), Res()
    r_ad, r_sd, r_csd = Res(), Res(), Res()

    dbg = {}

    def dbg_out(name, shape, dt=F32):
        t = nc.dram_tensor("dbg_" + name, list(shape), dt, kind="ExternalOutput").ap()
        dbg[name] = t
        return t

    final_ops = []

    def VEC(fn, reads, writes):
        return P.op("vector", fn, reads, writes)

    def ACT(fn, reads, writes):
        return P.op("scalar", fn, reads, writes)

    def POOL(fn, reads, writes):
        return P.op("gpsimd", fn, reads, writes)

    def PE(fn, reads, writes):
        return P.op("tensor", fn, reads, writes)

    def DMA(out, in_, reads, writes, q="sync"):
        return P.op(q, lambda e: e.dma_start(out=out, in_=in_), reads, writes, dma=True)

    def DMAC(out, in_, reads, writes):
        return P.op("gpsimd", lambda e: e.dma_start(out=out, in_=in_), reads, writes, dma=True)

    banks = []
    for i in range(8):
        t = P.psum(f"bank{i}", [128, 512], F32)
        banks.append((t, t.bitcast(BF16), Res(f"bank{i}")))
    bank_rr = [0]

    def next_bank(pool=(0, 1, 2, 3, 4, 5, 6, 7)):
        i = pool[bank_rr[0] % len(pool)]
        bank_rr[0] += 1
        return banks[i]

    identf = P.sbuf("identf", [128, 128], F32)
    ident = P.sbuf("ident", [128, 128], BF16)
    onesf = P.sbuf("onesf", [128, 128], F32)
    sel65 = P.sbuf("sel65", [65, 64], F32)
    onesb = P.sbuf("onesb", [128, 128], BF16)
    fing = P.sbuf("fing", [128, D], F32)
    mask16 = P.sbuf("mask16", [128, 128], F32)
    cols = P.sbuf("cols", [128, L, NCOL], F32)
    consts = P.sbuf("consts", [128, 2], F32)
    r_const = Res("const")
    POOL(lambda e: e.memset(identf[:], 1.0), [], [r_const])
    POOL(lambda e: e.affine_select(out=identf[:], in_=identf[:], pattern=[[-1, 128]], compare_op=ALU.is_equal,
                                   fill=0.0, base=0, channel_multiplier=1), [r_const], [r_const])
    VEC(lambda e: e.tensor_copy(out=ident[:], in_=identf[:]), [r_const], [r_const])
    VEC(lambda e: e.memset(onesf[:], 1.0), [], [r_const])
    VEC(lambda e: e.memset(onesb[:], 1.0), [], [r_const])
    DMA(fing[:], fing_d[0:1, :].to_broadcast([128, D]), [], [r_const])
    VEC(lambda e: e.memset(sel65[:], 0.0), [], [r_const])
    VEC(lambda e: e.memset(sel65[64:65, :], 1.0), [r_const], [r_const])
    DMA(mask16[:], mask16_d[:, :], [], [r_const])
    DMA(cols[:], cols_d[:, :, :], [], [r_const])
    DMA(consts[:], consts_d[:, :], [], [r_const])

    def col(l, c, n=1, p0=0, p1=128):
        return cols[p0:p1, l, c:c + n]

    ARENA_F32 = 33792
    arena_f = P.sbuf("arena", [128, ARENA_F32], F32)
    arena_b = arena_f.bitcast(BF16)
    arena_i = arena_f.bitcast(I32)
    arena_off = [0]

    def areset():
        arena_off[0] = 0

    def aalloc(shape, dt, name=""):
        n = 1
        for s_ in shape[1:]:
            n *= s_
        esz = 2 if dt == BF16 else 4
        off = (arena_off[0] + 3) // 4 * 4
        arena_off[0] = off + n * esz
        assert arena_off[0] <= ARENA_F32 * 4, (name, arena_off[0])
        base = {BF16: arena_b, F32: arena_f, I32: arena_i}[dt]
        e0 = off // esz
        ap = base[0:shape[0], e0:e0 + n]
        if len(shape) == 3:
            ap = ap.rearrange("p (a b) -> p a b", b=shape[2])
        elif len(shape) == 4:
            ap = ap.rearrange("p (a b c) -> p a b c", b=shape[2], c=shape[3])
        elif len(shape) == 5:
            ap = ap.rearrange("p (a b c d) -> p a b c d", b=shape[2], c=shape[3], d=shape[4])
        return ap, Res(name)

    class ARing:
        def __init__(self, n, shape, dt, name=""):
            self.bufs = [aalloc(shape, dt, f"{name}{i}") for i in range(n)]
            self.i = 0

        def next(self):
            b = self.bufs[self.i % len(self.bufs)]
            self.i += 1
            return b

    ht_ring = Ring(P, "ht", 2, [128, D], F32)
    junk_ring = Ring(P, "junk", 1, [128, D], BF16)
    xn_ring = Ring(P, "xn", 2, [128, D], BF16)
    xnT_ring = Ring(P, "xnT", 2, [128, 8, 512], BF16)
    st_ring = Ring(P, "st", 8, [128, 4], F32)
    pT_ring = Ring(P, "pT", 4, [128, 512], BF16)
    frA = (P.sbuf("frA", [128, 5632], F32), Res("frA"))
    frB = (P.sbuf("frB", [128, 2048], F32), Res("frB"))

    def rms_stats(src_ap, r_src, nfeat, st, r_st, c0):
        junk, r_junk = junk_ring.next()
        n = src_ap.shape[-1]
        ACT(lambda e: e.activation(out=junk[:, 0:n], in_=src_ap, func=AF.Square, accum_out=st[:, c0:c0 + 1]), [r_src], [r_junk, r_st])
        ACT(lambda e: e.activation(out=st[:, c0:c0 + 1], in_=st[:, c0:c0 + 1], func=AF.Sqrt, scale=1.0 / nfeat, bias=EPS), [r_st], [r_st])
        VEC(lambda e: e.reciprocal(out=st[:, c0:c0 + 1], in_=st[:, c0:c0 + 1]), [r_st], [r_st])

    def norm_transpose(ht, r_ht, gcol, xnT, r_xnT, sub):
        st, r_st = st_ring.next()
        rms_stats(ht[:], r_ht, D, st, r_st, 0)
        xn, r_xn = xn_ring.next()
        VEC(lambda e: e.tensor_scalar(out=xn[:], in0=ht[:], scalar1=st[:, 0:1], scalar2=None, op0=ALU.mult), [r_ht, r_st], [r_xn])
        bk, bkb, r_bk = next_bank()
        bkv = bkb[:, :].rearrange("p (a b) -> p a b", b=128)
        for k in range(8):
            PE(lambda e, k=k: e.transpose(out=bkv[:, k, :], in_=xn[:, k * 128:(k + 1) * 128], identity=ident[:]), [r_xn, r_const], [r_bk])
        VEC(lambda e: e.tensor_tensor(out=xnT[:, :, sub * 128:(sub + 1) * 128], in0=bkv, in1=gcol.to_broadcast([128, 8, 128]), op=ALU.mult),
            [r_bk, r_const], [r_xnT])

    RS = slice(64, 96)

    areset()
    posi, r_ra = aalloc([96, S], I32, "posi")
    tmpa, _ = aalloc([96, S], F32, "tmpa")
    tmpb, _ = aalloc([96, S], F32, "tmpb")
    cs_t, r_cs = aalloc([96, S], F32, "cs")
    sn_t, _ = aalloc([96, S], F32, "sn")
    DMA(posi[RS, :], pos_d[0:1, :].to_broadcast([32, S]), [], [r_ra])
    VEC(lambda e: e.tensor_copy(out=tmpa[RS, :], in_=posi[RS, :]), [r_ra], [r_ra])
    VEC(lambda e: e.tensor_scalar(out=tmpa[RS, :], in0=tmpa[RS, :], scalar1=consts[RS, 0:1], scalar2=1.0 / TWO_PI,
                                  op0=ALU.mult, op1=ALU.mult), [r_ra, r_const], [r_ra])
    VEC(lambda e: e.tensor_scalar(out=tmpb[RS, :], in0=tmpa[RS, :], scalar1=MAGIC, scalar2=MAGIC, op0=ALU.add, op1=ALU.subtract), [r_ra], [r_ra])
    VEC(lambda e: e.tensor_tensor(out=tmpb[RS, :], in0=tmpa[RS, :], in1=tmpb[RS, :], op=ALU.subtract), [r_ra], [r_ra])
    ACT(lambda e: e.activation(out=sn_t[RS, :], in_=tmpb[RS, :], func=AF.Sin, scale=consts[RS, 1:2]), [r_ra, r_const], [r_cs])
    VEC(lambda e: e.tensor_scalar(out=tmpa[RS, :], in0=tmpa[RS, :], scalar1=0.25, scalar2=None, op0=ALU.add), [r_ra, r_cs], [r_ra])
    VEC(lambda e: e.tensor_scalar(out=tmpb[RS, :], in0=tmpa[RS, :], scalar1=MAGIC, scalar2=MAGIC, op0=ALU.add, op1=ALU.subtract), [r_ra], [r_ra])
    VEC(lambda e: e.tensor_tensor(out=tmpb[RS, :], in0=tmpa[RS, :], in1=tmpb[RS, :], op=ALU.subtract), [r_ra], [r_ra])
    ACT(lambda e: e.activation(out=cs_t[RS, :], in_=tmpb[RS, :], func=AF.Sin, scale=SIN_SCALE), [r_ra], [r_cs])
    DMA(cos_d[:, :], cs_t[RS, :], [r_cs], [r_csd])
    DMA(sin_d[:, :], sn_t[RS, :], [r_cs], [r_csd])

    def emit_layer(l):
        src_d = x_d if l == 0 else h_d
        r_src = [Res() for _ in range(32)] if l == 0 else r_hd
        P.barrier()
        areset()
        w_in_sb, r_w = aalloc([128, 8, 928], BF16, "w_in")
        w_kr_sb, _ = aalloc([128, 8, 192], BF16)
        w_uq_sb, _ = aalloc([128, 2, 768], BF16)
        w_uqs_sb, _ = aalloc([128, 2, 768], BF16)
        w_ukv_sb, _ = aalloc([128, 1024], BF16)
        DMAC(w_in_sb, w_in_d[l].rearrange("(k p) n -> p k n", p=128), [], [r_w])
        DMAC(w_kr_sb, w_kr_d[l].rearrange("(k p) n -> p k n", p=128), [], [r_w])
        DMAC(w_uq_sb, w_uq_d[l].rearrange("(k p) n -> p k n", p=128), [], [r_w])
        DMAC(w_uqs_sb, w_uqs_d[l].rearrange("(k p) n -> p k n", p=128), [], [r_w])
        DMAC(w_ukv_sb, w_ukv_d[l], [], [r_w])
        qs, r_qs = aalloc([96, 8, 512], F32, "qs")
        qsw, r_qsw = aalloc([96, 8, 512], F32, "qsw")
        qb_ring = ARing(2, [96, 8, 512], BF16, "qb")
        kb_ring = ARing(2, [96, 8, 512], BF16, "kb")
        vb_ring = ARing(2, [128, 8, 4, 65], BF16, "vb")
        ub_ring = ARing(2, [128, 4, 512], BF16, "ub")
        cT_ring = ARing(2, [128, 3, 512], BF16, "cT")
        cs_ring = ARing(2, [96, 2, 512], F32, "csb")
        ta, r_ta = aalloc([96, 1024], F32, "ta")
        for vb, r_vb in vb_ring.bufs:
            VEC(lambda e, vb=vb: e.memset(vb[:, :, :, 64:65], 1.0), [], [r_vb])

        ctxA = {}

        htx = [(frB[0][:, 0:1024], Res("htx0")), (frB[0][:, 1024:2048], Res("htx1"))]
        frA_b = frA[0].bitcast(BF16)
        xnx = [(frA_b[:, 0:1024], Res("xnx0")), (frA_b[:, 1024:2048], Res("xnx1"))]
        jkx = [(frA_b[:, 2048 + i * 1024:2048 + (i + 1) * 1024], Res(f"jkx{i}")) for i in range(4)]

        def stageA1(b):
            T0 = b * 512
            xnT, r_xnT = xnT_ring.next()
            cT, r_cT = cT_ring.next()
            vb, r_vb = vb_ring.next()
            csb, r_csb = cs_ring.next()
            ctxA[b] = (xnT, r_xnT, cT, r_cT, csb, r_csb)
            DMA(csb[RS, 0, :], cos_d[:, T0:T0 + 512], [r_csd], [r_csb])
            DMA(csb[RS, 1, :], sin_d[:, T0:T0 + 512], [r_csd], [r_csb])
            hts = [ht_ring.next(), ht_ring.next(), htx[0], htx[1]]
            xns = [xn_ring.next(), xn_ring.next(), xnx[0], xnx[1]]
            sts = [st_ring.next() for _ in range(4)]
            gcol = col(l, C_GMIX, 8)
            gq = col(l, C_GQ, 3)
            for sub in range(4):
                ht, r_ht = hts[sub]
                DMA(ht[:, :], src_d[T0 + sub * 128:T0 + (sub + 1) * 128, :], [r_src[b * 4 + sub]], [r_ht])
            yield
            for sub in range(4):
                (ht, r_ht), (st, r_st), (jk, r_jk) = hts[sub], sts[sub], jkx[sub]
                ACT(lambda e, ht=ht, st=st, jk=jk: e.activation(out=jk, in_=ht[:, :], func=AF.Square, accum_out=st[:, 0:1]), [r_ht], [r_jk, r_st])
            for sub in range(4):
                st, r_st = sts[sub]
                ACT(lambda e, st=st: e.activation(out=st[:, 0:1], in_=st[:, 0:1], func=AF.Sqrt, scale=1.0 / D, bias=EPS), [r_st], [r_st])
            yield
            for sub in range(4):
                st, r_st = sts[sub]
                VEC(lambda e, st=st: e.reciprocal(out=st[:, 0:1], in_=st[:, 0:1]), [r_st], [r_st])
            for sub in range(4):
                (ht, r_ht), (st, r_st), (xn, r_xn) = hts[sub], sts[sub], xns[sub]
                VEC(lambda e, ht=ht, st=st, xn=xn: e.tensor_scalar(out=xn[:, :], in0=ht[:, :], scalar1=st[:, 0:1], scalar2=None, op0=ALU.mult),
                    [r_ht, r_st], [r_xn])
            yield
            bks = []
            for sub in range(4):
                xn, r_xn = xns[sub]
                bk, bkb, r_bk = next_bank()
                bkv = bkb[:, :].rearrange("p (a b) -> p a b", b=128)
                for k in range(8):
                    PE(lambda e, k=k, bkv=bkv, xn=xn: e.transpose(out=bkv[:, k, :], in_=xn[:, k * 128:(k + 1) * 128], identity=ident[:]), [r_xn, r_const], [r_bk])
                bks.append((bkv, r_bk))
                if sub % 2 == 1:
                    yield
            for sub in range(4):
                bkv, r_bk = bks[sub]
                VEC(lambda e, bkv=bkv, sub=sub: e.tensor_tensor(out=xnT[:, :, sub * 128:(sub + 1) * 128], in0=bkv, in1=gcol.to_broadcast([128, 8, 128]), op=ALU.mult),
                    [r_bk, r_const], [r_xnT])
            yield
            cbk = []
            for sub in range(4):
                bk, bkb, r_bk = next_bank()
                for k in range(8):
                    PE(lambda e, k=k, bk=bk, sub=sub: e.matmul(bk[:, 0:384], lhsT=xnT[:, k, sub * 128:(sub + 1) * 128], rhs=w_in_sb[:, k, 0:384],
                                                              start=(k == 0), stop=(k == 7)), [r_xnT, r_w], [r_bk])
                cbk.append((bk, r_bk))
                if sub % 2 == 1:
                    yield
            for sub in range(4):
                (bk, r_bk), (st, r_st), (jk, r_jk) = cbk[sub], sts[sub], jkx[sub]
                ACT(lambda e, bk=bk, st=st, jk=jk: e.activation(out=jk[:, 0:256], in_=bk[:, 0:256], func=AF.Square, accum_out=st[:, 1:2]), [r_bk], [r_jk, r_st])
                ACT(lambda e, bk=bk, st=st, jk=jk: e.activation(out=jk[:, 256:384], in_=bk[:, 256:384], func=AF.Square, accum_out=st[:, 2:3]), [r_bk], [r_jk, r_st])
            yield
            for sub in range(4):
                st, r_st = sts[sub]
                ACT(lambda e, st=st: e.activation(out=st[:, 1:2], in_=st[:, 1:2], func=AF.Sqrt, scale=1.0 / 256, bias=EPS), [r_st], [r_st])
                ACT(lambda e, st=st: e.activation(out=st[:, 2:3], in_=st[:, 2:3], func=AF.Sqrt, scale=1.0 / 128, bias=EPS), [r_st], [r_st])
            for sub in range(4):
                st, r_st = sts[sub]
                VEC(lambda e, st=st: e.reciprocal(out=st[:, 1:3], in_=st[:, 1:3]), [r_st], [r_st])
            yield
            for sub in range(4):
                (bk, r_bk), (st, r_st), (cn, r_cn) = cbk[sub], sts[sub], xns[sub]
                VEC(lambda e, bk=bk, cn=cn, st=st: e.tensor_scalar(out=cn[:, 0:256], in0=bk[:, 0:256], scalar1=st[:, 1:2], scalar2=None, op0=ALU.mult),
                    [r_bk, r_st], [r_cn])
                VEC(lambda e, bk=bk, cn=cn, st=st: e.tensor_scalar(out=cn[:, 256:384], in0=bk[:, 256:384], scalar1=st[:, 2:3], scalar2=None, op0=ALU.mult),
                    [r_bk, r_st], [r_cn])
            yield
            bk2s = []
            for sub in range(4):
                cn, r_cn = xns[sub]
                bk2, bk2b, r_bk2 = next_bank()
                bk2v = bk2b[:, :].rearrange("p (a b) -> p a b", b=128)
                for k in range(3):
                    PE(lambda e, k=k, bk2v=bk2v, cn=cn: e.transpose(out=bk2v[:, k, :], in_=cn[:, k * 128:(k + 1) * 128], identity=ident[:]), [r_cn, r_const], [r_bk2])
                bk2s.append((bk2v, r_bk2))
            yield
            for sub in range(4):
                bk2v, r_bk2 = bk2s[sub]
                VEC(lambda e, bk2v=bk2v, sub=sub: e.tensor_tensor(out=cT[:, :, sub * 128:(sub + 1) * 128], in0=bk2v[:, 0:3, :],
                                                                 in1=gq.to_broadcast([128, 3, 128]), op=ALU.mult), [r_bk2, r_const], [r_cT])
            yield
            for sub in range(4):
                bk3, _, r_bk3 = next_bank()
                PE(lambda e, bk3=bk3, sub=sub: e.matmul(bk3[:, :], lhsT=cT[:, 2, sub * 128:(sub + 1) * 128], rhs=w_ukv_sb[:, 512:1024],
                                                       start=True, stop=True), [r_cT, r_w], [r_bk3])
                ACT(lambda e, bk3=bk3, sub=sub: e.copy(out=vb[:, :, sub, 0:64], in_=bk3[:, :].rearrange("p (h d) -> p h d", d=64)), [r_bk3], [r_vb])
                if sub % 2 == 1:
                    yield
            DMA(V_d[:, :, b * 4:(b + 1) * 4, :].rearrange("h p t d -> p h t d"), vb, [r_vb], [r_vd], q="scalar")
            yield

        def stageA2(b):
            T0 = b * 512
            xnT, r_xnT, cT, r_cT, csb, r_csb = ctxA.pop(b)
            ub, r_ub = ub_ring.next()
            for ct in range(4):
                yield
                bk, _, r_bk = next_bank()
                for k in range(8):
                    PE(lambda e, k=k, bk=bk, xnT=xnT, ct=ct: e.matmul(bk[:, :], lhsT=w_in_sb[:, k, 416 + ct * 128:416 + (ct + 1) * 128],
                                                                      rhs=xnT[:, k, :], start=(k == 0), stop=(k == 7)), [r_xnT, r_w], [r_bk])
                ACT(lambda e, bk=bk, ct=ct, ub=ub: e.copy(out=ub[:, ct, :], in_=bk[:, :]), [r_bk], [r_ub])
            DMA(uT_d[:, :, T0:T0 + 512], ub, [r_ub], [r_ud], q="scalar")
            yield
            bka, _, r_bka = next_bank()
            bkb_, _, r_bkb = next_bank()
            for k in range(8):
                PE(lambda e, k=k, bka=bka, xnT=xnT: e.matmul(bka[0:96, :], lhsT=w_kr_sb[:, k, 0:96], rhs=xnT[:, k, :], start=(k == 0), stop=(k == 7)),
                   [r_xnT, r_w], [r_bka])
            for k in range(8):
                PE(lambda e, k=k, bkb_=bkb_, xnT=xnT: e.matmul(bkb_[0:96, :], lhsT=w_kr_sb[:, k, 96:192], rhs=xnT[:, k, :], start=(k == 0), stop=(k == 7)),
                   [r_xnT, r_w], [r_bkb])
            yield
            VEC(lambda e, bka=bka, csb=csb: e.tensor_tensor(out=ta[RS, 0:512], in0=bka[RS, :], in1=csb[RS, 0, :], op=ALU.mult), [r_bka, r_csb], [r_ta])
            VEC(lambda e, bkb_=bkb_, csb=csb: e.tensor_tensor(out=ta[RS, 512:1024], in0=bkb_[RS, :], in1=csb[RS, 1, :], op=ALU.mult), [r_bkb, r_csb], [r_ta])
            VEC(lambda e: e.tensor_tensor(out=ta[RS, 0:512], in0=ta[RS, 0:512], in1=ta[RS, 512:1024], op=ALU.add), [r_ta], [r_ta])
            kb, r_kb = kb_ring.next()
            POOL(lambda e, kb=kb: e.tensor_copy(out=kb[RS, :, :], in_=ta[RS, 0:512].rearrange("p (o t) -> p o t", o=1).to_broadcast([32, 8, 512])),
                 [r_ta], [r_kb])
            for h in range(8):
                yield
                bk, _, r_bk = next_bank()
                PE(lambda e, bk=bk, h=h, cT=cT: e.matmul(bk[0:64, :], lhsT=w_ukv_sb[:, h * 64:(h + 1) * 64], rhs=cT[:, 2, :], start=True, stop=True),
                   [r_cT, r_w], [r_bk])
                ACT(lambda e, bk=bk, h=h, kb=kb: e.copy(out=kb[0:64, h, :], in_=bk[0:64, :]), [r_bk], [r_kb])
            DMA(kT_d[:, :, T0:T0 + 512].rearrange("h p t -> p h t"), kb, [r_kb], [r_kd], q="scalar")
            for h in range(8):
                yield
                bka, _, r_bka = next_bank()
                bkb_, _, r_bkb = next_bank()
                for k in range(2):
                    PE(lambda e, k=k, bka=bka, h=h, cT=cT: e.matmul(bka[0:96, :], lhsT=w_uq_sb[:, k, h * 96:(h + 1) * 96], rhs=cT[:, k, :],
                                                                    start=(k == 0), stop=(k == 1)), [r_cT, r_w], [r_bka])
                for k in range(2):
                    PE(lambda e, k=k, bkb_=bkb_, h=h, cT=cT: e.matmul(bkb_[0:96, :], lhsT=w_uqs_sb[:, k, h * 96:(h + 1) * 96], rhs=cT[:, k, :],
                                                                      start=(k == 0), stop=(k == 1)), [r_cT, r_w], [r_bkb])
                ACT(lambda e, bka=bka, h=h: e.copy(out=qs[:, h, :], in_=bka[0:96, :]), [r_bka], [r_qs])
                ACT(lambda e, bkb_=bkb_, h=h: e.copy(out=qsw[RS, h, :], in_=bkb_[RS, :]), [r_bkb], [r_qsw])
            yield
            qb, r_qb = qb_ring.next()
            POOL(lambda e, qb=qb: e.tensor_copy(out=qb[0:64, :, :], in_=qs[0:64, :, :]), [r_qs], [r_qb])
            VEC(lambda e, csb=csb: e.tensor_tensor(out=qs[RS, :, :], in0=qs[RS, :, :],
                                                   in1=csb[RS, 0:1, :].to_broadcast([32, 8, 512]), op=ALU.mult), [r_qs, r_csb], [r_qs])
            VEC(lambda e, csb=csb: e.tensor_tensor(out=qsw[RS, :, :], in0=qsw[RS, :, :],
                                                   in1=csb[RS, 1:2, :].to_broadcast([32, 8, 512]), op=ALU.mult), [r_qsw, r_csb], [r_qsw])
            VEC(lambda e, qb=qb: e.tensor_tensor(out=qb[RS, :, :], in0=qs[RS, :, :], in1=qsw[RS, :, :], op=ALU.add), [r_qs, r_qsw], [r_qb])
            DMA(qT_d[:, :, T0:T0 + 512].rearrange("h p t -> p h t"), qb, [r_qb], [r_qd])
            yield

        def run_interleaved(gens):
            gens = list(gens)
            while gens:
                for g_ in list(gens):
                    try:
                        next(g_)
                    except StopIteration:
                        gens.remove(g_)

        for b in range(NB + 1):
            gl = []
            if b >= 1:
                gl.append(stageA2(b - 1))
            if b < NB:
                gl.append(stageA1(b))
            run_interleaved(gl)

        if stop_after == "A":
            return True

        P.barrier()
        areset()
        S_POOL = (0, 1, 2, 3)
        O_POOL = (4, 5)
        M_POOL = (6, 7)
        pT_ringB = ARing(6, [128, 512], BF16, "pTB")
        qh_ring = ARing(2, [96, S], BF16, "qh")
        kh_ring = ARing(2, [96, S], BF16, "kh")
        vh_ring = ARing(2, [128, 32, 65], BF16, "vh")
        oT_ring = ARing(3, [65, 1024], F32, "oT3")
        an_ring = ARing(3, [64, 512], F32, "an3")
        LA = 3

        def load_head(h):
            qh, r_qh = qh_ring.next()
            kh, r_kh = kh_ring.next()
            vh, r_vh = vh_ring.next()
            DMA(qh, qT_d[h], [r_qd], [r_qh])
            DMA(kh, kT_d[h], [r_kd], [r_kh])
            DMA(vh, V_d[h], [r_vd], [r_vh])
            return (qh, r_qh, kh, r_kh, vh, r_vh)

        heads = {0: load_head(0)}
        for h in range(8):
            qh, r_qh, kh, r_kh, vh, r_vh = heads[h]
            if h + 1 < 8:
                heads[h + 1] = load_head(h + 1)
            items = [(b, kt) for b in range(NB) for kt in range(4 * (b + 1))]
            pts = {}
            bos = {}
            deferred = []

            def stage1(i):
                b, kt = items[i]
                T0 = b * 512
                bs, _, r_bs = next_bank(S_POOL)
                PE(lambda e, bs=bs, kt=kt, kh=kh, qh=qh, T0=T0: e.matmul(bs[:, :], lhsT=kh[:, kt * 128:(kt + 1) * 128], rhs=qh[:, T0:T0 + 512],
                                                                         start=True, stop=True), [r_kh, r_qh], [r_bs])
                pT, r_pT = pT_ringB.next()
                ACT(lambda e, bs=bs, pT=pT: e.activation(out=pT[:], in_=bs[:, :], func=AF.Exp, scale=ATT_SCALE), [r_bs], [r_pT])
                if kt >= 4 * b:
                    base = T0 - kt * 128
                    POOL(lambda e, pT=pT, base=base: e.affine_select(out=pT[:], in_=pT[:], pattern=[[1, 512]], compare_op=ALU.is_ge,
                                                                     fill=0.0, base=base, channel_multiplier=-1), [r_pT], [r_pT])
                pts[i] = (pT, r_pT)

            def stage2(j, i_now):
                b, kt = items[j]
                nkt = 4 * (b + 1)
                if kt == 0:
                    bos[b] = next_bank(O_POOL)
                bo, _, r_bo = bos[b]
                pT, r_pT = pts.pop(j)
                PE(lambda e, bo=bo, pT=pT, kt=kt, nkt=nkt, vh=vh: e.matmul(bo[0:65, :], lhsT=vh[:, kt, :], rhs=pT[:],
                                                                           start=(kt == 0), stop=(kt == nkt - 1)), [r_vh, r_pT], [r_bo])
                if kt == nkt - 1:
                    oT, r_oT = oT_ring.next()
                    VEC(lambda e, bo=bo, oT=oT: e.tensor_copy(out=oT[:, 0:512], in_=bo[0:65, :]), [r_bo], [r_oT])

                    def epi(b=b, oT=oT, r_oT=r_oT):
                        bm, _, r_bm = next_bank(M_POOL)
                        PE(lambda e, bm=bm, oT=oT: e.matmul(bm[0:64, :], lhsT=sel65[:, :], rhs=oT[:, 0:512], start=True, stop=True), [r_oT, r_const], [r_bm])
                        VEC(lambda e, bm=bm, oT=oT: e.reciprocal(out=oT[0:64, 512:1024], in_=bm[0:64, :]), [r_bm, r_oT], [r_oT])
                        an, r_an = an_ring.next()
                        POOL(lambda e, oT=oT, an=an: e.tensor_tensor(out=an[:, :], in0=oT[0:64, 0:512], in1=oT[0:64, 512:1024], op=ALU.mult), [r_oT], [r_an])
                        DMA(aT_d[b, :, h, :], an, [r_an], [r_ad], q="gpsimd")
                    deferred.append((i_now + 3, epi))

            n_it = len(items)
            for i in range(n_it + LA):
                if i < n_it:
                    stage1(i)
                if i - LA >= 0:
                    stage2(i - LA, i)
                while deferred and deferred[0][0] <= i:
                    deferred.pop(0)[1]()
            while deferred:
                deferred.pop(0)[1]()
        if stop_after == "B1":
            return True
        P.barrier()
        areset()
        Wst, r_Wst = aalloc([128, 4, 8, 2, 128], BF16, "Wst")
        Wfir, r_Wfir = aalloc([128, 4, 8, 128], BF16, "Wfir")
        Wo_r, r_Wo = aalloc([128, 16, 8, 32], BF16, "Wo")
        Wo_i, _ = aalloc([128, 16, 8, 32], BF16)
        sm, r_sm = aalloc([128, 32, 16], F32, "sm")
        pw_r, _ = aalloc([128, 16, 9], F32)
        pw_i, _ = aalloc([128, 16, 9], F32)
        ph_r, _ = aalloc([128, 16, 9], F32)
        ph_i, _ = aalloc([128, 16, 9], F32)
        mark = arena_off[0]
        Bri, r_bc = aalloc([128, 2, 16, 16], F32, "Bri")
        Cri, _ = aalloc([128, 2, 16, 16], F32)
        Bb_r, r_T = aalloc([128, 16, 16], F32, "T")
        Bb_i, _ = aalloc([128, 16, 16], F32)
        T1, _ = aalloc([128, 16, 16], F32)
        T2, _ = aalloc([128, 16, 16], F32)
        T3, _ = aalloc([128, 16, 16], F32)
        T4, _ = aalloc([128, 16, 16], F32)
        ME_r, r_ME = aalloc([128, 8, 4, 128], F32, "ME")
        ME_i, _ = aalloc([128, 8, 4, 128], F32)
        MF_r, r_MF = aalloc([128, 4, 128], F32, "MF")
        MF_in, _ = aalloc([128, 4, 128], F32)
        tmpF, r_tmpF = aalloc([128, 128], F32, "tmpF")
        DMA(sm[:, 0:3, :], s5v_d[l], [], [r_sm])
        DMA(Bri, s5b_d[l], [], [r_bc])
        DMA(Cri, s5c_d[l], [], [r_bc])
        POOL(lambda e: e.memset(ME_r, 0.0), [], [r_ME])
        POOL(lambda e: e.memset(ME_i, 0.0), [], [r_ME])
        POOL(lambda e: e.memset(MF_r, 0.0), [], [r_MF])
        POOL(lambda e: e.memset(MF_in, 0.0), [], [r_MF])
        POOL(lambda e: e.memset(Wo_r, 0.0), [], [r_Wo])
        POOL(lambda e: e.memset(Wo_i, 0.0), [], [r_Wo])
        LRE, LIM, LDT, DT, TT_, ER, Y, RND, FR, SN_, CS_, AR, AI, ARM1, DEN, RDEN, CBR, CBI, U1, U2, RDEC, RR = range(22)

        def smtt(o, a, b, op):
            VEC(lambda e: e.tensor_tensor(out=sm[:, o, :], in0=sm[:, a, :], in1=sm[:, b, :], op=op), [r_sm], [r_sm])

        def smts(o, a, s1, op0, s2=None, op1=None):
            if op1 is None:
                VEC(lambda e: e.tensor_scalar(out=sm[:, o, :], in0=sm[:, a, :], scalar1=s1, scalar2=None, op0=op0), [r_sm], [r_sm])
            else:
                VEC(lambda e: e.tensor_scalar(out=sm[:, o, :], in0=sm[:, a, :], scalar1=s1, scalar2=s2, op0=op0, op1=op1), [r_sm], [r_sm])

        def smact(o, a, func, scale=1.0):
            ACT(lambda e: e.activation(out=sm[:, o, :], in_=sm[:, a, :], func=func, scale=scale), [r_sm], [r_sm])

        smact(DT, LDT, AF.Exp)
        smtt(TT_, LRE, DT, ALU.mult)
        smact(ER, TT_, AF.Exp)
        smact(RDEC, TT_, AF.Exp, 8.0)
        smtt(Y, LIM, DT, ALU.mult)
        smts(Y, Y, 1.0 / TWO_PI, ALU.mult)
        smts(RND, Y, MAGIC, ALU.add, MAGIC, ALU.subtract)
        smtt(FR, Y, RND, ALU.subtract)
        smact(SN_, FR, AF.Sin, SIN_SCALE)
        smts(Y, Y, 0.25, ALU.add)
        smts(RND, Y, MAGIC, ALU.add, MAGIC, ALU.subtract)
        smtt(FR, Y, RND, ALU.subtract)
        smact(CS_, FR, AF.Sin, SIN_SCALE)
        smtt(AR, ER, CS_, ALU.mult)
        smtt(AI, ER, SN_, ALU.mult)
        smts(ARM1, AR, -1.0, ALU.add)
        smtt(U1, LRE, LRE, ALU.mult)
        smtt(U2, LIM, LIM, ALU.mult)
        smtt(DEN, U1, U2, ALU.add)
        VEC(lambda e: e.reciprocal(out=sm[:, RDEN, :], in_=sm[:, DEN, :]), [r_sm], [r_sm])
        smtt(U1, ARM1, LRE, ALU.mult)
        smtt(U2, AI, LIM, ALU.mult)
        smtt(U1, U1, U2, ALU.add)
        smtt(CBR, U1, RDEN, ALU.mult)
        smtt(U1, AI, LRE, ALU.mult)
        smtt(U2, ARM1, LIM, ALU.mult)
        smtt(U1, U1, U2, ALU.subtract)
        smtt(CBI, U1, RDEN, ALU.mult)
        VEC(lambda e: e.memset(pw_r[:, :, 0:1], 1.0), [r_sm], [r_sm])
        VEC(lambda e: e.memset(pw_i[:, :, 0:1], 0.0), [r_sm], [r_sm])
        VEC(lambda e: e.tensor_copy(out=pw_r[:, :, 1], in_=sm[:, AR, :]), [r_sm], [r_sm])
        VEC(lambda e: e.tensor_copy(out=pw_i[:, :, 1], in_=sm[:, AI, :]), [r_sm], [r_sm])
        for k in range(1, 8):
            VEC(lambda e, k=k: e.tensor_tensor(out=sm[:, U1, :], in0=pw_r[:, :, k], in1=sm[:, AR, :], op=ALU.mult), [r_sm], [r_sm])
            VEC(lambda e, k=k: e.tensor_tensor(out=sm[:, U2, :], in0=pw_i[:, :, k], in1=sm[:, AI, :], op=ALU.mult), [r_sm], [r_sm])
            VEC(lambda e, k=k: e.tensor_tensor(out=pw_r[:, :, k + 1], in0=sm[:, U1, :], in1=sm[:, U2, :], op=ALU.subtract), [r_sm], [r_sm])
            VEC(lambda e, k=k: e.tensor_tensor(out=sm[:, U1, :], in0=pw_r[:, :, k], in1=sm[:, AI, :], op=ALU.mult), [r_sm], [r_sm])
            VEC(lambda e, k=k: e.tensor_tensor(out=sm[:, U2, :], in0=pw_i[:, :, k], in1=sm[:, AR, :], op=ALU.mult), [r_sm], [r_sm])
            VEC(lambda e, k=k: e.tensor_tensor(out=pw_i[:, :, k + 1], in0=sm[:, U1, :], in1=sm[:, U2, :], op=ALU.add), [r_sm], [r_sm])
        VEC(lambda e: e.reciprocal(out=sm[:, RR, :], in_=sm[:, RDEC, :]), [r_sm], [r_sm])
        VEC(lambda e: e.tensor_tensor(out=ph_r[:, :, 0], in0=pw_r[:, :, 8], in1=sm[:, RR, :], op=ALU.mult), [r_sm], [r_sm])
        VEC(lambda e: e.tensor_tensor(out=ph_i[:, :, 0], in0=pw_i[:, :, 8], in1=sm[:, RR, :], op=ALU.mult), [r_sm], [r_sm])
        for k in range(8):
            VEC(lambda e, k=k: e.tensor_tensor(out=sm[:, U1, :], in0=ph_r[:, :, k], in1=ph_r[:, :, k], op=ALU.mult), [r_sm], [r_sm])
            VEC(lambda e, k=k: e.tensor_tensor(out=sm[:, U2, :], in0=ph_i[:, :, k], in1=ph_i[:, :, k], op=ALU.mult), [r_sm], [r_sm])
            VEC(lambda e, k=k: e.tensor_tensor(out=ph_r[:, :, k + 1], in0=sm[:, U1, :], in1=sm[:, U2, :], op=ALU.subtract), [r_sm], [r_sm])
            VEC(lambda e, k=k: e.tensor_tensor(out=sm[:, U1, :], in0=ph_r[:, :, k], in1=ph_i[:, :, k], op=ALU.mult), [r_sm], [r_sm])
            VEC(lambda e, k=k: e.tensor_scalar(out=ph_i[:, :, k + 1], in0=sm[:, U1, :], scalar1=2.0, scalar2=None, op0=ALU.mult), [r_sm], [r_sm])

        def bc16(tile_idx_ap):
            return tile_idx_ap.rearrange("p (a o) -> p a o", o=1).to_broadcast([128, 16, 16])

        def cmul_bc(outr, outi, xr, xi, sr_ap, si_ap, rds, wrs):
            pass

        VEC(lambda e: e.tensor_tensor(out=T1, in0=Bri[:, 0], in1=bc16(sm[:, CBR, :]), op=ALU.mult), [r_sm, r_bc], [r_T])
        VEC(lambda e: e.tensor_tensor(out=T2, in0=Bri[:, 1], in1=bc16(sm[:, CBI, :]), op=ALU.mult), [r_sm, r_bc], [r_T])
        VEC(lambda e: e.tensor_tensor(out=Bb_r, in0=T1, in1=T2, op=ALU.subtract), [r_T], [r_T])
        VEC(lambda e: e.tensor_tensor(out=T1, in0=Bri[:, 1], in1=bc16(sm[:, CBR, :]), op=ALU.mult), [r_sm, r_bc, r_T], [r_T])
        VEC(lambda e: e.tensor_tensor(out=T2, in0=Bri[:, 0], in1=bc16(sm[:, CBI, :]), op=ALU.mult), [r_sm, r_bc], [r_T])
        VEC(lambda e: e.tensor_tensor(out=Bb_i, in0=T1, in1=T2, op=ALU.add), [r_T], [r_T])

        def blkME(M, lg, hf):
            return M[hf * 64:(hf + 1) * 64, lg, :, :].rearrange("p ct (q x) -> p ct q x", x=32)[:, :, :, hf * 16:(hf + 1) * 16]

        def halfT(T, hf):
            return T[hf * 64:(hf + 1) * 64, :, :].rearrange("p (ct q) c -> p ct q c", q=4)

        for lg in range(8):
            VEC(lambda e, lg=lg: e.tensor_tensor(out=T1, in0=Bb_r, in1=pw_r[:, :, lg:lg + 1].to_broadcast([128, 16, 16]), op=ALU.mult), [r_sm, r_T], [r_T])
            VEC(lambda e, lg=lg: e.tensor_tensor(out=T2, in0=Bb_i, in1=pw_i[:, :, lg:lg + 1].to_broadcast([128, 16, 16]), op=ALU.mult), [r_sm, r_T], [r_T])
            VEC(lambda e, lg=lg: e.tensor_tensor(out=T3, in0=Bb_i, in1=pw_r[:, :, lg:lg + 1].to_broadcast([128, 16, 16]), op=ALU.mult), [r_sm, r_T], [r_T])
            VEC(lambda e, lg=lg: e.tensor_tensor(out=T4, in0=Bb_r, in1=pw_i[:, :, lg:lg + 1].to_broadcast([128, 16, 16]), op=ALU.mult), [r_sm, r_T], [r_T])
            for hf in range(2):
                POOL(lambda e, lg=lg, hf=hf: e.tensor_tensor(out=blkME(ME_r, lg, hf), in0=halfT(T1, hf), in1=halfT(T2, hf), op=ALU.subtract), [r_T], [r_ME])
                POOL(lambda e, lg=lg, hf=hf: e.tensor_tensor(out=blkME(ME_i, lg, hf), in0=halfT(T3, hf), in1=halfT(T4, hf), op=ALU.add), [r_T], [r_ME])

        def blkMF(M, hf):
            return M[hf * 64:(hf + 1) * 64, :, :].rearrange("p ct (q x) -> p ct q x", x=32)[:, :, :, hf * 16:(hf + 1) * 16]

        for hf in range(2):
            POOL(lambda e, hf=hf: e.tensor_copy(out=blkMF(MF_r, hf), in_=halfT(Cri[:, 0], hf)), [r_bc], [r_MF])
            POOL(lambda e, hf=hf: e.tensor_scalar(out=blkMF(MF_in, hf), in0=halfT(Cri[:, 1], hf), scalar1=-1.0, scalar2=None, op0=ALU.mult), [r_bc], [r_MF])
        for ct in range(4):
            for ri, M in ((0, ME_r), (1, ME_i)):
                for j0 in (0, 4):
                    bk, _, r_bk = next_bank()
                    for jj in range(4):
                        j = j0 + jj
                        PE(lambda e, bk=bk, jj=jj, j=j, ct=ct, M=M: e.transpose(out=bk[:, jj * 128:(jj + 1) * 128], in_=M[:, 7 - j, ct, :], identity=identf[:]),
                           [r_ME, r_const], [r_bk])
                    ACT(lambda e, bk=bk, ct=ct, j0=j0, ri=ri: e.copy(out=Wst[:, ct, j0:j0 + 4, ri, :], in_=bk[:, :].rearrange("p (a b) -> p a b", b=128)),
                        [r_bk], [r_Wst])
        for ct in range(4):
            for l0 in (0, 4):
                bk, _, r_bk = next_bank()
                for ll in range(4):
                    lg = l0 + ll
                    PE(lambda e, bk=bk, ll=ll, lg=lg, ct=ct: e.matmul(bk[:, ll * 128:(ll + 1) * 128], lhsT=ME_r[:, lg, ct, :], rhs=MF_r[:, ct, :], start=True, stop=False),
                       [r_ME, r_MF], [r_bk])
                    PE(lambda e, bk=bk, ll=ll, lg=lg, ct=ct: e.matmul(bk[:, ll * 128:(ll + 1) * 128], lhsT=ME_i[:, lg, ct, :], rhs=MF_in[:, ct, :], start=False, stop=True),
                       [r_ME, r_MF], [r_bk])
                VEC(lambda e, bk=bk, ct=ct, l0=l0: e.tensor_tensor(out=Wfir[:, ct, l0:l0 + 4, :], in0=bk[:, :].rearrange("p (a b) -> p a b", b=128),
                                                                 in1=mask16[:, :].rearrange("p (o b) -> p o b", o=1).to_broadcast([128, 4, 128]), op=ALU.mult),
                    [r_bk, r_const], [r_Wfir])
                if l0 == 0:
                    VEC(lambda e, bk=bk: e.tensor_tensor(out=tmpF, in0=bk[:, 0:128], in1=mask16[:, :], op=ALU.mult), [r_bk, r_const], [r_tmpF])
                    VEC(lambda e, ct=ct: e.scalar_tensor_tensor(out=Wfir[:, ct, 0, :], in0=identf[:, :], scalar=col(l, C_D + ct), in1=tmpF, op0=ALU.mult, op1=ALU.add),
                        [r_tmpF, r_const], [r_Wfir])
        for i in range(8):
            VEC(lambda e, i=i: e.tensor_tensor(out=T1, in0=Cri[:, 0], in1=pw_r[:, :, i + 1:i + 2].to_broadcast([128, 16, 16]), op=ALU.mult), [r_sm, r_bc, r_T], [r_T])
            VEC(lambda e, i=i: e.tensor_tensor(out=T2, in0=Cri[:, 1], in1=pw_i[:, :, i + 1:i + 2].to_broadcast([128, 16, 16]), op=ALU.mult), [r_sm, r_bc, r_T], [r_T])
            VEC(lambda e, i=i: e.tensor_tensor(out=T3, in0=Cri[:, 1], in1=pw_r[:, :, i + 1:i + 2].to_broadcast([128, 16, 16]), op=ALU.mult), [r_sm, r_bc, r_T], [r_T])
            VEC(lambda e, i=i: e.tensor_tensor(out=T4, in0=Cri[:, 0], in1=pw_i[:, :, i + 1:i + 2].to_broadcast([128, 16, 16]), op=ALU.mult), [r_sm, r_bc, r_T], [r_T])
            for hf in range(2):
                hs = slice(hf * 64, (hf + 1) * 64)
                cs = slice(hf * 16, (hf + 1) * 16)
                POOL(lambda e, i=i, hs=hs, cs=cs: e.tensor_tensor(out=Wo_r[hs, :, i, cs], in0=T1[hs], in1=T2[hs], op=ALU.subtract), [r_T], [r_Wo])
                VEC(lambda e, i=i, hs=hs, cs=cs: e.scalar_tensor_tensor(out=Wo_i[hs, :, i, cs], in0=T3[hs], scalar=-1.0, in1=T4[hs], op0=ALU.mult, op1=ALU.subtract),
                    [r_T], [r_Wo])
        P.barrier()
        arena_off[0] = mark
        u_ring = ARing(2, [128, S], BF16, "uct")
        y_ring = ARing(1, [128, S], F32, "yct")
        tab_r, r_tab = aalloc([128, 4, 512], F32, "tab")
        tab_i, _ = aalloc([128, 4, 512], F32)
        tq1, r_tq = aalloc([128, 4, 256], F32, "tq")
        tq2, _ = aalloc([128, 4, 256], F32)
        xps = []
        for i_ in range(2):
            xr_, rx_ = aalloc([128, 4, 512], BF16, f"xpr{i_}")
            xi_, _ = aalloc([128, 4, 512], BF16)
            VEC(lambda e, xr_=xr_: e.memset(xr_[:, :, 0:1], 0.0), [], [rx_])
            VEC(lambda e, xi_=xi_: e.memset(xi_[:, :, 0:1], 0.0), [], [rx_])
            xps.append((xr_, xi_, rx_))

        class _AR:
            def __init__(self, bufs):
                self.bufs = bufs
                self.i = 0

            def next(self):
                b_ = self.bufs[self.i % len(self.bufs)]
                self.i += 1
                return b_
        A_ring = _AR([aalloc([128, 512], F32, f"A{i_}") for i_ in range(6)] + [(frB[0][:, i_ * 512:(i_ + 1) * 512], Res(f"AB{i_}")) for i_ in range(4)])
        ctxS = {}

        def stageS(ct):
            xp_r, xp_i, r_xp = xps[ct % 2]
            uct, r_uct = u_ring.next()
            DMA(uct, uT_d[:, ct, :], [r_ud], [r_uct])
            u8 = uct.rearrange("p (c j) -> p j c", j=8)
            ctxS[ct] = (uct, r_uct, u8)
            VEC(lambda e: e.memset(tab_r[:, :, 0:1], 1.0), [r_tab], [r_tab])
            VEC(lambda e: e.memset(tab_i[:, :, 0:1], 0.0), [r_tab], [r_tab])
            for k in range(9):
                yield
                s_ = 1 << k
                phr = ph_r[:, 4 * ct:4 * ct + 4, k:k + 1].to_broadcast([128, 4, s_])
                phi = ph_i[:, 4 * ct:4 * ct + 4, k:k + 1].to_broadcast([128, 4, s_])
                VEC(lambda e, s_=s_, phr=phr: e.tensor_tensor(out=tq1[:, :, 0:s_], in0=tab_r[:, :, 0:s_], in1=phr, op=ALU.mult), [r_tab, r_sm, r_tq], [r_tq])
                VEC(lambda e, s_=s_, phi=phi: e.tensor_tensor(out=tq2[:, :, 0:s_], in0=tab_i[:, :, 0:s_], in1=phi, op=ALU.mult), [r_tab, r_sm, r_tq], [r_tq])
                VEC(lambda e, s_=s_: e.tensor_tensor(out=tab_r[:, :, s_:2 * s_], in0=tq1[:, :, 0:s_], in1=tq2[:, :, 0:s_], op=ALU.subtract), [r_tq], [r_tab])
                VEC(lambda e, s_=s_, phi=phi: e.tensor_tensor(out=tq1[:, :, 0:s_], in0=tab_r[:, :, 0:s_], in1=phi, op=ALU.mult), [r_tab, r_sm, r_tq], [r_tq])
                VEC(lambda e, s_=s_, phr=phr: e.tensor_tensor(out=tq2[:, :, 0:s_], in0=tab_i[:, :, 0:s_], in1=phr, op=ALU.mult), [r_tab, r_sm, r_tq], [r_tq])
                VEC(lambda e, s_=s_: e.tensor_tensor(out=tab_i[:, :, s_:2 * s_], in0=tq1[:, :, 0:s_], in1=tq2[:, :, 0:s_], op=ALU.add), [r_tq], [r_tab])
            for q in range(4):
                yield
                pair = 4 * ct + q
                ps_ = slice(32 * q, 32 * q + 32)
                bks = []
                for ri in range(2):
                    bk, _, r_bk = next_bank()
                    for j in range(8):
                        PE(lambda e, bk=bk, j=j, ri=ri, ps_=ps_, q=q, ct=ct, u8=u8: e.matmul(bk[:, :], lhsT=Wst[ps_, ct, j, ri, :], rhs=u8[ps_, j, :],
                                                                                           start=(j == 0), stop=(j == 7), tile_position=(32 * q, 0)),
                           [r_Wst, r_uct], [r_bk])
                    bks.append((bk, r_bk))
                (Sr, r_Sr), (Si, r_Si) = bks
                tr, ti = tab_r[:, q, :], tab_i[:, q, :]
                a1, r_a1 = A_ring.next()
                a2, r_a2 = A_ring.next()
                a3, r_a3 = A_ring.next()
                a4, r_a4 = A_ring.next()
                VEC(lambda e, Sr=Sr, a1=a1, tr=tr: e.tensor_tensor(out=a1, in0=Sr[:, :], in1=tr, op=ALU.mult), [r_Sr, r_tab], [r_a1])
                VEC(lambda e, Si=Si, a2=a2, ti=ti: e.tensor_tensor(out=a2, in0=Si[:, :], in1=ti, op=ALU.mult), [r_Si, r_tab], [r_a2])
                VEC(lambda e, Si=Si, a3=a3, tr=tr: e.tensor_tensor(out=a3, in0=Si[:, :], in1=tr, op=ALU.mult), [r_Si, r_tab], [r_a3])
                VEC(lambda e, Sr=Sr, a4=a4, ti=ti: e.tensor_tensor(out=a4, in0=Sr[:, :], in1=ti, op=ALU.mult), [r_Sr, r_tab], [r_a4])
                POOL(lambda e, a1=a1, a2=a2: e.tensor_tensor(out=a1, in0=a1, in1=a2, op=ALU.add), [r_a1, r_a2], [r_a1])
                POOL(lambda e, a3=a3, a4=a4: e.tensor_tensor(out=a3, in0=a3, in1=a4, op=ALU.subtract), [r_a3, r_a4], [r_a3])
                yield
                rd = sm[:, RDEC, pair:pair + 1].to_broadcast([128, 512])
                VEC(lambda e, a1=a1, a2=a2, rd=rd: e.tensor_tensor_scan(out=a2, data0=rd, data1=a1, initial=0.0, op0=ALU.mult, op1=ALU.add), [r_a1, r_sm, r_a2], [r_a2])
                VEC(lambda e, a3=a3, a4=a4, rd=rd: e.tensor_tensor_scan(out=a4, data0=rd, data1=a3, initial=0.0, op0=ALU.mult, op1=ALU.add), [r_a3, r_sm, r_a4], [r_a4])
                yield
                b1, r_b1 = A_ring.next()
                b2, r_b2 = A_ring.next()
                POOL(lambda e, a2=a2, b1=b1, tr=tr: e.tensor_tensor(out=b1, in0=a2, in1=tr, op=ALU.mult), [r_a2, r_tab], [r_b1])
                POOL(lambda e, a4=a4, b2=b2, ti=ti: e.tensor_tensor(out=b2, in0=a4, in1=ti, op=ALU.mult), [r_a4, r_tab], [r_b2])
                VEC(lambda e, b1=b1, b2=b2, q=q: e.tensor_tensor(out=xp_r[:, q, 1:512], in0=b1[:, 0:511], in1=b2[:, 0:511], op=ALU.subtract), [r_b1, r_b2], [r_xp])
                POOL(lambda e, a2=a2, a1=a1, ti=ti: e.tensor_tensor(out=a1, in0=a2, in1=ti, op=ALU.mult), [r_a2, r_tab, r_a1], [r_a1])
                POOL(lambda e, a4=a4, a3=a3, tr=tr: e.tensor_tensor(out=a3, in0=a4, in1=tr, op=ALU.mult), [r_a4, r_tab, r_a3], [r_a3])
                VEC(lambda e, a1=a1, a3=a3, q=q: e.tensor_tensor(out=xp_i[:, q, 1:512], in0=a1[:, 0:511], in1=a3[:, 0:511], op=ALU.add), [r_a1, r_a3], [r_xp])

        def stageO(ct):
            xp_r, xp_i, r_xp = xps[ct % 2]
            uct, r_uct, u8 = ctxS.pop(ct)
            yct, r_yct = y_ring.next()
            y8 = yct.rearrange("p (c j) -> p j c", j=8)
            for i in range(8):
                yield
                bk, _, r_bk = next_bank()
                for lg in range(i + 1):
                    PE(lambda e, bk=bk, lg=lg, i=i, ct=ct, u8=u8: e.matmul(bk[:, :], lhsT=Wfir[:, ct, lg, :], rhs=u8[:, i - lg, :], start=(lg == 0), stop=False),
                       [r_Wfir, r_uct], [r_bk])
                for q in range(4):
                    pair = 4 * ct + q
                    PE(lambda e, bk=bk, q=q, pair=pair, i=i: e.matmul(bk[32 * q:32 * q + 32, :], lhsT=Wo_r[:, pair, i, :], rhs=xp_r[:, q, :], start=False, stop=False,
                                                                     tile_position=(0, 32 * q)), [r_Wo, r_xp], [r_bk])
                    PE(lambda e, bk=bk, q=q, pair=pair, i=i: e.matmul(bk[32 * q:32 * q + 32, :], lhsT=Wo_i[:, pair, i, :], rhs=xp_i[:, q, :], start=False, stop=True,
                                                                     tile_position=(0, 32 * q)), [r_Wo, r_xp], [r_bk])
                ACT(lambda e, bk=bk, y8=y8, i=i: e.copy(out=y8[:, i, :], in_=bk[:, :]), [r_bk], [r_yct])
            DMA(yT_d[:, ct, :], yct, [r_yct], [r_yd], q="scalar")

        for ct_ in range(5):
            gl = []
            if ct_ >= 1:
                gl.append(stageO(ct_ - 1))
            if ct_ < 4:
                gl.append(stageS(ct_))
            run_interleaved(gl)
        if stop_after == "B2a":
            return True
        P.barrier()
        areset()
        wglu, r_wglu = aalloc([128, 4, 512], BF16, "wglu")
        DMAC(wglu, w_glu_d[l].rearrange("(k p) n -> p k n", p=128), [], [r_wglu])
        yb_ring = ARing(2, [128, 4, 512], F32, "yb")
        g_ring = ARing(2, [128, 4, 512], BF16, "gT")
        sg_ring = ARing(2, [128, 4, 512], F32, "sg")
        sq_ring = ARing(2, [128, 4, 512], F32, "sq")
        rs_ring = ARing(2, [128, 512], F32, "rs")
        sn_ring = ARing(2, [128, 4, 512], BF16, "sn")
        glu_end = arena_off[0]
        wo_a, r_wc = aalloc([128, 4, 1024], BF16, "wo_a")
        wo_s, _ = aalloc([128, 4, 1024], BF16)
        wxq, _ = aalloc([128, 8, 1024], BF16)
        wxo, _ = aalloc([128, 8, 1024], BF16)
        KxT, r_kx = aalloc([128, 8, 256], BF16, "KxT")
        Vx, _ = aalloc([128, 2, 1024], BF16)
        c1w_end = arena_off[0]
        for par_ in range(2):
            DMAC(wo_a[par_ * 64:(par_ + 1) * 64, :, :], w_out_d[l, 0:512, :].rearrange("(hp par d) n -> par d hp n", par=2, d=64)[par_], [], [r_wc])
        VEC(lambda e: e.tensor_tensor(out=wo_a, in0=wo_a, in1=col(l, C_GA2, 4).rearrange("p (k o) -> p k o", o=1).to_broadcast([128, 4, 1024]), op=ALU.mult), [r_wc, r_const], [r_wc])
        DMAC(wo_s, w_out_d[l, 512:1024, :].rearrange("(k p) n -> p k n", p=128), [], [r_wc])
        DMAC(wxq, w_xq_d[l].rearrange("(k p) n -> p k n", p=128), [], [r_wc])
        DMAC(wxo, w_xo_d[l].rearrange("(k p) n -> p k n", p=128), [], [r_wc])
        def stageG(b):
            T0 = b * 512
            yb, r_yb = yb_ring.next()
            DMA(yb, yT_d[:, :, T0:T0 + 512], [r_yd], [r_yb])
            yield
            gT, r_gT = g_ring.next()
            ACT(lambda e, yb=yb, gT=gT: e.activation(out=gT, in_=yb, func=AF.Gelu_apprx_tanh), [r_yb], [r_gT])
            sg, r_sg = sg_ring.next()
            for co in range(4):
                yield
                bk, _, r_bk = next_bank()
                for ci in range(4):
                    PE(lambda e, bk=bk, ci=ci, co=co, gT=gT: e.matmul(bk[:, :], lhsT=wglu[:, ci, co * 128:(co + 1) * 128], rhs=gT[:, ci, :], start=(ci == 0), stop=(ci == 3)),
                       [r_wglu, r_gT], [r_bk])
                ACT(lambda e, bk=bk, co=co, sg=sg: e.activation(out=sg[:, co, :], in_=bk[:, :], func=AF.Sigmoid, bias=col(l, C_BGLU + co)), [r_bk, r_const], [r_sg])
            yield
            VEC(lambda e, sg=sg, yb=yb: e.tensor_tensor(out=sg, in0=sg, in1=yb, op=ALU.mult), [r_sg, r_yb], [r_sg])
            yield
            sq, r_sq = sq_ring.next()
            POOL(lambda e, sg=sg, sq=sq: e.tensor_tensor(out=sq, in0=sg, in1=sg, op=ALU.mult), [r_sg], [r_sq])
            yield
            bk, _, r_bk = next_bank()
            for co in range(4):
                PE(lambda e, bk=bk, co=co, sq=sq: e.matmul(bk[:, :], lhsT=onesf[:, :], rhs=sq[:, co, :], start=(co == 0), stop=(co == 3)), [r_sq, r_const], [r_bk])
            yield
            rs_, r_rs = rs_ring.next()
            ACT(lambda e, bk=bk, rs_=rs_: e.activation(out=rs_, in_=bk[:, :], func=AF.Sqrt, scale=1.0 / 512, bias=EPS), [r_bk], [r_rs])
            yield
            VEC(lambda e, rs_=rs_: e.reciprocal(out=rs_, in_=rs_), [r_rs], [r_rs])
            VEC(lambda e, sg=sg: e.tensor_tensor(out=sg, in0=sg, in1=col(l, C_GS, 4).rearrange("p (k o) -> p k o", o=1).to_broadcast([128, 4, 512]), op=ALU.mult),
                [r_sg, r_const], [r_sg])
            yield
            sn, r_sn = sn_ring.next()
            VEC(lambda e, sg=sg, sn=sn, rs_=rs_: e.tensor_tensor(out=sn, in0=sg, in1=rs_.rearrange("p (o t) -> p o t", o=1).to_broadcast([128, 4, 512]), op=ALU.mult),
                [r_sg, r_rs], [r_sn])
            DMA(sT_d[b], sn, [r_sn], [r_sd])

        for b_ in range(0, NB, 2):
            run_interleaved([stageG(b_), stageG(b_ + 1)])
        if stop_after == "B2":
            return True
        P.barrier()
        arena_off[0] = 0
        wxkv, r_wxkv = aalloc([128, 8, 2048], BF16, "wxkv")
        assert arena_off[0] <= glu_end
        DMAC(wxkv, w_xkv_d[l].rearrange("(k p) n -> p k n", p=128), [], [r_wxkv])
        memT, r_memT = xnT_ring.next()
        for mt in range(2):
            ht, r_ht = ht_ring.next()
            DMA(ht[:], mem_d[mt * 128:(mt + 1) * 128, :], [], [r_ht])
            norm_transpose(ht, r_ht, col(l, C_GMEM, 8), memT, r_memT, mt)
        for oc in range(8):
            bk, _, r_bk = next_bank()
            for k in range(8):
                PE(lambda e, bk=bk, k=k, oc=oc: e.matmul(bk[:, 0:256], lhsT=wxkv[:, k, oc * 128:(oc + 1) * 128], rhs=memT[:, k, 0:256], start=(k == 0), stop=(k == 7)),
                   [r_wxkv, r_memT], [r_bk])
            ACT(lambda e, bk=bk, oc=oc: e.copy(out=KxT[:, oc, :], in_=bk[:, 0:256]), [r_bk], [r_kx])
        for mt in range(2):
            for hf in range(2):
                bk, _, r_bk = next_bank()
                for k in range(8):
                    PE(lambda e, bk=bk, k=k, mt=mt, hf=hf: e.matmul(bk[:, :], lhsT=memT[:, k, mt * 128:(mt + 1) * 128], rhs=wxkv[:, k, 1024 + hf * 512:1024 + (hf + 1) * 512],
                                                                  start=(k == 0), stop=(k == 7)), [r_wxkv, r_memT], [r_bk])
                ACT(lambda e, bk=bk, mt=mt, hf=hf: e.copy(out=Vx[:, mt, hf * 512:(hf + 1) * 512], in_=bk[:, :]), [r_bk], [r_kx])
        P.barrier()
        arena_off[0] = 0
        h1_ring = ARing(8, [128, D], F32, "h1t")
        qx_ring = ARing(1, [128, 8, 512], BF16, "qxT")
        ox_ring = ARing(1, [128, 8, 512], BF16, "oxT")
        an_ring2 = ARing(1, [128, 4, 512], BF16, "an2")
        sn_ring2 = ARing(1, [128, 4, 512], BF16, "sn2")
        pX_ring = ARing(2, [128, 2, 512], BF16, "pX")
        rc_ring = ARing(2, [128, 512], F32, "rc")
        assert arena_off[0] <= glu_end, (arena_off[0], glu_end)
        araw_v = frA[0][0:64, 0:4096].rearrange("p (h t) -> p h t", t=512)
        araw4 = frA[0][0:64, 0:4096].rearrange("p (hp par t) -> p hp par t", par=2, t=512)
        r_araw = frA[1]
        sqv = frB[0][0:64, 0:2048].rearrange("p (h t) -> p h t", t=512)
        r_sqv = frB[1]
        r_h1src = [Res() for _ in range(32)] if l == 0 else r_hd
        ctx1 = {}

        def stageC1a(b):
            T0 = b * 512
            DMA(araw_v, aT_d[b], [r_ad], [r_araw])
            sn2, r_sn2 = sn_ring2.next()
            DMA(sn2, sT_d[b], [r_sd], [r_sn2])
            bk, _, r_bk = next_bank()
            for hg in range(2):
                POOL(lambda e, hg=hg: e.tensor_tensor(out=sqv, in0=araw_v[:, hg * 4:(hg + 1) * 4, :], in1=araw_v[:, hg * 4:(hg + 1) * 4, :], op=ALU.mult), [r_araw], [r_sqv])
                for hh in range(4):
                    PE(lambda e, bk=bk, hg=hg, hh=hh: e.matmul(bk[0:64, :], lhsT=onesf[0:64, 0:64], rhs=sqv[:, hh, :], start=(hg == 0 and hh == 0), stop=(hg == 1 and hh == 3)),
                       [r_sqv, r_const], [r_bk])
                yield
            rc, r_rc = rc_ring.next()
            ACT(lambda e, bk=bk, rc=rc: e.activation(out=rc[0:64, :], in_=bk[0:64, :], func=AF.Sqrt, scale=1.0 / 512, bias=EPS), [r_bk], [r_rc])
            VEC(lambda e, rc=rc: e.reciprocal(out=rc[0:64, :], in_=rc[0:64, :]), [r_rc], [r_rc])
            yield
            an2, r_an2 = an_ring2.next()
            rcb = rc[0:64, :].rearrange("p (o t) -> p o t", o=1).to_broadcast([64, 4, 512])
            VEC(lambda e, an2=an2, rcb=rcb: e.tensor_tensor(out=an2[0:64, :, :], in0=araw4[:, :, 0, :], in1=rcb, op=ALU.mult), [r_araw, r_rc], [r_an2])
            yield
            VEC(lambda e, an2=an2, rcb=rcb: e.tensor_tensor(out=an2[64:128, :, :], in0=araw4[:, :, 1, :], in1=rcb, op=ALU.mult), [r_araw, r_rc], [r_an2])
            yield
            xnT, r_xnT = xnT_ring.next()
            h1s = []
            ctx1[b] = (xnT, r_xnT, h1s)
            for sub in range(4):
                ti = b * 4 + sub
                ts_ = slice(sub * 128, (sub + 1) * 128)
                ht, r_ht = ht_ring.next()
                DMA(ht[:], src_d[T0 + sub * 128:T0 + (sub + 1) * 128, :], [r_h1src[ti]], [r_ht])
                h1t, r_h1t = h1_ring.next()
                for hf in range(2):
                    bk, _, r_bk = next_bank()
                    cs_ = slice(hf * 512, (hf + 1) * 512)
                    for k in range(4):
                        PE(lambda e, bk=bk, k=k, an2=an2, ts_=ts_, cs_=cs_: e.matmul(bk[:, :], lhsT=an2[:, k, ts_], rhs=wo_a[:, k, cs_], start=(k == 0), stop=False),
                           [r_an2, r_wc], [r_bk])
                    for k in range(4):
                        PE(lambda e, bk=bk, k=k, sn2=sn2, ts_=ts_, cs_=cs_: e.matmul(bk[:, :], lhsT=sn2[:, k, ts_], rhs=wo_s[:, k, cs_], start=False, stop=(k == 3)),
                           [r_sn2, r_wc], [r_bk])
                    VEC(lambda e, bk=bk, ht=ht, h1t=h1t, cs_=cs_: e.tensor_tensor(out=h1t[:, cs_], in0=bk[:, :], in1=ht[:, cs_], op=ALU.add), [r_bk, r_ht], [r_h1t])
                    yield
                norm_transpose(h1t, r_h1t, col(l, C_GX, 8), xnT, r_xnT, sub)
                h1s.append((h1t, r_h1t))
                yield

        def stageC1b(b):
            T0 = b * 512
            xnT, r_xnT, h1s = ctx1.pop(b)
            qxT, r_qxT = qx_ring.next()
            for oc in range(8):
                bk, _, r_bk = next_bank()
                for k in range(8):
                    PE(lambda e, bk=bk, k=k, oc=oc, xnT=xnT: e.matmul(bk[:, :], lhsT=wxq[:, k, oc * 128:(oc + 1) * 128], rhs=xnT[:, k, :], start=(k == 0), stop=(k == 7)),
                       [r_wc, r_xnT], [r_bk])
                ACT(lambda e, bk=bk, oc=oc, qxT=qxT: e.copy(out=qxT[:, oc, :], in_=bk[:, :]), [r_bk], [r_qxT])
                if oc % 2 == 1:
                    yield
            oxT, r_oxT = ox_ring.next()
            for hx in range(4):
                pX, r_pX = pX_ring.next()
                for mt in range(2):
                    bk, _, r_bk = next_bank()
                    for dc in range(2):
                        PE(lambda e, bk=bk, dc=dc, mt=mt, hx=hx, qxT=qxT: e.matmul(bk[:, :], lhsT=KxT[:, hx * 2 + dc, mt * 128:(mt + 1) * 128], rhs=qxT[:, hx * 2 + dc, :],
                                                                              start=(dc == 0), stop=(dc == 1)), [r_kx, r_qxT], [r_bk])
                    ACT(lambda e, bk=bk, mt=mt, pX=pX: e.activation(out=pX[:, mt, :], in_=bk[:, :], func=AF.Exp, scale=X_SCALE), [r_bk], [r_pX])
                yield
                bk, _, r_bk = next_bank()
                for mt in range(2):
                    PE(lambda e, bk=bk, mt=mt, pX=pX: e.matmul(bk[:, :], lhsT=onesb[:, :], rhs=pX[:, mt, :], start=(mt == 0), stop=(mt == 1)), [r_pX, r_const], [r_bk])
                rc, r_rc = rc_ring.next()
                VEC(lambda e, bk=bk, rc=rc: e.reciprocal(out=rc, in_=bk[:, :]), [r_bk], [r_rc])
                yield
                for dc in range(2):
                    bk, _, r_bk = next_bank()
                    for mt in range(2):
                        PE(lambda e, bk=bk, mt=mt, dc=dc, hx=hx, pX=pX: e.matmul(bk[:, :], lhsT=Vx[:, mt, (hx * 2 + dc) * 128:(hx * 2 + dc + 1) * 128], rhs=pX[:, mt, :],
                                                                             start=(mt == 0), stop=(mt == 1)), [r_kx, r_pX], [r_bk])
                    VEC(lambda e, bk=bk, dc=dc, hx=hx, rc=rc, oxT=oxT: e.tensor_tensor(out=oxT[:, hx * 2 + dc, :], in0=bk[:, :], in1=rc, op=ALU.mult), [r_bk, r_rc], [r_oxT])
                yield
            for sub in range(4):
                ti = b * 4 + sub
                ts_ = slice(sub * 128, (sub + 1) * 128)
                h1t, r_h1t = h1s[sub]
                for hf in range(2):
                    bk, _, r_bk = next_bank()
                    cs_ = slice(hf * 512, (hf + 1) * 512)
                    for k in range(8):
                        PE(lambda e, bk=bk, k=k, oxT=oxT, ts_=ts_, cs_=cs_: e.matmul(bk[:, :], lhsT=oxT[:, k, ts_], rhs=wxo[:, k, cs_], start=(k == 0), stop=(k == 7)),
                           [r_oxT, r_wc], [r_bk])
                    VEC(lambda e, bk=bk, h1t=h1t, cs_=cs_: e.tensor_tensor(out=h1t[:, cs_], in0=bk[:, :], in1=h1t[:, cs_], op=ALU.add), [r_bk, r_h1t], [r_h1t])
                    yield
                DMA(h1_d[T0 + sub * 128:T0 + (sub + 1) * 128, :], h1t[:], [r_h1t], [r_h1d[ti]])

        for b in range(NB + 1):
            gl = []
            if b >= 1:
                gl.append(stageC1b(b - 1))
            if b < NB:
                gl.append(stageC1a(b))
            run_interleaved(gl)
        if stop_after == "C1":
            return True

        P.barrier()
        areset()
        wg, r_wf = aalloc([128, 8, DFF], BF16, "wg")
        wu, _ = aalloc([128, 8, DFF], BF16)
        wd, _ = aalloc([128, NFF, 1024], BF16)
        FG = [(0, 768), (768, 1536), (1536, 2176), (2176, 2816)]
        r_wgu = [Res(f"wgu{g_}") for g_ in range(4)]
        r_wdk = [Res(f"wd{k_}") for k_ in range(NFF // 2)]
        for g_, (c0_, c1_) in enumerate(FG):
            DMAC(wg[:, :, c0_:c1_], w_gate_d[l][:, c0_:c1_].rearrange("(k p) n -> p k n", p=128), [], [r_wgu[g_]])
            DMAC(wu[:, :, c0_:c1_], w_up_d[l][:, c0_:c1_].rearrange("(k p) n -> p k n", p=128), [], [r_wgu[g_]])
        for k in range(0, NFF, 2):
            DMAC(wd[:, k:k + 2, :], w_down_d[l, k * 128:(k + 2) * 128, :].rearrange("(k p) n -> p k n", p=128), [], [r_wdk[k // 2]])

        def fgrp(fc):
            for g_, (c0_, c1_) in enumerate(FG):
                if c0_ <= fc * 128 < c1_:
                    return r_wgu[g_]
        actT = frA[0].bitcast(BF16) if hasattr(frA[0], "bitcast") else None
        actT = actT[:, 0:NFF * 512].rearrange("p (f t) -> p f t", t=512)
        r_actT = frA[1]
        sgs = [(frB[0][:, i * 512:(i + 1) * 512], Res()) for i in range(4)]
        sg_i = [0]
        last = (l == L - 1)
        ctxC = {}
        nt_ring = Ring(P, f"ntl{l}", 1, [128, 4], F32) if False else None

        def stageC2a(b):
            T0 = b * 512
            xnT, r_xnT = xnT_ring.next()
            ctxC[b] = (xnT, r_xnT)
            for sub in range(4):
                ti = b * 4 + sub
                ht, r_ht = ht_ring.next()
                DMA(ht[:], h1_d[T0 + sub * 128:T0 + (sub + 1) * 128, :], [r_h1d[ti]], [r_ht])
                norm_transpose(ht, r_ht, col(l, C_GFFN, 8), xnT, r_xnT, sub)
                yield

        def stageC2b(b):
            T0 = b * 512
            xnT, r_xnT = ctxC.pop(b)
            for fc in range(NFF):
                if fc % 3 == 0:
                    yield
                bkg, _, r_bkg = next_bank()
                bku, _, r_bku = next_bank()
                for k in range(8):
                    PE(lambda e, bkg=bkg, k=k, fc=fc, xnT=xnT: e.matmul(bkg[:, :], lhsT=wg[:, k, fc * 128:(fc + 1) * 128], rhs=xnT[:, k, :], start=(k == 0), stop=(k == 7)),
                       [fgrp(fc), r_xnT], [r_bkg])
                for k in range(8):
                    PE(lambda e, bku=bku, k=k, fc=fc, xnT=xnT: e.matmul(bku[:, :], lhsT=wu[:, k, fc * 128:(fc + 1) * 128], rhs=xnT[:, k, :], start=(k == 0), stop=(k == 7)),
                       [fgrp(fc), r_xnT], [r_bku])
                sgt, r_sgt = sgs[sg_i[0] % 4]
                sg_i[0] += 1
                ACT(lambda e, bkg=bkg, sgt=sgt: e.activation(out=sgt, in_=bkg[:, :], func=AF.Silu), [r_bkg], [r_sgt])
                VEC(lambda e, bku=bku, sgt=sgt, fc=fc: e.tensor_tensor(out=actT[:, fc, :], in0=bku[:, :], in1=sgt, op=ALU.mult), [r_bku, r_sgt], [r_actT])
            for sub in range(4):
                ti = b * 4 + sub
                ts_ = slice(sub * 128, (sub + 1) * 128)
                ht, r_ht = ht_ring.next()
                DMA(ht[:], h1_d[T0 + sub * 128:T0 + (sub + 1) * 128, :], [r_h1d[ti]], [r_ht])
                for hf in range(2):
                    yield
                    bk, _, r_bk = next_bank()
                    cs_ = slice(hf * 512, (hf + 1) * 512)
                    for fc in range(NFF):
                        PE(lambda e, bk=bk, fc=fc, ts_=ts_, cs_=cs_: e.matmul(bk[:, :], lhsT=actT[:, fc, ts_], rhs=wd[:, fc, cs_], start=(fc == 0), stop=(fc == NFF - 1)),
                           [r_actT, r_wdk[fc // 2]], [r_bk])
                    VEC(lambda e, bk=bk, ht=ht, cs_=cs_: e.tensor_tensor(out=ht[:, cs_], in0=bk[:, :], in1=ht[:, cs_], op=ALU.add), [r_bk, r_ht], [r_ht])
                if not last:
                    DMA(h_d[T0 + sub * 128:T0 + (sub + 1) * 128, :], ht[:], [r_ht], [r_hd[ti]])
                else:
                    st, r_st = st_ring.next()
                    rms_stats(ht[:], r_ht, D, st, r_st, 0)
                    VEC(lambda e, ht=ht, st=st: e.scalar_tensor_tensor(out=ht[:], in0=ht[:], scalar=st[:, 0:1], in1=fing[:], op0=ALU.mult, op1=ALU.mult),
                        [r_ht, r_st, r_const], [r_ht])
                    final_ops.append(DMA(out_d[T0 + sub * 128:T0 + (sub + 1) * 128, :], ht[:], [r_ht], []))


        def run_il(gens):
            gens = list(gens)
            while gens:
                for g_ in list(gens):
                    try:
                        next(g_)
                    except StopIteration:
                        gens.remove(g_)

        for b in range(NB + 1):
            gl = []
            if b >= 1:
                gl.append(stageC2b(b - 1))
            if b < NB:
                gl.append(stageC2a(b))
            run_il(gl)
        return False

    for l_ in range(n_layers):
        if emit_layer(l_):
            break

    if debug:
        def dump(name, src, shape, dt, r):
            final_ops.append(DMA(dbg_out(name, shape, dt), src, [r], []))
        P.barrier()
        dump("qT", qT_d[:, :, 3584:4096], [8, 96, 512], BF16, r_qd)
        dump("kT", kT_d[:, :, 3584:4096], [8, 96, 512], BF16, r_kd)
        dump("V", V_d[:, :, 28:32, :], [8, 128, 4, 65], BF16, r_vd)
        dump("uT", uT_d[:, :, 3584:4096], [128, 4, 512], BF16, r_ud)
        dump("aT0", aT_d[0], [64, 8, 512], F32, r_ad)
        dump("aT7", aT_d[7], [64, 8, 512], F32, r_ad)
        dump("yT", yT_d[:, :, 3584:4096], [128, 4, 512], F32, r_yd)
        dump("yT0", yT_d[:, :, 0:512], [128, 4, 512], F32, r_yd)
        dump("sT7", sT_d[7], [128, 4, 512], BF16, r_sd)
        dump("sT0", sT_d[0], [128, 4, 512], BF16, r_sd)
        dump("h1", h1_d[3968:4096, :], [128, D], F32, r_h1d[31])
        dump("h", h_d[3968:4096, :], [128, D], F32, r_hd[31])
    P.barrier()
    return P.build(final_waits=final_ops), dbg


def prep_inputs(inp):
    f = np.float32
    g = lambda k: np.asarray(inp[k])
    cols = np.zeros((128, L, NCOL), f)
    for l in range(L):
        cols[:, l, C_GMIX:C_GMIX + 8] = g("norm_mix_g")[l].reshape(8, 128).T
        cols[:, l, C_GX:C_GX + 8] = g("norm_x_g")[l].reshape(8, 128).T
        cols[:, l, C_GFFN:C_GFFN + 8] = g("norm_ffn_g")[l].reshape(8, 128).T
        cols[:, l, C_GMEM:C_GMEM + 8] = g("mem_norm_g")[l].reshape(8, 128).T
        cols[:, l, C_GQ:C_GQ + 2] = g("q_norm_g")[l].reshape(2, 128).T
        cols[:, l, C_GKV] = g("kv_norm_g")[l]
        cols[:, l, C_GS:C_GS + 4] = g("ssm_out_g")[l].reshape(4, 128).T
        cols[:, l, C_D:C_D + 4] = g("ssm_d")[l].reshape(4, 128).T
        cols[:, l, C_BGLU:C_BGLU + 4] = g("ssm_b_glu")[l].reshape(4, 128).T
        cols[0:64, l, C_GA:C_GA + 8] = g("attn_out_g")[l].reshape(8, 64).T
        cols[:, l, C_GA2:C_GA2 + 4] = g("attn_out_g")[l].reshape(4, 2, 64).transpose(1, 2, 0).reshape(128, 4)
    consts = np.zeros((128, 2), f)
    freqs = (np.float32(10000.0) ** (-np.arange(0, 32, 2, dtype=np.float32) / np.float32(32))).astype(f)
    consts[64:80, 0] = freqs
    consts[80:96, 0] = freqs
    consts[64:80, 1] = -SIN_SCALE
    consts[80:96, 1] = SIN_SCALE
    idx = np.arange(128) // 16
    mask16 = (idx[:, None] == idx[None, :]).astype(f)
    w_in = g("w_in")
    w_kr = np.zeros((L, D, 192), f)
    w_kr[:, :, 64:96] = w_in[:, :, 384:416]
    w_kr[:, :, 160:176] = w_in[:, :, 400:416]
    w_kr[:, :, 176:192] = w_in[:, :, 384:400]
    w_uq = g("w_uq")
    w_uqs = np.zeros_like(w_uq)
    for h in range(8):
        w_uqs[:, :, h * 96 + 64:h * 96 + 80] = w_uq[:, :, h * 96 + 80:h * 96 + 96]
        w_uqs[:, :, h * 96 + 80:h * 96 + 96] = w_uq[:, :, h * 96 + 64:h * 96 + 80]
    w_ukv = g("w_ukv").reshape(L, 128, 8, 2, 64).transpose(0, 1, 3, 2, 4).reshape(L, 128, 1024)

    def pair_layout(a):
        sh = a.shape
        a = a.reshape(L, 16, 2, 64, *sh[3:])
        a = np.moveaxis(a, 1, 3)
        return a.reshape(L, 128, 16, *sh[3:])

    lam_re = pair_layout(g("ssm_lambda_re"))
    lam_im = pair_layout(g("ssm_lambda_im"))
    logdt = pair_layout(np.repeat(g("ssm_log_dt")[:, :, None], 64, axis=2))
    s5v = np.stack([lam_re, lam_im, logdt], axis=2)
    b_re = pair_layout(g("ssm_b_re"))
    b_im = pair_layout(g("ssm_b_im"))
    s5b = np.stack([b_re, b_im], axis=2)
    c_re = pair_layout(np.swapaxes(g("ssm_c_re"), 2, 3))
    c_im = pair_layout(np.swapaxes(g("ssm_c_im"), 2, 3))
    s5c = np.stack([c_re, c_im], axis=2)
    common = dict(
        cols=cols, consts=consts, mask16=mask16, fing=g("final_norm_g").reshape(1, D).astype(f),
        w_in=w_in, w_kr=w_kr, w_uq=w_uq, w_uqs=w_uqs, w_ukv=np.ascontiguousarray(w_ukv),
        s5v=np.ascontiguousarray(s5v), s5b=np.ascontiguousarray(s5b), s5c=np.ascontiguousarray(s5c),
        w_glu=g("ssm_w_glu"), w_out=g("w_out"), w_xq=g("w_xq"), w_xkv=g("w_xkv"), w_xo=g("w_xo"),
        w_gate=g("w_gate"), w_up=g("w_up"), w_down=g("w_down"),
    )
    common = {k: np.ascontiguousarray(v, dtype=f) for k, v in common.items()}
    x = g("x")
    mem = g("mem")
    pos = g("positions").astype(np.int32)
    per_core = []
    for c in range(x.shape[0]):
        d = dict(common)
        d["x"] = np.ascontiguousarray(x[c], dtype=f)
        d["mem"] = np.ascontiguousarray(mem[c], dtype=f)
        d["pos"] = np.ascontiguousarray(pos[c].reshape(1, S))
        per_core.append(d)
    return per_core


def kernel(**inputs):
    per_core = prep_inputs(inputs)
    nc, _ = build_program()
    res = run_bass_kernel_spmd(nc, per_core, core_ids=list(range(8)))
    return np.stack([np.asarray(r["out"], dtype=np.float32) for r in res.results], axis=0)
```
